# Optimizing a Trainium2 kernel written in Bass

```python
import math
import jax
import jax.numpy as jnp
from jax import lax
import numpy as np

D_MODEL = 1024
BATCH = 8
SEQ = 4096
DEPTH = 2

BRANCH_W = D_MODEL // 2
N_BRANCH = 3
SSM_GROUP = 16
SSM_GROUPS = BRANCH_W // SSM_GROUP
SSM_STATE = 64
M_HEADS = 4
M_HEAD_DIM = BRANCH_W // M_HEADS
M_CHUNK = 64
CONV_WIDTH = 4
A_HEADS = 4
A_V_DIM = BRANCH_W // A_HEADS
A_QK_DIM = A_V_DIM // 2
Q_BLOCK = 128
REL_BUCKETS = 32
REL_MAX_DIST = 128
EPS = 1e-6
NEG_INF = -1e30

IN_SPLITS = (
    BRANCH_W, BRANCH_W,
    BRANCH_W, BRANCH_W, BRANCH_W, BRANCH_W, BRANCH_W,
    M_HEADS, M_HEADS,
    BRANCH_W, BRANCH_W, BRANCH_W, BRANCH_W,
    N_BRANCH * D_MODEL,
)
D_IN = sum(IN_SPLITS)

kernel_name = "hybrid_s5_mlstm_diffattn_gated_block"


def rms_norm(x, g):
    xf = x.astype(jnp.float32)
    y = xf * lax.rsqrt(jnp.mean(xf * xf, axis=-1, keepdims=True) + EPS)
    return (y * g.astype(jnp.float32)).astype(x.dtype)


def head_rms_norm(h, g, n_heads):
    shp = h.shape
    hf = h.astype(jnp.float32).reshape(shp[:-1] + (n_heads, -1))
    hf = hf * lax.rsqrt(jnp.mean(hf * hf, axis=-1, keepdims=True) + EPS)
    return (hf.reshape(shp) * g.astype(jnp.float32)).astype(h.dtype)


def split_cols(y):
    outs, start = [], 0
    for size in IN_SPLITS:
        outs.append(y[..., start:start + size])
        start += size
    return outs


def causal_conv(x, w, b):
    L = x.shape[1]
    xp = jnp.pad(x, ((0, 0), (CONV_WIDTH - 1, 0), (0, 0)))
    y = b
    for j in range(CONV_WIDTH):
        y = y + xp[:, j:j + L] * w[j]
    return y


def s5_mixer(u, lam_re, lam_im, b_re, b_im, c_re, c_im, d_skip, log_step, w_glu, b_glu):
    f32 = jnp.float32
    Bsz, L, W = u.shape
    uf = u.astype(f32).reshape(Bsz, L, SSM_GROUPS, SSM_GROUP)
    step = jnp.exp(log_step.astype(f32))[:, None]
    lre, lim = lam_re.astype(f32), lam_im.astype(f32)
    mag = jnp.exp(lre * step)
    abar_re, abar_im = mag * jnp.cos(lim * step), mag * jnp.sin(lim * step)
    den = lre * lre + lim * lim
    nre, nim = abar_re - 1.0, abar_im
    r_re = ((nre * lre + nim * lim) / den)[..., None]
    r_im = ((nim * lre - nre * lim) / den)[..., None]
    bre, bim = b_re.astype(f32), b_im.astype(f32)
    bbar_re = r_re * bre - r_im * bim
    bbar_im = r_re * bim + r_im * bre
    bu_re = jnp.einsum('blgc,gpc->blgp', uf, bbar_re)
    bu_im = jnp.einsum('blgc,gpc->blgp', uf, bbar_im)
    a_re = jnp.broadcast_to(abar_re, (1, L, SSM_GROUPS, SSM_STATE))
    a_im = jnp.broadcast_to(abar_im, (1, L, SSM_GROUPS, SSM_STATE))

    def combine(left, right):
        a1r, a1i, b1r, b1i = left
        a2r, a2i, b2r, b2i = right
        return (a2r * a1r - a2i * a1i,
                a2r * a1i + a2i * a1r,
                a2r * b1r - a2i * b1i + b2r,
                a2r * b1i + a2i * b1r + b2i)

    _, _, xr, xi = lax.associative_scan(combine, (a_re, a_im, bu_re, bu_im), axis=1)
    y = (jnp.einsum('blgp,gcp->blgc', xr, c_re.astype(f32))
         - jnp.einsum('blgp,gcp->blgc', xi, c_im.astype(f32)))
    y = y.reshape(Bsz, L, W) + d_skip.astype(f32) * uf.reshape(Bsz, L, W)
    g = jax.nn.gelu(y)
    out = g * jax.nn.sigmoid(g @ w_glu.astype(f32) + b_glu.astype(f32))
    return out.astype(u.dtype)


def mlstm_chunkwise(q, k, v, i_pre, f_pre):
    f32 = jnp.float32
    Bsz, L, H, Dh = q.shape
    T = M_CHUNK
    NC = L // T

    def to_chunks(a):
        return a.astype(f32).reshape(Bsz, NC, T, H, Dh).transpose(0, 3, 1, 2, 4)

    qc, kc, vc = to_chunks(q), to_chunks(k) * (Dh ** -0.5), to_chunks(v)
    ig = i_pre.astype(f32).reshape(Bsz, NC, T, H).transpose(0, 3, 1, 2)
    logf = jax.nn.log_sigmoid(f_pre.astype(f32)).reshape(Bsz, NC, T, H).transpose(0, 3, 1, 2)
    b = jnp.cumsum(logf, axis=-1)
    b_last = b[..., -1]

    a = b_last[..., None] - b + ig
    m_loc = jnp.max(a, axis=-1)
    w = jnp.exp(a - m_loc[..., None])
    C_loc = jnp.einsum('bhnt,bhntv,bhntk->bhnvk', w, vc, kc)
    n_loc = jnp.einsum('bhnt,bhntk->bhnk', w, kc)

    def step(carry, inp):
        C, n, m = carry
        bl, ml, Cl, nl = inp
        m_new = jnp.maximum(bl + m, ml)
        s_old = jnp.exp(bl + m - m_new)
        s_loc = jnp.exp(ml - m_new)
        C_new = s_old[..., None, None] * C + s_loc[..., None, None] * Cl
        n_new = s_old[..., None] * n + s_loc[..., None] * nl
        return (C_new, n_new, m_new), (C, n, m)

    init = (jnp.zeros((Bsz, H, Dh, Dh), f32), jnp.zeros((Bsz, H, Dh), f32), jnp.zeros((Bsz, H), f32))
    xs = (jnp.moveaxis(b_last, -1, 0), jnp.moveaxis(m_loc, -1, 0),
          jnp.moveaxis(C_loc, 2, 0), jnp.moveaxis(n_loc, 2, 0))
    _, (C_prev, n_prev, m_prev) = lax.scan(step, init, xs)
    C_prev = jnp.moveaxis(C_prev, 0, 2)
    n_prev = jnp.moveaxis(n_prev, 0, 2)
    m_prev = jnp.moveaxis(m_prev, 0, -1)

    causal = jnp.tril(jnp.ones((T, T), dtype=bool))
    d_mat = jnp.where(causal, b[..., :, None] - b[..., None, :] + ig[..., None, :], NEG_INF)
    m_intra = jnp.max(d_mat, axis=-1)
    m_inter = b + m_prev[..., None]
    m_t = jnp.maximum(m_inter, m_intra)
    s_mat = jnp.einsum('bhntd,bhnsd->bhnts', qc, kc) * jnp.exp(d_mat - m_t[..., None])
    inter = jnp.exp(m_inter - m_t)
    num = (jnp.einsum('bhnts,bhnsv->bhntv', s_mat, vc)
           + inter[..., None] * jnp.einsum('bhnvk,bhntk->bhntv', C_prev, qc))
    den = jnp.sum(s_mat, axis=-1) + inter * jnp.einsum('bhnk,bhntk->bhnt', n_prev, qc)
    h = num / jnp.maximum(jnp.abs(den), jnp.exp(-m_t))[..., None]
    return h.transpose(0, 2, 3, 1, 4).reshape(Bsz, L, H * Dh).astype(q.dtype)


def relative_bucket(rel):
    n = jnp.maximum(rel, 0)
    max_exact = REL_BUCKETS // 2
    large = max_exact + (jnp.log(jnp.maximum(n, 1).astype(jnp.float32) / max_exact)
                         / math.log(REL_MAX_DIST / max_exact)
                         * (REL_BUCKETS - max_exact)).astype(jnp.int32)
    large = jnp.minimum(large, REL_BUCKETS - 1)
    return jnp.where(n < max_exact, n, large)


def diff_attention(q, k, v, lam, rel_bias):
    f32 = jnp.float32
    Bsz, L, H, _, dqk = q.shape
    NB = L // Q_BLOCK
    qb = (q.astype(f32) * (dqk ** -0.5)).reshape(Bsz, NB, Q_BLOCK, H, 2, dqk).transpose(1, 0, 3, 4, 2, 5)
    kf = k.astype(f32).transpose(0, 2, 3, 1, 4)
    vf = v.astype(f32).transpose(0, 2, 1, 3)
    table = rel_bias.astype(f32)
    kpos = jnp.arange(L)

    def block(args):
        qblk, blk = args
        qpos = blk * Q_BLOCK + jnp.arange(Q_BLOCK)
        rel = qpos[:, None] - kpos[None, :]
        bias = table[relative_bucket(rel)].transpose(2, 0, 1)
        logits = jnp.einsum('bhcqd,bhckd->bhcqk', qblk, kf) + bias[None, :, None]
        logits = jnp.where(rel >= 0, logits, NEG_INF)
        p = jax.nn.softmax(logits, axis=-1)
        attn = p[:, :, 0] - lam * p[:, :, 1]
        return jnp.einsum('bhqk,bhkv->bhqv', attn, vf)

    out = lax.map(block, (qb, jnp.arange(NB)))
    return out.transpose(1, 0, 3, 2, 4).reshape(Bsz, L, H * v.shape[-1]).astype(v.dtype)


def hybrid_layer(x, layer_idx, norm_g, w_in, conv_w, conv_b, if_bias, m_norm_g,
                 lam_re, lam_im, b_re, b_im, c_re, c_im, d_skip, log_step, w_glu, b_glu,
                 diff_lam, a_norm_g, rel_bias, w_branch, w_out):
    Bsz, L, _ = x.shape
    h = rms_norm(x, norm_g)
    proj = h @ w_in
    (s_u, s_z, m_q, m_k, m_v, m_o, m_z, m_i, m_f,
     a_q, a_k, a_v, a_z, g_pre) = split_cols(proj)

    y_ssm = s5_mixer(s_u, lam_re, lam_im, b_re, b_im, c_re, c_im, d_skip, log_step, w_glu, b_glu)
    y_ssm = y_ssm * jax.nn.silu(s_z)

    qk = jax.nn.silu(causal_conv(jnp.concatenate([m_q, m_k], axis=-1), conv_w, conv_b))
    mq, mk = qk[..., :BRANCH_W], qk[..., BRANCH_W:]
    hm = mlstm_chunkwise(mq.reshape(Bsz, L, M_HEADS, M_HEAD_DIM),
                         mk.reshape(Bsz, L, M_HEADS, M_HEAD_DIM),
                         m_v.reshape(Bsz, L, M_HEADS, M_HEAD_DIM),
                         m_i + if_bias[0], m_f + if_bias[1])
    hm = jax.nn.sigmoid(m_o) * hm
    y_mlstm = head_rms_norm(hm, m_norm_g, M_HEADS) * jax.nn.silu(m_z)

    lam_init = 0.8 - 0.6 * math.exp(-0.3 * layer_idx)
    lf = diff_lam.astype(jnp.float32)
    lam = jnp.exp(jnp.sum(lf[0] * lf[1])) - jnp.exp(jnp.sum(lf[2] * lf[3])) + lam_init
    ha = diff_attention(a_q.reshape(Bsz, L, A_HEADS, 2, A_QK_DIM),
                        a_k.reshape(Bsz, L, A_HEADS, 2, A_QK_DIM),
                        a_v.reshape(Bsz, L, A_HEADS, A_V_DIM), lam, rel_bias)
    y_attn = head_rms_norm(ha, a_norm_g, A_HEADS) * (1.0 - lam_init) * jax.nn.silu(a_z)

    gates = jax.nn.sigmoid(g_pre).reshape(Bsz, L, N_BRANCH, D_MODEL)
    merged = (gates[:, :, 0] * (y_ssm @ w_branch[0])
              + gates[:, :, 1] * (y_mlstm @ w_branch[1])
              + gates[:, :, 2] * (y_attn @ w_branch[2]))
    return x + merged @ w_out


def setup_inputs(seed: int = 0) -> dict:
    key = jax.random.key(seed)
    ks = jax.random.split(key, 24)
    f32 = jnp.float32

    def nrm(k, shape, scale):
        return jax.random.normal(k, shape, f32) * scale

    G, P, W = SSM_GROUPS, SSM_STATE, BRANCH_W
    x = nrm(ks[0], (BATCH, SEQ, D_MODEL), 1.0)
    norm_g = 1.0 + nrm(ks[1], (DEPTH, D_MODEL), 0.02)
    w_in = nrm(ks[2], (DEPTH, D_MODEL, D_IN), D_MODEL ** -0.5)
    conv_w = nrm(ks[3], (DEPTH, CONV_WIDTH, 2 * W), CONV_WIDTH ** -0.5)
    conv_b = nrm(ks[4], (DEPTH, 2 * W), 0.02)
    f_bias = jnp.linspace(3.0, 6.0, M_HEADS, dtype=f32)
    if_bias = jnp.stack([nrm(ks[5], (DEPTH, M_HEADS), 0.1),
                         f_bias + nrm(ks[6], (DEPTH, M_HEADS), 0.1)], axis=1)
    m_norm_g = 1.0 + nrm(ks[7], (DEPTH, W), 0.02)
    ssm_lam_re = -0.5 + nrm(ks[8], (DEPTH, G, P), 0.01)
    ssm_lam_im = math.pi * jnp.arange(P, dtype=f32) + nrm(ks[9], (DEPTH, G, P), 0.01)
    ssm_b_re = nrm(ks[10], (DEPTH, G, P, SSM_GROUP), (2 * SSM_GROUP) ** -0.5)
    ssm_b_im = nrm(ks[11], (DEPTH, G, P, SSM_GROUP), (2 * SSM_GROUP) ** -0.5)
    ssm_c_re = nrm(ks[12], (DEPTH, G, SSM_GROUP, P), (2 * P) ** -0.5)
    ssm_c_im = nrm(ks[13], (DEPTH, G, SSM_GROUP, P), (2 * P) ** -0.5)
    ssm_d = nrm(ks[14], (DEPTH, W), 0.5)
    ssm_log_step = jax.random.uniform(ks[15], (DEPTH, G), f32, math.log(1e-3), math.log(1e-1))
    ssm_w_glu = nrm(ks[16], (DEPTH, W, W), W ** -0.5)
    ssm_b_glu = nrm(ks[17], (DEPTH, W), 0.02)
    diff_lam = nrm(ks[18], (DEPTH, 4, A_QK_DIM), 0.1)
    diff_norm_g = 1.0 + nrm(ks[19], (DEPTH, W), 0.02)
    rel_bias = nrm(ks[20], (REL_BUCKETS, A_HEADS), 0.2)
    w_branch = nrm(ks[21], (DEPTH, N_BRANCH, W, D_MODEL), W ** -0.5)
    w_out = nrm(ks[22], (DEPTH, D_MODEL, D_MODEL), D_MODEL ** -0.5)
    final_g = 1.0 + nrm(ks[23], (D_MODEL,), 0.02)
    return {"x": x, "norm_g": norm_g, "w_in": w_in, "conv_w": conv_w, "conv_b": conv_b,
            "if_bias": if_bias, "m_norm_g": m_norm_g, "ssm_lam_re": ssm_lam_re,
            "ssm_lam_im": ssm_lam_im, "ssm_b_re": ssm_b_re, "ssm_b_im": ssm_b_im,
            "ssm_c_re": ssm_c_re, "ssm_c_im": ssm_c_im, "ssm_d": ssm_d,
            "ssm_log_step": ssm_log_step, "ssm_w_glu": ssm_w_glu, "ssm_b_glu": ssm_b_glu,
            "diff_lam": diff_lam, "diff_norm_g": diff_norm_g, "rel_bias": rel_bias,
            "w_branch": w_branch, "w_out": w_out, "final_g": final_g}


def reference(x, norm_g, w_in, conv_w, conv_b, if_bias, m_norm_g, ssm_lam_re, ssm_lam_im,
              ssm_b_re, ssm_b_im, ssm_c_re, ssm_c_im, ssm_d, ssm_log_step, ssm_w_glu,
              ssm_b_glu, diff_lam, diff_norm_g, rel_bias, w_branch, w_out, final_g):
    for l in range(DEPTH):
        x = hybrid_layer(x, l, norm_g[l], w_in[l], conv_w[l], conv_b[l], if_bias[l], m_norm_g[l],
                         ssm_lam_re[l], ssm_lam_im[l], ssm_b_re[l], ssm_b_im[l],
                         ssm_c_re[l], ssm_c_im[l], ssm_d[l], ssm_log_step[l],
                         ssm_w_glu[l], ssm_b_glu[l], diff_lam[l], diff_norm_g[l],
                         rel_bias, w_branch[l], w_out[l])
    return rms_norm(x, final_g)
```

```python
import math
from contextlib import ExitStack
import numpy as np
import concourse.bass as bass
import concourse.mybir as mybir
from concourse.bass_utils import run_bass_kernel_spmd

F32 = mybir.dt.float32
BF16 = mybir.dt.bfloat16
I32 = mybir.dt.int32
ALU = mybir.AluOpType
AF = mybir.ActivationFunctionType
AX = mybir.AxisListType

D_MODEL = 1024
SEQ = 4096
DEPTH = 2
W = 512
D_IN = 8712
NCORES = 8
EPS = 1e-6


class Buf:
    __slots__ = ("name", "w", "r")

    def __init__(self, name=""):
        self.name = name
        self.w = None
        self.r = {}


class Sched:
    ENGS = (("pe", "tensor"), ("dve", "vector"), ("act", "scalar"), ("pool", "gpsimd"), ("sp", "sync"))
    NSLOT = 8

    def __init__(self, nc):
        self.nc = nc
        self.ops = {e: [] for e, _ in self.ENGS}
        self.cnt = {}
        self.seen = {e: {} for e, _ in self.ENGS}
        self.dma_i = {e: 0 for e, _ in self.ENGS}

    def _waits(self, eng, r, w, extra=None):
        deps = dict(extra or {})

        def add(k, v):
            if v > deps.get(k, 0):
                deps[k] = v

        for b in r:
            if b.w is not None:
                add(*b.w)
        for b in w:
            if b.w is not None and b.w[0] != eng:
                add(*b.w)
            for k, v in b.r.items():
                if k != eng:
                    add(k, v)
        waits = []
        seen = self.seen[eng]
        for k, v in deps.items():
            if k == "pe" and eng == "pe":
                continue
            if seen.get(k, 0) >= v:
                continue
            seen[k] = v
            waits.append((k, v))
        return waits

    def op(self, eng, fn, r=(), w=(), inc=True):
        waits = self._waits(eng, r, w)
        n = self.cnt.get(eng, 0) + 1
        if inc:
            self.cnt[eng] = n
        self.ops[eng].append((waits, fn, (eng, 1, n) if inc else None))
        for b in r:
            if b.r.get(eng, 0) < n:
                b.r[eng] = n
        for b in w:
            b.w = (eng, n)
            b.r = {}

    def dma(self, q, out, in_, r=(), w=(), **kw):
        i = self.dma_i[q]
        self.dma_i[q] += 1
        key = ("dma", q, i % self.NSLOT)
        n = self.cnt.get(key, 0) + 16
        extra = {key: n - 16} if n > 16 else None
        waits = self._waits(q, r, w, extra)
        self.cnt[key] = n
        self.ops[q].append((waits, lambda e: e.dma_start(out=out, in_=in_, **kw), (key, 16, n)))
        for b in r:
            b.r[key] = n
        for b in w:
            b.w = (key, n)
            b.r = {}

    def barrier(self):
        snap = dict(self.cnt)
        for eng, _ in self.ENGS:
            waits = [(k, v) for k, v in snap.items() if self.seen[eng].get(k, 0) < v and k != eng]
            for k, v in waits:
                self.seen[eng][k] = v
            if waits:
                self.ops[eng].append((waits, None, None))

    def finish(self, q="sp"):
        waits = [(k, v) for k, v in self.cnt.items() if self.seen[q].get(k, 0) < v]
        self.ops[q].append((waits, None, None))

    def emit(self):
        nc = self.nc
        targets = {}
        for eng, _ in self.ENGS:
            for waits, fn, inc in self.ops[eng]:
                for wk, wv in waits:
                    targets.setdefault(wk, set()).add(wv)
        rank = {}
        for key, vs in targets.items():
            step = 16 if isinstance(key, tuple) else 1
            rank[key] = {v: (i + 1) * step for i, v in enumerate(sorted(vs))}
        with ExitStack() as st:
            sems = {}
            for i, key in enumerate(self.cnt):
                sems[key] = st.enter_context(nc.semaphore("s%d" % i))
            block = st.enter_context(nc.Block())
            for eng, attr in self.ENGS:
                ops = self.ops[eng]
                if not ops:
                    continue

                def body(e, ops=ops):
                    for waits, fn, inc in ops:
                        for wk, wv in waits:
                            e.wait_ge(sems[wk], rank[wk][wv])
                        if fn is not None:
                            ins = fn(e)
                            if inc is not None and inc[2] in targets.get(inc[0], ()):
                                ins.then_inc(sems[inc[0]], inc[1])

                getattr(block, attr)(body)


class T:
    def __init__(self, t, name):
        self.t = t
        self.b = Buf(name)

    def __getitem__(self, k):
        return self.t[k]


class K:
    def __init__(self, L, depth, dbg=False):
        self.L = L
        self.depth = depth
        self.dbg = dbg
        self.nc = bass.Bass("TRN2", target_bir_lowering=False)
        self.S = Sched(self.nc)
        self.dram = {}
        self.dbuf = {}
        self.nps = 0
        self.sb_off = 16640
        self.sb_n = 0
        self.sb_mark = 0

    def sb(self, name, shape, dt=F32):
        nbytes = int(np.prod(shape[1:])) * (2 if dt == BF16 else 4)
        off = (self.sb_off + 63) // 64 * 64
        self.sb_off = off + nbytes
        assert self.sb_off <= 16640 + 206 * 1024, ("SBUF overflow", name, self.sb_off)
        self.sb_n += 1
        nm = "%s_%d" % (name, self.sb_n)
        return T(self.nc.alloc_sbuf_tensor_at(nm, list(shape), dt, offset=off), nm)

    def phase_mark(self):
        self.sb_mark = self.sb_off

    def phase_reset(self, keep_hT=True):
        self.S.barrier()
        self.sb_off = self.sb_mark if keep_hT else self.sb_mark0

    def ps(self, name, shape=(128, 512), dt=F32):
        return T(self.nc.alloc_psum_tensor(name, list(shape), dt), name)

    def din(self, name, shape, dt=F32):
        t = self.nc.dram_tensor(name, list(shape), dt, kind="ExternalInput")
        self.dram[name] = t
        self.dbuf[name] = Buf(name)
        return t

    def dscratch(self, name, shape, dt=F32, out=False):
        kind = "ExternalOutput" if (out or self.dbg) else "Internal"
        t = self.nc.dram_tensor(name, list(shape), dt, kind=kind)
        self.dram[name] = t
        self.dbuf[name] = Buf(name)
        return t

    def op(self, eng, fn, r=(), w=(), inc=True):
        self.S.op(eng, fn, [x.b if isinstance(x, T) else x for x in r],
                  [x.b if isinstance(x, T) else x for x in w], inc)

    def dma(self, q, out, in_, r=(), w=(), **kw):
        self.S.dma(q, out, in_, [x.b if isinstance(x, T) else x for x in r],
                   [x.b if isinstance(x, T) else x for x in w], **kw)

    def mm(self, out, lhsT, rhs, start, stop, r=(), w=(), inc=None):
        if inc is None:
            inc = stop
        self.op("pe", lambda e: e.matmul(out, lhsT, rhs, start=start, stop=stop), r=r, w=w, inc=inc)

    def act(self, out, in_, func, r=(), w=(), bias=None, scale=1.0, accum=None, eng="act"):
        kw = {}
        if bias is not None:
            kw["bias"] = bias
        if accum is not None:
            kw["accum_out"] = accum
        self.op(eng, lambda e: e.activation(out=out, in_=in_, func=func, scale=scale, **kw), r=r, w=w)

    def tt(self, out, in0, in1, op, r=(), w=(), eng="dve"):
        self.op(eng, lambda e: e.tensor_tensor(out=out, in0=in0, in1=in1, op=op), r=r, w=w)

    def ts(self, out, in0, s1, s2, op0, op1=None, r=(), w=(), eng="dve", accum=None):
        kw = {}
        if op1 is not None:
            kw["op1"] = op1
        if accum is not None:
            kw["accum_out"] = accum
        self.op(eng, lambda e: e.tensor_scalar(out=out, in0=in0, scalar1=s1, scalar2=s2, op0=op0, **kw), r=r, w=w)

    def stt(self, out, in0, scalar, in1, op0, op1, r=(), w=(), eng="dve"):
        self.op(eng, lambda e: e.scalar_tensor_tensor(out=out, in0=in0, scalar=scalar, in1=in1, op0=op0, op1=op1),
                r=r, w=w)

    def copy(self, out, in_, r=(), w=(), eng="dve"):
        self.op(eng, lambda e: e.tensor_copy(out=out, in_=in_), r=r, w=w)

    def memset(self, ap, val, w=(), eng="pool"):
        self.op(eng, lambda e: e.memset(ap, val), w=w)

    def scan(self, out, d0, d1, init, op0, op1, r=(), w=(), eng="dve"):
        self.op(eng, lambda e: e.tensor_tensor_scan(out=out, data0=d0, data1=d1, initial=init, op0=op0, op1=op1),
                r=r, w=w)

    def reduce(self, out, in_, op, axis, r=(), w=(), eng="dve"):
        self.op(eng, lambda e: e.tensor_reduce(out=out, in_=in_, axis=axis, op=op), r=r, w=w)

    def recip(self, out, in_, r=(), w=(), eng="dve"):
        self.op(eng, lambda e: e.reciprocal(out=out, in_=in_), r=r, w=w)

    def tr(self, out, in_, ident, r=(), w=(), inc=True):
        self.op("pe", lambda e: e.transpose(out, in_, ident), r=r, w=w, inc=inc)


def host_consts():
    c = {}
    c["ident"] = np.eye(128, dtype=np.float32)
    c["iota1"] = np.tile(np.arange(1, 513, dtype=np.float32)[None, :], (128, 1))
    s_ = np.arange(128)
    c["causal01"] = (s_[:, None] <= s_[None, :]).astype(np.float32)
    c["chmask"] = (s_[:, None] // 16 == np.arange(8)[None, :]).astype(np.float32)
    c["hmask"] = (s_[:, None] // 64 == np.arange(2)[None, :]).astype(np.float32)
    sel8 = np.zeros((8, 8), np.float32)
    for h in range(4):
        sel8[h, h] = 1.0
        sel8[4 + h, h] = -1.0
        sel8[4 + h, 4 + h] = 1.0
    c["sel8"] = sel8
    rm = np.zeros((8, 2), np.float32)
    rm[0:4, 0] = 1.0
    rm[4:8, 1] = 1.0
    c["rowmask8"] = rm
    sr4 = np.zeros((4, 4 * 128), np.float32)
    for h in range(4):
        sr4[h, h * 128:(h + 1) * 128] = 1.0
    c["selrow4"] = sr4
    bo = np.zeros((128, 2), np.float32)
    bo[0:64, 0] = 1.0
    bo[64:128, 1] = 1.0
    c["blockones"] = bo
    e = np.arange(1152)
    d = e - 511
    n = np.maximum(d, 0)
    large = 16 + (np.log(np.maximum(n, 1).astype(np.float32) / np.float32(16)) / np.float32(math.log(128 / 16))
                  * np.float32(16)).astype(np.int32)
    large = np.minimum(large, 31)
    bucket = np.where(n < 16, n, large)
    oh = np.zeros((32, 1152), np.float32)
    oh[bucket, e] = 1.0
    oh[:, d < 0] = 0.0
    c["oh"] = oh
    c["maskvec"] = np.where(d < 0, -30000.0, 0.0).astype(np.float32)[None, :]
    c["ones_row"] = np.ones((1, 128), np.float32)
    c["antiI"] = np.ascontiguousarray(np.eye(128, dtype=np.float32)[::-1])
    return c


CONST_SHAPES = {"ident": [128, 128], "iota1": [128, 512], "causal01": [128, 128], "chmask": [128, 8],
                "hmask": [128, 2], "sel8": [8, 8], "rowmask8": [8, 2], "selrow4": [4, 512],
                "blockones": [128, 2], "oh": [32, 1152], "maskvec": [1, 1152], "ones_row": [1, 128], "antiI": [128, 128]}

PARAM_SHAPES = lambda depth: {
    "norm_g": [depth, 1024], "w_in": [depth, 1024, D_IN], "conv_w": [depth, 4, 1024], "conv_b": [depth, 1024],
    "if_bias": [depth, 2, 4], "m_norm_g": [depth, 512], "ssm_lam_re": [depth, 32, 64],
    "ssm_lam_im": [depth, 32, 64], "ssm_b_re": [depth, 32, 64, 16], "ssm_b_im": [depth, 32, 64, 16],
    "ssm_c_re": [depth, 32, 16, 64], "ssm_c_im": [depth, 32, 16, 64], "ssm_d": [depth, 512],
    "ssm_log_step": [depth, 32], "ssm_w_glu": [depth, 512, 512], "ssm_b_glu": [depth, 512],
    "diff_lam": [depth, 4, 64], "diff_norm_g": [depth, 512], "rel_bias": [32, 4],
    "w_branch": [depth, 3, 512, 1024], "w_out": [depth, 1024, 1024], "final_g": [1024]}

PI = math.pi


def build(L=SEQ, depth=DEPTH, dbg=False, phases=("proj", "s5", "mlstm", "attn", "merge")):
    k = K(L, depth, dbg)
    nc = k.nc
    NT = L // 128
    NB = L // 512
    x_in = k.din("x", [L, D_MODEL])
    P_ = {n: k.din(n, shp) for n, shp in PARAM_SHAPES(depth).items()}
    C_ = {n: k.din(n, shp) for n, shp in CONST_SHAPES.items()}
    norm_g, w_in, conv_w, conv_b = P_["norm_g"], P_["w_in"], P_["conv_w"], P_["conv_b"]
    out_d = k.dscratch("out", [L, D_MODEL], F32, out=True)
    xs_d = k.dscratch("xs", [L, D_MODEL], F32)

    suT = k.dscratch("suT", [W, L], BF16)
    szT = k.dscratch("szT", [W, L], BF16)
    mqT = k.dscratch("mqT", [W, L], BF16)
    mkT = k.dscratch("mkT", [W, L], BF16)
    mv = k.dscratch("mv", [L, W], BF16)
    mo = k.dscratch("mo", [L, W], BF16)
    mz = k.dscratch("mz", [L, W], BF16)
    aqT = k.dscratch("aqT", [W, L], BF16)
    akT = k.dscratch("akT", [W, L], BF16)
    av = k.dscratch("av", [L, W], BF16)
    azT = k.dscratch("azT", [W, L], BF16)
    gT = k.dscratch("gT", [3 * D_MODEL, L], BF16)
    mifd = k.dscratch("mif", [8, L], F32)
    ysT = k.dscratch("ysT", [W, L], BF16)
    ymT = k.dscratch("ymT", [W, L], BF16)
    yaT = k.dscratch("yaT", [W, L], BF16)
    Gd = k.dscratch("Gd", [4, 1152], F32)

    def cload(name, dt=F32, parts=None):
        shp = CONST_SHAPES[name]
        t = k.sb(name, shp, F32)
        k.dma("sp", t[:], C_[name].ap(), w=[t])
        if dt == BF16:
            tb = k.sb(name + "_b", shp, BF16)
            k.copy(tb[:], t[:], r=[t], w=[tb])
            return t, tb
        return t

    ident_f, ident_b = cload("ident", BF16)
    causal01 = cload("causal01")
    chmask = cload("chmask")
    hmask = cload("hmask")
    sel8 = cload("sel8")
    rowmask8 = cload("rowmask8")
    selrow4 = cload("selrow4")
    blockones_f, blockones_b = cload("blockones", BF16)
    ones_row = cload("ones_row")
    gbc = k.sb("gbc", [128, D_MODEL], F32)
    wf = k.sb("wf", [128, NT, 8])
    dbc = k.sb("dbc", [128, 4, NT])
    consts = k.sb("consts", [128, 8], F32)
    CV = [EPS, -PI, 0.0, 1.0, 0.5 * PI]
    for i, v in enumerate(CV):
        k.memset(consts[:, i:i + 1], v, w=[consts])
    epsb = consts[:, 0:1]
    negpi = consts[:, 1:2]
    k.sb_mark0 = k.sb_off
    hT = k.sb("hT", [128, 8, L], BF16)
    psb = [k.ps("ps%d" % i) for i in range(8)]

    def next_ps():
        p = psb[k.nps % 8]
        k.nps += 1
        return p

    def rows_to_cols(src, R, nchunk, dst_ap, dstT):
        p = next_ps()
        for c in range(nchunk):
            k.mm(p[:, c * R:(c + 1) * R], src[0:R, c * 128:(c + 1) * 128], ident_f[0:R, 0:R],
                 start=True, stop=True, r=[src, ident_f], w=[p], inc=(c == nchunk - 1))
        k.copy(dst_ap, p[:, 0:nchunk * R].rearrange("p (c r) -> p c r", r=R), r=[p], w=[dstT])

    nrm = {}

    def norm_alloc():
        nrm["hb"] = [k.sb("hb%d" % i, [128, D_MODEL], BF16) for i in range(2)]
        nrm["sq"] = k.sb("sq", [128, D_MODEL], F32)
        nrm["ss"] = [k.sb("ss%d" % i, [128, 1], F32) for i in range(2)]
        nrm["rstd"] = [k.sb("rstd%d" % i, [128, 1], F32) for i in range(2)]
        nrm["i"] = 0

    def norm_rstd(xtile):
        i = nrm["i"]
        nrm["i"] += 1
        s_, r_ = nrm["ss"][i % 2], nrm["rstd"][i % 2]
        sq = nrm["sq"]
        k.act(sq[:], xtile[:], AF.Square, r=[xtile], w=[sq, s_], accum=s_[:])
        k.act(r_[:], s_[:], AF.Sqrt, r=[s_, consts], w=[r_], scale=1.0 / D_MODEL, bias=epsb)
        k.recip(r_[:], r_[:], r=[r_], w=[r_])
        return r_, i

    def norm_transpose(xtile, tok0):
        r_, i = norm_rstd(xtile)
        h_ = nrm["hb"][i % 2]
        k.stt(h_[:], xtile[:], r_[:, 0:1], gbc[:], ALU.mult, ALU.mult, r=[xtile, r_, gbc], w=[h_])
        p = next_ps()
        pv = p.t.bitcast(BF16)
        for c in range(8):
            k.tr(pv[:, c * 128:(c + 1) * 128], h_[:, c * 128:(c + 1) * 128], ident_b[:], r=[h_, ident_b], w=[p],
                 inc=(c == 7))
        k.act(hT[:, :, tok0:tok0 + 128], pv[:, :].rearrange("p (c t) -> p c t", c=8), AF.Copy, r=[p], w=[hT])

    def headnorm_out(ht, gtile, gate, stage, col0, ws):
        ssq, sqs, yb = ws["ssq"], ws["sqs"], ws["yb"]
        for h in range(4):
            k.act(sqs[:, :], ht[:, h * 128:(h + 1) * 128], AF.Square, r=[ht], w=[sqs, ssq], accum=ssq[:, h:h + 1])
        k.act(ssq[:, :], ssq[:, :], AF.Sqrt, r=[ssq, consts], w=[ssq], scale=1.0 / 128, bias=epsb)
        k.recip(ssq[:, :], ssq[:, :], r=[ssq], w=[ssq])
        for h in range(4):
            k.stt(ht[:, h * 128:(h + 1) * 128], ht[:, h * 128:(h + 1) * 128], ssq[:, h:h + 1],
                  gtile[:, h * 128:(h + 1) * 128], ALU.mult, ALU.mult, r=[ht, ssq, gtile], w=[ht])
        k.tt(yb[:, :], ht[:, :], gate[:, :], ALU.mult, r=[ht, gate], w=[yb])
        p = next_ps()
        pv = p.t.bitcast(BF16)
        for h in range(4):
            k.tr(pv[:, h * 128:(h + 1) * 128], yb[:, h * 128:(h + 1) * 128], ident_b[:], r=[yb, ident_b], w=[p],
                 inc=(h == 3))
        k.act(stage[:, :, col0:col0 + 128], pv[:, 0:512].rearrange("p (c t) -> p c t", c=4), AF.Copy,
              r=[p], w=[stage])

    k.phase_mark()
    if "attn" in phases:
        tab = k.sb("tab", [32, 4], F32)
        oh = k.sb("oh", [32, 1152], F32)
        mvec = k.sb("mvec", [1, 1152], F32)
        Gs = k.sb("Gs", [4, 1152], F32)
        k.dma("sp", tab[:], P_["rel_bias"].ap(), w=[tab])
        k.dma("sp", oh[:], C_["oh"].ap(), w=[oh])
        k.dma("sp", mvec[:], C_["maskvec"].ap(), w=[mvec])
        for c0 in range(0, 1152, 384):
            p = next_ps()
            k.mm(p[0:4, 0:384], tab[:, :], oh[:, c0:c0 + 384], start=True, stop=False, r=[tab, oh], w=[p])
            k.mm(p[0:4, 0:384], ones_row[0:1, 0:4], mvec[0:1, c0:c0 + 384], start=False, stop=True,
                 r=[ones_row, mvec], w=[p])
            k.copy(Gs[:, c0:c0 + 384], p[0:4, 0:384], r=[p], w=[Gs])
        k.dma("sp", Gd.ap(), Gs[:], r=[Gs], w=[k.dbuf["Gd"]])
    k.phase_reset()

    def phase0():
        norm_alloc()
        xt = [k.sb("xt%d" % i, [128, D_MODEL], F32) for i in range(2)]
        k.dma("sp", gbc[:], norm_g.ap()[0:1, :].partition_broadcast(128), w=[gbc])
        for i in range(NT):
            xt_ = xt[i % 2]
            k.dma("sp", xt_[:], x_in.ap()[i * 128:(i + 1) * 128, :], w=[xt_])
            norm_transpose(xt_, i * 128)

    phase0()
    k.phase_reset()

    def phase_proj(l):
        wst = [k.sb("wst%d" % i, [128, 8, 512], F32) for i in range(2)]
        wbf = [k.sb("wbf%d" % i, [128, 8, 512], BF16) for i in range(2)]
        stT = [k.sb("stT%d" % i, [128, L], BF16) for i in range(2)]
        stN = [k.sb("stN%d" % i, [128, 512], BF16) for i in range(6)]
        cbuf = k.sb("cbuf", [128, L + 8], F32)
        cacc = k.sb("cacc", [128, L], F32)
        cw = k.sb("cw", [128, 8, 5], F32)
        crow = k.sb("crow", [5, 2 * W], F32)
        st = {"wi": 0, "ti": 0, "ni": 0, "ev": 0}
        mif = k.sb("mif_sb", [8, L], F32)

        def load_w(col0, ncols=512):
            i = st["wi"]
            st["wi"] += 1
            ws, wb = wst[i % 2], wbf[i % 2]
            k.dma("pool", ws[:, :, 0:ncols],
                  w_in.ap()[l, :, col0:col0 + ncols].rearrange("(kc p) c -> p kc c", p=128), w=[ws])
            k.copy(wb[:, :, 0:ncols], ws[:, :, 0:ncols], r=[ws], w=[wb], eng="pool")
            return wb

        def evac(out_ap, in_ap, func, r, w, scale=1.0):
            if func is None:
                st["ev"] += 1
                if st["ev"] % 2 == 0:
                    if scale == 1.0:
                        k.copy(out_ap, in_ap, r=r, w=w)
                    else:
                        k.ts(out_ap, in_ap, scale, None, ALU.mult, r=r, w=w)
                    return
                func = AF.Copy
            k.act(out_ap, in_ap, func, r=r, w=w, scale=scale)

        def proj_T_block(wb, col0, dst, row0, func, scale=1.0, conv=None):
            for c in range(4):
                if conv is None:
                    stg = stT[st["ti"] % 2]
                    st["ti"] += 1
                for tb in range(NB):
                    p = next_ps()
                    for kc in range(8):
                        k.mm(p[:, :], wb[:, kc, c * 128:(c + 1) * 128], hT[:, kc, tb * 512:(tb + 1) * 512],
                             start=(kc == 0), stop=(kc == 7), r=[wb, hT], w=[p])
                    if conv is None:
                        evac(stg[:, tb * 512:(tb + 1) * 512], p[:, :], func, r=[p], w=[stg], scale=scale)
                    else:
                        k.act(cbuf[:, 8 + tb * 512:8 + (tb + 1) * 512], p[:, :], AF.Copy, r=[p], w=[cbuf])
                if conv is not None:
                    cc = conv * 4 + c
                    stg = stT[st["ti"] % 2]
                    st["ti"] += 1
                    k.ts(cacc[:], cbuf[:, 8:8 + L], cw[:, cc, 3:4], cw[:, cc, 4:5], ALU.mult, ALU.add,
                         r=[cbuf, cw], w=[cacc])
                    for j in range(3):
                        k.stt(cacc[:], cbuf[:, 5 + j:5 + j + L], cw[:, cc, j:j + 1], cacc[:], ALU.mult, ALU.add,
                              r=[cbuf, cw, cacc], w=[cacc])
                    k.act(stg[:], cacc[:], AF.Silu, r=[cacc], w=[stg])
                k.dma("sp", dst.ap()[row0 + c * 128:row0 + (c + 1) * 128, :], stg[:], r=[stg], w=[k.dbuf[dst.name]])

        def proj_N_block(wb, col0, dst, func):
            for t in range(NT):
                p = next_ps()
                stg = stN[st["ni"] % 6]
                st["ni"] += 1
                for kc in range(8):
                    k.mm(p[:, :], hT[:, kc, t * 128:(t + 1) * 128], wb[:, kc, :],
                         start=(kc == 0), stop=(kc == 7), r=[wb, hT], w=[p])
                evac(stg[:], p[:, :], func, r=[p], w=[stg])
                k.dma("sp", dst.ap()[t * 128:(t + 1) * 128, :], stg[:], r=[stg], w=[k.dbuf[dst.name]])

        def proj_if(wb, col0=3584):
            for tb in range(NB):
                p = next_ps()
                for kc in range(8):
                    k.mm(p[0:8, :], wb[:, kc, 0:8], hT[:, kc, tb * 512:(tb + 1) * 512],
                         start=(kc == 0), stop=(kc == 7), r=[wb, hT], w=[p])
                k.act(mif[:, tb * 512:(tb + 1) * 512], p[0:8, :], AF.Copy, r=[p], w=[mif])

        k.memset(cbuf[:, 0:8], 0.0, w=[cbuf])
        k.dma("sp", crow[0:4, :], conv_w.ap()[l], w=[crow])
        k.dma("sp", crow[4:5, :], conv_b.ap()[l:l + 1, :], w=[crow])
        rows_to_cols(crow, 5, 8, cw[:, :, :], cw)
        specs = [(proj_if, 3584, 8, ()),
                 (proj_T_block, 0, 512, (suT, 0, None)),
                 (proj_T_block, 512, 512, (szT, 0, AF.Silu)),
                 (proj_T_block, 1024, 512, (mqT, 0, None, 1.0, 0)),
                 (proj_T_block, 1536, 512, (mkT, 0, None, 1.0, 1)),
                 (proj_N_block, 2048, 512, (mv, None)),
                 (proj_N_block, 2560, 512, (mo, AF.Sigmoid)),
                 (proj_N_block, 3072, 512, (mz, AF.Silu)),
                 (proj_T_block, 3592, 512, (aqT, 0, None, 0.125)),
                 (proj_T_block, 4104, 512, (akT, 0, None)),
                 (proj_N_block, 4616, 512, (av, None)),
                 (proj_T_block, 5128, 512, (azT, 0, AF.Silu))]
        for gb in range(6):
            specs.append((proj_T_block, 5640 + gb * 512, 512, (gT, gb * 512, AF.Sigmoid)))
        wbl = [None] * len(specs)
        wbl[0] = load_w(specs[0][1], specs[0][2])
        for i, (fn, col0, ncols, args) in enumerate(specs):
            if i + 1 < len(specs):
                wbl[i + 1] = load_w(specs[i + 1][1], specs[i + 1][2])
            fn(wbl[i], col0, *args)
            if i == 0:
                k.dma("sp", mifd.ap(), mif[:], r=[mif], w=[k.dbuf["mif"]])

    def sincos(ph_ap, shape, sin_out, cos_out, u, r, wT):
        ti = u["ti"]
        tf = u["tf"]
        uu = u["u"]
        for which, out_ap in (("sin", sin_out), ("cos", cos_out)):
            if which == "sin":
                k.ts(ti[:], ph_ap, 1.0 / (2.0 * PI), None, ALU.mult, r=r, w=[ti])
            else:
                k.ts(ti[:], ph_ap, 1.0 / (2.0 * PI), 0.25, ALU.mult, ALU.add, r=r, w=[ti])
            k.copy(tf[:], ti[:], r=[ti], w=[tf])
            k.stt(uu[:], tf[:], -2.0 * PI, ph_ap, ALU.mult, ALU.add, r=[tf] + list(r), w=[uu])
            if which == "sin":
                k.ts(uu[:], uu[:], -PI + 1e-5, PI - 1e-5, ALU.max, ALU.min, r=[uu], w=[uu])
                k.act(out_ap, uu[:], AF.Sin, r=[uu], w=wT)
            else:
                k.ts(uu[:], uu[:], -1.5 * PI + 1e-5, 0.5 * PI - 1e-5, ALU.max, ALU.min, r=[uu], w=[uu])
                k.act(out_ap, uu[:], AF.Sin, r=[uu, consts], w=wT, bias=consts[:, 4:5])

    def phase_s5(l):
        pre_ = {}
        for nm_, shp_, dt_ in (("mag", [128, 16], F32), ("th", [128, 16], F32), ("BTr", [128, 16, 128], BF16),
                               ("BTi", [128, 16, 128], BF16), ("CTr", [128, 16, 128], BF16),
                               ("CTi", [128, 16, 128], BF16), ("CTn", [128, 16, 128], BF16), ("Dg", [128, 4, 128], BF16), ("dcol", [128, 4, 2], F32),
                               ("wg", [128, 4, 512], BF16), ("cosT", [128, 16, 512], F32),
                               ("sinT", [128, 16, 512], F32)):
            pre_[nm_] = k.sb(nm_, shp_, dt_)
        mark_s5 = k.sb_off
        f = lambda name, shp, dt=F32: pre_[name] if name in pre_ else k.sb(name, shp, dt)
        lre, lim, lst = f("lre", [128, 16]), f("lim", [128, 16]), f("lst", [128, 16])
        mag, th, are, aim = f("mag", [128, 16]), f("th", [128, 16]), f("are", [128, 16]), f("aim", [128, 16])
        den, rre, rim, t16 = (f("den", [128, 16]), f("rre", [128, 16]),
                              f("rim", [128, 16]), f("t16", [128, 16]))
        u16 = {"ti": f("u16i", [128, 16], I32), "tf": f("u16f", [128, 16]), "u": f("u16u", [128, 16])}
        q = "sp"
        k.dma(q, lre[:], P_["ssm_lam_re"].ap()[l].rearrange("(s g) p -> (g p) s", g=2), w=[lre],
              allow_slow_non_contiguous=True)
        k.dma(q, lim[:], P_["ssm_lam_im"].ap()[l].rearrange("(s g) p -> (g p) s", g=2), w=[lim],
              allow_slow_non_contiguous=True)
        for h in range(2):
            k.dma(q, lst[h * 64:(h + 1) * 64, :],
                  P_["ssm_log_step"].ap()[l].rearrange("(s g) -> g s", g=2)[h:h + 1, :].partition_broadcast(64),
                  w=[lst], allow_slow_non_contiguous=True)
        k.act(lst[:], lst[:], AF.Exp, r=[lst], w=[lst])
        k.tt(mag[:], lre[:], lst[:], ALU.mult, r=[lre, lst], w=[mag])
        k.act(mag[:], mag[:], AF.Exp, r=[mag], w=[mag])
        k.tt(th[:], lim[:], lst[:], ALU.mult, r=[lim, lst], w=[th])
        sincos(th[:], [128, 16], aim[:], are[:], u16, [th], [aim, are])
        k.tt(are[:], are[:], mag[:], ALU.mult, r=[are, mag], w=[are])
        k.tt(aim[:], aim[:], mag[:], ALU.mult, r=[aim, mag], w=[aim])
        k.tt(den[:], lre[:], lre[:], ALU.mult, r=[lre], w=[den])
        k.tt(t16[:], lim[:], lim[:], ALU.mult, r=[lim], w=[t16])
        k.tt(den[:], den[:], t16[:], ALU.add, r=[den, t16], w=[den])
        k.recip(den[:], den[:], r=[den], w=[den])
        nre = f("nre", [128, 16])
        k.ts(nre[:], are[:], -1.0, None, ALU.add, r=[are], w=[nre])
        k.tt(rre[:], nre[:], lre[:], ALU.mult, r=[nre, lre], w=[rre])
        k.tt(t16[:], aim[:], lim[:], ALU.mult, r=[aim, lim], w=[t16])
        k.tt(rre[:], rre[:], t16[:], ALU.add, r=[rre, t16], w=[rre])
        k.tt(rre[:], rre[:], den[:], ALU.mult, r=[rre, den], w=[rre])
        k.tt(rim[:], aim[:], lre[:], ALU.mult, r=[aim, lre], w=[rim])
        k.tt(t16[:], nre[:], lim[:], ALU.mult, r=[nre, lim], w=[t16])
        k.tt(rim[:], rim[:], t16[:], ALU.subtract, r=[rim, t16], w=[rim])
        k.tt(rim[:], rim[:], den[:], ALU.mult, r=[rim, den], w=[rim])
        bre, bim = f("bre", [128, 16, 16]), f("bim", [128, 16, 16])
        bbr, bbi, tb_ = f("bbr", [128, 16, 16]), f("bbi", [128, 16, 16]), f("tb_", [128, 16, 16])
        k.dma(q, bre[:], P_["ssm_b_re"].ap()[l].rearrange("(s g) p c -> (g p) s c", g=2), w=[bre])
        k.dma(q, bim[:], P_["ssm_b_im"].ap()[l].rearrange("(s g) p c -> (g p) s c", g=2), w=[bim])
        rre_b = rre[:, :].unsqueeze(2).to_broadcast([128, 16, 16])
        rim_b = rim[:, :].unsqueeze(2).to_broadcast([128, 16, 16])
        k.tt(bbr[:], bre[:], rre_b, ALU.mult, r=[bre, rre], w=[bbr])
        k.tt(tb_[:], bim[:], rim_b, ALU.mult, r=[bim, rim], w=[tb_])
        k.tt(bbr[:], bbr[:], tb_[:], ALU.subtract, r=[bbr, tb_], w=[bbr])
        k.tt(bbi[:], bim[:], rre_b, ALU.mult, r=[bim, rre], w=[bbi])
        k.tt(tb_[:], bre[:], rim_b, ALU.mult, r=[bre, rim], w=[tb_])
        k.tt(bbi[:], bbi[:], tb_[:], ALU.add, r=[bbi, tb_], w=[bbi])
        BTr, BTi = f("BTr", [128, 16, 128], BF16), f("BTi", [128, 16, 128], BF16)
        bfull = [f("bfull%d" % i, [128, 128], BF16) for i in range(4)]
        for s in range(16):
            for ri, (src, dstT) in enumerate(((bbr, BTr), (bbi, BTi))):
                bf_ = bfull[(s * 3 + ri) % 4]
                k.memset(bf_[:], 0.0, w=[bf_], eng="dve")
                for h in range(2):
                    c0 = 32 * (s % 4) + 16 * h
                    k.ts(bf_[:, c0:c0 + 16], src[:, s, :], hmask[:, h:h + 1], None, ALU.mult, r=[src, hmask, bf_],
                         w=[bf_])
                p = next_ps()
                pv = p.t.bitcast(BF16)
                k.tr(pv[:, 0:128], bf_[:], ident_b[:], r=[bf_, ident_b], w=[p])
                k.act(dstT[:, s, :], pv[:, 0:128], AF.Copy, r=[p], w=[dstT])
        cnr, cni = f("cnr", [128, 4, 64]), f("cni", [128, 4, 64])
        k.dma(q, cnr[:], P_["ssm_c_re"].ap()[l].rearrange("(c4 gg) c p -> (gg c) c4 p", c4=4), w=[cnr])
        k.dma(q, cni[:], P_["ssm_c_im"].ap()[l].rearrange("(c4 gg) c p -> (gg c) c4 p", c4=4), w=[cni])
        CTr, CTi, CTn = f("CTr", [128, 16, 128], BF16), f("CTi", [128, 16, 128], BF16), f("CTn", [128, 16, 128], BF16)
        for s in range(16):
            for ri, (src, dstT, sgn) in enumerate(((cnr, CTr, 1.0), (cni, CTi, -1.0), (cnr, CTn, -1.0))):
                bf_ = bfull[(s * 3 + ri) % 4]
                for h in range(2):
                    j = (s % 4) * 2 + h
                    k.ts(bf_[:, 64 * h:64 * h + 64], src[:, s // 4, :], chmask[:, j:j + 1], sgn, ALU.mult, ALU.mult,
                         r=[src, chmask, bf_], w=[bf_])
                p = next_ps()
                pv = p.t.bitcast(BF16)
                k.tr(pv[:, 0:128], bf_[:], ident_b[:], r=[bf_, ident_b], w=[p])
                k.act(dstT[:, s, :], pv[:, 0:128], AF.Copy, r=[p], w=[dstT])
        drow = f("drow", [2, W])
        dcol = f("dcol", [128, 4, 2])
        k.dma(q, drow[0:1, :], P_["ssm_d"].ap()[l:l + 1, :], w=[drow])
        k.dma(q, drow[1:2, :], P_["ssm_b_glu"].ap()[l:l + 1, :], w=[drow])
        rows_to_cols(drow, 2, 4, dcol[:, :, :], dcol)
        Dg = f("Dg", [128, 4, 128], BF16)
        for c in range(4):
            k.ts(Dg[:, c, :], ident_f[:, :], dcol[:, c, 0:1], None, ALU.mult, r=[ident_f, dcol], w=[Dg])
        wg = f("wg", [128, 4, 512], BF16)
        wgs = f("wgs", [128, 4, 512])
        k.dma(q, wgs[:], P_["ssm_w_glu"].ap()[l].rearrange("(kc p) c -> p kc c", p=128), w=[wgs])
        k.copy(wg[:], wgs[:], r=[wgs], w=[wg], eng="pool")
        cosT, sinT = f("cosT", [128, 16, 512]), f("sinT", [128, 16, 512])
        iota1 = f("iota1", [128, 512])
        k.dma(q, iota1[:], C_["iota1"].ap(), w=[iota1])
        ph = [f("ph%d" % i, [128, 512]) for i in range(2)]
        us = [{"ti": f("usi%d" % i, [128, 512], I32), "tf": f("usf%d" % i, [128, 512]),
               "u": f("usu%d" % i, [128, 512])} for i in range(1)] * 2
        for s in range(16):
            ph_, u_ = ph[s % 2], us[s % 2]
            k.ts(ph_[:], iota1[:], th[:, s:s + 1], None, ALU.mult, r=[iota1, th], w=[ph_])
            sincos(ph_[:], [128, 512], sinT[:, s, :], cosT[:, s, :], u_, [ph_], [sinT, cosT])
        k.S.barrier()
        k.sb_off = mark_s5
        ccol, scol, nscol = f("ccol", [128, 16]), f("scol", [128, 16]), f("nscol", [128, 16])
        k.copy(ccol[:], cosT[:, :, 511], r=[cosT], w=[ccol])
        k.copy(scol[:], sinT[:, :, 511], r=[sinT], w=[scol])
        k.ts(nscol[:], sinT[:, :, 511], -1.0, None, ALU.mult, r=[sinT], w=[nscol])
        ctmp = [f("ctmp%d" % i, [128, 2]) for i in range(2)]
        cr, ci = f("cr", [128, 16]), f("ci", [128, 16])
        k.memset(cr[:], 0.0, w=[cr])
        k.memset(ci[:], 0.0, w=[ci])
        k.S.barrier()
        ub = [f("ub%d" % i, [128, 4, 512], BF16) for i in range(2)]
        zb = [f("zb%d" % i, [128, 4, 512], BF16) for i in range(2)]
        wk = {n: [f("%s%d" % (n, i), [128, 512]) for i in range(2)] for n in
              ("t1", "t2", "t3", "t4", "wre", "wim", "zre", "zim")}
        for a_ in ("d1", "d2", "d3", "d4"):
            wk[a_] = [f("%s%d" % (a_, i), [128, 512], BF16) for i in range(2)]
        gf = f("gf", [128, 4, 512])
        gb = f("gb", [128, 4, 512], BF16)
        y2 = [f("y2%d" % i, [128, 512]) for i in range(2)]
        sg = [f("sg%d" % i, [128, 512]) for i in range(2)]
        yo = [f("yo%d" % i, [128, 4, 512], BF16) for i in range(2)]
        crb = [Buf("cr%d" % s_) for s_ in range(16)]
        cib = [Buf("ci%d" % s_) for s_ in range(16)]

        def load_blk(j):
            k.dma("sp", ub[j % 2][:], suT.ap()[:, j * 512:(j + 1) * 512].rearrange("(c p) t -> p c t", p=128),
                  r=[k.dbuf["suT"]], w=[ub[j % 2]])
            k.dma("sp", zb[j % 2][:], szT.ap()[:, j * 512:(j + 1) * 512].rearrange("(c p) t -> p c t", p=128),
                  r=[k.dbuf["szT"]], w=[zb[j % 2]])

        iters = [(j, c4, s_) for j in range(NB) for c4 in range(4) for s_ in range(4 * c4, 4 * c4 + 4)]

        def stA(i):
            j, c4, s_ = iters[i]
            if c4 == 0 and s_ == 1 and j + 1 < NB:
                load_blk(j + 1)
            ub_ = ub[j % 2]
            W_ = {n: v[i % 2] for n, v in wk.items()}
            pa, pb = psb[2 + (i * 2) % 6], psb[2 + (i * 2 + 1) % 6]
            k.mm(pa[:, :], BTr[:, s_, :], ub_[:, c4, :], True, True, r=[BTr, ub_], w=[pa])
            k.mm(pb[:, :], BTi[:, s_, :], ub_[:, c4, :], True, True, r=[BTi, ub_], w=[pb])
            cs, sn = cosT[:, s_, :], sinT[:, s_, :]
            k.tt(W_["t1"][:], pa[:, :], cs, ALU.mult, r=[pa, cosT], w=[W_["t1"]])
            k.tt(W_["t2"][:], pb[:, :], sn, ALU.mult, r=[pb, sinT], w=[W_["t2"]])
            k.tt(W_["wre"][:], W_["t1"][:], W_["t2"][:], ALU.add, r=[W_["t1"], W_["t2"]], w=[W_["wre"]])
            k.tt(W_["t3"][:], pb[:, :], cs, ALU.mult, r=[pb, cosT], w=[W_["t3"]])
            k.tt(W_["t4"][:], pa[:, :], sn, ALU.mult, r=[pa, sinT], w=[W_["t4"]])
            k.tt(W_["wim"][:], W_["t3"][:], W_["t4"][:], ALU.subtract, r=[W_["t3"], W_["t4"]], w=[W_["wim"]])
            magb = mag[:, s_:s_ + 1].to_broadcast([128, 512])
            k.scan(W_["zre"][:], magb, W_["wre"][:], cr[:, s_:s_ + 1], ALU.mult, ALU.add,
                   r=[mag, W_["wre"], crb[s_]], w=[W_["zre"]])
            k.scan(W_["zim"][:], magb, W_["wim"][:], ci[:, s_:s_ + 1], ALU.mult, ALU.add,
                   r=[mag, W_["wim"], cib[s_]], w=[W_["zim"]])

        def stB(i):
            j, c4, s_ = iters[i]
            W_ = {n: v[i % 2] for n, v in wk.items()}
            cs, sn = cosT[:, s_, :], sinT[:, s_, :]
            k.tt(W_["d1"][:], W_["zre"][:], cs, ALU.mult, r=[W_["zre"], cosT], w=[W_["d1"]], eng="pool")
            k.tt(W_["d2"][:], W_["zim"][:], sn, ALU.mult, r=[W_["zim"], sinT], w=[W_["d2"]], eng="pool")
            k.tt(W_["d3"][:], W_["zre"][:], sn, ALU.mult, r=[W_["zre"], sinT], w=[W_["d3"]], eng="pool")
            k.tt(W_["d4"][:], W_["zim"][:], cs, ALU.mult, r=[W_["zim"], cosT], w=[W_["d4"]], eng="pool")

        def stC(i):
            j, c4, s_ = iters[i]
            ub_, zb_, yo_ = ub[j % 2], zb[j % 2], yo[j % 2]
            W_ = {n: v[i % 2] for n, v in wk.items()}
            yps = psb[c4 % 2]
            ct = ctmp[i % 2]
            k.act(ct[:, 0:1], W_["zim"][:, 511:512], AF.Copy, r=[W_["zim"], nscol], w=[ct], scale=nscol[:, s_:s_ + 1])
            k.act(cr[:, s_:s_ + 1], W_["zre"][:, 511:512], AF.Identity, r=[W_["zre"], ccol, ct], w=[crb[s_]],
                  scale=ccol[:, s_:s_ + 1], bias=ct[:, 0:1])
            k.act(ct[:, 1:2], W_["zim"][:, 511:512], AF.Copy, r=[W_["zim"], ccol], w=[ct], scale=ccol[:, s_:s_ + 1])
            k.act(ci[:, s_:s_ + 1], W_["zre"][:, 511:512], AF.Identity, r=[W_["zre"], scol, ct], w=[cib[s_]],
                  scale=scol[:, s_:s_ + 1], bias=ct[:, 1:2])
            k.mm(yps[:, :], CTr[:, s_, :], W_["d1"][:], start=(s_ == 4 * c4), stop=False, r=[CTr, W_["d1"]], w=[yps],
                 inc=True)
            k.mm(yps[:, :], CTn[:, s_, :], W_["d2"][:], start=False, stop=False, r=[CTn, W_["d2"]], w=[yps], inc=True)
            k.mm(yps[:, :], CTi[:, s_, :], W_["d3"][:], start=False, stop=False, r=[CTi, W_["d3"]], w=[yps], inc=True)
            k.mm(yps[:, :], CTi[:, s_, :], W_["d4"][:], start=False, stop=False, r=[CTi, W_["d4"]], w=[yps], inc=True)
            if s_ != 4 * c4 + 3:
                return
            k.mm(yps[:, :], Dg[:, c4, :], ub_[:, c4, :], start=False, stop=True, r=[Dg, ub_], w=[yps])
            y2_, sg_ = y2[c4 % 2], sg[c4 % 2]
            k.act(y2_[:], yps[:, :], AF.Square, r=[yps], w=[y2_])
            k.ts(y2_[:], y2_[:], 0.044715, 1.0, ALU.mult, ALU.add, r=[y2_], w=[y2_])
            k.tt(y2_[:], y2_[:], yps[:, :], ALU.mult, r=[y2_, yps], w=[y2_])
            k.act(sg_[:], y2_[:], AF.Sigmoid, r=[y2_], w=[sg_], scale=2.0 * math.sqrt(2.0 / PI))
            k.tt(gf[:, c4, :], sg_[:], yps[:, :], ALU.mult, r=[sg_, yps], w=[gf])
            k.act(gb[:, c4, :], gf[:, c4, :], AF.Copy, r=[gf], w=[gb])
            if c4 != 3:
                return
            for jc in range(4):
                pg = psb[2 + jc]
                for ic in range(4):
                    k.mm(pg[:, :], wg[:, ic, jc * 128:(jc + 1) * 128], gb[:, ic, :], start=(ic == 0), stop=(ic == 3),
                         r=[wg, gb], w=[pg])
                sg_ = sg[jc % 2]
                k.act(sg_[:], pg[:, :], AF.Sigmoid, r=[pg, dcol], w=[sg_], bias=dcol[:, jc, 1:2])
                k.tt(sg_[:], sg_[:], gf[:, jc, :], ALU.mult, r=[sg_, gf], w=[sg_])
                k.tt(yo_[:, jc, :], sg_[:], zb_[:, jc, :], ALU.mult, r=[sg_, zb_], w=[yo_])
            k.dma("sp", ysT.ap()[:, j * 512:(j + 1) * 512].rearrange("(c p) t -> p c t", p=128), yo_[:],
                  r=[yo_], w=[k.dbuf["ysT"]])

        load_blk(0)
        stA(0)
        for i in range(len(iters)):
            stB(i)
            if i + 1 < len(iters):
                stA(i + 1)
            stC(i)

    def phase_gates(l, pre):
        f = lambda name, shp, dt=F32: k.sb(name, shp, dt)
        NC = NT
        qT = f("mq", [128, 4, L], BF16)
        kTt = f("mk", [128, 4, L], BF16)
        Vp = f("mVp", [128, NT, 4, 130], BF16)
        pre.update(qT=qT, kTt=kTt, Vp=Vp, mark=k.sb_off)
        mif = f("mif_g", [8, L])
        k.dma("sp", mif[:], mifd.ap(), r=[k.dbuf["mif"]], w=[mif])
        ifb = f("ifb", [8, 1])
        k.dma("sp", ifb[:], P_["if_bias"].ap()[l].rearrange("a (h o) -> (a h) o", o=1), w=[ifb],
              allow_slow_non_contiguous=True)
        k.dma("sp", qT[:], mqT.ap().rearrange("(c p) t -> p c t", p=128), r=[k.dbuf["mqT"]], w=[qT])
        k.dma("sp", kTt[:], mkT.ap().rearrange("(c p) t -> p c t", p=128), r=[k.dbuf["mkT"]], w=[kTt])
        for n in range(NT):
            k.dma("sp", Vp[:, n, :, 0:128], mv.ap()[n * 128:(n + 1) * 128, :].rearrange("p (h d) -> p h d", h=4),
                  r=[k.dbuf["mv"]], w=[Vp])
        pre = f("pre", [8, L])
        lsg = f("lsg", [8, L])
        tmp8 = f("tmp8", [8, L])
        k.ts(pre[:], mif[:], ifb[:, 0:1], None, ALU.add, r=[mif, ifb], w=[pre])
        k.act(tmp8[:], pre[:], AF.Abs, r=[pre], w=[tmp8])
        k.act(tmp8[:], tmp8[:], AF.Exp, r=[tmp8], w=[tmp8], scale=-1.0)
        k.act(tmp8[:], tmp8[:], AF.Ln, r=[tmp8, consts], w=[tmp8], bias=consts[0:8, 3:4])
        k.ts(lsg[:], pre[:], 0.0, None, ALU.min, r=[pre], w=[lsg])
        k.tt(lsg[:], lsg[:], tmp8[:], ALU.subtract, r=[lsg, tmp8], w=[lsg])
        cs8 = tmp8
        k.scan(cs8[:], consts[0:8, 3:4].to_broadcast([8, L]), lsg[:], 0.0, ALU.mult, ALU.add, r=[consts, lsg],
               w=[cs8])
        R8 = lsg
        k.ts(R8[:], cs8[:], rowmask8[:, 1:2], None, ALU.mult, r=[cs8, rowmask8], w=[R8])
        k.stt(R8[:], pre[:], rowmask8[:, 0:1], R8[:], ALU.mult, ALU.add, r=[pre, rowmask8, R8], w=[R8])
        a4, B4 = f("a4", [4, L]), f("B4", [4, L])
        for tb in range(NB):
            p = next_ps()
            k.mm(p[0:4, :], sel8[:, 0:4], R8[:, tb * 512:(tb + 1) * 512], True, True, r=[sel8, R8], w=[p])
            k.copy(a4[:, tb * 512:(tb + 1) * 512], p[0:4, :], r=[p], w=[a4])
            p = next_ps()
            k.mm(p[0:4, :], sel8[:, 4:8], R8[:, tb * 512:(tb + 1) * 512], True, True, r=[sel8, R8], w=[p])
            k.copy(B4[:, tb * 512:(tb + 1) * 512], p[0:4, :], r=[p], w=[B4])
        cm, Mx, Mp, dd = f("cm", [4, NC]), f("Mx", [4, NC]), f("Mp", [4, NC]), f("dd", [4, NC])
        k.reduce(cm[:], a4[:, :].rearrange("p (n t) -> p n t", t=128), ALU.max, AX.X, r=[a4], w=[cm])
        k.scan(Mx[:], cm[:], cm[:], 0.0, ALU.max, ALU.max, r=[cm], w=[Mx])
        k.memset(Mp[:, 0:1], 0.0, w=[Mp])
        if NC > 1:
            k.copy(Mp[:, 1:NC], Mx[:, 0:NC - 1], r=[Mx], w=[Mp])
        k.tt(dd[:], Mp[:], Mx[:], ALU.subtract, r=[Mp, Mx], w=[dd])
        k.act(dd[:], dd[:], AF.Exp, r=[dd], w=[dd])
        Mb = Mx[:, :].unsqueeze(2).to_broadcast([4, NC, 128])
        wrow, frow = a4, B4
        k.tt(wrow[:, :].rearrange("p (n t) -> p n t", t=128), a4[:, :].rearrange("p (n t) -> p n t", t=128), Mb,
             ALU.subtract, r=[a4, Mx], w=[wrow])
        k.act(wrow[:], wrow[:], AF.Exp, r=[wrow], w=[wrow])
        k.tt(frow[:, :].rearrange("p (n t) -> p n t", t=128), B4[:, :].rearrange("p (n t) -> p n t", t=128), Mb,
             ALU.add, r=[B4, Mx], w=[frow])
        k.act(frow[:], frow[:], AF.Exp, r=[frow], w=[frow], scale=-1.0)
        for n0 in range(0, NT, 32):
            nn = min(32, NT - n0)
            p = next_ps()
            for n in range(n0, n0 + nn):
                c0 = (n - n0) * 8
                k.mm(p[:, c0:c0 + 4], wrow[0:4, n * 128:(n + 1) * 128], ident_f[0:4, 0:4], True, True,
                     r=[wrow, ident_f], w=[p], inc=False)
                k.mm(p[:, c0 + 4:c0 + 8], frow[0:4, n * 128:(n + 1) * 128], ident_f[0:4, 0:4], True, True,
                     r=[frow, ident_f], w=[p], inc=(n == n0 + nn - 1))
            k.copy(wf[:, n0:n0 + nn, :], p[:, 0:nn * 8].rearrange("p (n e) -> p n e", e=8), r=[p], w=[wf])
        for h in range(4):
            p = next_ps()
            k.mm(p[:, 0:NC], selrow4[0:4, h * 128:(h + 1) * 128], dd[0:4, :], True, True, r=[selrow4, dd], w=[p])
            k.copy(dbc[:, h, :], p[:, 0:NC], r=[p], w=[dbc])

    def phase_mlstm(l, pre):
        f = lambda name, shp, dt=F32: k.sb(name, shp, dt)
        NC = NT
        SC = 128.0 ** -0.5
        qT, kTt, Vp = pre["qT"], pre["kTt"], pre["Vp"]
        k.memset(Vp[:, :, :, 128:129], 1.0, w=[Vp], eng="dve")
        k.memset(Vp[:, :, :, 129:130], 0.0, w=[Vp], eng="dve")
        for n in range(NT):
            k.tt(Vp[:, n, :, :], Vp[:, n, :, :], wf[:, n, 0:4].unsqueeze(2).to_broadcast([128, 4, 130]), ALU.mult,
                 r=[Vp, wf], w=[Vp])
        gM = f("gM", [128, W])
        k.dma("sp", gM[:], P_["m_norm_g"].ap()[l:l + 1, :].partition_broadcast(128), w=[gM])
        Cst4 = f("Cst4", [128, 4, 130])
        Tm4 = f("Tm4", [128, 4, 130])
        Cs4 = [f("Cs4_%d" % i, [128, 4, 130], BF16) for i in range(2)]
        k.memset(Cst4[:], 0.0, w=[Cst4], eng="dve")
        Sm4 = [f("Sm4_%d" % i, [128, 4, 128], BF16) for i in range(2)]
        kN4 = [f("kN4_%d" % i, [128, 4, 128], BF16) for i in range(2)]
        puS4 = [f("puS4_%d" % i, [128, 4, 129]) for i in range(2)]
        d4 = [f("d4_%d" % i, [128, 8]) for i in range(2)]
        hmt = [f("hmt%d" % i, [128, W]) for i in range(2)]
        mot = [f("mot%d" % i, [128, W], BF16) for i in range(2)]
        mzt = [f("mzt%d" % i, [128, W], BF16) for i in range(2)]
        ws = {"ssq": f("m_ssq", [128, 4]), "sqs": f("m_sqs", [128, 128]), "yb": f("m_yb", [128, W], BF16)}
        stage = [f("mstage%d" % i, [128, 4, 512], BF16) for i in range(2)]
        cz_b = causal01[:, :].unsqueeze(1).to_broadcast([128, 4, 128])

        def stL(n):
            i2 = n % 2
            tk = slice(n * 128, (n + 1) * 128)
            psS = next_ps()
            for h in range(4):
                k.mm(psS[:, h * 128:(h + 1) * 128], kTt[:, h, tk], qT[:, h, tk], True, True, r=[kTt, qT], w=[psS],
                     inc=(h == 3))
            k.tt(Sm4[i2][:], psS[:, :].rearrange("p (h t) -> p h t", h=4), cz_b, ALU.mult, r=[psS, causal01],
                 w=[Sm4[i2]])
            pk = next_ps()
            pkv = pk.t.bitcast(BF16)
            for h in range(4):
                k.tr(pkv[:, h * 128:(h + 1) * 128], kTt[:, h, tk], ident_b[:], r=[kTt, ident_b], w=[pk],
                     inc=(h == 3))
            k.act(kN4[i2][:], pkv[:, 0:512].rearrange("p (h t) -> p h t", h=4), AF.Copy, r=[pk], w=[kN4[i2]])
            for hp in range(2):
                pu = next_ps()
                for hh in range(2):
                    h = hp * 2 + hh
                    k.mm(pu[:, hh * 129:(hh + 1) * 129], kN4[i2][:, h, :], Vp[:, n, h, 0:129], True, True,
                         r=[kN4[i2], Vp], w=[pu], inc=(hh == 1))
                k.act(puS4[i2][:, hp * 2:hp * 2 + 2, :], pu[:, 0:258].rearrange("p (h e) -> p h e", h=2), AF.Copy,
                      r=[pu], w=[puS4[i2]])

        def stR(n):
            i2 = n % 2
            tk = slice(n * 128, (n + 1) * 128)
            hm_ = hmt[n % 2]
            d_ = d4[i2]
            dsc = dbc[:, :, n:n + 1].to_broadcast([128, 4, 130])
            k.tt(Tm4[:], Cst4[:], dsc, ALU.mult, r=[Cst4, dbc], w=[Tm4])
            cs_ = Cs4[i2]
            if n > 0:
                k.act(cs_[:], Tm4[:], AF.Copy, r=[Tm4], w=[cs_])
            k.tt(Cst4[:, :, 0:129], Tm4[:, :, 0:129], puS4[i2][:], ALU.add, r=[Tm4, puS4[i2]], w=[Cst4])
            for hp in range(2):
                po = next_ps()
                for hh in range(2):
                    h = hp * 2 + hh
                    k.mm(po[:, hh * 129:(hh + 1) * 129], Sm4[i2][:, h, :], Vp[:, n, h, 0:129], True, (n == 0),
                         r=[Sm4[i2], Vp], w=[po], inc=(n == 0 and hh == 1))
                    if n > 0:
                        k.mm(po[:, hh * 129:(hh + 1) * 129], qT[:, h, tk], cs_[:, h, 0:129], False, True,
                             r=[qT, cs_], w=[po], inc=(hh == 1))
                pov = po[:, 0:258].rearrange("p (h e) -> p h e", h=2)
                k.act(d_[:, hp * 2:hp * 2 + 2], pov[:, :, 128], AF.Abs, r=[po], w=[d_], scale=SC)
                k.tt(d_[:, hp * 2:hp * 2 + 2], d_[:, hp * 2:hp * 2 + 2], wf[:, n, 4 + hp * 2:6 + hp * 2], ALU.max,
                     r=[d_, wf], w=[d_])
                k.recip(d_[:, 4 + hp * 2:6 + hp * 2], d_[:, hp * 2:hp * 2 + 2], r=[d_], w=[d_])
                k.ts(d_[:, 4 + hp * 2:6 + hp * 2], d_[:, 4 + hp * 2:6 + hp * 2], SC, None, ALU.mult, r=[d_], w=[d_])
                k.tt(hm_[:, hp * 256:(hp + 1) * 256].rearrange("p (h t) -> p h t", h=2), pov[:, :, 0:128],
                     d_[:, 4 + hp * 2:6 + hp * 2].unsqueeze(2).to_broadcast([128, 2, 128]), ALU.mult,
                     r=[po, d_], w=[hm_])

        stL(0)
        for n in range(NT):
            hm_, mo_, mz_ = hmt[n % 2], mot[n % 2], mzt[n % 2]
            k.dma("sp", mo_[:], mo.ap()[n * 128:(n + 1) * 128, :], r=[k.dbuf["mo"]], w=[mo_])
            k.dma("sp", mz_[:], mz.ap()[n * 128:(n + 1) * 128, :], r=[k.dbuf["mz"]], w=[mz_])
            if n + 1 < NT:
                stL(n + 1)
            stR(n)
            k.tt(hm_[:], hm_[:], mo_[:], ALU.mult, r=[hm_, mo_], w=[hm_])
            stg = stage[(n // 4) % 2]
            headnorm_out(hm_, gM, mz_, stg, (n % 4) * 128, ws)
            if n % 4 == 3:
                j = n // 4
                k.dma("sp", ymT.ap()[:, j * 512:(j + 1) * 512].rearrange("(c p) t -> p c t", p=128), stg[:],
                      r=[stg], w=[k.dbuf["ymT"]])

    def phase_attn(l):
        pre_ = {}
        for nm_, shp_, dt_ in (("aq", [128, 4, L], BF16), ("ak", [128, 4, L], BF16), ("aVp", [128, NT, 4, 130], BF16),
                               ("strip0", [128, 1024], F32), ("strip1", [128, 1024], F32),
                               ("strip2", [128, 1024], F32), ("strip3", [128, 1024], F32), ("b31bc", [128, 4], F32),
                               ("negMb", [128, 2, 4], F32), ("farb", [128, 2, 4], F32), ("neglam", [128, 1], F32),
                               ("gAcol", [128, 4, 1], F32), ("ones_b", [128, 128], BF16)):
            pre_[nm_] = k.sb(nm_, shp_, dt_)
        mark_a = k.sb_off
        f = lambda name, shp, dt=F32: pre_[name] if name in pre_ else k.sb(name, shp, dt)
        lam_init = 0.8 - 0.6 * math.exp(-0.3 * l)
        qT = f("aq", [128, 4, L], BF16)
        kTt = f("ak", [128, 4, L], BF16)
        Vp = f("aVp", [128, NT, 4, 130], BF16)
        k.dma("sp", qT[:], aqT.ap().rearrange("(c p) t -> p c t", p=128), r=[k.dbuf["aqT"]], w=[qT])
        k.dma("sp", kTt[:], akT.ap().rearrange("(c p) t -> p c t", p=128), r=[k.dbuf["akT"]], w=[kTt])
        vstg = f("vstg_a", [128, NT, W], BF16)
        for n0 in range(0, NT, 8):
            nn = min(8, NT - n0)
            k.dma("sp", vstg[:, n0:n0 + nn, :],
                  av.ap()[n0 * 128:(n0 + nn) * 128, :].rearrange("(n p) c -> p n c", p=128),
                  r=[k.dbuf["av"]], w=[vstg])
        k.copy(Vp[:, :, :, 0:128], vstg[:, :, :].rearrange("p n (h d) -> p n h d", h=4), r=[vstg], w=[Vp])
        k.memset(Vp[:, :, :, 128:129], 1.0, w=[Vp], eng="dve")
        k.memset(Vp[:, :, :, 129:130], 0.0, w=[Vp], eng="dve")
        strips = [f("strip%d" % h, [128, 1024]) for h in range(4)]
        b31bc = f("b31bc", [128, 4])
        antiI = f("antiI", [128, 128])
        k.dma("sp", antiI[:], C_["antiI"].ap(), w=[antiI])
        srev = [f("srev%d" % i, [128, 1024]) for i in range(2)]
        for h in range(4):
            sr = srev[h % 2]
            k.dma("sp", sr[:], bass.AP(Gd, h * 1152, [[1, 128], [1, 1024]]), r=[k.dbuf["Gd"]], w=[sr])
            for hf in range(2):
                p = next_ps()
                k.mm(p[:, :], antiI[:, :], sr[:, hf * 512:(hf + 1) * 512], True, True, r=[antiI, sr], w=[p])
                k.copy(strips[h][:, hf * 512:(hf + 1) * 512], p[:, :], r=[p], w=[strips[h]])
        k.dma("sp", b31bc[:], P_["rel_bias"].ap()[31:32, :].partition_broadcast(128), w=[b31bc])
        sqb = [f("sqb%d" % i, [128, 512], BF16) for i in range(2)]
        blkmax = f("blkmax", [2, NB])
        stat = f("stat", [2, 8])
        ii = 0
        for qi, src in enumerate((qT, kTt)):
            for h in range(4):
                for tb in range(NB):
                    s_ = sqb[ii % 2]
                    ii += 1
                    k.act(s_[:], src[:, h, tb * 512:(tb + 1) * 512], AF.Square, r=[src], w=[s_])
                    p = next_ps()
                    k.mm(p[0:2, :], blockones_b[:, 0:2], s_[:], True, True, r=[blockones_b, s_], w=[p])
                    k.reduce(blkmax[:, tb:tb + 1], p[0:2, :], ALU.max, AX.X, r=[p], w=[blkmax])
                k.reduce(stat[:, qi * 4 + h:qi * 4 + h + 1], blkmax[:, :], ALU.max, AX.X, r=[blkmax], w=[stat])
        M2 = f("M2", [2, 4])
        k.tt(M2[:], stat[:, 0:4], stat[:, 4:8], ALU.mult, r=[stat], w=[M2])
        k.act(M2[:], M2[:], AF.Sqrt, r=[M2], w=[M2])
        k.ts(M2[:], M2[:], -1.05, -2.0, ALU.mult, ALU.add, r=[M2], w=[M2])
        negMb = f("negMb", [128, 2, 4])
        farb = f("farb", [128, 2, 4])
        selc = f("selc", [2, 2, 128])
        k.memset(selc[:], 0.0, w=[selc])
        for c in range(2):
            k.ts(selc[:, c, :], ident_f[0:2, c:c + 1].to_broadcast([2, 128]), 1.0, None, ALU.mult, r=[ident_f, selc],
                 w=[selc])
        for c in range(2):
            p = next_ps()
            k.mm(p[:, 0:4], selc[0:2, c, :], M2[0:2, 0:4], True, True, r=[selc, M2], w=[p])
            k.copy(negMb[:, c, :], p[:, 0:4], r=[p], w=[negMb])
            k.tt(farb[:, c, :], negMb[:, c, :], b31bc[:, :], ALU.add, r=[negMb, b31bc], w=[farb])
        dl = f("dl", [1, 256])
        k.dma("sp", dl[:], P_["diff_lam"].ap()[l:l + 1].rearrange("o a d -> o (a d)"), w=[dl])
        pr = f("pr", [1, 128])
        s12 = f("s12", [1, 2])
        k.tt(pr[:, 0:64], dl[:, 0:64], dl[:, 64:128], ALU.mult, r=[dl], w=[pr])
        k.tt(pr[:, 64:128], dl[:, 128:192], dl[:, 192:256], ALU.mult, r=[dl], w=[pr])
        k.reduce(s12[:], pr[:, :].rearrange("p (a d) -> p a d", a=2), ALU.add, AX.X, r=[pr], w=[s12])
        k.act(s12[:], s12[:], AF.Exp, r=[s12], w=[s12])
        lamv = f("lamv", [1, 1])
        k.tt(lamv[:], s12[:, 1:2], s12[:, 0:1], ALU.subtract, r=[s12], w=[lamv])
        k.ts(lamv[:], lamv[:], -lam_init, None, ALU.add, r=[lamv], w=[lamv])
        neglam = f("neglam", [128, 1])
        p = next_ps()
        k.mm(p[:, 0:1], ones_row[0:1, :], lamv[0:1, 0:1], True, True, r=[ones_row, lamv], w=[p])
        k.copy(neglam[:], p[:, 0:1], r=[p], w=[neglam])
        garow = f("garow", [1, W])
        gAcol = f("gAcol", [128, 4, 1])
        k.dma("sp", garow[:], P_["diff_norm_g"].ap()[l:l + 1, :], w=[garow])
        rows_to_cols(garow, 1, 4, gAcol[:, :, :], gAcol)
        k.ts(gAcol[:], gAcol[:], 1.0 - lam_init, None, ALU.mult, r=[gAcol], w=[gAcol])
        ones_b = f("ones_b", [128, 128], BF16)
        k.memset(ones_b[:], 1.0, w=[ones_b], eng="dve")
        k.S.barrier()
        k.sb_off = mark_a
        Pt = [f("Pt%d" % i, [128, 512], BF16) for i in range(3)]
        tmpb = [f("tmpb%d" % i, [128, 512]) for i in range(2)]
        rsb = [f("rsb%d" % i, [128, 512]) for i in range(2)]
        o0 = [f("o0_%d" % i, [128, 512]) for i in range(2)]
        t1 = [f("at1_%d" % i, [128, 512]) for i in range(2)]
        haT = [f("haT%d" % i, [128, 512]) for i in range(2)]
        sqb_ = [f("asq%d" % i, [128, 512], BF16) for i in range(2)]
        rstd = [f("arstd%d" % i, [128, 512]) for i in range(2)]
        azt = [f("azt%d" % i, [128, 512], BF16) for i in range(2)]
        stage = [f("astage%d" % i, [128, 512], BF16) for i in range(2)]
        qz = [[f("qz%d_%d" % (i, c), [128, 512], BF16) for c in range(2)] for i in range(2)]
        for i in range(2):
            for c in range(2):
                k.memset(qz[i][c][:], 0.0, w=[qz[i][c]], eng="dve")
        it = 0
        gh = 0
        ic = 0
        for g in range(NB):
            for h in range(4):
                qz_ = qz[gh % 2]
                az_ = azt[gh % 2]
                ha_ = haT[gh % 2]
                k.dma("sp", az_[:], azT.ap()[h * 128:(h + 1) * 128, g * 512:(g + 1) * 512], r=[k.dbuf["azT"]],
                      w=[az_])
                for c in range(2):
                    k.copy(qz_[c][c * 64:(c + 1) * 64, :], qT[c * 64:(c + 1) * 64, h, g * 512:(g + 1) * 512],
                           r=[qT], w=[qz_[c]], eng="pool")
                for c in range(2):
                    nkb = 4 * g + 4
                    DEPTH_PF = 2
                    base = it
                    pss = {}
                    accV, accS = psb[(ic % 2) * 2], psb[(ic % 2) * 2 + 1]
                    ic += 1

                    def issue_qk(kb):
                        ps = psb[4 + (base + kb) % 4]
                        pss[kb] = ps
                        k.mm(ps[:, :], kTt[:, h, kb * 128:(kb + 1) * 128], qz_[c][:, :], True, True,
                             r=[kTt, qz_[c]], w=[ps])

                    for kb in range(min(DEPTH_PF, nkb)):
                        issue_qk(kb)
                    for kb in range(nkb):
                        it += 1
                        if kb + DEPTH_PF < nkb:
                            issue_qk(kb + DEPTH_PF)
                        ps = pss.pop(kb)
                        P_t = Pt[it % 3]
                        j = kb - (4 * g - 1)
                        c0 = 0
                        if j >= 0:
                            tb_ = tmpb[it % 2]
                            c0 = max(j - 1, 0) * 128
                            k.tt(tb_[:, c0:512], ps[:, c0:512], strips[h][:, (4 - j) * 128 + c0:(4 - j) * 128 + 512],
                                 ALU.add, r=[ps, strips[h]], w=[tb_])
                            k.act(P_t[:, c0:512], tb_[:, c0:512], AF.Exp, r=[tb_, negMb], w=[P_t],
                                  bias=negMb[:, c, h:h + 1])
                        else:
                            k.act(P_t[:], ps[:, :], AF.Exp, r=[ps, farb], w=[P_t], bias=farb[:, c, h:h + 1])
                        last = (kb == nkb - 1)
                        k.mm(accV[:, c0:512], Vp[:, kb, h, 0:128], P_t[:, c0:512], start=(kb == 0), stop=last,
                             r=[Vp, P_t], w=[accV], inc=last)
                        k.mm(accS[:, c0:512], ones_b[:, :], P_t[:, c0:512], start=(kb == 0), stop=last,
                             r=[ones_b, P_t], w=[accS], inc=True)
                    rs_ = rsb[ic % 2]
                    k.recip(rs_[:], accS[:, :], r=[accS], w=[rs_])
                    if c == 0:
                        o0_ = o0[gh % 2]
                        k.tt(o0_[:], accV[:, :], rs_[:], ALU.mult, r=[accV, rs_], w=[o0_])
                    else:
                        t_ = t1[gh % 2]
                        k.tt(t_[:], accV[:, :], rs_[:], ALU.mult, r=[accV, rs_], w=[t_])
                        k.stt(ha_[:], t_[:], neglam[:, 0:1], o0_[:], ALU.mult, ALU.add, r=[t_, neglam, o0_], w=[ha_])
                sq_ = sqb_[gh % 2]
                rd_ = rstd[gh % 2]
                stg = stage[gh % 2]
                k.act(sq_[:], ha_[:], AF.Square, r=[ha_], w=[sq_])
                pn = psb[4 + it % 4]
                k.mm(pn[:, :], ones_b[:, :], sq_[:], True, True, r=[ones_b, sq_], w=[pn])
                k.act(rd_[:], pn[:, :], AF.Sqrt, r=[pn, consts], w=[rd_], scale=1.0 / 128, bias=epsb)
                k.recip(rd_[:], rd_[:], r=[rd_], w=[rd_])
                k.tt(ha_[:], ha_[:], rd_[:], ALU.mult, r=[ha_, rd_], w=[ha_])
                k.stt(stg[:], ha_[:], gAcol[:, h, 0:1], az_[:], ALU.mult, ALU.mult, r=[ha_, gAcol, az_], w=[stg])
                k.dma("sp", yaT.ap()[h * 128:(h + 1) * 128, g * 512:(g + 1) * 512], stg[:], r=[stg],
                      w=[k.dbuf["yaT"]])
                gh += 1

    def phase_merge(l):
        f = lambda name, shp, dt=F32: k.sb(name, shp, dt)
        last = (l == depth - 1)
        norm_alloc()
        wbr = f("wbr", [128, 3, 4, 1024], BF16)
        wo = f("wo", [128, 8, 1024], BF16)
        mark_m = k.sb_off
        wbs = [f("wbs%d" % i, [128, 4, 1024]) for i in range(2)]
        for b in range(3):
            s_ = wbs[b % 2]
            k.dma("sp", s_[:], P_["w_branch"].ap()[l, b].rearrange("(kc p) c -> p kc c", p=128), w=[s_])
            k.copy(wbr[:, b, :, :], s_[:], r=[s_], w=[wbr])
        for hf in range(2):
            s_ = wbs[(3 + hf) % 2]
            k.dma("sp", s_[:], P_["w_out"].ap()[l, hf * 512:(hf + 1) * 512, :].rearrange("(kc p) c -> p kc c", p=128),
                  w=[s_])
            k.copy(wo[:, hf * 4:(hf + 1) * 4, :], s_[:], r=[s_], w=[wo])
        if last:
            k.dma("sp", gbc[:], P_["final_g"].ap().rearrange("(o d) -> o d", o=1).partition_broadcast(128), w=[gbc])
        else:
            k.dma("sp", gbc[:], norm_g.ap()[l + 1:l + 2, :].partition_broadcast(128), w=[gbc])
        k.S.barrier()
        k.sb_off = mark_m
        yb_ = [[f("y%d_%d" % (b, i), [128, 4, 512], BF16) for i in range(2)] for b in range(3)]
        gt = [f("gt%d" % i, [128, 512], BF16) for i in range(6)]
        m1 = [f("m1_%d" % i, [128, 512]) for i in range(2)]
        m2 = [f("m2_%d" % i, [128, 512]) for i in range(2)]
        mT = [f("mT%d" % i, [128, 8, 512], BF16) for i in range(2)]
        xo = [f("xo%d" % i, [128, D_MODEL]) for i in range(2)]
        xn = [f("xn%d" % i, [128, D_MODEL]) for i in range(2)]
        x_src = x_in if l == 0 else xs_d
        srcs = (ysT, ymT, yaT)
        st_ = {"gi": 0}

        def load_y(j):
            for b in range(3):
                k.dma("sp", yb_[b][j % 2][:],
                      srcs[b].ap()[:, j * 512:(j + 1) * 512].rearrange("(c p) t -> p c t", p=128),
                      r=[k.dbuf[srcs[b].name]], w=[yb_[b][j % 2]])

        def branch(j):
            ys_ = [yb_[b][j % 2] for b in range(3)]
            mT_ = mT[j % 2]
            for dc in range(8):
                m1_, m2_ = m1[dc % 2], m2[dc % 2]
                for b in range(3):
                    g_ = gt[st_["gi"] % 6]
                    st_["gi"] += 1
                    r0 = b * 1024 + dc * 128
                    k.dma("sp", g_[:], gT.ap()[r0:r0 + 128, j * 512:(j + 1) * 512], r=[k.dbuf["gT"]], w=[g_])
                    p = next_ps()
                    for cc in range(4):
                        k.mm(p[:, :], wbr[:, b, cc, dc * 128:(dc + 1) * 128], ys_[b][:, cc, :], start=(cc == 0),
                             stop=(cc == 3), r=[wbr, ys_[b]], w=[p])
                    if b == 0:
                        k.tt(m1_[:], p[:, :], g_[:], ALU.mult, r=[p, g_], w=[m1_])
                    else:
                        k.tt(m2_[:], p[:, :], g_[:], ALU.mult, r=[p, g_], w=[m2_])
                        if b == 1:
                            k.tt(m1_[:], m1_[:], m2_[:], ALU.add, r=[m1_, m2_], w=[m1_], eng="pool")
                        else:
                            k.tt(mT_[:, dc, :], m1_[:], m2_[:], ALU.add, r=[m1_, m2_], w=[mT_])

        def outproj(j):
            mT_ = mT[j % 2]
            for t4 in range(4):
                n = j * 4 + t4
                xo_, xn_ = xo[n % 2], xn[n % 2]
                k.dma("sp", xo_[:], x_src.ap()[n * 128:(n + 1) * 128, :], r=[k.dbuf[x_src.name]], w=[xo_])
                for hf in range(2):
                    p = next_ps()
                    for dc in range(8):
                        k.mm(p[:, :], mT_[:, dc, t4 * 128:(t4 + 1) * 128], wo[:, dc, hf * 512:(hf + 1) * 512],
                             start=(dc == 0), stop=(dc == 7), r=[mT_, wo], w=[p])
                    k.tt(xn_[:, hf * 512:(hf + 1) * 512], p[:, :], xo_[:, hf * 512:(hf + 1) * 512], ALU.add,
                         r=[p, xo_], w=[xn_])
                if last:
                    r_, _ = norm_rstd(xn_)
                    k.stt(xo_[:], xn_[:], r_[:, 0:1], gbc[:], ALU.mult, ALU.mult, r=[xn_, r_, gbc], w=[xo_])
                    k.dma("sp", out_d.ap()[n * 128:(n + 1) * 128, :], xo_[:], r=[xo_], w=[k.dbuf["out"]])
                else:
                    k.dma("sp", xs_d.ap()[n * 128:(n + 1) * 128, :], xn_[:], r=[xn_], w=[k.dbuf["xs"]])
                    norm_transpose(xn_, n * 128)

        load_y(0)
        branch(0)
        for j in range(NB):
            if j + 1 < NB:
                load_y(j + 1)
                branch(j + 1)
            outproj(j)

    for l in range(depth):
        if "proj" in phases:
            phase_proj(l)
        k.phase_reset(keep_hT=False)
        if "mlstm" in phases:
            pre_m = {}
            phase_gates(l, pre_m)
            k.S.barrier()
            k.sb_off = pre_m["mark"]
            phase_mlstm(l, pre_m)
        k.phase_reset(keep_hT=False)
        if "s5" in phases:
            phase_s5(l)
        k.phase_reset(keep_hT=False)
        if "attn" in phases:
            phase_attn(l)
        k.phase_reset(keep_hT=True)
        if "merge" in phases:
            phase_merge(l)
        k.phase_reset(keep_hT=True)

    k.S.finish("sp")
    k.S.emit()
    return k


_CACHE = {}


def kernel(**inputs):
    L, depth = SEQ, DEPTH
    if "k" not in _CACHE:
        _CACHE["k"] = build(L, depth)
    kb = _CACHE["k"]
    consts = host_consts()
    x = np.ascontiguousarray(inputs["x"], dtype=np.float32)
    shared = {n: np.ascontiguousarray(inputs[n], dtype=np.float32) for n in PARAM_SHAPES(depth)}
    shared.update(consts)
    in_maps = []
    for c in range(NCORES):
        m = dict(shared)
        m["x"] = x[c]
        in_maps.append(m)
    res = run_bass_kernel_spmd(kb.nc, in_maps, core_ids=list(range(NCORES)))
    return np.stack([np.asarray(r["out"], dtype=np.float32) for r in res.results], axis=0)
```

```python
import math
from contextlib import ExitStack
import numpy as np
import concourse.bass as bass
import concourse.mybir as mybir
from concourse.bass_utils import run_bass_kernel_spmd

F32 = mybir.dt.float32
BF16 = mybir.dt.bfloat16
I32 = mybir.dt.int32
ALU = mybir.AluOpType
AF = mybir.ActivationFunctionType
AX = mybir.AxisListType

D_MODEL = 1024
SEQ = 4096
DEPTH = 2
W = 512
D_IN = 8712
NCORES = 8
EPS = 1e-6


class Buf:
    __slots__ = ("name", "w", "r")

    def __init__(self, name=""):
        self.name = name
        self.w = None
        self.r = {}


class Sched:
    ENGS = (("pe", "tensor"), ("dve", "vector"), ("act", "scalar"), ("pool", "gpsimd"), ("sp", "sync"))
    NSLOT = 8

    def __init__(self, nc):
        self.nc = nc
        self.ops = {e: [] for e, _ in self.ENGS}
        self.cnt = {}
        self.seen = {e: {} for e, _ in self.ENGS}
        self.dma_i = {e: 0 for e, _ in self.ENGS}

    def _waits(self, eng, r, w, extra=None):
        deps = dict(extra or {})

        def add(k, v):
            if v > deps.get(k, 0):
                deps[k] = v

        for b in r:
            if b.w is not None:
                add(*b.w)
        for b in w:
            if b.w is not None and b.w[0] != eng:
                add(*b.w)
            for k, v in b.r.items():
                if k != eng:
                    add(k, v)
        waits = []
        seen = self.seen[eng]
        for k, v in deps.items():
            if k == "pe" and eng == "pe":
                continue
            if seen.get(k, 0) >= v:
                continue
            seen[k] = v
            waits.append((k, v))
        return waits

    def op(self, eng, fn, r=(), w=(), inc=True):
        waits = self._waits(eng, r, w)
        n = self.cnt.get(eng, 0) + 1
        if inc:
            self.cnt[eng] = n
        self.ops[eng].append((waits, fn, (eng, 1, n) if inc else None))
        for b in r:
            if b.r.get(eng, 0) < n:
                b.r[eng] = n
        for b in w:
            b.w = (eng, n)
            b.r = {}

    def dma(self, q, out, in_, r=(), w=(), **kw):
        i = self.dma_i[q]
        self.dma_i[q] += 1
        key = ("dma", q, i % self.NSLOT)
        n = self.cnt.get(key, 0) + 16
        extra = {key: n - 16} if n > 16 else None
        waits = self._waits(q, r, w, extra)
        self.cnt[key] = n
        self.ops[q].append((waits, lambda e: e.dma_start(out=out, in_=in_, **kw), (key, 16, n)))
        for b in r:
            b.r[key] = n
        for b in w:
            b.w = (key, n)
            b.r = {}

    def barrier(self):
        snap = dict(self.cnt)
        for eng, _ in self.ENGS:
            waits = [(k, v) for k, v in snap.items() if self.seen[eng].get(k, 0) < v and k != eng]
            for k, v in waits:
                self.seen[eng][k] = v
            if waits:
                self.ops[eng].append((waits, None, None))

    def finish(self, q="sp"):
        waits = [(k, v) for k, v in self.cnt.items() if self.seen[q].get(k, 0) < v]
        self.ops[q].append((waits, None, None))

    def emit(self):
        nc = self.nc
        targets = {}
        for eng, _ in self.ENGS:
            for waits, fn, inc in self.ops[eng]:
                for wk, wv in waits:
                    targets.setdefault(wk, set()).add(wv)
        rank = {}
        for key, vs in targets.items():
            step = 16 if isinstance(key, tuple) else 1
            rank[key] = {v: (i + 1) * step for i, v in enumerate(sorted(vs))}
        with ExitStack() as st:
            sems = {}
            for i, key in enumerate(self.cnt):
                sems[key] = st.enter_context(nc.semaphore("s%d" % i))
            block = st.enter_context(nc.Block())
            for eng, attr in self.ENGS:
                ops = self.ops[eng]
                if not ops:
                    continue

                def body(e, ops=ops):
                    for waits, fn, inc in ops:
                        for wk, wv in waits:
                            e.wait_ge(sems[wk], rank[wk][wv])
                        if fn is not None:
                            ins = fn(e)
                            if inc is not None and inc[2] in targets.get(inc[0], ()):
                                ins.then_inc(sems[inc[0]], inc[1])

                getattr(block, attr)(body)


class T:
    def __init__(self, t, name):
        self.t = t
        self.b = Buf(name)

    def __getitem__(self, k):
        return self.t[k]


class K:
    def __init__(self, L, depth, dbg=False):
        self.L = L
        self.depth = depth
        self.dbg = dbg
        self.nc = bass.Bass("TRN2", target_bir_lowering=False)
        self.S = Sched(self.nc)
        self.dram = {}
        self.dbuf = {}
        self.nps = 0
        self.sb_off = 16640
        self.sb_n = 0
        self.sb_mark = 0

    def sb(self, name, shape, dt=F32):
        nbytes = int(np.prod(shape[1:])) * (2 if dt == BF16 else 4)
        off = (self.sb_off + 63) // 64 * 64
        self.sb_off = off + nbytes
        assert self.sb_off <= 16640 + 206 * 1024, ("SBUF overflow", name, self.sb_off)
        self.sb_n += 1
        nm = "%s_%d" % (name, self.sb_n)
        return T(self.nc.alloc_sbuf_tensor_at(nm, list(shape), dt, offset=off), nm)

    def phase_mark(self):
        self.sb_mark = self.sb_off

    def phase_reset(self, keep_hT=True):
        self.S.barrier()
        self.sb_off = self.sb_mark if keep_hT else self.sb_mark0

    def ps(self, name, shape=(128, 512), dt=F32):
        return T(self.nc.alloc_psum_tensor(name, list(shape), dt), name)

    def din(self, name, shape, dt=F32):
        t = self.nc.dram_tensor(name, list(shape), dt, kind="ExternalInput")
        self.dram[name] = t
        self.dbuf[name] = Buf(name)
        return t

    def dscratch(self, name, shape, dt=F32, out=False):
        kind = "ExternalOutput" if (out or self.dbg) else "Internal"
        t = self.nc.dram_tensor(name, list(shape), dt, kind=kind)
        self.dram[name] = t
        self.dbuf[name] = Buf(name)
        return t

    def op(self, eng, fn, r=(), w=(), inc=True):
        self.S.op(eng, fn, [x.b if isinstance(x, T) else x for x in r],
                  [x.b if isinstance(x, T) else x for x in w], inc)

    def dma(self, q, out, in_, r=(), w=(), **kw):
        self.S.dma(q, out, in_, [x.b if isinstance(x, T) else x for x in r],
                   [x.b if isinstance(x, T) else x for x in w], **kw)

    def mm(self, out, lhsT, rhs, start, stop, r=(), w=(), inc=None):
        if inc is None:
            inc = stop
        self.op("pe", lambda e: e.matmul(out, lhsT, rhs, start=start, stop=stop), r=r, w=w, inc=inc)

    def act(self, out, in_, func, r=(), w=(), bias=None, scale=1.0, accum=None, eng="act"):
        kw = {}
        if bias is not None:
            kw["bias"] = bias
        if accum is not None:
            kw["accum_out"] = accum
        self.op(eng, lambda e: e.activation(out=out, in_=in_, func=func, scale=scale, **kw), r=r, w=w)

    def tt(self, out, in0, in1, op, r=(), w=(), eng="dve"):
        self.op(eng, lambda e: e.tensor_tensor(out=out, in0=in0, in1=in1, op=op), r=r, w=w)

    def ts(self, out, in0, s1, s2, op0, op1=None, r=(), w=(), eng="dve", accum=None):
        kw = {}
        if op1 is not None:
            kw["op1"] = op1
        if accum is not None:
            kw["accum_out"] = accum
        self.op(eng, lambda e: e.tensor_scalar(out=out, in0=in0, scalar1=s1, scalar2=s2, op0=op0, **kw), r=r, w=w)

    def stt(self, out, in0, scalar, in1, op0, op1, r=(), w=(), eng="dve"):
        self.op(eng, lambda e: e.scalar_tensor_tensor(out=out, in0=in0, scalar=scalar, in1=in1, op0=op0, op1=op1),
                r=r, w=w)

    def copy(self, out, in_, r=(), w=(), eng="dve"):
        self.op(eng, lambda e: e.tensor_copy(out=out, in_=in_), r=r, w=w)

    def memset(self, ap, val, w=(), eng="pool"):
        self.op(eng, lambda e: e.memset(ap, val), w=w)

    def scan(self, out, d0, d1, init, op0, op1, r=(), w=(), eng="dve"):
        self.op(eng, lambda e: e.tensor_tensor_scan(out=out, data0=d0, data1=d1, initial=init, op0=op0, op1=op1),
                r=r, w=w)

    def reduce(self, out, in_, op, axis, r=(), w=(), eng="dve"):
        self.op(eng, lambda e: e.tensor_reduce(out=out, in_=in_, axis=axis, op=op), r=r, w=w)

    def recip(self, out, in_, r=(), w=(), eng="dve"):
        self.op(eng, lambda e: e.reciprocal(out=out, in_=in_), r=r, w=w)

    def tr(self, out, in_, ident, r=(), w=(), inc=True):
        self.op("pe", lambda e: e.transpose(out, in_, ident), r=r, w=w, inc=inc)


def host_consts():
    c = {}
    c["ident"] = np.eye(128, dtype=np.float32)
    c["iota1"] = np.tile(np.arange(1, 513, dtype=np.float32)[None, :], (128, 1))
    s_ = np.arange(128)
    c["causal01"] = (s_[:, None] <= s_[None, :]).astype(np.float32)
    c["chmask"] = (s_[:, None] // 16 == np.arange(8)[None, :]).astype(np.float32)
    c["hmask"] = (s_[:, None] // 64 == np.arange(2)[None, :]).astype(np.float32)
    sel8 = np.zeros((8, 8), np.float32)
    for h in range(4):
        sel8[h, h] = 1.0
        sel8[4 + h, h] = -1.0
        sel8[4 + h, 4 + h] = 1.0
    c["sel8"] = sel8
    rm = np.zeros((8, 2), np.float32)
    rm[0:4, 0] = 1.0
    rm[4:8, 1] = 1.0
    c["rowmask8"] = rm
    sr4 = np.zeros((4, 4 * 128), np.float32)
    for h in range(4):
        sr4[h, h * 128:(h + 1) * 128] = 1.0
    c["selrow4"] = sr4
    bo = np.zeros((128, 2), np.float32)
    bo[0:64, 0] = 1.0
    bo[64:128, 1] = 1.0
    c["blockones"] = bo
    e = np.arange(1152)
    d = e - 511
    n = np.maximum(d, 0)
    large = 16 + (np.log(np.maximum(n, 1).astype(np.float32) / np.float32(16)) / np.float32(math.log(128 / 16))
                  * np.float32(16)).astype(np.int32)
    large = np.minimum(large, 31)
    bucket = np.where(n < 16, n, large)
    oh = np.zeros((32, 1152), np.float32)
    oh[bucket, e] = 1.0
    oh[:, d < 0] = 0.0
    c["oh"] = oh
    c["maskvec"] = np.where(d < 0, -30000.0, 0.0).astype(np.float32)[None, :]
    c["ones_row"] = np.ones((1, 128), np.float32)
    c["antiI"] = np.ascontiguousarray(np.eye(128, dtype=np.float32)[::-1])
    return c


CONST_SHAPES = {"ident": [128, 128], "iota1": [128, 512], "causal01": [128, 128], "chmask": [128, 8],
                "hmask": [128, 2], "sel8": [8, 8], "rowmask8": [8, 2], "selrow4": [4, 512],
                "blockones": [128, 2], "oh": [32, 1152], "maskvec": [1, 1152], "ones_row": [1, 128], "antiI": [128, 128]}

PARAM_SHAPES = lambda depth: {
    "norm_g": [depth, 1024], "w_in": [depth, 1024, D_IN], "conv_w": [depth, 4, 1024], "conv_b": [depth, 1024],
    "if_bias": [depth, 2, 4], "m_norm_g": [depth, 512], "ssm_lam_re": [depth, 32, 64],
    "ssm_lam_im": [depth, 32, 64], "ssm_b_re": [depth, 32, 64, 16], "ssm_b_im": [depth, 32, 64, 16],
    "ssm_c_re": [depth, 32, 16, 64], "ssm_c_im": [depth, 32, 16, 64], "ssm_d": [depth, 512],
    "ssm_log_step": [depth, 32], "ssm_w_glu": [depth, 512, 512], "ssm_b_glu": [depth, 512],
    "diff_lam": [depth, 4, 64], "diff_norm_g": [depth, 512], "rel_bias": [32, 4],
    "w_branch": [depth, 3, 512, 1024], "w_out": [depth, 1024, 1024], "final_g": [1024]}

PI = math.pi


def build(L=SEQ, depth=DEPTH, dbg=False, phases=("proj", "s5", "mlstm", "attn", "merge")):
    k = K(L, depth, dbg)
    nc = k.nc
    NT = L // 128
    NB = L // 512
    x_in = k.din("x", [L, D_MODEL])
    P_ = {n: k.din(n, shp) for n, shp in PARAM_SHAPES(depth).items()}
    C_ = {n: k.din(n, shp) for n, shp in CONST_SHAPES.items()}
    norm_g, w_in, conv_w, conv_b = P_["norm_g"], P_["w_in"], P_["conv_w"], P_["conv_b"]
    out_d = k.dscratch("out", [L, D_MODEL], F32, out=True)
    xs_d = k.dscratch("xs", [L, D_MODEL], F32)

    suT = k.dscratch("suT", [W, L], BF16)
    szT = k.dscratch("szT", [W, L], BF16)
    mqT = k.dscratch("mqT", [W, L], BF16)
    mkT = k.dscratch("mkT", [W, L], BF16)
    mv = k.dscratch("mv", [L, W], BF16)
    mo = k.dscratch("mo", [L, W], BF16)
    mz = k.dscratch("mz", [L, W], BF16)
    aqT = k.dscratch("aqT", [W, L], BF16)
    akT = k.dscratch("akT", [W, L], BF16)
    av = k.dscratch("av", [L, W], BF16)
    azT = k.dscratch("azT", [W, L], BF16)
    gT = k.dscratch("gT", [3 * D_MODEL, L], BF16)
    mifd = k.dscratch("mif", [8, L], F32)
    ysT = k.dscratch("ysT", [W, L], BF16)
    ymT = k.dscratch("ymT", [W, L], BF16)
    yaT = k.dscratch("yaT", [W, L], BF16)
    Gd = k.dscratch("Gd", [4, 1152], F32)

    def cload(name, dt=F32, parts=None):
        shp = CONST_SHAPES[name]
        t = k.sb(name, shp, F32)
        k.dma("sp", t[:], C_[name].ap(), w=[t])
        if dt == BF16:
            tb = k.sb(name + "_b", shp, BF16)
            k.copy(tb[:], t[:], r=[t], w=[tb])
            return t, tb
        return t

    ident_f, ident_b = cload("ident", BF16)
    causal01 = cload("causal01")
    chmask = cload("chmask")
    hmask = cload("hmask")
    sel8 = cload("sel8")
    rowmask8 = cload("rowmask8")
    selrow4 = cload("selrow4")
    blockones_f, blockones_b = cload("blockones", BF16)
    ones_row = cload("ones_row")
    gbc = k.sb("gbc", [128, D_MODEL], F32)
    wf = k.sb("wf", [128, NT, 8])
    dbc = k.sb("dbc", [128, 4, NT])
    consts = k.sb("consts", [128, 8], F32)
    CV = [EPS, -PI, 0.0, 1.0, 0.5 * PI]
    for i, v in enumerate(CV):
        k.memset(consts[:, i:i + 1], v, w=[consts])
    epsb = consts[:, 0:1]
    negpi = consts[:, 1:2]
    k.sb_mark0 = k.sb_off
    hT = k.sb("hT", [128, 8, L], BF16)
    psb = [k.ps("ps%d" % i) for i in range(8)]

    def next_ps():
        p = psb[k.nps % 8]
        k.nps += 1
        return p

    def rows_to_cols(src, R, nchunk, dst_ap, dstT):
        p = next_ps()
        for c in range(nchunk):
            k.mm(p[:, c * R:(c + 1) * R], src[0:R, c * 128:(c + 1) * 128], ident_f[0:R, 0:R],
                 start=True, stop=True, r=[src, ident_f], w=[p], inc=(c == nchunk - 1))
        k.copy(dst_ap, p[:, 0:nchunk * R].rearrange("p (c r) -> p c r", r=R), r=[p], w=[dstT])

    nrm = {}

    def norm_alloc():
        nrm["hb"] = [k.sb("hb%d" % i, [128, D_MODEL], BF16) for i in range(2)]
        nrm["sq"] = k.sb("sq", [128, D_MODEL], F32)
        nrm["ss"] = [k.sb("ss%d" % i, [128, 1], F32) for i in range(2)]
        nrm["rstd"] = [k.sb("rstd%d" % i, [128, 1], F32) for i in range(2)]
        nrm["i"] = 0

    def norm_rstd(xtile):
        i = nrm["i"]
        nrm["i"] += 1
        s_, r_ = nrm["ss"][i % 2], nrm["rstd"][i % 2]
        sq = nrm["sq"]
        k.act(sq[:], xtile[:], AF.Square, r=[xtile], w=[sq, s_], accum=s_[:])
        k.act(r_[:], s_[:], AF.Sqrt, r=[s_, consts], w=[r_], scale=1.0 / D_MODEL, bias=epsb)
        k.recip(r_[:], r_[:], r=[r_], w=[r_])
        return r_, i

    def norm_transpose(xtile, tok0):
        r_, i = norm_rstd(xtile)
        h_ = nrm["hb"][i % 2]
        k.stt(h_[:], xtile[:], r_[:, 0:1], gbc[:], ALU.mult, ALU.mult, r=[xtile, r_, gbc], w=[h_])
        p = next_ps()
        pv = p.t.bitcast(BF16)
        for c in range(8):
            k.tr(pv[:, c * 128:(c + 1) * 128], h_[:, c * 128:(c + 1) * 128], ident_b[:], r=[h_, ident_b], w=[p],
                 inc=(c == 7))
        k.act(hT[:, :, tok0:tok0 + 128], pv[:, :].rearrange("p (c t) -> p c t", c=8), AF.Copy, r=[p], w=[hT])

    def headnorm_out(ht, gtile, gate, stage, col0, ws):
        ssq, sqs, yb = ws["ssq"], ws["sqs"], ws["yb"]
        for h in range(4):
            k.act(sqs[:, :], ht[:, h * 128:(h + 1) * 128], AF.Square, r=[ht], w=[sqs, ssq], accum=ssq[:, h:h + 1])
        k.act(ssq[:, :], ssq[:, :], AF.Sqrt, r=[ssq, consts], w=[ssq], scale=1.0 / 128, bias=epsb)
        k.recip(ssq[:, :], ssq[:, :], r=[ssq], w=[ssq])
        for h in range(4):
            k.stt(ht[:, h * 128:(h + 1) * 128], ht[:, h * 128:(h + 1) * 128], ssq[:, h:h + 1],
                  gtile[:, h * 128:(h + 1) * 128], ALU.mult, ALU.mult, r=[ht, ssq, gtile], w=[ht])
        k.tt(yb[:, :], ht[:, :], gate[:, :], ALU.mult, r=[ht, gate], w=[yb])
        p = next_ps()
        pv = p.t.bitcast(BF16)
        for h in range(4):
            k.tr(pv[:, h * 128:(h + 1) * 128], yb[:, h * 128:(h + 1) * 128], ident_b[:], r=[yb, ident_b], w=[p],
                 inc=(h == 3))
        k.act(stage[:, :, col0:col0 + 128], pv[:, 0:512].rearrange("p (c t) -> p c t", c=4), AF.Copy,
              r=[p], w=[stage])

    k.phase_mark()
    if "attn" in phases:
        tab = k.sb("tab", [32, 4], F32)
        oh = k.sb("oh", [32, 1152], F32)
        mvec = k.sb("mvec", [1, 1152], F32)
        Gs = k.sb("Gs", [4, 1152], F32)
        k.dma("sp", tab[:], P_["rel_bias"].ap(), w=[tab])
        k.dma("sp", oh[:], C_["oh"].ap(), w=[oh])
        k.dma("sp", mvec[:], C_["maskvec"].ap(), w=[mvec])
        for c0 in range(0, 1152, 384):
            p = next_ps()
            k.mm(p[0:4, 0:384], tab[:, :], oh[:, c0:c0 + 384], start=True, stop=False, r=[tab, oh], w=[p])
            k.mm(p[0:4, 0:384], ones_row[0:1, 0:4], mvec[0:1, c0:c0 + 384], start=False, stop=True,
                 r=[ones_row, mvec], w=[p])
            k.copy(Gs[:, c0:c0 + 384], p[0:4, 0:384], r=[p], w=[Gs])
        k.dma("sp", Gd.ap(), Gs[:], r=[Gs], w=[k.dbuf["Gd"]])
    k.phase_reset()

    def phase0():
        norm_alloc()
        xt = [k.sb("xt%d" % i, [128, D_MODEL], F32) for i in range(2)]
        k.dma("sp", gbc[:], norm_g.ap()[0:1, :].partition_broadcast(128), w=[gbc])
        for i in range(NT):
            xt_ = xt[i % 2]
            k.dma("sp", xt_[:], x_in.ap()[i * 128:(i + 1) * 128, :], w=[xt_])
            norm_transpose(xt_, i * 128)

    phase0()
    k.phase_reset()

    def phase_proj(l):
        wst = [k.sb("wst%d" % i, [128, 8, 512], F32) for i in range(2)]
        wbf = [k.sb("wbf%d" % i, [128, 8, 512], BF16) for i in range(2)]
        stT = [k.sb("stT%d" % i, [128, L], BF16) for i in range(2)]
        stN = [k.sb("stN%d" % i, [128, 512], BF16) for i in range(6)]
        cbuf = k.sb("cbuf", [128, L + 8], F32)
        cacc = k.sb("cacc", [128, L], F32)
        cw = k.sb("cw", [128, 8, 5], F32)
        crow = k.sb("crow", [5, 2 * W], F32)
        st = {"wi": 0, "ti": 0, "ni": 0, "ev": 0}
        mif = k.sb("mif_sb", [8, L], F32)

        def load_w(col0, ncols=512):
            i = st["wi"]
            st["wi"] += 1
            ws, wb = wst[i % 2], wbf[i % 2]
            k.dma("pool", ws[:, :, 0:ncols],
                  w_in.ap()[l, :, col0:col0 + ncols].rearrange("(kc p) c -> p kc c", p=128), w=[ws])
            k.copy(wb[:, :, 0:ncols], ws[:, :, 0:ncols], r=[ws], w=[wb], eng="pool")
            return wb

        def evac(out_ap, in_ap, func, r, w, scale=1.0):
            if func is None:
                st["ev"] += 1
                if st["ev"] % 2 == 0:
                    if scale == 1.0:
                        k.copy(out_ap, in_ap, r=r, w=w)
                    else:
                        k.ts(out_ap, in_ap, scale, None, ALU.mult, r=r, w=w)
                    return
                func = AF.Copy
            k.act(out_ap, in_ap, func, r=r, w=w, scale=scale)

        def proj_T_block(wb, col0, dst, row0, func, scale=1.0, conv=None):
            for c in range(4):
                if conv is None:
                    stg = stT[st["ti"] % 2]
                    st["ti"] += 1
                for tb in range(NB):
                    p = next_ps()
                    for kc in range(8):
                        k.mm(p[:, :], wb[:, kc, c * 128:(c + 1) * 128], hT[:, kc, tb * 512:(tb + 1) * 512],
                             start=(kc == 0), stop=(kc == 7), r=[wb, hT], w=[p])
                    if conv is None:
                        evac(stg[:, tb * 512:(tb + 1) * 512], p[:, :], func, r=[p], w=[stg], scale=scale)
                    else:
                        k.act(cbuf[:, 8 + tb * 512:8 + (tb + 1) * 512], p[:, :], AF.Copy, r=[p], w=[cbuf])
                if conv is not None:
                    cc = conv * 4 + c
                    stg = stT[st["ti"] % 2]
                    st["ti"] += 1
                    k.ts(cacc[:], cbuf[:, 8:8 + L], cw[:, cc, 3:4], cw[:, cc, 4:5], ALU.mult, ALU.add,
                         r=[cbuf, cw], w=[cacc])
                    for j in range(3):
                        k.stt(cacc[:], cbuf[:, 5 + j:5 + j + L], cw[:, cc, j:j + 1], cacc[:], ALU.mult, ALU.add,
                              r=[cbuf, cw, cacc], w=[cacc])
                    k.act(stg[:], cacc[:], AF.Silu, r=[cacc], w=[stg])
                k.dma("sp", dst.ap()[row0 + c * 128:row0 + (c + 1) * 128, :], stg[:], r=[stg], w=[k.dbuf[dst.name]])

        def proj_N_block(wb, col0, dst, func):
            for t in range(NT):
                p = next_ps()
                stg = stN[st["ni"] % 6]
                st["ni"] += 1
                for kc in range(8):
                    k.mm(p[:, :], hT[:, kc, t * 128:(t + 1) * 128], wb[:, kc, :],
                         start=(kc == 0), stop=(kc == 7), r=[wb, hT], w=[p])
                evac(stg[:], p[:, :], func, r=[p], w=[stg])
                k.dma("sp", dst.ap()[t * 128:(t + 1) * 128, :], stg[:], r=[stg], w=[k.dbuf[dst.name]])

        def proj_if(wb, col0=3584):
            for tb in range(NB):
                p = next_ps()
                for kc in range(8):
                    k.mm(p[0:8, :], wb[:, kc, 0:8], hT[:, kc, tb * 512:(tb + 1) * 512],
                         start=(kc == 0), stop=(kc == 7), r=[wb, hT], w=[p])
                k.act(mif[:, tb * 512:(tb + 1) * 512], p[0:8, :], AF.Copy, r=[p], w=[mif])

        k.memset(cbuf[:, 0:8], 0.0, w=[cbuf])
        k.dma("sp", crow[0:4, :], conv_w.ap()[l], w=[crow])
        k.dma("sp", crow[4:5, :], conv_b.ap()[l:l + 1, :], w=[crow])
        rows_to_cols(crow, 5, 8, cw[:, :, :], cw)
        specs = [(proj_if, 3584, 8, ()),
                 (proj_T_block, 0, 512, (suT, 0, None)),
                 (proj_T_block, 512, 512, (szT, 0, AF.Silu)),
                 (proj_T_block, 1024, 512, (mqT, 0, None, 1.0, 0)),
                 (proj_T_block, 1536, 512, (mkT, 0, None, 1.0, 1)),
                 (proj_N_block, 2048, 512, (mv, None)),
                 (proj_N_block, 2560, 512, (mo, AF.Sigmoid)),
                 (proj_N_block, 3072, 512, (mz, AF.Silu)),
                 (proj_T_block, 3592, 512, (aqT, 0, None, 0.125)),
                 (proj_T_block, 4104, 512, (akT, 0, None)),
                 (proj_N_block, 4616, 512, (av, None)),
                 (proj_T_block, 5128, 512, (azT, 0, AF.Silu))]
        for gb in range(6):
            specs.append((proj_T_block, 5640 + gb * 512, 512, (gT, gb * 512, AF.Sigmoid)))
        wbl = [None] * len(specs)
        wbl[0] = load_w(specs[0][1], specs[0][2])
        for i, (fn, col0, ncols, args) in enumerate(specs):
            if i + 1 < len(specs):
                wbl[i + 1] = load_w(specs[i + 1][1], specs[i + 1][2])
            fn(wbl[i], col0, *args)
            if i == 0:
                k.dma("sp", mifd.ap(), mif[:], r=[mif], w=[k.dbuf["mif"]])

    def sincos(ph_ap, shape, sin_out, cos_out, u, r, wT):
        ti = u["ti"]
        tf = u["tf"]
        uu = u["u"]
        for which, out_ap in (("sin", sin_out), ("cos", cos_out)):
            if which == "sin":
                k.ts(ti[:], ph_ap, 1.0 / (2.0 * PI), None, ALU.mult, r=r, w=[ti])
            else:
                k.ts(ti[:], ph_ap, 1.0 / (2.0 * PI), 0.25, ALU.mult, ALU.add, r=r, w=[ti])
            k.copy(tf[:], ti[:], r=[ti], w=[tf])
            k.stt(uu[:], tf[:], -2.0 * PI, ph_ap, ALU.mult, ALU.add, r=[tf] + list(r), w=[uu])
            if which == "sin":
                k.ts(uu[:], uu[:], -PI + 1e-5, PI - 1e-5, ALU.max, ALU.min, r=[uu], w=[uu])
                k.act(out_ap, uu[:], AF.Sin, r=[uu], w=wT)
            else:
                k.ts(uu[:], uu[:], -1.5 * PI + 1e-5, 0.5 * PI - 1e-5, ALU.max, ALU.min, r=[uu], w=[uu])
                k.act(out_ap, uu[:], AF.Sin, r=[uu, consts], w=wT, bias=consts[:, 4:5])

    def phase_s5(l):
        pre_ = {}
        for nm_, shp_, dt_ in (("mag", [128, 16], F32), ("th", [128, 16], F32), ("BTr", [128, 16, 128], BF16),
                               ("BTi", [128, 16, 128], BF16), ("CTr", [128, 16, 128], BF16),
                               ("CTi", [128, 16, 128], BF16), ("CTn", [128, 16, 128], BF16), ("Dg", [128, 4, 128], BF16), ("dcol", [128, 4, 2], F32),
                               ("wg", [128, 4, 512], BF16), ("cosT", [128, 16, 512], F32),
                               ("sinT", [128, 16, 512], F32)):
            pre_[nm_] = k.sb(nm_, shp_, dt_)
        mark_s5 = k.sb_off
        f = lambda name, shp, dt=F32: pre_[name] if name in pre_ else k.sb(name, shp, dt)
        lre, lim, lst = f("lre", [128, 16]), f("lim", [128, 16]), f("lst", [128, 16])
        mag, th, are, aim = f("mag", [128, 16]), f("th", [128, 16]), f("are", [128, 16]), f("aim", [128, 16])
        den, rre, rim, t16 = (f("den", [128, 16]), f("rre", [128, 16]),
                              f("rim", [128, 16]), f("t16", [128, 16]))
        u16 = {"ti": f("u16i", [128, 16], I32), "tf": f("u16f", [128, 16]), "u": f("u16u", [128, 16])}
        q = "sp"
        k.dma(q, lre[:], P_["ssm_lam_re"].ap()[l].rearrange("(s g) p -> (g p) s", g=2), w=[lre],
              allow_slow_non_contiguous=True)
        k.dma(q, lim[:], P_["ssm_lam_im"].ap()[l].rearrange("(s g) p -> (g p) s", g=2), w=[lim],
              allow_slow_non_contiguous=True)
        for h in range(2):
            k.dma(q, lst[h * 64:(h + 1) * 64, :],
                  P_["ssm_log_step"].ap()[l].rearrange("(s g) -> g s", g=2)[h:h + 1, :].partition_broadcast(64),
                  w=[lst], allow_slow_non_contiguous=True)
        k.act(lst[:], lst[:], AF.Exp, r=[lst], w=[lst])
        k.tt(mag[:], lre[:], lst[:], ALU.mult, r=[lre, lst], w=[mag])
        k.act(mag[:], mag[:], AF.Exp, r=[mag], w=[mag])
        k.tt(th[:], lim[:], lst[:], ALU.mult, r=[lim, lst], w=[th])
        sincos(th[:], [128, 16], aim[:], are[:], u16, [th], [aim, are])
        k.tt(are[:], are[:], mag[:], ALU.mult, r=[are, mag], w=[are])
        k.tt(aim[:], aim[:], mag[:], ALU.mult, r=[aim, mag], w=[aim])
        k.tt(den[:], lre[:], lre[:], ALU.mult, r=[lre], w=[den])
        k.tt(t16[:], lim[:], lim[:], ALU.mult, r=[lim], w=[t16])
        k.tt(den[:], den[:], t16[:], ALU.add, r=[den, t16], w=[den])
        k.recip(den[:], den[:], r=[den], w=[den])
        nre = f("nre", [128, 16])
        k.ts(nre[:], are[:], -1.0, None, ALU.add, r=[are], w=[nre])
        k.tt(rre[:], nre[:], lre[:], ALU.mult, r=[nre, lre], w=[rre])
        k.tt(t16[:], aim[:], lim[:], ALU.mult, r=[aim, lim], w=[t16])
        k.tt(rre[:], rre[:], t16[:], ALU.add, r=[rre, t16], w=[rre])
        k.tt(rre[:], rre[:], den[:], ALU.mult, r=[rre, den], w=[rre])
        k.tt(rim[:], aim[:], lre[:], ALU.mult, r=[aim, lre], w=[rim])
        k.tt(t16[:], nre[:], lim[:], ALU.mult, r=[nre, lim], w=[t16])
        k.tt(rim[:], rim[:], t16[:], ALU.subtract, r=[rim, t16], w=[rim])
        k.tt(rim[:], rim[:], den[:], ALU.mult, r=[rim, den], w=[rim])
        bre, bim = f("bre", [128, 16, 16]), f("bim", [128, 16, 16])
        bbr, bbi, tb_ = f("bbr", [128, 16, 16]), f("bbi", [128, 16, 16]), f("tb_", [128, 16, 16])
        k.dma(q, bre[:], P_["ssm_b_re"].ap()[l].rearrange("(s g) p c -> (g p) s c", g=2), w=[bre])
        k.dma(q, bim[:], P_["ssm_b_im"].ap()[l].rearrange("(s g) p c -> (g p) s c", g=2), w=[bim])
        rre_b = rre[:, :].unsqueeze(2).to_broadcast([128, 16, 16])
        rim_b = rim[:, :].unsqueeze(2).to_broadcast([128, 16, 16])
        k.tt(bbr[:], bre[:], rre_b, ALU.mult, r=[bre, rre], w=[bbr])
        k.tt(tb_[:], bim[:], rim_b, ALU.mult, r=[bim, rim], w=[tb_])
        k.tt(bbr[:], bbr[:], tb_[:], ALU.subtract, r=[bbr, tb_], w=[bbr])
        k.tt(bbi[:], bim[:], rre_b, ALU.mult, r=[bim, rre], w=[bbi])
        k.tt(tb_[:], bre[:], rim_b, ALU.mult, r=[bre, rim], w=[tb_])
        k.tt(bbi[:], bbi[:], tb_[:], ALU.add, r=[bbi, tb_], w=[bbi])
        BTr, BTi = f("BTr", [128, 16, 128], BF16), f("BTi", [128, 16, 128], BF16)
        bfull = [f("bfull%d" % i, [128, 128], BF16) for i in range(4)]
        for s in range(16):
            for ri, (src, dstT) in enumerate(((bbr, BTr), (bbi, BTi))):
                bf_ = bfull[(s * 3 + ri) % 4]
                k.memset(bf_[:], 0.0, w=[bf_], eng="dve")
                for h in range(2):
                    c0 = 32 * (s % 4) + 16 * h
                    k.ts(bf_[:, c0:c0 + 16], src[:, s, :], hmask[:, h:h + 1], None, ALU.mult, r=[src, hmask, bf_],
                         w=[bf_])
                p = next_ps()
                pv = p.t.bitcast(BF16)
                k.tr(pv[:, 0:128], bf_[:], ident_b[:], r=[bf_, ident_b], w=[p])
                k.act(dstT[:, s, :], pv[:, 0:128], AF.Copy, r=[p], w=[dstT])
        cnr, cni = f("cnr", [128, 4, 64]), f("cni", [128, 4, 64])
        k.dma(q, cnr[:], P_["ssm_c_re"].ap()[l].rearrange("(c4 gg) c p -> (gg c) c4 p", c4=4), w=[cnr])
        k.dma(q, cni[:], P_["ssm_c_im"].ap()[l].rearrange("(c4 gg) c p -> (gg c) c4 p", c4=4), w=[cni])
        CTr, CTi, CTn = f("CTr", [128, 16, 128], BF16), f("CTi", [128, 16, 128], BF16), f("CTn", [128, 16, 128], BF16)
        for s in range(16):
            for ri, (src, dstT, sgn) in enumerate(((cnr, CTr, 1.0), (cni, CTi, -1.0), (cnr, CTn, -1.0))):
                bf_ = bfull[(s * 3 + ri) % 4]
                for h in range(2):
                    j = (s % 4) * 2 + h
                    k.ts(bf_[:, 64 * h:64 * h + 64], src[:, s // 4, :], chmask[:, j:j + 1], sgn, ALU.mult, ALU.mult,
                         r=[src, chmask, bf_], w=[bf_])
                p = next_ps()
                pv = p.t.bitcast(BF16)
                k.tr(pv[:, 0:128], bf_[:], ident_b[:], r=[bf_, ident_b], w=[p])
                k.act(dstT[:, s, :], pv[:, 0:128], AF.Copy, r=[p], w=[dstT])
        drow = f("drow", [2, W])
        dcol = f("dcol", [128, 4, 2])
        k.dma(q, drow[0:1, :], P_["ssm_d"].ap()[l:l + 1, :], w=[drow])
        k.dma(q, drow[1:2, :], P_["ssm_b_glu"].ap()[l:l + 1, :], w=[drow])
        rows_to_cols(drow, 2, 4, dcol[:, :, :], dcol)
        Dg = f("Dg", [128, 4, 128], BF16)
        for c in range(4):
            k.ts(Dg[:, c, :], ident_f[:, :], dcol[:, c, 0:1], None, ALU.mult, r=[ident_f, dcol], w=[Dg])
        wg = f("wg", [128, 4, 512], BF16)
        wgs = f("wgs", [128, 4, 512])
        k.dma(q, wgs[:], P_["ssm_w_glu"].ap()[l].rearrange("(kc p) c -> p kc c", p=128), w=[wgs])
        k.copy(wg[:], wgs[:], r=[wgs], w=[wg], eng="pool")
        cosT, sinT = f("cosT", [128, 16, 512]), f("sinT", [128, 16, 512])
        iota1 = f("iota1", [128, 512])
        k.dma(q, iota1[:], C_["iota1"].ap(), w=[iota1])
        ph = [f("ph%d" % i, [128, 512]) for i in range(2)]
        us = [{"ti": f("usi%d" % i, [128, 512], I32), "tf": f("usf%d" % i, [128, 512]),
               "u": f("usu%d" % i, [128, 512])} for i in range(1)] * 2
        for s in range(16):
            ph_, u_ = ph[s % 2], us[s % 2]
            k.ts(ph_[:], iota1[:], th[:, s:s + 1], None, ALU.mult, r=[iota1, th], w=[ph_])
            sincos(ph_[:], [128, 512], sinT[:, s, :], cosT[:, s, :], u_, [ph_], [sinT, cosT])
        k.S.barrier()
        k.sb_off = mark_s5
        ccol, scol, nscol = f("ccol", [128, 16]), f("scol", [128, 16]), f("nscol", [128, 16])
        k.copy(ccol[:], cosT[:, :, 511], r=[cosT], w=[ccol])
        k.copy(scol[:], sinT[:, :, 511], r=[sinT], w=[scol])
        k.ts(nscol[:], sinT[:, :, 511], -1.0, None, ALU.mult, r=[sinT], w=[nscol])
        ctmp = [f("ctmp%d" % i, [128, 2]) for i in range(2)]
        cr, ci = f("cr", [128, 16]), f("ci", [128, 16])
        k.memset(cr[:], 0.0, w=[cr])
        k.memset(ci[:], 0.0, w=[ci])
        k.S.barrier()
        ub = [f("ub%d" % i, [128, 4, 512], BF16) for i in range(2)]
        zb = [f("zb%d" % i, [128, 4, 512], BF16) for i in range(2)]
        wk = {n: [f("%s%d" % (n, i), [128, 512]) for i in range(2)] for n in
              ("t1", "t2", "t3", "t4", "wre", "wim", "zre", "zim")}
        for a_ in ("d1", "d2", "d3", "d4"):
            wk[a_] = [f("%s%d" % (a_, i), [128, 512], BF16) for i in range(2)]
        gf = f("gf", [128, 4, 512])
        gb = f("gb", [128, 4, 512], BF16)
        y2 = [f("y2%d" % i, [128, 512]) for i in range(2)]
        sg = [f("sg%d" % i, [128, 512]) for i in range(2)]
        yo = [f("yo%d" % i, [128, 4, 512], BF16) for i in range(2)]
        crb = [Buf("cr%d" % s_) for s_ in range(16)]
        cib = [Buf("ci%d" % s_) for s_ in range(16)]

        def load_blk(j):
            k.dma("sp", ub[j % 2][:], suT.ap()[:, j * 512:(j + 1) * 512].rearrange("(c p) t -> p c t", p=128),
                  r=[k.dbuf["suT"]], w=[ub[j % 2]])
            k.dma("sp", zb[j % 2][:], szT.ap()[:, j * 512:(j + 1) * 512].rearrange("(c p) t -> p c t", p=128),
                  r=[k.dbuf["szT"]], w=[zb[j % 2]])

        iters = [(j, c4, s_) for j in range(NB) for c4 in range(4) for s_ in range(4 * c4, 4 * c4 + 4)]

        def stA(i):
            j, c4, s_ = iters[i]
            if c4 == 0 and s_ == 1 and j + 1 < NB:
                load_blk(j + 1)
            ub_ = ub[j % 2]
            W_ = {n: v[i % 2] for n, v in wk.items()}
            pa, pb = psb[2 + (i * 2) % 6], psb[2 + (i * 2 + 1) % 6]
            k.mm(pa[:, :], BTr[:, s_, :], ub_[:, c4, :], True, True, r=[BTr, ub_], w=[pa])
            k.mm(pb[:, :], BTi[:, s_, :], ub_[:, c4, :], True, True, r=[BTi, ub_], w=[pb])
            cs, sn = cosT[:, s_, :], sinT[:, s_, :]
            k.tt(W_["t1"][:], pa[:, :], cs, ALU.mult, r=[pa, cosT], w=[W_["t1"]])
            k.tt(W_["t2"][:], pb[:, :], sn, ALU.mult, r=[pb, sinT], w=[W_["t2"]])
            k.tt(W_["wre"][:], W_["t1"][:], W_["t2"][:], ALU.add, r=[W_["t1"], W_["t2"]], w=[W_["wre"]])
            k.tt(W_["t3"][:], pb[:, :], cs, ALU.mult, r=[pb, cosT], w=[W_["t3"]])
            k.tt(W_["t4"][:], pa[:, :], sn, ALU.mult, r=[pa, sinT], w=[W_["t4"]])
            k.tt(W_["wim"][:], W_["t3"][:], W_["t4"][:], ALU.subtract, r=[W_["t3"], W_["t4"]], w=[W_["wim"]])
            magb = mag[:, s_:s_ + 1].to_broadcast([128, 512])
            k.scan(W_["zre"][:], magb, W_["wre"][:], cr[:, s_:s_ + 1], ALU.mult, ALU.add,
                   r=[mag, W_["wre"], crb[s_]], w=[W_["zre"]])
            k.scan(W_["zim"][:], magb, W_["wim"][:], ci[:, s_:s_ + 1], ALU.mult, ALU.add,
                   r=[mag, W_["wim"], cib[s_]], w=[W_["zim"]])

        def stB(i):
            j, c4, s_ = iters[i]
            W_ = {n: v[i % 2] for n, v in wk.items()}
            cs, sn = cosT[:, s_, :], sinT[:, s_, :]
            k.tt(W_["d1"][:], W_["zre"][:], cs, ALU.mult, r=[W_["zre"], cosT], w=[W_["d1"]], eng="pool")
            k.tt(W_["d2"][:], W_["zim"][:], sn, ALU.mult, r=[W_["zim"], sinT], w=[W_["d2"]], eng="pool")
            k.tt(W_["d3"][:], W_["zre"][:], sn, ALU.mult, r=[W_["zre"], sinT], w=[W_["d3"]], eng="pool")
            k.tt(W_["d4"][:], W_["zim"][:], cs, ALU.mult, r=[W_["zim"], cosT], w=[W_["d4"]], eng="pool")

        def stC(i):
            j, c4, s_ = iters[i]
            ub_, zb_, yo_ = ub[j % 2], zb[j % 2], yo[j % 2]
            W_ = {n: v[i % 2] for n, v in wk.items()}
            yps = psb[c4 % 2]
            ct = ctmp[i % 2]
            k.act(ct[:, 0:1], W_["zim"][:, 511:512], AF.Copy, r=[W_["zim"], nscol], w=[ct], scale=nscol[:, s_:s_ + 1])
            k.act(cr[:, s_:s_ + 1], W_["zre"][:, 511:512], AF.Identity, r=[W_["zre"], ccol, ct], w=[crb[s_]],
                  scale=ccol[:, s_:s_ + 1], bias=ct[:, 0:1])
            k.act(ct[:, 1:2], W_["zim"][:, 511:512], AF.Copy, r=[W_["zim"], ccol], w=[ct], scale=ccol[:, s_:s_ + 1])
            k.act(ci[:, s_:s_ + 1], W_["zre"][:, 511:512], AF.Identity, r=[W_["zre"], scol, ct], w=[cib[s_]],
                  scale=scol[:, s_:s_ + 1], bias=ct[:, 1:2])
            k.mm(yps[:, :], CTr[:, s_, :], W_["d1"][:], start=(s_ == 4 * c4), stop=False, r=[CTr, W_["d1"]], w=[yps],
                 inc=True)
            k.mm(yps[:, :], CTn[:, s_, :], W_["d2"][:], start=False, stop=False, r=[CTn, W_["d2"]], w=[yps], inc=True)
            k.mm(yps[:, :], CTi[:, s_, :], W_["d3"][:], start=False, stop=False, r=[CTi, W_["d3"]], w=[yps], inc=True)
            k.mm(yps[:, :], CTi[:, s_, :], W_["d4"][:], start=False, stop=False, r=[CTi, W_["d4"]], w=[yps], inc=True)
            if s_ != 4 * c4 + 3:
                return
            k.mm(yps[:, :], Dg[:, c4, :], ub_[:, c4, :], start=False, stop=True, r=[Dg, ub_], w=[yps])
            y2_, sg_ = y2[c4 % 2], sg[c4 % 2]
            k.act(y2_[:], yps[:, :], AF.Square, r=[yps], w=[y2_])
            k.ts(y2_[:], y2_[:], 0.044715, 1.0, ALU.mult, ALU.add, r=[y2_], w=[y2_])
            k.tt(y2_[:], y2_[:], yps[:, :], ALU.mult, r=[y2_, yps], w=[y2_])
            k.act(sg_[:], y2_[:], AF.Sigmoid, r=[y2_], w=[sg_], scale=2.0 * math.sqrt(2.0 / PI))
            k.tt(gf[:, c4, :], sg_[:], yps[:, :], ALU.mult, r=[sg_, yps], w=[gf])
            k.act(gb[:, c4, :], gf[:, c4, :], AF.Copy, r=[gf], w=[gb])
            if c4 != 3:
                return
            for jc in range(4):
                pg = psb[2 + jc]
                for ic in range(4):
                    k.mm(pg[:, :], wg[:, ic, jc * 128:(jc + 1) * 128], gb[:, ic, :], start=(ic == 0), stop=(ic == 3),
                         r=[wg, gb], w=[pg])
                sg_ = sg[jc % 2]
                k.act(sg_[:], pg[:, :], AF.Sigmoid, r=[pg, dcol], w=[sg_], bias=dcol[:, jc, 1:2])
                k.tt(sg_[:], sg_[:], gf[:, jc, :], ALU.mult, r=[sg_, gf], w=[sg_])
                k.tt(yo_[:, jc, :], sg_[:], zb_[:, jc, :], ALU.mult, r=[sg_, zb_], w=[yo_])
            k.dma("sp", ysT.ap()[:, j * 512:(j + 1) * 512].rearrange("(c p) t -> p c t", p=128), yo_[:],
                  r=[yo_], w=[k.dbuf["ysT"]])

        load_blk(0)
        stA(0)
        for i in range(len(iters)):
            stB(i)
            if i + 1 < len(iters):
                stA(i + 1)
            stC(i)

    def phase_gates(l, pre):
        f = lambda name, shp, dt=F32: k.sb(name, shp, dt)
        NC = NT
        qT = f("mq", [128, 4, L], BF16)
        kTt = f("mk", [128, 4, L], BF16)
        Vp = f("mVp", [128, NT, 4, 130], BF16)
        pre.update(qT=qT, kTt=kTt, Vp=Vp, mark=k.sb_off)
        mif = f("mif_g", [8, L])
        k.dma("sp", mif[:], mifd.ap(), r=[k.dbuf["mif"]], w=[mif])
        ifb = f("ifb", [8, 1])
        k.dma("sp", ifb[:], P_["if_bias"].ap()[l].rearrange("a (h o) -> (a h) o", o=1), w=[ifb],
              allow_slow_non_contiguous=True)
        k.dma("sp", qT[:], mqT.ap().rearrange("(c p) t -> p c t", p=128), r=[k.dbuf["mqT"]], w=[qT])
        k.dma("sp", kTt[:], mkT.ap().rearrange("(c p) t -> p c t", p=128), r=[k.dbuf["mkT"]], w=[kTt])
        for n in range(NT):
            k.dma("sp", Vp[:, n, :, 0:128], mv.ap()[n * 128:(n + 1) * 128, :].rearrange("p (h d) -> p h d", h=4),
                  r=[k.dbuf["mv"]], w=[Vp])
        pre = f("pre", [8, L])
        lsg = f("lsg", [8, L])
        tmp8 = f("tmp8", [8, L])
        k.ts(pre[:], mif[:], ifb[:, 0:1], None, ALU.add, r=[mif, ifb], w=[pre])
        k.act(tmp8[:], pre[:], AF.Abs, r=[pre], w=[tmp8])
        k.act(tmp8[:], tmp8[:], AF.Exp, r=[tmp8], w=[tmp8], scale=-1.0)
        k.act(tmp8[:], tmp8[:], AF.Ln, r=[tmp8, consts], w=[tmp8], bias=consts[0:8, 3:4])
        k.ts(lsg[:], pre[:], 0.0, None, ALU.min, r=[pre], w=[lsg])
        k.tt(lsg[:], lsg[:], tmp8[:], ALU.subtract, r=[lsg, tmp8], w=[lsg])
        cs8 = tmp8
        k.scan(cs8[:], consts[0:8, 3:4].to_broadcast([8, L]), lsg[:], 0.0, ALU.mult, ALU.add, r=[consts, lsg],
               w=[cs8])
        R8 = lsg
        k.ts(R8[:], cs8[:], rowmask8[:, 1:2], None, ALU.mult, r=[cs8, rowmask8], w=[R8])
        k.stt(R8[:], pre[:], rowmask8[:, 0:1], R8[:], ALU.mult, ALU.add, r=[pre, rowmask8, R8], w=[R8])
        a4, B4 = f("a4", [4, L]), f("B4", [4, L])
        for tb in range(NB):
            p = next_ps()
            k.mm(p[0:4, :], sel8[:, 0:4], R8[:, tb * 512:(tb + 1) * 512], True, True, r=[sel8, R8], w=[p])
            k.copy(a4[:, tb * 512:(tb + 1) * 512], p[0:4, :], r=[p], w=[a4])
            p = next_ps()
            k.mm(p[0:4, :], sel8[:, 4:8], R8[:, tb * 512:(tb + 1) * 512], True, True, r=[sel8, R8], w=[p])
            k.copy(B4[:, tb * 512:(tb + 1) * 512], p[0:4, :], r=[p], w=[B4])
        cm, Mx, Mp, dd = f("cm", [4, NC]), f("Mx", [4, NC]), f("Mp", [4, NC]), f("dd", [4, NC])
        k.reduce(cm[:], a4[:, :].rearrange("p (n t) -> p n t", t=128), ALU.max, AX.X, r=[a4], w=[cm])
        k.scan(Mx[:], cm[:], cm[:], 0.0, ALU.max, ALU.max, r=[cm], w=[Mx])
        k.memset(Mp[:, 0:1], 0.0, w=[Mp])
        if NC > 1:
            k.copy(Mp[:, 1:NC], Mx[:, 0:NC - 1], r=[Mx], w=[Mp])
        k.tt(dd[:], Mp[:], Mx[:], ALU.subtract, r=[Mp, Mx], w=[dd])
        k.act(dd[:], dd[:], AF.Exp, r=[dd], w=[dd])
        Mb = Mx[:, :].unsqueeze(2).to_broadcast([4, NC, 128])
        wrow, frow = a4, B4
        k.tt(wrow[:, :].rearrange("p (n t) -> p n t", t=128), a4[:, :].rearrange("p (n t) -> p n t", t=128), Mb,
             ALU.subtract, r=[a4, Mx], w=[wrow])
        k.act(wrow[:], wrow[:], AF.Exp, r=[wrow], w=[wrow])
        k.tt(frow[:, :].rearrange("p (n t) -> p n t", t=128), B4[:, :].rearrange("p (n t) -> p n t", t=128), Mb,
             ALU.add, r=[B4, Mx], w=[frow])
        k.act(frow[:], frow[:], AF.Exp, r=[frow], w=[frow], scale=-1.0)
        for n0 in range(0, NT, 32):
            nn = min(32, NT - n0)
            p = next_ps()
            for n in range(n0, n0 + nn):
                c0 = (n - n0) * 8
                k.mm(p[:, c0:c0 + 4], wrow[0:4, n * 128:(n + 1) * 128], ident_f[0:4, 0:4], True, True,
                     r=[wrow, ident_f], w=[p], inc=False)
                k.mm(p[:, c0 + 4:c0 + 8], frow[0:4, n * 128:(n + 1) * 128], ident_f[0:4, 0:4], True, True,
                     r=[frow, ident_f], w=[p], inc=(n == n0 + nn - 1))
            k.copy(wf[:, n0:n0 + nn, :], p[:, 0:nn * 8].rearrange("p (n e) -> p n e", e=8), r=[p], w=[wf])
        for h in range(4):
            p = next_ps()
            k.mm(p[:, 0:NC], selrow4[0:4, h * 128:(h + 1) * 128], dd[0:4, :], True, True, r=[selrow4, dd], w=[p])
            k.copy(dbc[:, h, :], p[:, 0:NC], r=[p], w=[dbc])

    def phase_mlstm(l, pre):
        f = lambda name, shp, dt=F32: k.sb(name, shp, dt)
        NC = NT
        SC = 128.0 ** -0.5
        qT, kTt, Vp = pre["qT"], pre["kTt"], pre["Vp"]
        k.memset(Vp[:, :, :, 128:129], 1.0, w=[Vp], eng="dve")
        k.memset(Vp[:, :, :, 129:130], 0.0, w=[Vp], eng="dve")
        for n in range(NT):
            k.tt(Vp[:, n, :, :], Vp[:, n, :, :], wf[:, n, 0:4].unsqueeze(2).to_broadcast([128, 4, 130]), ALU.mult,
                 r=[Vp, wf], w=[Vp])
        gM = f("gM", [128, W])
        k.dma("sp", gM[:], P_["m_norm_g"].ap()[l:l + 1, :].partition_broadcast(128), w=[gM])
        Cst4 = f("Cst4", [128, 4, 130])
        Tm4 = f("Tm4", [128, 4, 130])
        Cs4 = [f("Cs4_%d" % i, [128, 4, 130], BF16) for i in range(2)]
        k.memset(Cst4[:], 0.0, w=[Cst4], eng="dve")
        Sm4 = [f("Sm4_%d" % i, [128, 4, 128], BF16) for i in range(2)]
        kN4 = [f("kN4_%d" % i, [128, 4, 128], BF16) for i in range(2)]
        puS4 = [f("puS4_%d" % i, [128, 4, 129]) for i in range(2)]
        d4 = [f("d4_%d" % i, [128, 8]) for i in range(2)]
        hmt = [f("hmt%d" % i, [128, W]) for i in range(2)]
        mot = [f("mot%d" % i, [128, W], BF16) for i in range(2)]
        mzt = [f("mzt%d" % i, [128, W], BF16) for i in range(2)]
        ws = {"ssq": f("m_ssq", [128, 4]), "sqs": f("m_sqs", [128, 128]), "yb": f("m_yb", [128, W], BF16)}
        stage = [f("mstage%d" % i, [128, 4, 512], BF16) for i in range(2)]
        cz_b = causal01[:, :].unsqueeze(1).to_broadcast([128, 4, 128])

        def stL(n):
            i2 = n % 2
            tk = slice(n * 128, (n + 1) * 128)
            psS = next_ps()
            for h in range(4):
                k.mm(psS[:, h * 128:(h + 1) * 128], kTt[:, h, tk], qT[:, h, tk], True, True, r=[kTt, qT], w=[psS],
                     inc=(h == 3))
            k.tt(Sm4[i2][:], psS[:, :].rearrange("p (h t) -> p h t", h=4), cz_b, ALU.mult, r=[psS, causal01],
                 w=[Sm4[i2]])
            pk = next_ps()
            pkv = pk.t.bitcast(BF16)
            for h in range(4):
                k.tr(pkv[:, h * 128:(h + 1) * 128], kTt[:, h, tk], ident_b[:], r=[kTt, ident_b], w=[pk],
                     inc=(h == 3))
            k.act(kN4[i2][:], pkv[:, 0:512].rearrange("p (h t) -> p h t", h=4), AF.Copy, r=[pk], w=[kN4[i2]])
            for hp in range(2):
                pu = next_ps()
                for hh in range(2):
                    h = hp * 2 + hh
                    k.mm(pu[:, hh * 129:(hh + 1) * 129], kN4[i2][:, h, :], Vp[:, n, h, 0:129], True, True,
                         r=[kN4[i2], Vp], w=[pu], inc=(hh == 1))
                k.act(puS4[i2][:, hp * 2:hp * 2 + 2, :], pu[:, 0:258].rearrange("p (h e) -> p h e", h=2), AF.Copy,
                      r=[pu], w=[puS4[i2]])

        def stR(n):
            i2 = n % 2
            tk = slice(n * 128, (n + 1) * 128)
            hm_ = hmt[n % 2]
            d_ = d4[i2]
            dsc = dbc[:, :, n:n + 1].to_broadcast([128, 4, 130])
            k.tt(Tm4[:], Cst4[:], dsc, ALU.mult, r=[Cst4, dbc], w=[Tm4])
            cs_ = Cs4[i2]
            if n > 0:
                k.act(cs_[:], Tm4[:], AF.Copy, r=[Tm4], w=[cs_])
            k.tt(Cst4[:, :, 0:129], Tm4[:, :, 0:129], puS4[i2][:], ALU.add, r=[Tm4, puS4[i2]], w=[Cst4])
            for hp in range(2):
                po = next_ps()
                for hh in range(2):
                    h = hp * 2 + hh
                    k.mm(po[:, hh * 129:(hh + 1) * 129], Sm4[i2][:, h, :], Vp[:, n, h, 0:129], True, (n == 0),
                         r=[Sm4[i2], Vp], w=[po], inc=(n == 0 and hh == 1))
                    if n > 0:
                        k.mm(po[:, hh * 129:(hh + 1) * 129], qT[:, h, tk], cs_[:, h, 0:129], False, True,
                             r=[qT, cs_], w=[po], inc=(hh == 1))
                pov = po[:, 0:258].rearrange("p (h e) -> p h e", h=2)
                k.act(d_[:, hp * 2:hp * 2 + 2], pov[:, :, 128], AF.Abs, r=[po], w=[d_], scale=SC)
                k.tt(d_[:, hp * 2:hp * 2 + 2], d_[:, hp * 2:hp * 2 + 2], wf[:, n, 4 + hp * 2:6 + hp * 2], ALU.max,
                     r=[d_, wf], w=[d_])
                k.recip(d_[:, 4 + hp * 2:6 + hp * 2], d_[:, hp * 2:hp * 2 + 2], r=[d_], w=[d_])
                k.ts(d_[:, 4 + hp * 2:6 + hp * 2], d_[:, 4 + hp * 2:6 + hp * 2], SC, None, ALU.mult, r=[d_], w=[d_])
                k.tt(hm_[:, hp * 256:(hp + 1) * 256].rearrange("p (h t) -> p h t", h=2), pov[:, :, 0:128],
                     d_[:, 4 + hp * 2:6 + hp * 2].unsqueeze(2).to_broadcast([128, 2, 128]), ALU.mult,
                     r=[po, d_], w=[hm_])

        stL(0)
        for n in range(NT):
            hm_, mo_, mz_ = hmt[n % 2], mot[n % 2], mzt[n % 2]
            k.dma("sp", mo_[:], mo.ap()[n * 128:(n + 1) * 128, :], r=[k.dbuf["mo"]], w=[mo_])
            k.dma("sp", mz_[:], mz.ap()[n * 128:(n + 1) * 128, :], r=[k.dbuf["mz"]], w=[mz_])
            if n + 1 < NT:
                stL(n + 1)
            stR(n)
            k.tt(hm_[:], hm_[:], mo_[:], ALU.mult, r=[hm_, mo_], w=[hm_])
            stg = stage[(n // 4) % 2]
            headnorm_out(hm_, gM, mz_, stg, (n % 4) * 128, ws)
            if n % 4 == 3:
                j = n // 4
                k.dma("sp", ymT.ap()[:, j * 512:(j + 1) * 512].rearrange("(c p) t -> p c t", p=128), stg[:],
                      r=[stg], w=[k.dbuf["ymT"]])

    def phase_attn(l):
        pre_ = {}
        for nm_, shp_, dt_ in (("aq", [128, 4, L], BF16), ("ak", [128, 4, L], BF16), ("aVp", [128, NT, 4, 130], BF16),
                               ("strip0", [128, 1024], F32), ("strip1", [128, 1024], F32),
                               ("strip2", [128, 1024], F32), ("strip3", [128, 1024], F32), ("b31bc", [128, 4], F32),
                               ("negMb", [128, 2, 4], F32), ("farb", [128, 2, 4], F32), ("neglam", [128, 1], F32),
                               ("gAcol", [128, 4, 1], F32), ("ones_b", [128, 128], BF16)):
            pre_[nm_] = k.sb(nm_, shp_, dt_)
        mark_a = k.sb_off
        f = lambda name, shp, dt=F32: pre_[name] if name in pre_ else k.sb(name, shp, dt)
        lam_init = 0.8 - 0.6 * math.exp(-0.3 * l)
        qT = f("aq", [128, 4, L], BF16)
        kTt = f("ak", [128, 4, L], BF16)
        Vp = f("aVp", [128, NT, 4, 130], BF16)
        k.dma("sp", qT[:], aqT.ap().rearrange("(c p) t -> p c t", p=128), r=[k.dbuf["aqT"]], w=[qT])
        k.dma("sp", kTt[:], akT.ap().rearrange("(c p) t -> p c t", p=128), r=[k.dbuf["akT"]], w=[kTt])
        vstg = f("vstg_a", [128, NT, W], BF16)
        for n0 in range(0, NT, 8):
            nn = min(8, NT - n0)
            k.dma("sp", vstg[:, n0:n0 + nn, :],
                  av.ap()[n0 * 128:(n0 + nn) * 128, :].rearrange("(n p) c -> p n c", p=128),
                  r=[k.dbuf["av"]], w=[vstg])
        k.copy(Vp[:, :, :, 0:128], vstg[:, :, :].rearrange("p n (h d) -> p n h d", h=4), r=[vstg], w=[Vp])
        k.memset(Vp[:, :, :, 128:129], 1.0, w=[Vp], eng="dve")
        k.memset(Vp[:, :, :, 129:130], 0.0, w=[Vp], eng="dve")
        strips = [f("strip%d" % h, [128, 1024]) for h in range(4)]
        b31bc = f("b31bc", [128, 4])
        antiI = f("antiI", [128, 128])
        k.dma("sp", antiI[:], C_["antiI"].ap(), w=[antiI])
        srev = [f("srev%d" % i, [128, 1024]) for i in range(2)]
        for h in range(4):
            sr = srev[h % 2]
            k.dma("sp", sr[:], bass.AP(Gd, h * 1152, [[1, 128], [1, 1024]]), r=[k.dbuf["Gd"]], w=[sr])
            for hf in range(2):
                p = next_ps()
                k.mm(p[:, :], antiI[:, :], sr[:, hf * 512:(hf + 1) * 512], True, True, r=[antiI, sr], w=[p])
                k.copy(strips[h][:, hf * 512:(hf + 1) * 512], p[:, :], r=[p], w=[strips[h]])
        k.dma("sp", b31bc[:], P_["rel_bias"].ap()[31:32, :].partition_broadcast(128), w=[b31bc])
        sqb = [f("sqb%d" % i, [128, 512], BF16) for i in range(2)]
        blkmax = f("blkmax", [2, NB])
        stat = f("stat", [2, 8])
        ii = 0
        for qi, src in enumerate((qT, kTt)):
            for h in range(4):
                for tb in range(NB):
                    s_ = sqb[ii % 2]
                    ii += 1
                    k.act(s_[:], src[:, h, tb * 512:(tb + 1) * 512], AF.Square, r=[src], w=[s_])
                    p = next_ps()
                    k.mm(p[0:2, :], blockones_b[:, 0:2], s_[:], True, True, r=[blockones_b, s_], w=[p])
                    k.reduce(blkmax[:, tb:tb + 1], p[0:2, :], ALU.max, AX.X, r=[p], w=[blkmax])
                k.reduce(stat[:, qi * 4 + h:qi * 4 + h + 1], blkmax[:, :], ALU.max, AX.X, r=[blkmax], w=[stat])
        M2 = f("M2", [2, 4])
        k.tt(M2[:], stat[:, 0:4], stat[:, 4:8], ALU.mult, r=[stat], w=[M2])
        k.act(M2[:], M2[:], AF.Sqrt, r=[M2], w=[M2])
        k.ts(M2[:], M2[:], -1.05, -2.0, ALU.mult, ALU.add, r=[M2], w=[M2])
        negMb = f("negMb", [128, 2, 4])
        farb = f("farb", [128, 2, 4])
        selc = f("selc", [2, 2, 128])
        k.memset(selc[:], 0.0, w=[selc])
        for c in range(2):
            k.ts(selc[:, c, :], ident_f[0:2, c:c + 1].to_broadcast([2, 128]), 1.0, None, ALU.mult, r=[ident_f, selc],
                 w=[selc])
        for c in range(2):
            p = next_ps()
            k.mm(p[:, 0:4], selc[0:2, c, :], M2[0:2, 0:4], True, True, r=[selc, M2], w=[p])
            k.copy(negMb[:, c, :], p[:, 0:4], r=[p], w=[negMb])
            k.tt(farb[:, c, :], negMb[:, c, :], b31bc[:, :], ALU.add, r=[negMb, b31bc], w=[farb])
        dl = f("dl", [1, 256])
        k.dma("sp", dl[:], P_["diff_lam"].ap()[l:l + 1].rearrange("o a d -> o (a d)"), w=[dl])
        pr = f("pr", [1, 128])
        s12 = f("s12", [1, 2])
        k.tt(pr[:, 0:64], dl[:, 0:64], dl[:, 64:128], ALU.mult, r=[dl], w=[pr])
        k.tt(pr[:, 64:128], dl[:, 128:192], dl[:, 192:256], ALU.mult, r=[dl], w=[pr])
        k.reduce(s12[:], pr[:, :].rearrange("p (a d) -> p a d", a=2), ALU.add, AX.X, r=[pr], w=[s12])
        k.act(s12[:], s12[:], AF.Exp, r=[s12], w=[s12])
        lamv = f("lamv", [1, 1])
        k.tt(lamv[:], s12[:, 1:2], s12[:, 0:1], ALU.subtract, r=[s12], w=[lamv])
        k.ts(lamv[:], lamv[:], -lam_init, None, ALU.add, r=[lamv], w=[lamv])
        neglam = f("neglam", [128, 1])
        p = next_ps()
        k.mm(p[:, 0:1], ones_row[0:1, :], lamv[0:1, 0:1], True, True, r=[ones_row, lamv], w=[p])
        k.copy(neglam[:], p[:, 0:1], r=[p], w=[neglam])
        garow = f("garow", [1, W])
        gAcol = f("gAcol", [128, 4, 1])
        k.dma("sp", garow[:], P_["diff_norm_g"].ap()[l:l + 1, :], w=[garow])
        rows_to_cols(garow, 1, 4, gAcol[:, :, :], gAcol)
        k.ts(gAcol[:], gAcol[:], 1.0 - lam_init, None, ALU.mult, r=[gAcol], w=[gAcol])
        ones_b = f("ones_b", [128, 128], BF16)
        k.memset(ones_b[:], 1.0, w=[ones_b], eng="dve")
        k.S.barrier()
        k.sb_off = mark_a
        Pt = [f("Pt%d" % i, [128, 512], BF16) for i in range(3)]
        tmpb = [f("tmpb%d" % i, [128, 512]) for i in range(2)]
        rsb = [f("rsb%d" % i, [128, 512]) for i in range(2)]
        o0 = [f("o0_%d" % i, [128, 512]) for i in range(2)]
        t1 = [f("at1_%d" % i, [128, 512]) for i in range(2)]
        haT = [f("haT%d" % i, [128, 512]) for i in range(2)]
        sqb_ = [f("asq%d" % i, [128, 512], BF16) for i in range(2)]
        rstd = [f("arstd%d" % i, [128, 512]) for i in range(2)]
        azt = [f("azt%d" % i, [128, 512], BF16) for i in range(2)]
        stage = [f("astage%d" % i, [128, 512], BF16) for i in range(2)]
        qz = [[f("qz%d_%d" % (i, c), [128, 512], BF16) for c in range(2)] for i in range(2)]
        for i in range(2):
            for c in range(2):
                k.memset(qz[i][c][:], 0.0, w=[qz[i][c]], eng="dve")
        it = 0
        gh = 0
        ic = 0
        pending = []
        my_ep = None
        for g in range(NB):
            for h in range(4):
                qz_ = qz[gh % 2]
                az_ = azt[gh % 2]
                ha_ = haT[gh % 2]
                k.dma("sp", az_[:], azT.ap()[h * 128:(h + 1) * 128, g * 512:(g + 1) * 512], r=[k.dbuf["azT"]],
                      w=[az_])
                for c in range(2):
                    k.copy(qz_[c][c * 64:(c + 1) * 64, :], qT[c * 64:(c + 1) * 64, h, g * 512:(g + 1) * 512],
                           r=[qT], w=[qz_[c]], eng="pool")
                for c in range(2):
                    nkb = 4 * g + 4
                    DEPTH_PF = 2
                    base = it
                    pss = {}
                    accV, accS = psb[(ic % 2) * 2], psb[(ic % 2) * 2 + 1]
                    ic += 1

                    def issue_qk(kb):
                        ps = psb[4 + (base + kb) % 4]
                        pss[kb] = ps
                        k.mm(ps[:, :], kTt[:, h, kb * 128:(kb + 1) * 128], qz_[c][:, :], True, True,
                             r=[kTt, qz_[c]], w=[ps])

                    for kb in range(min(DEPTH_PF, nkb)):
                        issue_qk(kb)
                    for kb in range(nkb):
                        it += 1
                        if kb + DEPTH_PF < nkb:
                            issue_qk(kb + DEPTH_PF)
                        if c == 0 and kb == min(3, nkb - 1) and pending and pending[0] is not my_ep:
                            pending.pop(0)()
                        ps = pss.pop(kb)
                        P_t = Pt[it % 3]
                        j = kb - (4 * g - 1)
                        c0 = 0
                        if j >= 0:
                            tb_ = tmpb[it % 2]
                            c0 = max(j - 1, 0) * 128
                            k.tt(tb_[:, c0:512], ps[:, c0:512], strips[h][:, (4 - j) * 128 + c0:(4 - j) * 128 + 512],
                                 ALU.add, r=[ps, strips[h]], w=[tb_])
                            k.act(P_t[:, c0:512], tb_[:, c0:512], AF.Exp, r=[tb_, negMb], w=[P_t],
                                  bias=negMb[:, c, h:h + 1])
                        else:
                            k.act(P_t[:], ps[:, :], AF.Exp, r=[ps, farb], w=[P_t], bias=farb[:, c, h:h + 1])
                        last = (kb == nkb - 1)
                        k.mm(accV[:, c0:512], Vp[:, kb, h, 0:128], P_t[:, c0:512], start=(kb == 0), stop=last,
                             r=[Vp, P_t], w=[accV], inc=last)
                        k.mm(accS[:, c0:512], ones_b[:, :], P_t[:, c0:512], start=(kb == 0), stop=last,
                             r=[ones_b, P_t], w=[accS], inc=True)
                    rs_ = rsb[ic % 2]
                    k.recip(rs_[:], accS[:, :], r=[accS], w=[rs_])
                    if c == 0:
                        o0_ = o0[gh % 2]
                        k.tt(o0_[:], accV[:, :], rs_[:], ALU.mult, r=[accV, rs_], w=[o0_])
                    else:
                        t_ = t1[gh % 2]
                        k.tt(t_[:], accV[:, :], rs_[:], ALU.mult, r=[accV, rs_], w=[t_])
                        k.stt(ha_[:], t_[:], neglam[:, 0:1], o0_[:], ALU.mult, ALU.add, r=[t_, neglam, o0_], w=[ha_])
                def epilogue(ha_=ha_, az_=az_, h=h, g=g, ghl=gh):
                    sq_ = sqb_[ghl % 2]
                    rd_ = rstd[ghl % 2]
                    stg = stage[ghl % 2]
                    k.act(sq_[:], ha_[:], AF.Square, r=[ha_], w=[sq_])
                    pn = psb[4 + (it + 2) % 4]
                    k.mm(pn[:, :], ones_b[:, :], sq_[:], True, True, r=[ones_b, sq_], w=[pn])
                    k.act(rd_[:], pn[:, :], AF.Ln, r=[pn, consts], w=[rd_], scale=1.0 / 128, bias=epsb)
                    k.act(rd_[:], rd_[:], AF.Exp, r=[rd_], w=[rd_], scale=-0.5)
                    k.tt(ha_[:], ha_[:], rd_[:], ALU.mult, r=[ha_, rd_], w=[ha_])
                    k.stt(stg[:], ha_[:], gAcol[:, h, 0:1], az_[:], ALU.mult, ALU.mult, r=[ha_, gAcol, az_], w=[stg])
                    k.dma("sp", yaT.ap()[h * 128:(h + 1) * 128, g * 512:(g + 1) * 512], stg[:], r=[stg],
                          w=[k.dbuf["yaT"]])

                pending.append(epilogue)
                gh += 1
        while pending:
            pending.pop(0)()

    def phase_merge(l):
        f = lambda name, shp, dt=F32: k.sb(name, shp, dt)
        last = (l == depth - 1)
        norm_alloc()
        wbr = f("wbr", [128, 3, 4, 1024], BF16)
        wo = f("wo", [128, 8, 1024], BF16)
        mark_m = k.sb_off
        wbs = [f("wbs%d" % i, [128, 4, 1024]) for i in range(2)]
        for b in range(3):
            s_ = wbs[b % 2]
            k.dma("sp", s_[:], P_["w_branch"].ap()[l, b].rearrange("(kc p) c -> p kc c", p=128), w=[s_])
            k.copy(wbr[:, b, :, :], s_[:], r=[s_], w=[wbr])
        for hf in range(2):
            s_ = wbs[(3 + hf) % 2]
            k.dma("sp", s_[:], P_["w_out"].ap()[l, hf * 512:(hf + 1) * 512, :].rearrange("(kc p) c -> p kc c", p=128),
                  w=[s_])
            k.copy(wo[:, hf * 4:(hf + 1) * 4, :], s_[:], r=[s_], w=[wo])
        if last:
            k.dma("sp", gbc[:], P_["final_g"].ap().rearrange("(o d) -> o d", o=1).partition_broadcast(128), w=[gbc])
        else:
            k.dma("sp", gbc[:], norm_g.ap()[l + 1:l + 2, :].partition_broadcast(128), w=[gbc])
        k.S.barrier()
        k.sb_off = mark_m
        yb_ = [[f("y%d_%d" % (b, i), [128, 4, 512], BF16) for i in range(2)] for b in range(3)]
        gt = [f("gt%d" % i, [128, 512], BF16) for i in range(6)]
        m1 = [f("m1_%d" % i, [128, 512]) for i in range(2)]
        m2 = [f("m2_%d" % i, [128, 512]) for i in range(2)]
        mT = [f("mT%d" % i, [128, 8, 512], BF16) for i in range(2)]
        xo = [f("xo%d" % i, [128, D_MODEL]) for i in range(2)]
        xn = [f("xn%d" % i, [128, D_MODEL]) for i in range(2)]
        x_src = x_in if l == 0 else xs_d
        srcs = (ysT, ymT, yaT)
        st_ = {"gi": 0}

        def load_y(j):
            for b in range(3):
                k.dma("sp", yb_[b][j % 2][:],
                      srcs[b].ap()[:, j * 512:(j + 1) * 512].rearrange("(c p) t -> p c t", p=128),
                      r=[k.dbuf[srcs[b].name]], w=[yb_[b][j % 2]])

        def branch(j):
            ys_ = [yb_[b][j % 2] for b in range(3)]
            mT_ = mT[j % 2]
            for dc in range(8):
                m1_, m2_ = m1[dc % 2], m2[dc % 2]
                for b in range(3):
                    g_ = gt[st_["gi"] % 6]
                    st_["gi"] += 1
                    r0 = b * 1024 + dc * 128
                    k.dma("sp", g_[:], gT.ap()[r0:r0 + 128, j * 512:(j + 1) * 512], r=[k.dbuf["gT"]], w=[g_])
                    p = next_ps()
                    for cc in range(4):
                        k.mm(p[:, :], wbr[:, b, cc, dc * 128:(dc + 1) * 128], ys_[b][:, cc, :], start=(cc == 0),
                             stop=(cc == 3), r=[wbr, ys_[b]], w=[p])
                    if b == 0:
                        k.tt(m1_[:], p[:, :], g_[:], ALU.mult, r=[p, g_], w=[m1_])
                    else:
                        k.tt(m2_[:], p[:, :], g_[:], ALU.mult, r=[p, g_], w=[m2_])
                        if b == 1:
                            k.tt(m1_[:], m1_[:], m2_[:], ALU.add, r=[m1_, m2_], w=[m1_], eng="pool")
                        else:
                            k.tt(mT_[:, dc, :], m1_[:], m2_[:], ALU.add, r=[m1_, m2_], w=[mT_])

        def outproj(j):
            mT_ = mT[j % 2]
            for t4 in range(4):
                n = j * 4 + t4
                xo_, xn_ = xo[n % 2], xn[n % 2]
                k.dma("sp", xo_[:], x_src.ap()[n * 128:(n + 1) * 128, :], r=[k.dbuf[x_src.name]], w=[xo_])
                for hf in range(2):
                    p = next_ps()
                    for dc in range(8):
                        k.mm(p[:, :], mT_[:, dc, t4 * 128:(t4 + 1) * 128], wo[:, dc, hf * 512:(hf + 1) * 512],
                             start=(dc == 0), stop=(dc == 7), r=[mT_, wo], w=[p])
                    k.tt(xn_[:, hf * 512:(hf + 1) * 512], p[:, :], xo_[:, hf * 512:(hf + 1) * 512], ALU.add,
                         r=[p, xo_], w=[xn_])
                if last:
                    r_, _ = norm_rstd(xn_)
                    k.stt(xo_[:], xn_[:], r_[:, 0:1], gbc[:], ALU.mult, ALU.mult, r=[xn_, r_, gbc], w=[xo_])
                    k.dma("sp", out_d.ap()[n * 128:(n + 1) * 128, :], xo_[:], r=[xo_], w=[k.dbuf["out"]])
                else:
                    k.dma("sp", xs_d.ap()[n * 128:(n + 1) * 128, :], xn_[:], r=[xn_], w=[k.dbuf["xs"]])
                    norm_transpose(xn_, n * 128)

        load_y(0)
        branch(0)
        for j in range(NB):
            if j + 1 < NB:
                load_y(j + 1)
                branch(j + 1)
            outproj(j)

    for l in range(depth):
        if "proj" in phases:
            phase_proj(l)
        k.phase_reset(keep_hT=False)
        if "mlstm" in phases:
            pre_m = {}
            phase_gates(l, pre_m)
            k.S.barrier()
            k.sb_off = pre_m["mark"]
            phase_mlstm(l, pre_m)
        k.phase_reset(keep_hT=False)
        if "s5" in phases:
            phase_s5(l)
        k.phase_reset(keep_hT=False)
        if "attn" in phases:
            phase_attn(l)
        k.phase_reset(keep_hT=True)
        if "merge" in phases:
            phase_merge(l)
        k.phase_reset(keep_hT=True)

    k.S.finish("sp")
    k.S.emit()
    return k


_CACHE = {}


def kernel(**inputs):
    L, depth = SEQ, DEPTH
    if "k" not in _CACHE:
        _CACHE["k"] = build(L, depth)
    kb = _CACHE["k"]
    consts = host_consts()
    x = np.ascontiguousarray(inputs["x"], dtype=np.float32)
    shared = {n: np.ascontiguousarray(inputs[n], dtype=np.float32) for n in PARAM_SHAPES(depth)}
    shared.update(consts)
    in_maps = []
    for c in range(NCORES):
        m = dict(shared)
        m["x"] = x[c]
        in_maps.append(m)
    res = run_bass_kernel_spmd(kb.nc, in_maps, core_ids=list(range(NCORES)))
    return np.stack([np.asarray(r["out"], dtype=np.float32) for r in res.results], axis=0)
```

```python
import math
from contextlib import ExitStack
import numpy as np
import concourse.bass as bass
import concourse.mybir as mybir
from concourse.bass_utils import run_bass_kernel_spmd

F32 = mybir.dt.float32
BF16 = mybir.dt.bfloat16
I32 = mybir.dt.int32
ALU = mybir.AluOpType
AF = mybir.ActivationFunctionType
AX = mybir.AxisListType

D_MODEL = 1024
SEQ = 4096
DEPTH = 2
W = 512
D_IN = 8712
NCORES = 8
EPS = 1e-6


class Buf:
    __slots__ = ("name", "w", "r")

    def __init__(self, name=""):
        self.name = name
        self.w = None
        self.r = {}


class Sched:
    ENGS = (("pe", "tensor"), ("dve", "vector"), ("act", "scalar"), ("pool", "gpsimd"), ("sp", "sync"))
    NSLOT = 8

    def __init__(self, nc):
        self.nc = nc
        self.ops = {e: [] for e, _ in self.ENGS}
        self.cnt = {}
        self.seen = {e: {} for e, _ in self.ENGS}
        self.dma_i = {e: 0 for e, _ in self.ENGS}

    def _waits(self, eng, r, w, extra=None):
        deps = dict(extra or {})

        def add(k, v):
            if v > deps.get(k, 0):
                deps[k] = v

        for b in r:
            if b.w is not None:
                add(*b.w)
        for b in w:
            if b.w is not None and b.w[0] != eng:
                add(*b.w)
            for k, v in b.r.items():
                if k != eng:
                    add(k, v)
        waits = []
        seen = self.seen[eng]
        for k, v in deps.items():
            if k == "pe" and eng == "pe":
                continue
            if seen.get(k, 0) >= v:
                continue
            seen[k] = v
            waits.append((k, v))
        return waits

    def op(self, eng, fn, r=(), w=(), inc=True):
        waits = self._waits(eng, r, w)
        n = self.cnt.get(eng, 0) + 1
        if inc:
            self.cnt[eng] = n
        self.ops[eng].append((waits, fn, (eng, 1, n) if inc else None))
        for b in r:
            if b.r.get(eng, 0) < n:
                b.r[eng] = n
        for b in w:
            b.w = (eng, n)
            b.r = {}

    def dma(self, q, out, in_, r=(), w=(), **kw):
        i = self.dma_i[q]
        self.dma_i[q] += 1
        key = ("dma", q, i % self.NSLOT)
        n = self.cnt.get(key, 0) + 16
        extra = {key: n - 16} if n > 16 else None
        waits = self._waits(q, r, w, extra)
        self.cnt[key] = n
        self.ops[q].append((waits, lambda e: e.dma_start(out=out, in_=in_, **kw), (key, 16, n)))
        for b in r:
            b.r[key] = n
        for b in w:
            b.w = (key, n)
            b.r = {}

    def barrier(self):
        snap = dict(self.cnt)
        for eng, _ in self.ENGS:
            waits = [(k, v) for k, v in snap.items() if self.seen[eng].get(k, 0) < v and k != eng]
            for k, v in waits:
                self.seen[eng][k] = v
            if waits:
                self.ops[eng].append((waits, None, None))

    def finish(self, q="sp"):
        waits = [(k, v) for k, v in self.cnt.items() if self.seen[q].get(k, 0) < v]
        self.ops[q].append((waits, None, None))

    def emit(self):
        nc = self.nc
        targets = {}
        for eng, _ in self.ENGS:
            for waits, fn, inc in self.ops[eng]:
                for wk, wv in waits:
                    targets.setdefault(wk, set()).add(wv)
        rank = {}
        for key, vs in targets.items():
            step = 16 if isinstance(key, tuple) else 1
            rank[key] = {v: (i + 1) * step for i, v in enumerate(sorted(vs))}
        with ExitStack() as st:
            sems = {}
            for i, key in enumerate(self.cnt):
                sems[key] = st.enter_context(nc.semaphore("s%d" % i))
            block = st.enter_context(nc.Block())
            for eng, attr in self.ENGS:
                ops = self.ops[eng]
                if not ops:
                    continue

                def body(e, ops=ops):
                    for waits, fn, inc in ops:
                        for wk, wv in waits:
                            e.wait_ge(sems[wk], rank[wk][wv])
                        if fn is not None:
                            ins = fn(e)
                            if inc is not None and inc[2] in targets.get(inc[0], ()):
                                ins.then_inc(sems[inc[0]], inc[1])

                getattr(block, attr)(body)


class T:
    def __init__(self, t, name):
        self.t = t
        self.b = Buf(name)

    def __getitem__(self, k):
        return self.t[k]


class K:
    def __init__(self, L, depth, dbg=False):
        self.L = L
        self.depth = depth
        self.dbg = dbg
        self.nc = bass.Bass("TRN2", target_bir_lowering=False)
        self.S = Sched(self.nc)
        self.dram = {}
        self.dbuf = {}
        self.nps = 0
        self.sb_off = 16640
        self.sb_n = 0
        self.sb_mark = 0

    def sb(self, name, shape, dt=F32):
        nbytes = int(np.prod(shape[1:])) * (2 if dt == BF16 else 4)
        off = (self.sb_off + 63) // 64 * 64
        self.sb_off = off + nbytes
        assert self.sb_off <= 16640 + 206 * 1024, ("SBUF overflow", name, self.sb_off)
        self.sb_n += 1
        nm = "%s_%d" % (name, self.sb_n)
        return T(self.nc.alloc_sbuf_tensor_at(nm, list(shape), dt, offset=off), nm)

    def phase_mark(self):
        self.sb_mark = self.sb_off

    def phase_reset(self, keep_hT=True):
        self.S.barrier()
        self.sb_off = self.sb_mark if keep_hT else self.sb_mark0

    def ps(self, name, shape=(128, 512), dt=F32):
        return T(self.nc.alloc_psum_tensor(name, list(shape), dt), name)

    def din(self, name, shape, dt=F32):
        t = self.nc.dram_tensor(name, list(shape), dt, kind="ExternalInput")
        self.dram[name] = t
        self.dbuf[name] = Buf(name)
        return t

    def dscratch(self, name, shape, dt=F32, out=False):
        kind = "ExternalOutput" if (out or self.dbg) else "Internal"
        t = self.nc.dram_tensor(name, list(shape), dt, kind=kind)
        self.dram[name] = t
        self.dbuf[name] = Buf(name)
        return t

    def op(self, eng, fn, r=(), w=(), inc=True):
        self.S.op(eng, fn, [x.b if isinstance(x, T) else x for x in r],
                  [x.b if isinstance(x, T) else x for x in w], inc)

    def dma(self, q, out, in_, r=(), w=(), **kw):
        self.S.dma(q, out, in_, [x.b if isinstance(x, T) else x for x in r],
                   [x.b if isinstance(x, T) else x for x in w], **kw)

    def mm(self, out, lhsT, rhs, start, stop, r=(), w=(), inc=None):
        if inc is None:
            inc = stop
        self.op("pe", lambda e: e.matmul(out, lhsT, rhs, start=start, stop=stop), r=r, w=w, inc=inc)

    def act(self, out, in_, func, r=(), w=(), bias=None, scale=1.0, accum=None, eng="act"):
        kw = {}
        if bias is not None:
            kw["bias"] = bias
        if accum is not None:
            kw["accum_out"] = accum
        self.op(eng, lambda e: e.activation(out=out, in_=in_, func=func, scale=scale, **kw), r=r, w=w)

    def tt(self, out, in0, in1, op, r=(), w=(), eng="dve"):
        self.op(eng, lambda e: e.tensor_tensor(out=out, in0=in0, in1=in1, op=op), r=r, w=w)

    def ts(self, out, in0, s1, s2, op0, op1=None, r=(), w=(), eng="dve", accum=None):
        kw = {}
        if op1 is not None:
            kw["op1"] = op1
        if accum is not None:
            kw["accum_out"] = accum
        self.op(eng, lambda e: e.tensor_scalar(out=out, in0=in0, scalar1=s1, scalar2=s2, op0=op0, **kw), r=r, w=w)

    def stt(self, out, in0, scalar, in1, op0, op1, r=(), w=(), eng="dve"):
        self.op(eng, lambda e: e.scalar_tensor_tensor(out=out, in0=in0, scalar=scalar, in1=in1, op0=op0, op1=op1),
                r=r, w=w)

    def copy(self, out, in_, r=(), w=(), eng="dve"):
        self.op(eng, lambda e: e.tensor_copy(out=out, in_=in_), r=r, w=w)

    def memset(self, ap, val, w=(), eng="pool"):
        self.op(eng, lambda e: e.memset(ap, val), w=w)

    def scan(self, out, d0, d1, init, op0, op1, r=(), w=(), eng="dve"):
        self.op(eng, lambda e: e.tensor_tensor_scan(out=out, data0=d0, data1=d1, initial=init, op0=op0, op1=op1),
                r=r, w=w)

    def reduce(self, out, in_, op, axis, r=(), w=(), eng="dve"):
        self.op(eng, lambda e: e.tensor_reduce(out=out, in_=in_, axis=axis, op=op), r=r, w=w)

    def recip(self, out, in_, r=(), w=(), eng="dve"):
        self.op(eng, lambda e: e.reciprocal(out=out, in_=in_), r=r, w=w)

    def tr(self, out, in_, ident, r=(), w=(), inc=True):
        self.op("pe", lambda e: e.transpose(out, in_, ident), r=r, w=w, inc=inc)


def host_consts():
    c = {}
    c["ident"] = np.eye(128, dtype=np.float32)
    c["iota1"] = np.tile(np.arange(1, 513, dtype=np.float32)[None, :], (128, 1))
    s_ = np.arange(128)
    c["causal01"] = (s_[:, None] <= s_[None, :]).astype(np.float32)
    c["chmask"] = (s_[:, None] // 16 == np.arange(8)[None, :]).astype(np.float32)
    c["hmask"] = (s_[:, None] // 64 == np.arange(2)[None, :]).astype(np.float32)
    sel8 = np.zeros((8, 8), np.float32)
    for h in range(4):
        sel8[h, h] = 1.0
        sel8[4 + h, h] = -1.0
        sel8[4 + h, 4 + h] = 1.0
    c["sel8"] = sel8
    rm = np.zeros((8, 2), np.float32)
    rm[0:4, 0] = 1.0
    rm[4:8, 1] = 1.0
    c["rowmask8"] = rm
    sr4 = np.zeros((4, 4 * 128), np.float32)
    for h in range(4):
        sr4[h, h * 128:(h + 1) * 128] = 1.0
    c["selrow4"] = sr4
    bo = np.zeros((128, 2), np.float32)
    bo[0:64, 0] = 1.0
    bo[64:128, 1] = 1.0
    c["blockones"] = bo
    e = np.arange(1152)
    d = e - 511
    n = np.maximum(d, 0)
    large = 16 + (np.log(np.maximum(n, 1).astype(np.float32) / np.float32(16)) / np.float32(math.log(128 / 16))
                  * np.float32(16)).astype(np.int32)
    large = np.minimum(large, 31)
    bucket = np.where(n < 16, n, large)
    oh = np.zeros((32, 1152), np.float32)
    oh[bucket, e] = 1.0
    oh[:, d < 0] = 0.0
    c["oh"] = oh
    c["maskvec"] = np.where(d < 0, -30000.0, 0.0).astype(np.float32)[None, :]
    c["ones_row"] = np.ones((1, 128), np.float32)
    c["antiI"] = np.ascontiguousarray(np.eye(128, dtype=np.float32)[::-1])
    return c


CONST_SHAPES = {"ident": [128, 128], "iota1": [128, 512], "causal01": [128, 128], "chmask": [128, 8],
                "hmask": [128, 2], "sel8": [8, 8], "rowmask8": [8, 2], "selrow4": [4, 512],
                "blockones": [128, 2], "oh": [32, 1152], "maskvec": [1, 1152], "ones_row": [1, 128], "antiI": [128, 128]}

PARAM_SHAPES = lambda depth: {
    "norm_g": [depth, 1024], "w_in": [depth, 1024, D_IN], "conv_w": [depth, 4, 1024], "conv_b": [depth, 1024],
    "if_bias": [depth, 2, 4], "m_norm_g": [depth, 512], "ssm_lam_re": [depth, 32, 64],
    "ssm_lam_im": [depth, 32, 64], "ssm_b_re": [depth, 32, 64, 16], "ssm_b_im": [depth, 32, 64, 16],
    "ssm_c_re": [depth, 32, 16, 64], "ssm_c_im": [depth, 32, 16, 64], "ssm_d": [depth, 512],
    "ssm_log_step": [depth, 32], "ssm_w_glu": [depth, 512, 512], "ssm_b_glu": [depth, 512],
    "diff_lam": [depth, 4, 64], "diff_norm_g": [depth, 512], "rel_bias": [32, 4],
    "w_branch": [depth, 3, 512, 1024], "w_out": [depth, 1024, 1024], "final_g": [1024]}

PI = math.pi


def build(L=SEQ, depth=DEPTH, dbg=False, phases=("proj", "s5", "mlstm", "attn", "merge")):
    k = K(L, depth, dbg)
    nc = k.nc
    NT = L // 128
    NB = L // 512
    x_in = k.din("x", [L, D_MODEL])
    P_ = {n: k.din(n, shp) for n, shp in PARAM_SHAPES(depth).items()}
    C_ = {n: k.din(n, shp) for n, shp in CONST_SHAPES.items()}
    norm_g, w_in, conv_w, conv_b = P_["norm_g"], P_["w_in"], P_["conv_w"], P_["conv_b"]
    out_d = k.dscratch("out", [L, D_MODEL], F32, out=True)
    xs_d = k.dscratch("xs", [L, D_MODEL], F32)

    suT = k.dscratch("suT", [W, L], BF16)
    szT = k.dscratch("szT", [W, L], BF16)
    mqT = k.dscratch("mqT", [W, L], BF16)
    mkT = k.dscratch("mkT", [W, L], BF16)
    mv = k.dscratch("mv", [L, W], BF16)
    mo = k.dscratch("mo", [L, W], BF16)
    mz = k.dscratch("mz", [L, W], BF16)
    aqT = k.dscratch("aqT", [W, L], BF16)
    akT = k.dscratch("akT", [W, L], BF16)
    av = k.dscratch("av", [L, W], BF16)
    azT = k.dscratch("azT", [W, L], BF16)
    gT = k.dscratch("gT", [3 * D_MODEL, L], BF16)
    mifd = k.dscratch("mif", [8, L], F32)
    ysT = k.dscratch("ysT", [W, L], BF16)
    ymT = k.dscratch("ymT", [W, L], BF16)
    yaT = k.dscratch("yaT", [W, L], BF16)
    Gd = k.dscratch("Gd", [4, 1152], F32)

    def cload(name, dt=F32, parts=None):
        shp = CONST_SHAPES[name]
        t = k.sb(name, shp, F32)
        k.dma("sp", t[:], C_[name].ap(), w=[t])
        if dt == BF16:
            tb = k.sb(name + "_b", shp, BF16)
            k.copy(tb[:], t[:], r=[t], w=[tb])
            return t, tb
        return t

    ident_f, ident_b = cload("ident", BF16)
    causal01 = cload("causal01")
    chmask = cload("chmask")
    hmask = cload("hmask")
    sel8 = cload("sel8")
    rowmask8 = cload("rowmask8")
    selrow4 = cload("selrow4")
    blockones_f, blockones_b = cload("blockones", BF16)
    ones_row = cload("ones_row")
    gbc = k.sb("gbc", [128, D_MODEL], F32)
    wf = k.sb("wf", [128, NT, 8])
    dbc = k.sb("dbc", [128, 4, NT])
    consts = k.sb("consts", [128, 8], F32)
    CV = [EPS, -PI, 0.0, 1.0, 0.5 * PI]
    for i, v in enumerate(CV):
        k.memset(consts[:, i:i + 1], v, w=[consts])
    epsb = consts[:, 0:1]
    negpi = consts[:, 1:2]
    k.sb_mark0 = k.sb_off
    hT = k.sb("hT", [128, 8, L], BF16)
    psb = [k.ps("ps%d" % i) for i in range(8)]

    def next_ps():
        p = psb[k.nps % 8]
        k.nps += 1
        return p

    def rows_to_cols(src, R, nchunk, dst_ap, dstT):
        p = next_ps()
        for c in range(nchunk):
            k.mm(p[:, c * R:(c + 1) * R], src[0:R, c * 128:(c + 1) * 128], ident_f[0:R, 0:R],
                 start=True, stop=True, r=[src, ident_f], w=[p], inc=(c == nchunk - 1))
        k.copy(dst_ap, p[:, 0:nchunk * R].rearrange("p (c r) -> p c r", r=R), r=[p], w=[dstT])

    nrm = {}

    def norm_alloc():
        nrm["hb"] = [k.sb("hb%d" % i, [128, D_MODEL], BF16) for i in range(2)]
        nrm["sq"] = k.sb("sq", [128, D_MODEL], F32)
        nrm["ss"] = [k.sb("ss%d" % i, [128, 1], F32) for i in range(2)]
        nrm["rstd"] = [k.sb("rstd%d" % i, [128, 1], F32) for i in range(2)]
        nrm["i"] = 0

    def norm_rstd(xtile):
        i = nrm["i"]
        nrm["i"] += 1
        s_, r_ = nrm["ss"][i % 2], nrm["rstd"][i % 2]
        sq = nrm["sq"]
        k.act(sq[:], xtile[:], AF.Square, r=[xtile], w=[sq, s_], accum=s_[:])
        k.act(r_[:], s_[:], AF.Sqrt, r=[s_, consts], w=[r_], scale=1.0 / D_MODEL, bias=epsb)
        k.recip(r_[:], r_[:], r=[r_], w=[r_])
        return r_, i

    def norm_transpose(xtile, tok0):
        r_, i = norm_rstd(xtile)
        h_ = nrm["hb"][i % 2]
        k.stt(h_[:], xtile[:], r_[:, 0:1], gbc[:], ALU.mult, ALU.mult, r=[xtile, r_, gbc], w=[h_])
        p = next_ps()
        pv = p.t.bitcast(BF16)
        for c in range(8):
            k.tr(pv[:, c * 128:(c + 1) * 128], h_[:, c * 128:(c + 1) * 128], ident_b[:], r=[h_, ident_b], w=[p],
                 inc=(c == 7))
        k.act(hT[:, :, tok0:tok0 + 128], pv[:, :].rearrange("p (c t) -> p c t", c=8), AF.Copy, r=[p], w=[hT])

    def headnorm_out(ht, gtile, gate, stage, col0, ws):
        ssq, sqs, yb = ws["ssq"], ws["sqs"], ws["yb"]
        for h in range(4):
            k.act(sqs[:, :], ht[:, h * 128:(h + 1) * 128], AF.Square, r=[ht], w=[sqs, ssq], accum=ssq[:, h:h + 1])
        k.act(ssq[:, :], ssq[:, :], AF.Sqrt, r=[ssq, consts], w=[ssq], scale=1.0 / 128, bias=epsb)
        k.recip(ssq[:, :], ssq[:, :], r=[ssq], w=[ssq])
        for h in range(4):
            k.stt(ht[:, h * 128:(h + 1) * 128], ht[:, h * 128:(h + 1) * 128], ssq[:, h:h + 1],
                  gtile[:, h * 128:(h + 1) * 128], ALU.mult, ALU.mult, r=[ht, ssq, gtile], w=[ht])
        k.tt(yb[:, :], ht[:, :], gate[:, :], ALU.mult, r=[ht, gate], w=[yb])
        p = next_ps()
        pv = p.t.bitcast(BF16)
        for h in range(4):
            k.tr(pv[:, h * 128:(h + 1) * 128], yb[:, h * 128:(h + 1) * 128], ident_b[:], r=[yb, ident_b], w=[p],
                 inc=(h == 3))
        k.act(stage[:, :, col0:col0 + 128], pv[:, 0:512].rearrange("p (c t) -> p c t", c=4), AF.Copy,
              r=[p], w=[stage])

    k.phase_mark()
    if "attn" in phases:
        tab = k.sb("tab", [32, 4], F32)
        oh = k.sb("oh", [32, 1152], F32)
        mvec = k.sb("mvec", [1, 1152], F32)
        Gs = k.sb("Gs", [4, 1152], F32)
        k.dma("sp", tab[:], P_["rel_bias"].ap(), w=[tab])
        k.dma("sp", oh[:], C_["oh"].ap(), w=[oh])
        k.dma("sp", mvec[:], C_["maskvec"].ap(), w=[mvec])
        for c0 in range(0, 1152, 384):
            p = next_ps()
            k.mm(p[0:4, 0:384], tab[:, :], oh[:, c0:c0 + 384], start=True, stop=False, r=[tab, oh], w=[p])
            k.mm(p[0:4, 0:384], ones_row[0:1, 0:4], mvec[0:1, c0:c0 + 384], start=False, stop=True,
                 r=[ones_row, mvec], w=[p])
            k.copy(Gs[:, c0:c0 + 384], p[0:4, 0:384], r=[p], w=[Gs])
        k.dma("sp", Gd.ap(), Gs[:], r=[Gs], w=[k.dbuf["Gd"]])
    k.phase_reset()

    def phase0():
        norm_alloc()
        xt = [k.sb("xt%d" % i, [128, D_MODEL], F32) for i in range(2)]
        k.dma("sp", gbc[:], norm_g.ap()[0:1, :].partition_broadcast(128), w=[gbc])
        for i in range(NT):
            xt_ = xt[i % 2]
            k.dma("sp", xt_[:], x_in.ap()[i * 128:(i + 1) * 128, :], w=[xt_])
            norm_transpose(xt_, i * 128)

    phase0()
    k.phase_reset()

    def phase_proj(l):
        wst = [k.sb("wst%d" % i, [128, 8, 512], F32) for i in range(2)]
        wbf = [k.sb("wbf%d" % i, [128, 8, 512], BF16) for i in range(2)]
        stT = [k.sb("stT%d" % i, [128, L], BF16) for i in range(2)]
        stN = [k.sb("stN%d" % i, [128, 512], BF16) for i in range(6)]
        cbuf = k.sb("cbuf", [128, L + 8], F32)
        cacc = k.sb("cacc", [128, L], F32)
        cw = k.sb("cw", [128, 8, 5], F32)
        crow = k.sb("crow", [5, 2 * W], F32)
        st = {"wi": 0, "ti": 0, "ni": 0, "ev": 0}
        mif = k.sb("mif_sb", [8, L], F32)

        def load_w(col0, ncols=512):
            i = st["wi"]
            st["wi"] += 1
            ws, wb = wst[i % 2], wbf[i % 2]
            k.dma("pool", ws[:, :, 0:ncols],
                  w_in.ap()[l, :, col0:col0 + ncols].rearrange("(kc p) c -> p kc c", p=128), w=[ws])
            k.copy(wb[:, :, 0:ncols], ws[:, :, 0:ncols], r=[ws], w=[wb], eng="pool")
            return wb

        def evac(out_ap, in_ap, func, r, w, scale=1.0):
            if func is None:
                st["ev"] += 1
                if st["ev"] % 2 == 0:
                    if scale == 1.0:
                        k.copy(out_ap, in_ap, r=r, w=w)
                    else:
                        k.ts(out_ap, in_ap, scale, None, ALU.mult, r=r, w=w)
                    return
                func = AF.Copy
            k.act(out_ap, in_ap, func, r=r, w=w, scale=scale)

        def proj_T_block(wb, col0, dst, row0, func, scale=1.0, conv=None):
            for c in range(4):
                if conv is None:
                    stg = stT[st["ti"] % 2]
                    st["ti"] += 1
                for tb in range(NB):
                    p = next_ps()
                    for kc in range(8):
                        k.mm(p[:, :], wb[:, kc, c * 128:(c + 1) * 128], hT[:, kc, tb * 512:(tb + 1) * 512],
                             start=(kc == 0), stop=(kc == 7), r=[wb, hT], w=[p])
                    if conv is None:
                        evac(stg[:, tb * 512:(tb + 1) * 512], p[:, :], func, r=[p], w=[stg], scale=scale)
                    else:
                        k.act(cbuf[:, 8 + tb * 512:8 + (tb + 1) * 512], p[:, :], AF.Copy, r=[p], w=[cbuf])
                if conv is not None:
                    cc = conv * 4 + c
                    stg = stT[st["ti"] % 2]
                    st["ti"] += 1
                    k.ts(cacc[:], cbuf[:, 8:8 + L], cw[:, cc, 3:4], cw[:, cc, 4:5], ALU.mult, ALU.add,
                         r=[cbuf, cw], w=[cacc])
                    for j in range(3):
                        k.stt(cacc[:], cbuf[:, 5 + j:5 + j + L], cw[:, cc, j:j + 1], cacc[:], ALU.mult, ALU.add,
                              r=[cbuf, cw, cacc], w=[cacc])
                    k.act(stg[:], cacc[:], AF.Silu, r=[cacc], w=[stg])
                k.dma("sp", dst.ap()[row0 + c * 128:row0 + (c + 1) * 128, :], stg[:], r=[stg], w=[k.dbuf[dst.name]])

        def proj_N_block(wb, col0, dst, func):
            for t in range(NT):
                p = next_ps()
                stg = stN[st["ni"] % 6]
                st["ni"] += 1
                for kc in range(8):
                    k.mm(p[:, :], hT[:, kc, t * 128:(t + 1) * 128], wb[:, kc, :],
                         start=(kc == 0), stop=(kc == 7), r=[wb, hT], w=[p])
                evac(stg[:], p[:, :], func, r=[p], w=[stg])
                k.dma("sp", dst.ap()[t * 128:(t + 1) * 128, :], stg[:], r=[stg], w=[k.dbuf[dst.name]])

        def proj_if(wb, col0=3584):
            for tb in range(NB):
                p = next_ps()
                for kc in range(8):
                    k.mm(p[0:8, :], wb[:, kc, 0:8], hT[:, kc, tb * 512:(tb + 1) * 512],
                         start=(kc == 0), stop=(kc == 7), r=[wb, hT], w=[p])
                k.act(mif[:, tb * 512:(tb + 1) * 512], p[0:8, :], AF.Copy, r=[p], w=[mif])

        k.memset(cbuf[:, 0:8], 0.0, w=[cbuf])
        k.dma("sp", crow[0:4, :], conv_w.ap()[l], w=[crow])
        k.dma("sp", crow[4:5, :], conv_b.ap()[l:l + 1, :], w=[crow])
        rows_to_cols(crow, 5, 8, cw[:, :, :], cw)
        specs = [(proj_if, 3584, 8, ()),
                 (proj_T_block, 0, 512, (suT, 0, None)),
                 (proj_T_block, 512, 512, (szT, 0, AF.Silu)),
                 (proj_T_block, 1024, 512, (mqT, 0, None, 1.0, 0)),
                 (proj_T_block, 1536, 512, (mkT, 0, None, 1.0, 1)),
                 (proj_N_block, 2048, 512, (mv, None)),
                 (proj_N_block, 2560, 512, (mo, AF.Sigmoid)),
                 (proj_N_block, 3072, 512, (mz, AF.Silu)),
                 (proj_T_block, 3592, 512, (aqT, 0, None, 0.125)),
                 (proj_T_block, 4104, 512, (akT, 0, None)),
                 (proj_N_block, 4616, 512, (av, None)),
                 (proj_T_block, 5128, 512, (azT, 0, AF.Silu))]
        for gb in range(6):
            specs.append((proj_T_block, 5640 + gb * 512, 512, (gT, gb * 512, AF.Sigmoid)))
        wbl = [None] * len(specs)
        wbl[0] = load_w(specs[0][1], specs[0][2])
        for i, (fn, col0, ncols, args) in enumerate(specs):
            if i + 1 < len(specs):
                wbl[i + 1] = load_w(specs[i + 1][1], specs[i + 1][2])
            fn(wbl[i], col0, *args)
            if i == 0:
                k.dma("sp", mifd.ap(), mif[:], r=[mif], w=[k.dbuf["mif"]])

    def sincos(ph_ap, shape, sin_out, cos_out, u, r, wT):
        ti = u["ti"]
        tf = u["tf"]
        uu = u["u"]
        for which, out_ap in (("sin", sin_out), ("cos", cos_out)):
            if which == "sin":
                k.ts(ti[:], ph_ap, 1.0 / (2.0 * PI), None, ALU.mult, r=r, w=[ti])
            else:
                k.ts(ti[:], ph_ap, 1.0 / (2.0 * PI), 0.25, ALU.mult, ALU.add, r=r, w=[ti])
            k.copy(tf[:], ti[:], r=[ti], w=[tf])
            k.stt(uu[:], tf[:], -2.0 * PI, ph_ap, ALU.mult, ALU.add, r=[tf] + list(r), w=[uu])
            if which == "sin":
                k.ts(uu[:], uu[:], -PI + 1e-5, PI - 1e-5, ALU.max, ALU.min, r=[uu], w=[uu])
                k.act(out_ap, uu[:], AF.Sin, r=[uu], w=wT)
            else:
                k.ts(uu[:], uu[:], -1.5 * PI + 1e-5, 0.5 * PI - 1e-5, ALU.max, ALU.min, r=[uu], w=[uu])
                k.act(out_ap, uu[:], AF.Sin, r=[uu, consts], w=wT, bias=consts[:, 4:5])

    def phase_s5(l):
        pre_ = {}
        for nm_, shp_, dt_ in (("mag", [128, 16], F32), ("th", [128, 16], F32), ("BTr", [128, 16, 128], BF16),
                               ("BTi", [128, 16, 128], BF16), ("CTr", [128, 16, 128], BF16),
                               ("CTi", [128, 16, 128], BF16), ("CTn", [128, 16, 128], BF16), ("Dg", [128, 4, 128], BF16), ("dcol", [128, 4, 2], F32),
                               ("wg", [128, 4, 512], BF16), ("cosT", [128, 16, 512], F32),
                               ("sinT", [128, 16, 512], F32)):
            pre_[nm_] = k.sb(nm_, shp_, dt_)
        mark_s5 = k.sb_off
        f = lambda name, shp, dt=F32: pre_[name] if name in pre_ else k.sb(name, shp, dt)
        lre, lim, lst = f("lre", [128, 16]), f("lim", [128, 16]), f("lst", [128, 16])
        mag, th, are, aim = f("mag", [128, 16]), f("th", [128, 16]), f("are", [128, 16]), f("aim", [128, 16])
        den, rre, rim, t16 = (f("den", [128, 16]), f("rre", [128, 16]),
                              f("rim", [128, 16]), f("t16", [128, 16]))
        u16 = {"ti": f("u16i", [128, 16], I32), "tf": f("u16f", [128, 16]), "u": f("u16u", [128, 16])}
        q = "sp"
        k.dma(q, lre[:], P_["ssm_lam_re"].ap()[l].rearrange("(s g) p -> (g p) s", g=2), w=[lre],
              allow_slow_non_contiguous=True)
        k.dma(q, lim[:], P_["ssm_lam_im"].ap()[l].rearrange("(s g) p -> (g p) s", g=2), w=[lim],
              allow_slow_non_contiguous=True)
        for h in range(2):
            k.dma(q, lst[h * 64:(h + 1) * 64, :],
                  P_["ssm_log_step"].ap()[l].rearrange("(s g) -> g s", g=2)[h:h + 1, :].partition_broadcast(64),
                  w=[lst], allow_slow_non_contiguous=True)
        k.act(lst[:], lst[:], AF.Exp, r=[lst], w=[lst])
        k.tt(mag[:], lre[:], lst[:], ALU.mult, r=[lre, lst], w=[mag])
        k.act(mag[:], mag[:], AF.Exp, r=[mag], w=[mag])
        k.tt(th[:], lim[:], lst[:], ALU.mult, r=[lim, lst], w=[th])
        sincos(th[:], [128, 16], aim[:], are[:], u16, [th], [aim, are])
        k.tt(are[:], are[:], mag[:], ALU.mult, r=[are, mag], w=[are])
        k.tt(aim[:], aim[:], mag[:], ALU.mult, r=[aim, mag], w=[aim])
        k.tt(den[:], lre[:], lre[:], ALU.mult, r=[lre], w=[den])
        k.tt(t16[:], lim[:], lim[:], ALU.mult, r=[lim], w=[t16])
        k.tt(den[:], den[:], t16[:], ALU.add, r=[den, t16], w=[den])
        k.recip(den[:], den[:], r=[den], w=[den])
        nre = f("nre", [128, 16])
        k.ts(nre[:], are[:], -1.0, None, ALU.add, r=[are], w=[nre])
        k.tt(rre[:], nre[:], lre[:], ALU.mult, r=[nre, lre], w=[rre])
        k.tt(t16[:], aim[:], lim[:], ALU.mult, r=[aim, lim], w=[t16])
        k.tt(rre[:], rre[:], t16[:], ALU.add, r=[rre, t16], w=[rre])
        k.tt(rre[:], rre[:], den[:], ALU.mult, r=[rre, den], w=[rre])
        k.tt(rim[:], aim[:], lre[:], ALU.mult, r=[aim, lre], w=[rim])
        k.tt(t16[:], nre[:], lim[:], ALU.mult, r=[nre, lim], w=[t16])
        k.tt(rim[:], rim[:], t16[:], ALU.subtract, r=[rim, t16], w=[rim])
        k.tt(rim[:], rim[:], den[:], ALU.mult, r=[rim, den], w=[rim])
        bre, bim = f("bre", [128, 16, 16]), f("bim", [128, 16, 16])
        bbr, bbi, tb_ = f("bbr", [128, 16, 16]), f("bbi", [128, 16, 16]), f("tb_", [128, 16, 16])
        k.dma(q, bre[:], P_["ssm_b_re"].ap()[l].rearrange("(s g) p c -> (g p) s c", g=2), w=[bre])
        k.dma(q, bim[:], P_["ssm_b_im"].ap()[l].rearrange("(s g) p c -> (g p) s c", g=2), w=[bim])
        rre_b = rre[:, :].unsqueeze(2).to_broadcast([128, 16, 16])
        rim_b = rim[:, :].unsqueeze(2).to_broadcast([128, 16, 16])
        k.tt(bbr[:], bre[:], rre_b, ALU.mult, r=[bre, rre], w=[bbr])
        k.tt(tb_[:], bim[:], rim_b, ALU.mult, r=[bim, rim], w=[tb_])
        k.tt(bbr[:], bbr[:], tb_[:], ALU.subtract, r=[bbr, tb_], w=[bbr])
        k.tt(bbi[:], bim[:], rre_b, ALU.mult, r=[bim, rre], w=[bbi])
        k.tt(tb_[:], bre[:], rim_b, ALU.mult, r=[bre, rim], w=[tb_])
        k.tt(bbi[:], bbi[:], tb_[:], ALU.add, r=[bbi, tb_], w=[bbi])
        BTr, BTi = f("BTr", [128, 16, 128], BF16), f("BTi", [128, 16, 128], BF16)
        fulls = [f("fulls%d" % i, [128, 16, 128], BF16) for i in range(2)]
        nfl = [0]

        def transpose_all(fa, dstT):
            for g4 in range(4):
                p = next_ps()
                pv = p.t.bitcast(BF16)
                for i4 in range(4):
                    k.tr(pv[:, i4 * 128:(i4 + 1) * 128], fa[:, g4 * 4 + i4, :], ident_b[:], r=[fa, ident_b], w=[p],
                         inc=(i4 == 3))
                k.act(dstT[:, g4 * 4:g4 * 4 + 4, :], pv[:, 0:512].rearrange("p (s c) -> p s c", s=4), AF.Copy,
                      r=[p], w=[dstT])

        for src, dstT in ((bbr, BTr), (bbi, BTi)):
            fa = fulls[nfl[0] % 2]
            nfl[0] += 1
            k.memset(fa[:], 0.0, w=[fa], eng="dve")
            for m in range(4):
                for h in range(2):
                    c0 = 32 * m + 16 * h
                    k.ts(fa[:, m::4, c0:c0 + 16], src[:, m::4, :], hmask[:, h:h + 1], None, ALU.mult,
                         r=[src, hmask, fa], w=[fa])
            transpose_all(fa, dstT)
        cnr, cni = f("cnr", [128, 4, 64]), f("cni", [128, 4, 64])
        k.dma(q, cnr[:], P_["ssm_c_re"].ap()[l].rearrange("(c4 gg) c p -> (gg c) c4 p", c4=4), w=[cnr])
        k.dma(q, cni[:], P_["ssm_c_im"].ap()[l].rearrange("(c4 gg) c p -> (gg c) c4 p", c4=4), w=[cni])
        CTr, CTi, CTn = f("CTr", [128, 16, 128], BF16), f("CTi", [128, 16, 128], BF16), f("CTn", [128, 16, 128], BF16)
        for src, dstT, sgn in ((cnr, CTr, 1.0), (cni, CTi, -1.0), (cnr, CTn, -1.0)):
            fa = fulls[nfl[0] % 2]
            nfl[0] += 1
            for m in range(4):
                for h in range(2):
                    j = m * 2 + h
                    k.ts(fa[:, m::4, 64 * h:64 * h + 64], src[:, 0:4, :], chmask[:, j:j + 1], sgn, ALU.mult, ALU.mult,
                         r=[src, chmask, fa], w=[fa])
            transpose_all(fa, dstT)
        drow = f("drow", [2, W])
        dcol = f("dcol", [128, 4, 2])
        k.dma(q, drow[0:1, :], P_["ssm_d"].ap()[l:l + 1, :], w=[drow])
        k.dma(q, drow[1:2, :], P_["ssm_b_glu"].ap()[l:l + 1, :], w=[drow])
        rows_to_cols(drow, 2, 4, dcol[:, :, :], dcol)
        Dg = f("Dg", [128, 4, 128], BF16)
        for c in range(4):
            k.ts(Dg[:, c, :], ident_f[:, :], dcol[:, c, 0:1], None, ALU.mult, r=[ident_f, dcol], w=[Dg])
        wg = f("wg", [128, 4, 512], BF16)
        wgs = f("wgs", [128, 4, 512])
        k.dma(q, wgs[:], P_["ssm_w_glu"].ap()[l].rearrange("(kc p) c -> p kc c", p=128), w=[wgs])
        k.copy(wg[:], wgs[:], r=[wgs], w=[wg], eng="pool")
        cosT, sinT = f("cosT", [128, 16, 512]), f("sinT", [128, 16, 512])
        iota1 = f("iota1", [128, 512])
        k.dma(q, iota1[:], C_["iota1"].ap(), w=[iota1])
        ph = [f("ph%d" % i, [128, 512]) for i in range(2)]
        us = [{"ti": f("usi%d" % i, [128, 512], I32), "tf": f("usf%d" % i, [128, 512]),
               "u": f("usu%d" % i, [128, 512])} for i in range(1)] * 2
        for s in range(16):
            ph_, u_ = ph[s % 2], us[s % 2]
            k.ts(ph_[:], iota1[:], th[:, s:s + 1], None, ALU.mult, r=[iota1, th], w=[ph_])
            sincos(ph_[:], [128, 512], sinT[:, s, :], cosT[:, s, :], u_, [ph_], [sinT, cosT])
        k.S.barrier()
        k.sb_off = mark_s5
        ccol, scol, nscol = f("ccol", [128, 16]), f("scol", [128, 16]), f("nscol", [128, 16])
        k.copy(ccol[:], cosT[:, :, 511], r=[cosT], w=[ccol])
        k.copy(scol[:], sinT[:, :, 511], r=[sinT], w=[scol])
        k.ts(nscol[:], sinT[:, :, 511], -1.0, None, ALU.mult, r=[sinT], w=[nscol])
        ctmp = [f("ctmp%d" % i, [128, 2]) for i in range(2)]
        cr, ci = f("cr", [128, 16]), f("ci", [128, 16])
        k.memset(cr[:], 0.0, w=[cr])
        k.memset(ci[:], 0.0, w=[ci])
        k.S.barrier()
        ub = [f("ub%d" % i, [128, 4, 512], BF16) for i in range(2)]
        zb = [f("zb%d" % i, [128, 4, 512], BF16) for i in range(2)]
        wk = {n: [f("%s%d" % (n, i), [128, 512]) for i in range(2)] for n in
              ("t1", "t2", "t3", "t4", "wre", "wim", "zre", "zim")}
        for a_ in ("d1", "d2", "d3", "d4"):
            wk[a_] = [f("%s%d" % (a_, i), [128, 512], BF16) for i in range(2)]
        gf = f("gf", [128, 4, 512])
        gb = f("gb", [128, 4, 512], BF16)
        y2 = [f("y2%d" % i, [128, 512]) for i in range(2)]
        sg = [f("sg%d" % i, [128, 512]) for i in range(2)]
        yo = [f("yo%d" % i, [128, 4, 512], BF16) for i in range(2)]
        crb = [Buf("cr%d" % s_) for s_ in range(16)]
        cib = [Buf("ci%d" % s_) for s_ in range(16)]

        def load_blk(j):
            k.dma("sp", ub[j % 2][:], suT.ap()[:, j * 512:(j + 1) * 512].rearrange("(c p) t -> p c t", p=128),
                  r=[k.dbuf["suT"]], w=[ub[j % 2]])
            k.dma("sp", zb[j % 2][:], szT.ap()[:, j * 512:(j + 1) * 512].rearrange("(c p) t -> p c t", p=128),
                  r=[k.dbuf["szT"]], w=[zb[j % 2]])

        iters = [(j, c4, s_) for j in range(NB) for c4 in range(4) for s_ in range(4 * c4, 4 * c4 + 4)]

        def stA(i):
            j, c4, s_ = iters[i]
            if c4 == 0 and s_ == 1 and j + 1 < NB:
                load_blk(j + 1)
            ub_ = ub[j % 2]
            W_ = {n: v[i % 2] for n, v in wk.items()}
            pa, pb = psb[2 + (i * 2) % 6], psb[2 + (i * 2 + 1) % 6]
            k.mm(pa[:, :], BTr[:, s_, :], ub_[:, c4, :], True, True, r=[BTr, ub_], w=[pa])
            k.mm(pb[:, :], BTi[:, s_, :], ub_[:, c4, :], True, True, r=[BTi, ub_], w=[pb])
            cs, sn = cosT[:, s_, :], sinT[:, s_, :]
            k.tt(W_["t1"][:], pa[:, :], cs, ALU.mult, r=[pa, cosT], w=[W_["t1"]])
            k.tt(W_["t2"][:], pb[:, :], sn, ALU.mult, r=[pb, sinT], w=[W_["t2"]])
            k.tt(W_["wre"][:], W_["t1"][:], W_["t2"][:], ALU.add, r=[W_["t1"], W_["t2"]], w=[W_["wre"]])
            k.tt(W_["t3"][:], pb[:, :], cs, ALU.mult, r=[pb, cosT], w=[W_["t3"]])
            k.tt(W_["t4"][:], pa[:, :], sn, ALU.mult, r=[pa, sinT], w=[W_["t4"]])
            k.tt(W_["wim"][:], W_["t3"][:], W_["t4"][:], ALU.subtract, r=[W_["t3"], W_["t4"]], w=[W_["wim"]])
            magb = mag[:, s_:s_ + 1].to_broadcast([128, 512])
            k.scan(W_["zre"][:], magb, W_["wre"][:], cr[:, s_:s_ + 1], ALU.mult, ALU.add,
                   r=[mag, W_["wre"], crb[s_]], w=[W_["zre"]])
            k.scan(W_["zim"][:], magb, W_["wim"][:], ci[:, s_:s_ + 1], ALU.mult, ALU.add,
                   r=[mag, W_["wim"], cib[s_]], w=[W_["zim"]])

        def stB(i):
            j, c4, s_ = iters[i]
            W_ = {n: v[i % 2] for n, v in wk.items()}
            cs, sn = cosT[:, s_, :], sinT[:, s_, :]
            k.tt(W_["d1"][:], W_["zre"][:], cs, ALU.mult, r=[W_["zre"], cosT], w=[W_["d1"]], eng="pool")
            k.tt(W_["d2"][:], W_["zim"][:], sn, ALU.mult, r=[W_["zim"], sinT], w=[W_["d2"]], eng="pool")
            k.tt(W_["d3"][:], W_["zre"][:], sn, ALU.mult, r=[W_["zre"], sinT], w=[W_["d3"]], eng="pool")
            k.tt(W_["d4"][:], W_["zim"][:], cs, ALU.mult, r=[W_["zim"], cosT], w=[W_["d4"]], eng="pool")

        def stC(i):
            j, c4, s_ = iters[i]
            ub_, zb_, yo_ = ub[j % 2], zb[j % 2], yo[j % 2]
            W_ = {n: v[i % 2] for n, v in wk.items()}
            yps = psb[c4 % 2]
            ct = ctmp[i % 2]
            k.act(ct[:, 0:1], W_["zim"][:, 511:512], AF.Copy, r=[W_["zim"], nscol], w=[ct], scale=nscol[:, s_:s_ + 1])
            k.act(cr[:, s_:s_ + 1], W_["zre"][:, 511:512], AF.Identity, r=[W_["zre"], ccol, ct], w=[crb[s_]],
                  scale=ccol[:, s_:s_ + 1], bias=ct[:, 0:1])
            k.act(ct[:, 1:2], W_["zim"][:, 511:512], AF.Copy, r=[W_["zim"], ccol], w=[ct], scale=ccol[:, s_:s_ + 1])
            k.act(ci[:, s_:s_ + 1], W_["zre"][:, 511:512], AF.Identity, r=[W_["zre"], scol, ct], w=[cib[s_]],
                  scale=scol[:, s_:s_ + 1], bias=ct[:, 1:2])
            k.mm(yps[:, :], CTr[:, s_, :], W_["d1"][:], start=(s_ == 4 * c4), stop=False, r=[CTr, W_["d1"]], w=[yps],
                 inc=True)
            k.mm(yps[:, :], CTn[:, s_, :], W_["d2"][:], start=False, stop=False, r=[CTn, W_["d2"]], w=[yps], inc=True)
            k.mm(yps[:, :], CTi[:, s_, :], W_["d3"][:], start=False, stop=False, r=[CTi, W_["d3"]], w=[yps], inc=True)
            k.mm(yps[:, :], CTi[:, s_, :], W_["d4"][:], start=False, stop=False, r=[CTi, W_["d4"]], w=[yps], inc=True)
            if s_ != 4 * c4 + 3:
                return
            k.mm(yps[:, :], Dg[:, c4, :], ub_[:, c4, :], start=False, stop=True, r=[Dg, ub_], w=[yps])
            y2_, sg_ = y2[c4 % 2], sg[c4 % 2]
            k.act(y2_[:], yps[:, :], AF.Square, r=[yps], w=[y2_])
            k.ts(y2_[:], y2_[:], 0.044715, 1.0, ALU.mult, ALU.add, r=[y2_], w=[y2_])
            k.tt(y2_[:], y2_[:], yps[:, :], ALU.mult, r=[y2_, yps], w=[y2_])
            k.act(sg_[:], y2_[:], AF.Sigmoid, r=[y2_], w=[sg_], scale=2.0 * math.sqrt(2.0 / PI))
            k.tt(gf[:, c4, :], sg_[:], yps[:, :], ALU.mult, r=[sg_, yps], w=[gf])
            k.act(gb[:, c4, :], gf[:, c4, :], AF.Copy, r=[gf], w=[gb])
            if c4 != 3:
                return
            for jc in range(4):
                pg = psb[2 + jc]
                for ic in range(4):
                    k.mm(pg[:, :], wg[:, ic, jc * 128:(jc + 1) * 128], gb[:, ic, :], start=(ic == 0), stop=(ic == 3),
                         r=[wg, gb], w=[pg])
                sg_ = sg[jc % 2]
                k.act(sg_[:], pg[:, :], AF.Sigmoid, r=[pg, dcol], w=[sg_], bias=dcol[:, jc, 1:2])
                k.tt(sg_[:], sg_[:], gf[:, jc, :], ALU.mult, r=[sg_, gf], w=[sg_])
                k.tt(yo_[:, jc, :], sg_[:], zb_[:, jc, :], ALU.mult, r=[sg_, zb_], w=[yo_])
            k.dma("sp", ysT.ap()[:, j * 512:(j + 1) * 512].rearrange("(c p) t -> p c t", p=128), yo_[:],
                  r=[yo_], w=[k.dbuf["ysT"]])

        load_blk(0)
        stA(0)
        for i in range(len(iters)):
            stB(i)
            if i + 1 < len(iters):
                stA(i + 1)
            stC(i)

    def phase_gates(l, pre):
        f = lambda name, shp, dt=F32: k.sb(name, shp, dt)
        NC = NT
        qT = f("mq", [128, 4, L], BF16)
        kTt = f("mk", [128, 4, L], BF16)
        Vp = f("mVp", [128, NT, 4, 130], BF16)
        pre.update(qT=qT, kTt=kTt, Vp=Vp, mark=k.sb_off)
        mif = f("mif_g", [8, L])
        k.dma("sp", mif[:], mifd.ap(), r=[k.dbuf["mif"]], w=[mif])
        ifb = f("ifb", [8, 1])
        k.dma("sp", ifb[:], P_["if_bias"].ap()[l].rearrange("a (h o) -> (a h) o", o=1), w=[ifb],
              allow_slow_non_contiguous=True)
        k.dma("sp", qT[:], mqT.ap().rearrange("(c p) t -> p c t", p=128), r=[k.dbuf["mqT"]], w=[qT])
        k.dma("sp", kTt[:], mkT.ap().rearrange("(c p) t -> p c t", p=128), r=[k.dbuf["mkT"]], w=[kTt])
        for n in range(NT):
            k.dma("sp", Vp[:, n, :, 0:128], mv.ap()[n * 128:(n + 1) * 128, :].rearrange("p (h d) -> p h d", h=4),
                  r=[k.dbuf["mv"]], w=[Vp])
        pre = f("pre", [8, L])
        lsg = f("lsg", [8, L])
        tmp8 = f("tmp8", [8, L])
        k.ts(pre[:], mif[:], ifb[:, 0:1], None, ALU.add, r=[mif, ifb], w=[pre])
        k.act(tmp8[:], pre[:], AF.Abs, r=[pre], w=[tmp8])
        k.act(tmp8[:], tmp8[:], AF.Exp, r=[tmp8], w=[tmp8], scale=-1.0)
        k.act(tmp8[:], tmp8[:], AF.Ln, r=[tmp8, consts], w=[tmp8], bias=consts[0:8, 3:4])
        k.ts(lsg[:], pre[:], 0.0, None, ALU.min, r=[pre], w=[lsg])
        k.tt(lsg[:], lsg[:], tmp8[:], ALU.subtract, r=[lsg, tmp8], w=[lsg])
        cs8 = tmp8
        k.scan(cs8[:], consts[0:8, 3:4].to_broadcast([8, L]), lsg[:], 0.0, ALU.mult, ALU.add, r=[consts, lsg],
               w=[cs8])
        R8 = lsg
        k.ts(R8[:], cs8[:], rowmask8[:, 1:2], None, ALU.mult, r=[cs8, rowmask8], w=[R8])
        k.stt(R8[:], pre[:], rowmask8[:, 0:1], R8[:], ALU.mult, ALU.add, r=[pre, rowmask8, R8], w=[R8])
        a4, B4 = f("a4", [4, L]), f("B4", [4, L])
        for tb in range(NB):
            p = next_ps()
            k.mm(p[0:4, :], sel8[:, 0:4], R8[:, tb * 512:(tb + 1) * 512], True, True, r=[sel8, R8], w=[p])
            k.copy(a4[:, tb * 512:(tb + 1) * 512], p[0:4, :], r=[p], w=[a4])
            p = next_ps()
            k.mm(p[0:4, :], sel8[:, 4:8], R8[:, tb * 512:(tb + 1) * 512], True, True, r=[sel8, R8], w=[p])
            k.copy(B4[:, tb * 512:(tb + 1) * 512], p[0:4, :], r=[p], w=[B4])
        cm, Mx, Mp, dd = f("cm", [4, NC]), f("Mx", [4, NC]), f("Mp", [4, NC]), f("dd", [4, NC])
        k.reduce(cm[:], a4[:, :].rearrange("p (n t) -> p n t", t=128), ALU.max, AX.X, r=[a4], w=[cm])
        k.scan(Mx[:], cm[:], cm[:], 0.0, ALU.max, ALU.max, r=[cm], w=[Mx])
        k.memset(Mp[:, 0:1], 0.0, w=[Mp])
        if NC > 1:
            k.copy(Mp[:, 1:NC], Mx[:, 0:NC - 1], r=[Mx], w=[Mp])
        k.tt(dd[:], Mp[:], Mx[:], ALU.subtract, r=[Mp, Mx], w=[dd])
        k.act(dd[:], dd[:], AF.Exp, r=[dd], w=[dd])
        Mb = Mx[:, :].unsqueeze(2).to_broadcast([4, NC, 128])
        wrow, frow = a4, B4
        k.tt(wrow[:, :].rearrange("p (n t) -> p n t", t=128), a4[:, :].rearrange("p (n t) -> p n t", t=128), Mb,
             ALU.subtract, r=[a4, Mx], w=[wrow])
        k.act(wrow[:], wrow[:], AF.Exp, r=[wrow], w=[wrow])
        k.tt(frow[:, :].rearrange("p (n t) -> p n t", t=128), B4[:, :].rearrange("p (n t) -> p n t", t=128), Mb,
             ALU.add, r=[B4, Mx], w=[frow])
        k.act(frow[:], frow[:], AF.Exp, r=[frow], w=[frow], scale=-1.0)
        for n0 in range(0, NT, 32):
            nn = min(32, NT - n0)
            p = next_ps()
            for n in range(n0, n0 + nn):
                c0 = (n - n0) * 8
                k.mm(p[:, c0:c0 + 4], wrow[0:4, n * 128:(n + 1) * 128], ident_f[0:4, 0:4], True, True,
                     r=[wrow, ident_f], w=[p], inc=False)
                k.mm(p[:, c0 + 4:c0 + 8], frow[0:4, n * 128:(n + 1) * 128], ident_f[0:4, 0:4], True, True,
                     r=[frow, ident_f], w=[p], inc=(n == n0 + nn - 1))
            k.copy(wf[:, n0:n0 + nn, :], p[:, 0:nn * 8].rearrange("p (n e) -> p n e", e=8), r=[p], w=[wf])
        for h in range(4):
            p = next_ps()
            k.mm(p[:, 0:NC], selrow4[0:4, h * 128:(h + 1) * 128], dd[0:4, :], True, True, r=[selrow4, dd], w=[p])
            k.copy(dbc[:, h, :], p[:, 0:NC], r=[p], w=[dbc])

    def phase_mlstm(l, pre):
        f = lambda name, shp, dt=F32: k.sb(name, shp, dt)
        NC = NT
        SC = 128.0 ** -0.5
        qT, kTt, Vp = pre["qT"], pre["kTt"], pre["Vp"]
        k.memset(Vp[:, :, :, 128:129], 1.0, w=[Vp], eng="dve")
        k.memset(Vp[:, :, :, 129:130], 0.0, w=[Vp], eng="dve")
        for n in range(NT):
            k.tt(Vp[:, n, :, :], Vp[:, n, :, :], wf[:, n, 0:4].unsqueeze(2).to_broadcast([128, 4, 130]), ALU.mult,
                 r=[Vp, wf], w=[Vp])
        gM = f("gM", [128, W])
        k.dma("sp", gM[:], P_["m_norm_g"].ap()[l:l + 1, :].partition_broadcast(128), w=[gM])
        Cst4 = f("Cst4", [128, 4, 130])
        Tm4 = f("Tm4", [128, 4, 130])
        Cs4 = [f("Cs4_%d" % i, [128, 4, 130], BF16) for i in range(2)]
        k.memset(Cst4[:], 0.0, w=[Cst4], eng="dve")
        Sm4 = [f("Sm4_%d" % i, [128, 4, 128], BF16) for i in range(2)]
        kN4 = [f("kN4_%d" % i, [128, 4, 128], BF16) for i in range(2)]
        puS4 = [f("puS4_%d" % i, [128, 4, 129]) for i in range(2)]
        d4 = [f("d4_%d" % i, [128, 8]) for i in range(2)]
        hmt = [f("hmt%d" % i, [128, W]) for i in range(2)]
        mot = [f("mot%d" % i, [128, W], BF16) for i in range(2)]
        mzt = [f("mzt%d" % i, [128, W], BF16) for i in range(2)]
        ws = {"ssq": f("m_ssq", [128, 4]), "sqs": f("m_sqs", [128, 128]), "yb": f("m_yb", [128, W], BF16)}
        stage = [f("mstage%d" % i, [128, 4, 512], BF16) for i in range(2)]
        cz_b = causal01[:, :].unsqueeze(1).to_broadcast([128, 4, 128])

        def stL(n):
            i2 = n % 2
            tk = slice(n * 128, (n + 1) * 128)
            psS = next_ps()
            for h in range(4):
                k.mm(psS[:, h * 128:(h + 1) * 128], kTt[:, h, tk], qT[:, h, tk], True, True, r=[kTt, qT], w=[psS],
                     inc=(h == 3))
            k.tt(Sm4[i2][:], psS[:, :].rearrange("p (h t) -> p h t", h=4), cz_b, ALU.mult, r=[psS, causal01],
                 w=[Sm4[i2]])
            pk = next_ps()
            pkv = pk.t.bitcast(BF16)
            for h in range(4):
                k.tr(pkv[:, h * 128:(h + 1) * 128], kTt[:, h, tk], ident_b[:], r=[kTt, ident_b], w=[pk],
                     inc=(h == 3))
            k.act(kN4[i2][:], pkv[:, 0:512].rearrange("p (h t) -> p h t", h=4), AF.Copy, r=[pk], w=[kN4[i2]])
            for hp in range(2):
                pu = next_ps()
                for hh in range(2):
                    h = hp * 2 + hh
                    k.mm(pu[:, hh * 129:(hh + 1) * 129], kN4[i2][:, h, :], Vp[:, n, h, 0:129], True, True,
                         r=[kN4[i2], Vp], w=[pu], inc=(hh == 1))
                k.act(puS4[i2][:, hp * 2:hp * 2 + 2, :], pu[:, 0:258].rearrange("p (h e) -> p h e", h=2), AF.Copy,
                      r=[pu], w=[puS4[i2]])

        def stR(n):
            i2 = n % 2
            tk = slice(n * 128, (n + 1) * 128)
            hm_ = hmt[n % 2]
            d_ = d4[i2]
            dsc = dbc[:, :, n:n + 1].to_broadcast([128, 4, 130])
            k.tt(Tm4[:], Cst4[:], dsc, ALU.mult, r=[Cst4, dbc], w=[Tm4])
            cs_ = Cs4[i2]
            if n > 0:
                k.act(cs_[:], Tm4[:], AF.Copy, r=[Tm4], w=[cs_])
            k.tt(Cst4[:, :, 0:129], Tm4[:, :, 0:129], puS4[i2][:], ALU.add, r=[Tm4, puS4[i2]], w=[Cst4])
            for hp in range(2):
                po = next_ps()
                for hh in range(2):
                    h = hp * 2 + hh
                    k.mm(po[:, hh * 129:(hh + 1) * 129], Sm4[i2][:, h, :], Vp[:, n, h, 0:129], True, (n == 0),
                         r=[Sm4[i2], Vp], w=[po], inc=(n == 0 and hh == 1))
                    if n > 0:
                        k.mm(po[:, hh * 129:(hh + 1) * 129], qT[:, h, tk], cs_[:, h, 0:129], False, True,
                             r=[qT, cs_], w=[po], inc=(hh == 1))
                pov = po[:, 0:258].rearrange("p (h e) -> p h e", h=2)
                k.act(d_[:, hp * 2:hp * 2 + 2], pov[:, :, 128], AF.Abs, r=[po], w=[d_], scale=SC)
                k.tt(d_[:, hp * 2:hp * 2 + 2], d_[:, hp * 2:hp * 2 + 2], wf[:, n, 4 + hp * 2:6 + hp * 2], ALU.max,
                     r=[d_, wf], w=[d_])
                k.recip(d_[:, 4 + hp * 2:6 + hp * 2], d_[:, hp * 2:hp * 2 + 2], r=[d_], w=[d_])
                k.ts(d_[:, 4 + hp * 2:6 + hp * 2], d_[:, 4 + hp * 2:6 + hp * 2], SC, None, ALU.mult, r=[d_], w=[d_])
                k.tt(hm_[:, hp * 256:(hp + 1) * 256].rearrange("p (h t) -> p h t", h=2), pov[:, :, 0:128],
                     d_[:, 4 + hp * 2:6 + hp * 2].unsqueeze(2).to_broadcast([128, 2, 128]), ALU.mult,
                     r=[po, d_], w=[hm_])

        stL(0)
        for n in range(NT):
            hm_, mo_, mz_ = hmt[n % 2], mot[n % 2], mzt[n % 2]
            k.dma("sp", mo_[:], mo.ap()[n * 128:(n + 1) * 128, :], r=[k.dbuf["mo"]], w=[mo_])
            k.dma("sp", mz_[:], mz.ap()[n * 128:(n + 1) * 128, :], r=[k.dbuf["mz"]], w=[mz_])
            if n + 1 < NT:
                stL(n + 1)
            stR(n)
            k.tt(hm_[:], hm_[:], mo_[:], ALU.mult, r=[hm_, mo_], w=[hm_])
            stg = stage[(n // 4) % 2]
            headnorm_out(hm_, gM, mz_, stg, (n % 4) * 128, ws)
            if n % 4 == 3:
                j = n // 4
                k.dma("sp", ymT.ap()[:, j * 512:(j + 1) * 512].rearrange("(c p) t -> p c t", p=128), stg[:],
                      r=[stg], w=[k.dbuf["ymT"]])

    def phase_attn(l):
        pre_ = {}
        for nm_, shp_, dt_ in (("aq", [128, 4, L], BF16), ("ak", [128, 4, L], BF16), ("aVp", [128, NT, 4, 130], BF16),
                               ("strip0", [128, 1024], F32), ("strip1", [128, 1024], F32),
                               ("strip2", [128, 1024], F32), ("strip3", [128, 1024], F32), ("b31bc", [128, 4], F32),
                               ("negMb", [128, 2, 4], F32), ("farb", [128, 2, 4], F32), ("neglam", [128, 1], F32),
                               ("gAcol", [128, 4, 1], F32), ("ones_b", [128, 128], BF16)):
            pre_[nm_] = k.sb(nm_, shp_, dt_)
        mark_a = k.sb_off
        f = lambda name, shp, dt=F32: pre_[name] if name in pre_ else k.sb(name, shp, dt)
        lam_init = 0.8 - 0.6 * math.exp(-0.3 * l)
        qT = f("aq", [128, 4, L], BF16)
        kTt = f("ak", [128, 4, L], BF16)
        Vp = f("aVp", [128, NT, 4, 130], BF16)
        k.dma("sp", qT[:], aqT.ap().rearrange("(c p) t -> p c t", p=128), r=[k.dbuf["aqT"]], w=[qT])
        k.dma("sp", kTt[:], akT.ap().rearrange("(c p) t -> p c t", p=128), r=[k.dbuf["akT"]], w=[kTt])
        vstg = f("vstg_a", [128, NT, W], BF16)
        for n0 in range(0, NT, 8):
            nn = min(8, NT - n0)
            k.dma("sp", vstg[:, n0:n0 + nn, :],
                  av.ap()[n0 * 128:(n0 + nn) * 128, :].rearrange("(n p) c -> p n c", p=128),
                  r=[k.dbuf["av"]], w=[vstg])
        k.copy(Vp[:, :, :, 0:128], vstg[:, :, :].rearrange("p n (h d) -> p n h d", h=4), r=[vstg], w=[Vp])
        k.memset(Vp[:, :, :, 128:129], 1.0, w=[Vp], eng="dve")
        k.memset(Vp[:, :, :, 129:130], 0.0, w=[Vp], eng="dve")
        strips = [f("strip%d" % h, [128, 1024]) for h in range(4)]
        b31bc = f("b31bc", [128, 4])
        antiI = f("antiI", [128, 128])
        k.dma("sp", antiI[:], C_["antiI"].ap(), w=[antiI])
        srev = [f("srev%d" % i, [128, 1024]) for i in range(2)]
        for h in range(4):
            sr = srev[h % 2]
            k.dma("sp", sr[:], bass.AP(Gd, h * 1152, [[1, 128], [1, 1024]]), r=[k.dbuf["Gd"]], w=[sr])
            for hf in range(2):
                p = next_ps()
                k.mm(p[:, :], antiI[:, :], sr[:, hf * 512:(hf + 1) * 512], True, True, r=[antiI, sr], w=[p])
                k.copy(strips[h][:, hf * 512:(hf + 1) * 512], p[:, :], r=[p], w=[strips[h]])
        k.dma("sp", b31bc[:], P_["rel_bias"].ap()[31:32, :].partition_broadcast(128), w=[b31bc])
        sqb = [f("sqb%d" % i, [128, 512], BF16) for i in range(2)]
        blkmax = f("blkmax", [2, NB])
        stat = f("stat", [2, 8])
        ii = 0
        for qi, src in enumerate((qT, kTt)):
            for h in range(4):
                for tb in range(NB):
                    s_ = sqb[ii % 2]
                    ii += 1
                    k.act(s_[:], src[:, h, tb * 512:(tb + 1) * 512], AF.Square, r=[src], w=[s_])
                    p = next_ps()
                    k.mm(p[0:2, :], blockones_b[:, 0:2], s_[:], True, True, r=[blockones_b, s_], w=[p])
                    k.reduce(blkmax[:, tb:tb + 1], p[0:2, :], ALU.max, AX.X, r=[p], w=[blkmax])
                k.reduce(stat[:, qi * 4 + h:qi * 4 + h + 1], blkmax[:, :], ALU.max, AX.X, r=[blkmax], w=[stat])
        M2 = f("M2", [2, 4])
        k.tt(M2[:], stat[:, 0:4], stat[:, 4:8], ALU.mult, r=[stat], w=[M2])
        k.act(M2[:], M2[:], AF.Sqrt, r=[M2], w=[M2])
        k.ts(M2[:], M2[:], -1.05, -2.0, ALU.mult, ALU.add, r=[M2], w=[M2])
        negMb = f("negMb", [128, 2, 4])
        farb = f("farb", [128, 2, 4])
        selc = f("selc", [2, 2, 128])
        k.memset(selc[:], 0.0, w=[selc])
        for c in range(2):
            k.ts(selc[:, c, :], ident_f[0:2, c:c + 1].to_broadcast([2, 128]), 1.0, None, ALU.mult, r=[ident_f, selc],
                 w=[selc])
        for c in range(2):
            p = next_ps()
            k.mm(p[:, 0:4], selc[0:2, c, :], M2[0:2, 0:4], True, True, r=[selc, M2], w=[p])
            k.copy(negMb[:, c, :], p[:, 0:4], r=[p], w=[negMb])
            k.tt(farb[:, c, :], negMb[:, c, :], b31bc[:, :], ALU.add, r=[negMb, b31bc], w=[farb])
        dl = f("dl", [1, 256])
        k.dma("sp", dl[:], P_["diff_lam"].ap()[l:l + 1].rearrange("o a d -> o (a d)"), w=[dl])
        pr = f("pr", [1, 128])
        s12 = f("s12", [1, 2])
        k.tt(pr[:, 0:64], dl[:, 0:64], dl[:, 64:128], ALU.mult, r=[dl], w=[pr])
        k.tt(pr[:, 64:128], dl[:, 128:192], dl[:, 192:256], ALU.mult, r=[dl], w=[pr])
        k.reduce(s12[:], pr[:, :].rearrange("p (a d) -> p a d", a=2), ALU.add, AX.X, r=[pr], w=[s12])
        k.act(s12[:], s12[:], AF.Exp, r=[s12], w=[s12])
        lamv = f("lamv", [1, 1])
        k.tt(lamv[:], s12[:, 1:2], s12[:, 0:1], ALU.subtract, r=[s12], w=[lamv])
        k.ts(lamv[:], lamv[:], -lam_init, None, ALU.add, r=[lamv], w=[lamv])
        neglam = f("neglam", [128, 1])
        p = next_ps()
        k.mm(p[:, 0:1], ones_row[0:1, :], lamv[0:1, 0:1], True, True, r=[ones_row, lamv], w=[p])
        k.copy(neglam[:], p[:, 0:1], r=[p], w=[neglam])
        garow = f("garow", [1, W])
        gAcol = f("gAcol", [128, 4, 1])
        k.dma("sp", garow[:], P_["diff_norm_g"].ap()[l:l + 1, :], w=[garow])
        rows_to_cols(garow, 1, 4, gAcol[:, :, :], gAcol)
        k.ts(gAcol[:], gAcol[:], 1.0 - lam_init, None, ALU.mult, r=[gAcol], w=[gAcol])
        ones_b = f("ones_b", [128, 128], BF16)
        k.memset(ones_b[:], 1.0, w=[ones_b], eng="dve")
        k.S.barrier()
        k.sb_off = mark_a
        Pt = [f("Pt%d" % i, [128, 512], BF16) for i in range(3)]
        tmpb = [f("tmpb%d" % i, [128, 512]) for i in range(2)]
        rsb = [f("rsb%d" % i, [128, 512]) for i in range(2)]
        o0 = [f("o0_%d" % i, [128, 512]) for i in range(2)]
        t1 = [f("at1_%d" % i, [128, 512]) for i in range(2)]
        haT = [f("haT%d" % i, [128, 512]) for i in range(2)]
        sqb_ = [f("asq%d" % i, [128, 512], BF16) for i in range(2)]
        rstd = [f("arstd%d" % i, [128, 512]) for i in range(2)]
        azt = [f("azt%d" % i, [128, 512], BF16) for i in range(2)]
        stage = [f("astage%d" % i, [128, 512], BF16) for i in range(2)]
        qz = [[f("qz%d_%d" % (i, c), [128, 512], BF16) for c in range(2)] for i in range(2)]
        for i in range(2):
            for c in range(2):
                k.memset(qz[i][c][:], 0.0, w=[qz[i][c]], eng="dve")
        it = 0
        gh = 0
        ic = 0
        pending = []
        my_ep = None
        for g in range(NB):
            for h in range(4):
                qz_ = qz[gh % 2]
                az_ = azt[gh % 2]
                ha_ = haT[gh % 2]
                k.dma("sp", az_[:], azT.ap()[h * 128:(h + 1) * 128, g * 512:(g + 1) * 512], r=[k.dbuf["azT"]],
                      w=[az_])
                for c in range(2):
                    k.copy(qz_[c][c * 64:(c + 1) * 64, :], qT[c * 64:(c + 1) * 64, h, g * 512:(g + 1) * 512],
                           r=[qT], w=[qz_[c]], eng="pool")
                for c in range(2):
                    nkb = 4 * g + 4
                    DEPTH_PF = 2
                    base = it
                    pss = {}
                    accV, accS = psb[(ic % 2) * 2], psb[(ic % 2) * 2 + 1]
                    ic += 1

                    def issue_qk(kb):
                        ps = psb[4 + (base + kb) % 4]
                        pss[kb] = ps
                        k.mm(ps[:, :], kTt[:, h, kb * 128:(kb + 1) * 128], qz_[c][:, :], True, True,
                             r=[kTt, qz_[c]], w=[ps])

                    for kb in range(min(DEPTH_PF, nkb)):
                        issue_qk(kb)
                    for kb in range(nkb):
                        it += 1
                        if kb + DEPTH_PF < nkb:
                            issue_qk(kb + DEPTH_PF)
                        if c == 0 and kb == min(3, nkb - 1) and pending and pending[0] is not my_ep:
                            pending.pop(0)()
                        ps = pss.pop(kb)
                        P_t = Pt[it % 3]
                        j = kb - (4 * g - 1)
                        c0 = 0
                        if j >= 0:
                            tb_ = tmpb[it % 2]
                            c0 = max(j - 1, 0) * 128
                            k.tt(tb_[:, c0:512], ps[:, c0:512], strips[h][:, (4 - j) * 128 + c0:(4 - j) * 128 + 512],
                                 ALU.add, r=[ps, strips[h]], w=[tb_])
                            k.act(P_t[:, c0:512], tb_[:, c0:512], AF.Exp, r=[tb_, negMb], w=[P_t],
                                  bias=negMb[:, c, h:h + 1])
                        else:
                            k.act(P_t[:], ps[:, :], AF.Exp, r=[ps, farb], w=[P_t], bias=farb[:, c, h:h + 1])
                        last = (kb == nkb - 1)
                        k.mm(accV[:, c0:512], Vp[:, kb, h, 0:128], P_t[:, c0:512], start=(kb == 0), stop=last,
                             r=[Vp, P_t], w=[accV], inc=last)
                        k.mm(accS[:, c0:512], ones_b[:, :], P_t[:, c0:512], start=(kb == 0), stop=last,
                             r=[ones_b, P_t], w=[accS], inc=True)
                    rs_ = rsb[ic % 2]
                    k.recip(rs_[:], accS[:, :], r=[accS], w=[rs_])
                    if c == 0:
                        o0_ = o0[gh % 2]
                        k.tt(o0_[:], accV[:, :], rs_[:], ALU.mult, r=[accV, rs_], w=[o0_])
                    else:
                        t_ = t1[gh % 2]
                        k.tt(t_[:], accV[:, :], rs_[:], ALU.mult, r=[accV, rs_], w=[t_])
                        k.stt(ha_[:], t_[:], neglam[:, 0:1], o0_[:], ALU.mult, ALU.add, r=[t_, neglam, o0_], w=[ha_])
                def epilogue(ha_=ha_, az_=az_, h=h, g=g, ghl=gh):
                    sq_ = sqb_[ghl % 2]
                    rd_ = rstd[ghl % 2]
                    stg = stage[ghl % 2]
                    k.act(sq_[:], ha_[:], AF.Square, r=[ha_], w=[sq_])
                    pn = psb[4 + (it + 2) % 4]
                    k.mm(pn[:, :], ones_b[:, :], sq_[:], True, True, r=[ones_b, sq_], w=[pn])
                    k.act(rd_[:], pn[:, :], AF.Ln, r=[pn, consts], w=[rd_], scale=1.0 / 128, bias=epsb)
                    k.act(rd_[:], rd_[:], AF.Exp, r=[rd_], w=[rd_], scale=-0.5)
                    k.tt(ha_[:], ha_[:], rd_[:], ALU.mult, r=[ha_, rd_], w=[ha_])
                    k.stt(stg[:], ha_[:], gAcol[:, h, 0:1], az_[:], ALU.mult, ALU.mult, r=[ha_, gAcol, az_], w=[stg])
                    k.dma("sp", yaT.ap()[h * 128:(h + 1) * 128, g * 512:(g + 1) * 512], stg[:], r=[stg],
                          w=[k.dbuf["yaT"]])

                pending.append(epilogue)
                gh += 1
        while pending:
            pending.pop(0)()

    def phase_merge(l):
        f = lambda name, shp, dt=F32: k.sb(name, shp, dt)
        last = (l == depth - 1)
        norm_alloc()
        wbr = f("wbr", [128, 3, 4, 1024], BF16)
        wo = f("wo", [128, 8, 1024], BF16)
        mark_m = k.sb_off
        wbs = [f("wbs%d" % i, [128, 4, 1024]) for i in range(2)]
        for b in range(3):
            s_ = wbs[b % 2]
            k.dma("sp", s_[:], P_["w_branch"].ap()[l, b].rearrange("(kc p) c -> p kc c", p=128), w=[s_])
            k.copy(wbr[:, b, :, :], s_[:], r=[s_], w=[wbr])
        for hf in range(2):
            s_ = wbs[(3 + hf) % 2]
            k.dma("sp", s_[:], P_["w_out"].ap()[l, hf * 512:(hf + 1) * 512, :].rearrange("(kc p) c -> p kc c", p=128),
                  w=[s_])
            k.copy(wo[:, hf * 4:(hf + 1) * 4, :], s_[:], r=[s_], w=[wo])
        if last:
            k.dma("sp", gbc[:], P_["final_g"].ap().rearrange("(o d) -> o d", o=1).partition_broadcast(128), w=[gbc])
        else:
            k.dma("sp", gbc[:], norm_g.ap()[l + 1:l + 2, :].partition_broadcast(128), w=[gbc])
        k.S.barrier()
        k.sb_off = mark_m
        yb_ = [[f("y%d_%d" % (b, i), [128, 4, 512], BF16) for i in range(2)] for b in range(3)]
        gt = [f("gt%d" % i, [128, 512], BF16) for i in range(6)]
        m1 = [f("m1_%d" % i, [128, 512]) for i in range(2)]
        m2 = [f("m2_%d" % i, [128, 512]) for i in range(2)]
        mT = [f("mT%d" % i, [128, 8, 512], BF16) for i in range(2)]
        xo = [f("xo%d" % i, [128, D_MODEL]) for i in range(2)]
        xn = [f("xn%d" % i, [128, D_MODEL]) for i in range(2)]
        x_src = x_in if l == 0 else xs_d
        srcs = (ysT, ymT, yaT)
        st_ = {"gi": 0}

        def load_y(j):
            for b in range(3):
                k.dma("sp", yb_[b][j % 2][:],
                      srcs[b].ap()[:, j * 512:(j + 1) * 512].rearrange("(c p) t -> p c t", p=128),
                      r=[k.dbuf[srcs[b].name]], w=[yb_[b][j % 2]])

        def branch(j):
            ys_ = [yb_[b][j % 2] for b in range(3)]
            mT_ = mT[j % 2]
            for dc in range(8):
                m1_, m2_ = m1[dc % 2], m2[dc % 2]
                for b in range(3):
                    g_ = gt[st_["gi"] % 6]
                    st_["gi"] += 1
                    r0 = b * 1024 + dc * 128
                    k.dma("sp", g_[:], gT.ap()[r0:r0 + 128, j * 512:(j + 1) * 512], r=[k.dbuf["gT"]], w=[g_])
                    p = next_ps()
                    for cc in range(4):
                        k.mm(p[:, :], wbr[:, b, cc, dc * 128:(dc + 1) * 128], ys_[b][:, cc, :], start=(cc == 0),
                             stop=(cc == 3), r=[wbr, ys_[b]], w=[p])
                    if b == 0:
                        k.tt(m1_[:], p[:, :], g_[:], ALU.mult, r=[p, g_], w=[m1_])
                    else:
                        k.tt(m2_[:], p[:, :], g_[:], ALU.mult, r=[p, g_], w=[m2_])
                        if b == 1:
                            k.tt(m1_[:], m1_[:], m2_[:], ALU.add, r=[m1_, m2_], w=[m1_], eng="pool")
                        else:
                            k.tt(mT_[:, dc, :], m1_[:], m2_[:], ALU.add, r=[m1_, m2_], w=[mT_])

        def outproj(j):
            mT_ = mT[j % 2]
            for t4 in range(4):
                n = j * 4 + t4
                xo_, xn_ = xo[n % 2], xn[n % 2]
                k.dma("sp", xo_[:], x_src.ap()[n * 128:(n + 1) * 128, :], r=[k.dbuf[x_src.name]], w=[xo_])
                for hf in range(2):
                    p = next_ps()
                    for dc in range(8):
                        k.mm(p[:, :], mT_[:, dc, t4 * 128:(t4 + 1) * 128], wo[:, dc, hf * 512:(hf + 1) * 512],
                             start=(dc == 0), stop=(dc == 7), r=[mT_, wo], w=[p])
                    k.tt(xn_[:, hf * 512:(hf + 1) * 512], p[:, :], xo_[:, hf * 512:(hf + 1) * 512], ALU.add,
                         r=[p, xo_], w=[xn_])
                if last:
                    r_, _ = norm_rstd(xn_)
                    k.stt(xo_[:], xn_[:], r_[:, 0:1], gbc[:], ALU.mult, ALU.mult, r=[xn_, r_, gbc], w=[xo_])
                    k.dma("sp", out_d.ap()[n * 128:(n + 1) * 128, :], xo_[:], r=[xo_], w=[k.dbuf["out"]])
                else:
                    k.dma("sp", xs_d.ap()[n * 128:(n + 1) * 128, :], xn_[:], r=[xn_], w=[k.dbuf["xs"]])
                    norm_transpose(xn_, n * 128)

        load_y(0)
        branch(0)
        for j in range(NB):
            if j + 1 < NB:
                load_y(j + 1)
                branch(j + 1)
            outproj(j)

    for l in range(depth):
        if "proj" in phases:
            phase_proj(l)
        k.phase_reset(keep_hT=False)
        if "mlstm" in phases:
            pre_m = {}
            phase_gates(l, pre_m)
            k.S.barrier()
            k.sb_off = pre_m["mark"]
            phase_mlstm(l, pre_m)
        k.phase_reset(keep_hT=False)
        if "s5" in phases:
            phase_s5(l)
        k.phase_reset(keep_hT=False)
        if "attn" in phases:
            phase_attn(l)
        k.phase_reset(keep_hT=True)
        if "merge" in phases:
            phase_merge(l)
        k.phase_reset(keep_hT=True)

    k.S.finish("sp")
    k.S.emit()
    return k


_CACHE = {}


def kernel(**inputs):
    L, depth = SEQ, DEPTH
    if "k" not in _CACHE:
        _CACHE["k"] = build(L, depth)
    kb = _CACHE["k"]
    consts = host_consts()
    x = np.ascontiguousarray(inputs["x"], dtype=np.float32)
    shared = {n: np.ascontiguousarray(inputs[n], dtype=np.float32) for n in PARAM_SHAPES(depth)}
    shared.update(consts)
    in_maps = []
    for c in range(NCORES):
        m = dict(shared)
        m["x"] = x[c]
        in_maps.append(m)
    res = run_bass_kernel_spmd(kb.nc, in_maps, core_ids=list(range(NCORES)))
    return np.stack([np.asarray(r["out"], dtype=np.float32) for r in res.results], axis=0)
```

```python
import math
from contextlib import ExitStack
import numpy as np
import concourse.bass as bass
import concourse.mybir as mybir
from concourse.bass_utils import run_bass_kernel_spmd

F32 = mybir.dt.float32
BF16 = mybir.dt.bfloat16
I32 = mybir.dt.int32
ALU = mybir.AluOpType
AF = mybir.ActivationFunctionType
AX = mybir.AxisListType

D_MODEL = 1024
SEQ = 4096
DEPTH = 2
W = 512
D_IN = 8712
NCORES = 8
EPS = 1e-6


class Buf:
    __slots__ = ("name", "w", "r")

    def __init__(self, name=""):
        self.name = name
        self.w = None
        self.r = {}


class Sched:
    ENGS = (("pe", "tensor"), ("dve", "vector"), ("act", "scalar"), ("pool", "gpsimd"), ("sp", "sync"))
    NSLOT = 8

    def __init__(self, nc):
        self.nc = nc
        self.ops = {e: [] for e, _ in self.ENGS}
        self.cnt = {}
        self.seen = {e: {} for e, _ in self.ENGS}
        self.dma_i = {e: 0 for e, _ in self.ENGS}

    def _waits(self, eng, r, w, extra=None):
        deps = dict(extra or {})

        def add(k, v):
            if v > deps.get(k, 0):
                deps[k] = v

        for b in r:
            if b.w is not None:
                add(*b.w)
        for b in w:
            if b.w is not None and b.w[0] != eng:
                add(*b.w)
            for k, v in b.r.items():
                if k != eng:
                    add(k, v)
        waits = []
        seen = self.seen[eng]
        for k, v in deps.items():
            if k == "pe" and eng == "pe":
                continue
            if seen.get(k, 0) >= v:
                continue
            seen[k] = v
            waits.append((k, v))
        return waits

    def op(self, eng, fn, r=(), w=(), inc=True):
        waits = self._waits(eng, r, w)
        n = self.cnt.get(eng, 0) + 1
        if inc:
            self.cnt[eng] = n
        self.ops[eng].append((waits, fn, (eng, 1, n) if inc else None))
        for b in r:
            if b.r.get(eng, 0) < n:
                b.r[eng] = n
        for b in w:
            b.w = (eng, n)
            b.r = {}

    def dma(self, q, out, in_, r=(), w=(), **kw):
        i = self.dma_i[q]
        self.dma_i[q] += 1
        key = ("dma", q, i % self.NSLOT)
        n = self.cnt.get(key, 0) + 16
        extra = {key: n - 16} if n > 16 else None
        waits = self._waits(q, r, w, extra)
        self.cnt[key] = n
        self.ops[q].append((waits, lambda e: e.dma_start(out=out, in_=in_, **kw), (key, 16, n)))
        for b in r:
            b.r[key] = n
        for b in w:
            b.w = (key, n)
            b.r = {}

    def barrier(self):
        snap = dict(self.cnt)
        for eng, _ in self.ENGS:
            waits = [(k, v) for k, v in snap.items() if self.seen[eng].get(k, 0) < v and k != eng]
            for k, v in waits:
                self.seen[eng][k] = v
            if waits:
                self.ops[eng].append((waits, None, None))

    def finish(self, q="sp"):
        waits = [(k, v) for k, v in self.cnt.items() if self.seen[q].get(k, 0) < v]
        self.ops[q].append((waits, None, None))

    def emit(self):
        nc = self.nc
        targets = {}
        for eng, _ in self.ENGS:
            for waits, fn, inc in self.ops[eng]:
                for wk, wv in waits:
                    targets.setdefault(wk, set()).add(wv)
        rank = {}
        for key, vs in targets.items():
            step = 16 if isinstance(key, tuple) else 1
            rank[key] = {v: (i + 1) * step for i, v in enumerate(sorted(vs))}
        with ExitStack() as st:
            sems = {}
            for i, key in enumerate(self.cnt):
                sems[key] = st.enter_context(nc.semaphore("s%d" % i))
            block = st.enter_context(nc.Block())
            for eng, attr in self.ENGS:
                ops = self.ops[eng]
                if not ops:
                    continue

                def body(e, ops=ops):
                    for waits, fn, inc in ops:
                        for wk, wv in waits:
                            e.wait_ge(sems[wk], rank[wk][wv])
                        if fn is not None:
                            ins = fn(e)
                            if inc is not None and inc[2] in targets.get(inc[0], ()):
                                ins.then_inc(sems[inc[0]], inc[1])

                getattr(block, attr)(body)


class T:
    def __init__(self, t, name):
        self.t = t
        self.b = Buf(name)

    def __getitem__(self, k):
        return self.t[k]


class K:
    def __init__(self, L, depth, dbg=False):
        self.L = L
        self.depth = depth
        self.dbg = dbg
        self.nc = bass.Bass("TRN2", target_bir_lowering=False)
        self.S = Sched(self.nc)
        self.dram = {}
        self.dbuf = {}
        self.nps = 0
        self.sb_off = 16640
        self.sb_n = 0
        self.sb_mark = 0

    def sb(self, name, shape, dt=F32):
        nbytes = int(np.prod(shape[1:])) * (2 if dt == BF16 else 4)
        off = (self.sb_off + 63) // 64 * 64
        self.sb_off = off + nbytes
        assert self.sb_off <= 16640 + 206 * 1024, ("SBUF overflow", name, self.sb_off)
        self.sb_n += 1
        nm = "%s_%d" % (name, self.sb_n)
        return T(self.nc.alloc_sbuf_tensor_at(nm, list(shape), dt, offset=off), nm)

    def phase_mark(self):
        self.sb_mark = self.sb_off

    def phase_reset(self, keep_hT=True):
        self.S.barrier()
        self.sb_off = self.sb_mark if keep_hT else self.sb_mark0

    def ps(self, name, shape=(128, 512), dt=F32):
        return T(self.nc.alloc_psum_tensor(name, list(shape), dt), name)

    def din(self, name, shape, dt=F32):
        t = self.nc.dram_tensor(name, list(shape), dt, kind="ExternalInput")
        self.dram[name] = t
        self.dbuf[name] = Buf(name)
        return t

    def dscratch(self, name, shape, dt=F32, out=False):
        kind = "ExternalOutput" if (out or self.dbg) else "Internal"
        t = self.nc.dram_tensor(name, list(shape), dt, kind=kind)
        self.dram[name] = t
        self.dbuf[name] = Buf(name)
        return t

    def op(self, eng, fn, r=(), w=(), inc=True):
        self.S.op(eng, fn, [x.b if isinstance(x, T) else x for x in r],
                  [x.b if isinstance(x, T) else x for x in w], inc)

    def dma(self, q, out, in_, r=(), w=(), **kw):
        self.S.dma(q, out, in_, [x.b if isinstance(x, T) else x for x in r],
                   [x.b if isinstance(x, T) else x for x in w], **kw)

    def mm(self, out, lhsT, rhs, start, stop, r=(), w=(), inc=None):
        if inc is None:
            inc = stop
        self.op("pe", lambda e: e.matmul(out, lhsT, rhs, start=start, stop=stop), r=r, w=w, inc=inc)

    def act(self, out, in_, func, r=(), w=(), bias=None, scale=1.0, accum=None, eng="act"):
        kw = {}
        if bias is not None:
            kw["bias"] = bias
        if accum is not None:
            kw["accum_out"] = accum
        self.op(eng, lambda e: e.activation(out=out, in_=in_, func=func, scale=scale, **kw), r=r, w=w)

    def tt(self, out, in0, in1, op, r=(), w=(), eng="dve"):
        self.op(eng, lambda e: e.tensor_tensor(out=out, in0=in0, in1=in1, op=op), r=r, w=w)

    def ts(self, out, in0, s1, s2, op0, op1=None, r=(), w=(), eng="dve", accum=None):
        kw = {}
        if op1 is not None:
            kw["op1"] = op1
        if accum is not None:
            kw["accum_out"] = accum
        self.op(eng, lambda e: e.tensor_scalar(out=out, in0=in0, scalar1=s1, scalar2=s2, op0=op0, **kw), r=r, w=w)

    def stt(self, out, in0, scalar, in1, op0, op1, r=(), w=(), eng="dve"):
        self.op(eng, lambda e: e.scalar_tensor_tensor(out=out, in0=in0, scalar=scalar, in1=in1, op0=op0, op1=op1),
                r=r, w=w)

    def copy(self, out, in_, r=(), w=(), eng="dve"):
        self.op(eng, lambda e: e.tensor_copy(out=out, in_=in_), r=r, w=w)

    def memset(self, ap, val, w=(), eng="pool"):
        self.op(eng, lambda e: e.memset(ap, val), w=w)

    def scan(self, out, d0, d1, init, op0, op1, r=(), w=(), eng="dve"):
        self.op(eng, lambda e: e.tensor_tensor_scan(out=out, data0=d0, data1=d1, initial=init, op0=op0, op1=op1),
                r=r, w=w)

    def reduce(self, out, in_, op, axis, r=(), w=(), eng="dve"):
        self.op(eng, lambda e: e.tensor_reduce(out=out, in_=in_, axis=axis, op=op), r=r, w=w)

    def recip(self, out, in_, r=(), w=(), eng="dve"):
        self.op(eng, lambda e: e.reciprocal(out=out, in_=in_), r=r, w=w)

    def tr(self, out, in_, ident, r=(), w=(), inc=True):
        self.op("pe", lambda e: e.transpose(out, in_, ident), r=r, w=w, inc=inc)


def host_consts():
    c = {}
    c["ident"] = np.eye(128, dtype=np.float32)
    c["iota1"] = np.tile(np.arange(1, 513, dtype=np.float32)[None, :], (128, 1))
    s_ = np.arange(128)
    c["causal01"] = (s_[:, None] <= s_[None, :]).astype(np.float32)
    c["chmask"] = (s_[:, None] // 16 == np.arange(8)[None, :]).astype(np.float32)
    c["hmask"] = (s_[:, None] // 64 == np.arange(2)[None, :]).astype(np.float32)
    sel8 = np.zeros((8, 8), np.float32)
    for h in range(4):
        sel8[h, h] = 1.0
        sel8[4 + h, h] = -1.0
        sel8[4 + h, 4 + h] = 1.0
    c["sel8"] = sel8
    rm = np.zeros((8, 2), np.float32)
    rm[0:4, 0] = 1.0
    rm[4:8, 1] = 1.0
    c["rowmask8"] = rm
    sr4 = np.zeros((4, 4 * 128), np.float32)
    for h in range(4):
        sr4[h, h * 128:(h + 1) * 128] = 1.0
    c["selrow4"] = sr4
    bo = np.zeros((128, 2), np.float32)
    bo[0:64, 0] = 1.0
    bo[64:128, 1] = 1.0
    c["blockones"] = bo
    e = np.arange(1152)
    d = e - 511
    n = np.maximum(d, 0)
    large = 16 + (np.log(np.maximum(n, 1).astype(np.float32) / np.float32(16)) / np.float32(math.log(128 / 16))
                  * np.float32(16)).astype(np.int32)
    large = np.minimum(large, 31)
    bucket = np.where(n < 16, n, large)
    oh = np.zeros((32, 1152), np.float32)
    oh[bucket, e] = 1.0
    oh[:, d < 0] = 0.0
    c["oh"] = oh
    c["maskvec"] = np.where(d < 0, -30000.0, 0.0).astype(np.float32)[None, :]
    c["ones_row"] = np.ones((1, 128), np.float32)
    c["antiI"] = np.ascontiguousarray(np.eye(128, dtype=np.float32)[::-1])
    return c


CONST_SHAPES = {"ident": [128, 128], "iota1": [128, 512], "causal01": [128, 128], "chmask": [128, 8],
                "hmask": [128, 2], "sel8": [8, 8], "rowmask8": [8, 2], "selrow4": [4, 512],
                "blockones": [128, 2], "oh": [32, 1152], "maskvec": [1, 1152], "ones_row": [1, 128], "antiI": [128, 128]}

PARAM_SHAPES = lambda depth: {
    "norm_g": [depth, 1024], "w_in": [depth, 1024, D_IN], "conv_w": [depth, 4, 1024], "conv_b": [depth, 1024],
    "if_bias": [depth, 2, 4], "m_norm_g": [depth, 512], "ssm_lam_re": [depth, 32, 64],
    "ssm_lam_im": [depth, 32, 64], "ssm_b_re": [depth, 32, 64, 16], "ssm_b_im": [depth, 32, 64, 16],
    "ssm_c_re": [depth, 32, 16, 64], "ssm_c_im": [depth, 32, 16, 64], "ssm_d": [depth, 512],
    "ssm_log_step": [depth, 32], "ssm_w_glu": [depth, 512, 512], "ssm_b_glu": [depth, 512],
    "diff_lam": [depth, 4, 64], "diff_norm_g": [depth, 512], "rel_bias": [32, 4],
    "w_branch": [depth, 3, 512, 1024], "w_out": [depth, 1024, 1024], "final_g": [1024]}

PI = math.pi


def build(L=SEQ, depth=DEPTH, dbg=False, phases=("proj", "s5", "mlstm", "attn", "merge")):
    k = K(L, depth, dbg)
    nc = k.nc
    NT = L // 128
    NB = L // 512
    x_in = k.din("x", [L, D_MODEL])
    P_ = {n: k.din(n, shp) for n, shp in PARAM_SHAPES(depth).items()}
    C_ = {n: k.din(n, shp) for n, shp in CONST_SHAPES.items()}
    norm_g, w_in, conv_w, conv_b = P_["norm_g"], P_["w_in"], P_["conv_w"], P_["conv_b"]
    out_d = k.dscratch("out", [L, D_MODEL], F32, out=True)
    xs_d = k.dscratch("xs", [L, D_MODEL], F32)

    suT = k.dscratch("suT", [W, L], BF16)
    szT = k.dscratch("szT", [W, L], BF16)
    mqT = k.dscratch("mqT", [W, L], BF16)
    mkT = k.dscratch("mkT", [W, L], BF16)
    mv = k.dscratch("mv", [L, W], BF16)
    mo = k.dscratch("mo", [L, W], BF16)
    mz = k.dscratch("mz", [L, W], BF16)
    aqT = k.dscratch("aqT", [W, L], BF16)
    akT = k.dscratch("akT", [W, L], BF16)
    av = k.dscratch("av", [L, W], BF16)
    azT = k.dscratch("azT", [W, L], BF16)
    gT = k.dscratch("gT", [3 * D_MODEL, L], BF16)
    mifd = k.dscratch("mif", [8, L], F32)
    ysT = k.dscratch("ysT", [W, L], BF16)
    ymT = k.dscratch("ymT", [W, L], BF16)
    yaT = k.dscratch("yaT", [W, L], BF16)
    Gd = k.dscratch("Gd", [4, 1152], F32)

    def cload(name, dt=F32, parts=None):
        shp = CONST_SHAPES[name]
        t = k.sb(name, shp, F32)
        k.dma("sp", t[:], C_[name].ap(), w=[t])
        if dt == BF16:
            tb = k.sb(name + "_b", shp, BF16)
            k.copy(tb[:], t[:], r=[t], w=[tb])
            return t, tb
        return t

    ident_f, ident_b = cload("ident", BF16)
    causal01 = cload("causal01")
    chmask = cload("chmask")
    hmask = cload("hmask")
    sel8 = cload("sel8")
    rowmask8 = cload("rowmask8")
    selrow4 = cload("selrow4")
    blockones_f, blockones_b = cload("blockones", BF16)
    ones_row = cload("ones_row")
    gbc = k.sb("gbc", [128, D_MODEL], F32)
    wf = k.sb("wf", [128, NT, 8])
    dbc = k.sb("dbc", [128, 4, NT])
    consts = k.sb("consts", [128, 8], F32)
    CV = [EPS, -PI, 0.0, 1.0, 0.5 * PI]
    for i, v in enumerate(CV):
        k.memset(consts[:, i:i + 1], v, w=[consts])
    epsb = consts[:, 0:1]
    negpi = consts[:, 1:2]
    k.sb_mark0 = k.sb_off
    hT = k.sb("hT", [128, 8, L], BF16)
    psb = [k.ps("ps%d" % i) for i in range(8)]

    def next_ps():
        p = psb[k.nps % 8]
        k.nps += 1
        return p

    def rows_to_cols(src, R, nchunk, dst_ap, dstT):
        p = next_ps()
        for c in range(nchunk):
            k.mm(p[:, c * R:(c + 1) * R], src[0:R, c * 128:(c + 1) * 128], ident_f[0:R, 0:R],
                 start=True, stop=True, r=[src, ident_f], w=[p], inc=(c == nchunk - 1))
        k.copy(dst_ap, p[:, 0:nchunk * R].rearrange("p (c r) -> p c r", r=R), r=[p], w=[dstT])

    nrm = {}

    def norm_alloc():
        nrm["hb"] = [k.sb("hb%d" % i, [128, D_MODEL], BF16) for i in range(2)]
        nrm["sq"] = k.sb("sq", [128, D_MODEL], F32)
        nrm["ss"] = [k.sb("ss%d" % i, [128, 1], F32) for i in range(2)]
        nrm["rstd"] = [k.sb("rstd%d" % i, [128, 1], F32) for i in range(2)]
        nrm["i"] = 0

    def norm_rstd(xtile):
        i = nrm["i"]
        nrm["i"] += 1
        s_, r_ = nrm["ss"][i % 2], nrm["rstd"][i % 2]
        sq = nrm["sq"]
        k.act(sq[:], xtile[:], AF.Square, r=[xtile], w=[sq, s_], accum=s_[:])
        k.act(r_[:], s_[:], AF.Sqrt, r=[s_, consts], w=[r_], scale=1.0 / D_MODEL, bias=epsb)
        k.recip(r_[:], r_[:], r=[r_], w=[r_])
        return r_, i

    def norm_transpose(xtile, tok0):
        r_, i = norm_rstd(xtile)
        h_ = nrm["hb"][i % 2]
        k.stt(h_[:], xtile[:], r_[:, 0:1], gbc[:], ALU.mult, ALU.mult, r=[xtile, r_, gbc], w=[h_])
        p = next_ps()
        pv = p.t.bitcast(BF16)
        for c in range(8):
            k.tr(pv[:, c * 128:(c + 1) * 128], h_[:, c * 128:(c + 1) * 128], ident_b[:], r=[h_, ident_b], w=[p],
                 inc=(c == 7))
        k.act(hT[:, :, tok0:tok0 + 128], pv[:, :].rearrange("p (c t) -> p c t", c=8), AF.Copy, r=[p], w=[hT])

    def headnorm_out(ht, gtile, gate, stage, col0, ws):
        ssq, sqs, yb = ws["ssq"], ws["sqs"], ws["yb"]
        for h in range(4):
            k.act(sqs[:, :], ht[:, h * 128:(h + 1) * 128], AF.Square, r=[ht], w=[sqs, ssq], accum=ssq[:, h:h + 1])
        k.act(ssq[:, :], ssq[:, :], AF.Sqrt, r=[ssq, consts], w=[ssq], scale=1.0 / 128, bias=epsb)
        k.recip(ssq[:, :], ssq[:, :], r=[ssq], w=[ssq])
        for h in range(4):
            k.stt(ht[:, h * 128:(h + 1) * 128], ht[:, h * 128:(h + 1) * 128], ssq[:, h:h + 1],
                  gtile[:, h * 128:(h + 1) * 128], ALU.mult, ALU.mult, r=[ht, ssq, gtile], w=[ht])
        k.tt(yb[:, :], ht[:, :], gate[:, :], ALU.mult, r=[ht, gate], w=[yb])
        p = next_ps()
        pv = p.t.bitcast(BF16)
        for h in range(4):
            k.tr(pv[:, h * 128:(h + 1) * 128], yb[:, h * 128:(h + 1) * 128], ident_b[:], r=[yb, ident_b], w=[p],
                 inc=(h == 3))
        k.act(stage[:, :, col0:col0 + 128], pv[:, 0:512].rearrange("p (c t) -> p c t", c=4), AF.Copy,
              r=[p], w=[stage])

    k.phase_mark()
    if "attn" in phases:
        tab = k.sb("tab", [32, 4], F32)
        oh = k.sb("oh", [32, 1152], F32)
        mvec = k.sb("mvec", [1, 1152], F32)
        Gs = k.sb("Gs", [4, 1152], F32)
        k.dma("sp", tab[:], P_["rel_bias"].ap(), w=[tab])
        k.dma("sp", oh[:], C_["oh"].ap(), w=[oh])
        k.dma("sp", mvec[:], C_["maskvec"].ap(), w=[mvec])
        for c0 in range(0, 1152, 384):
            p = next_ps()
            k.mm(p[0:4, 0:384], tab[:, :], oh[:, c0:c0 + 384], start=True, stop=False, r=[tab, oh], w=[p])
            k.mm(p[0:4, 0:384], ones_row[0:1, 0:4], mvec[0:1, c0:c0 + 384], start=False, stop=True,
                 r=[ones_row, mvec], w=[p])
            k.copy(Gs[:, c0:c0 + 384], p[0:4, 0:384], r=[p], w=[Gs])
        k.dma("sp", Gd.ap(), Gs[:], r=[Gs], w=[k.dbuf["Gd"]])
    k.phase_reset()

    def phase0():
        norm_alloc()
        xt = [k.sb("xt%d" % i, [128, D_MODEL], F32) for i in range(2)]
        k.dma("sp", gbc[:], norm_g.ap()[0:1, :].partition_broadcast(128), w=[gbc])
        for i in range(NT):
            xt_ = xt[i % 2]
            k.dma("sp", xt_[:], x_in.ap()[i * 128:(i + 1) * 128, :], w=[xt_])
            norm_transpose(xt_, i * 128)

    phase0()
    k.phase_reset()

    def phase_proj(l):
        wst = [k.sb("wst%d" % i, [128, 8, 512], F32) for i in range(2)]
        wbf = [k.sb("wbf%d" % i, [128, 8, 512], BF16) for i in range(2)]
        stT = [k.sb("stT%d" % i, [128, L], BF16) for i in range(2)]
        stN = [k.sb("stN%d" % i, [128, 512], BF16) for i in range(6)]
        cbuf = k.sb("cbuf", [128, L + 8], F32)
        cacc = k.sb("cacc", [128, L], F32)
        cw = k.sb("cw", [128, 8, 5], F32)
        crow = k.sb("crow", [5, 2 * W], F32)
        st = {"wi": 0, "ti": 0, "ni": 0, "ev": 0}
        mif = k.sb("mif_sb", [8, L], F32)

        def load_w(col0, ncols=512):
            i = st["wi"]
            st["wi"] += 1
            ws, wb = wst[i % 2], wbf[i % 2]
            k.dma("pool", ws[:, :, 0:ncols],
                  w_in.ap()[l, :, col0:col0 + ncols].rearrange("(kc p) c -> p kc c", p=128), w=[ws])
            k.copy(wb[:, :, 0:ncols], ws[:, :, 0:ncols], r=[ws], w=[wb], eng="pool")
            return wb

        def evac(out_ap, in_ap, func, r, w, scale=1.0):
            if func is None:
                st["ev"] += 1
                if st["ev"] % 2 == 0:
                    if scale == 1.0:
                        k.copy(out_ap, in_ap, r=r, w=w)
                    else:
                        k.ts(out_ap, in_ap, scale, None, ALU.mult, r=r, w=w)
                    return
                func = AF.Copy
            k.act(out_ap, in_ap, func, r=r, w=w, scale=scale)

        def proj_T_block(wb, col0, dst, row0, func, scale=1.0, conv=None):
            for c in range(4):
                if conv is None:
                    stg = stT[st["ti"] % 2]
                    st["ti"] += 1
                for tb in range(NB):
                    p = next_ps()
                    for kc in range(8):
                        k.mm(p[:, :], wb[:, kc, c * 128:(c + 1) * 128], hT[:, kc, tb * 512:(tb + 1) * 512],
                             start=(kc == 0), stop=(kc == 7), r=[wb, hT], w=[p])
                    if conv is None:
                        evac(stg[:, tb * 512:(tb + 1) * 512], p[:, :], func, r=[p], w=[stg], scale=scale)
                    else:
                        k.act(cbuf[:, 8 + tb * 512:8 + (tb + 1) * 512], p[:, :], AF.Copy, r=[p], w=[cbuf])
                if conv is not None:
                    cc = conv * 4 + c
                    stg = stT[st["ti"] % 2]
                    st["ti"] += 1
                    k.ts(cacc[:], cbuf[:, 8:8 + L], cw[:, cc, 3:4], cw[:, cc, 4:5], ALU.mult, ALU.add,
                         r=[cbuf, cw], w=[cacc])
                    for j in range(3):
                        k.stt(cacc[:], cbuf[:, 5 + j:5 + j + L], cw[:, cc, j:j + 1], cacc[:], ALU.mult, ALU.add,
                              r=[cbuf, cw, cacc], w=[cacc])
                    k.act(stg[:], cacc[:], AF.Silu, r=[cacc], w=[stg])
                k.dma("sp", dst.ap()[row0 + c * 128:row0 + (c + 1) * 128, :], stg[:], r=[stg], w=[k.dbuf[dst.name]])

        def proj_N_block(wb, col0, dst, func):
            for t in range(NT):
                p = next_ps()
                stg = stN[st["ni"] % 6]
                st["ni"] += 1
                for kc in range(8):
                    k.mm(p[:, :], hT[:, kc, t * 128:(t + 1) * 128], wb[:, kc, :],
                         start=(kc == 0), stop=(kc == 7), r=[wb, hT], w=[p])
                evac(stg[:], p[:, :], func, r=[p], w=[stg])
                k.dma("sp", dst.ap()[t * 128:(t + 1) * 128, :], stg[:], r=[stg], w=[k.dbuf[dst.name]])

        def proj_if(wb, col0=3584):
            for tb in range(NB):
                p = next_ps()
                for kc in range(8):
                    k.mm(p[0:8, :], wb[:, kc, 0:8], hT[:, kc, tb * 512:(tb + 1) * 512],
                         start=(kc == 0), stop=(kc == 7), r=[wb, hT], w=[p])
                k.act(mif[:, tb * 512:(tb + 1) * 512], p[0:8, :], AF.Copy, r=[p], w=[mif])

        k.memset(cbuf[:, 0:8], 0.0, w=[cbuf])
        k.dma("sp", crow[0:4, :], conv_w.ap()[l], w=[crow])
        k.dma("sp", crow[4:5, :], conv_b.ap()[l:l + 1, :], w=[crow])
        rows_to_cols(crow, 5, 8, cw[:, :, :], cw)
        specs = [(proj_if, 3584, 8, ()),
                 (proj_T_block, 0, 512, (suT, 0, None)),
                 (proj_T_block, 512, 512, (szT, 0, AF.Silu)),
                 (proj_T_block, 1024, 512, (mqT, 0, None, 1.0, 0)),
                 (proj_T_block, 1536, 512, (mkT, 0, None, 1.0, 1)),
                 (proj_N_block, 2048, 512, (mv, None)),
                 (proj_N_block, 2560, 512, (mo, AF.Sigmoid)),
                 (proj_N_block, 3072, 512, (mz, AF.Silu)),
                 (proj_T_block, 3592, 512, (aqT, 0, None, 0.125)),
                 (proj_T_block, 4104, 512, (akT, 0, None)),
                 (proj_N_block, 4616, 512, (av, None)),
                 (proj_T_block, 5128, 512, (azT, 0, AF.Silu))]
        for gb in range(6):
            specs.append((proj_T_block, 5640 + gb * 512, 512, (gT, gb * 512, AF.Sigmoid)))
        wbl = [None] * len(specs)
        wbl[0] = load_w(specs[0][1], specs[0][2])
        for i, (fn, col0, ncols, args) in enumerate(specs):
            if i + 1 < len(specs):
                wbl[i + 1] = load_w(specs[i + 1][1], specs[i + 1][2])
            fn(wbl[i], col0, *args)
            if i == 0:
                k.dma("sp", mifd.ap(), mif[:], r=[mif], w=[k.dbuf["mif"]])

    def sincos(ph_ap, shape, sin_out, cos_out, u, r, wT):
        ti = u["ti"]
        tf = u["tf"]
        uu = u["u"]
        for which, out_ap in (("sin", sin_out), ("cos", cos_out)):
            if which == "sin":
                k.ts(ti[:], ph_ap, 1.0 / (2.0 * PI), None, ALU.mult, r=r, w=[ti])
            else:
                k.ts(ti[:], ph_ap, 1.0 / (2.0 * PI), 0.25, ALU.mult, ALU.add, r=r, w=[ti])
            k.copy(tf[:], ti[:], r=[ti], w=[tf])
            k.stt(uu[:], tf[:], -2.0 * PI, ph_ap, ALU.mult, ALU.add, r=[tf] + list(r), w=[uu])
            if which == "sin":
                k.ts(uu[:], uu[:], -PI + 1e-5, PI - 1e-5, ALU.max, ALU.min, r=[uu], w=[uu])
                k.act(out_ap, uu[:], AF.Sin, r=[uu], w=wT)
            else:
                k.ts(uu[:], uu[:], -1.5 * PI + 1e-5, 0.5 * PI - 1e-5, ALU.max, ALU.min, r=[uu], w=[uu])
                k.act(out_ap, uu[:], AF.Sin, r=[uu, consts], w=wT, bias=consts[:, 4:5])

    def phase_s5(l):
        pre_ = {}
        for nm_, shp_, dt_ in (("mag", [128, 16], F32), ("th", [128, 16], F32), ("BTr", [128, 16, 128], BF16),
                               ("BTi", [128, 16, 128], BF16), ("CTr", [128, 16, 128], BF16),
                               ("CTi", [128, 16, 128], BF16), ("CTn", [128, 16, 128], BF16), ("Dg", [128, 4, 128], BF16), ("dcol", [128, 4, 2], F32),
                               ("wg", [128, 4, 512], BF16), ("cosT", [128, 16, 512], F32),
                               ("sinT", [128, 16, 512], F32)):
            pre_[nm_] = k.sb(nm_, shp_, dt_)
        mark_s5 = k.sb_off
        f = lambda name, shp, dt=F32: pre_[name] if name in pre_ else k.sb(name, shp, dt)
        lre, lim, lst = f("lre", [128, 16]), f("lim", [128, 16]), f("lst", [128, 16])
        mag, th, are, aim = f("mag", [128, 16]), f("th", [128, 16]), f("are", [128, 16]), f("aim", [128, 16])
        den, rre, rim, t16 = (f("den", [128, 16]), f("rre", [128, 16]),
                              f("rim", [128, 16]), f("t16", [128, 16]))
        u16 = {"ti": f("u16i", [128, 16], I32), "tf": f("u16f", [128, 16]), "u": f("u16u", [128, 16])}
        q = "sp"
        k.dma(q, lre[:], P_["ssm_lam_re"].ap()[l].rearrange("(s g) p -> (g p) s", g=2), w=[lre],
              allow_slow_non_contiguous=True)
        k.dma(q, lim[:], P_["ssm_lam_im"].ap()[l].rearrange("(s g) p -> (g p) s", g=2), w=[lim],
              allow_slow_non_contiguous=True)
        for h in range(2):
            k.dma(q, lst[h * 64:(h + 1) * 64, :],
                  P_["ssm_log_step"].ap()[l].rearrange("(s g) -> g s", g=2)[h:h + 1, :].partition_broadcast(64),
                  w=[lst], allow_slow_non_contiguous=True)
        k.act(lst[:], lst[:], AF.Exp, r=[lst], w=[lst])
        k.tt(mag[:], lre[:], lst[:], ALU.mult, r=[lre, lst], w=[mag])
        k.act(mag[:], mag[:], AF.Exp, r=[mag], w=[mag])
        k.tt(th[:], lim[:], lst[:], ALU.mult, r=[lim, lst], w=[th])
        sincos(th[:], [128, 16], aim[:], are[:], u16, [th], [aim, are])
        k.tt(are[:], are[:], mag[:], ALU.mult, r=[are, mag], w=[are])
        k.tt(aim[:], aim[:], mag[:], ALU.mult, r=[aim, mag], w=[aim])
        k.tt(den[:], lre[:], lre[:], ALU.mult, r=[lre], w=[den])
        k.tt(t16[:], lim[:], lim[:], ALU.mult, r=[lim], w=[t16])
        k.tt(den[:], den[:], t16[:], ALU.add, r=[den, t16], w=[den])
        k.recip(den[:], den[:], r=[den], w=[den])
        nre = f("nre", [128, 16])
        k.ts(nre[:], are[:], -1.0, None, ALU.add, r=[are], w=[nre])
        k.tt(rre[:], nre[:], lre[:], ALU.mult, r=[nre, lre], w=[rre])
        k.tt(t16[:], aim[:], lim[:], ALU.mult, r=[aim, lim], w=[t16])
        k.tt(rre[:], rre[:], t16[:], ALU.add, r=[rre, t16], w=[rre])
        k.tt(rre[:], rre[:], den[:], ALU.mult, r=[rre, den], w=[rre])
        k.tt(rim[:], aim[:], lre[:], ALU.mult, r=[aim, lre], w=[rim])
        k.tt(t16[:], nre[:], lim[:], ALU.mult, r=[nre, lim], w=[t16])
        k.tt(rim[:], rim[:], t16[:], ALU.subtract, r=[rim, t16], w=[rim])
        k.tt(rim[:], rim[:], den[:], ALU.mult, r=[rim, den], w=[rim])
        bre, bim = f("bre", [128, 16, 16]), f("bim", [128, 16, 16])
        bbr, bbi, tb_ = f("bbr", [128, 16, 16]), f("bbi", [128, 16, 16]), f("tb_", [128, 16, 16])
        k.dma(q, bre[:], P_["ssm_b_re"].ap()[l].rearrange("(s g) p c -> (g p) s c", g=2), w=[bre])
        k.dma(q, bim[:], P_["ssm_b_im"].ap()[l].rearrange("(s g) p c -> (g p) s c", g=2), w=[bim])
        rre_b = rre[:, :].unsqueeze(2).to_broadcast([128, 16, 16])
        rim_b = rim[:, :].unsqueeze(2).to_broadcast([128, 16, 16])
        k.tt(bbr[:], bre[:], rre_b, ALU.mult, r=[bre, rre], w=[bbr])
        k.tt(tb_[:], bim[:], rim_b, ALU.mult, r=[bim, rim], w=[tb_])
        k.tt(bbr[:], bbr[:], tb_[:], ALU.subtract, r=[bbr, tb_], w=[bbr])
        k.tt(bbi[:], bim[:], rre_b, ALU.mult, r=[bim, rre], w=[bbi])
        k.tt(tb_[:], bre[:], rim_b, ALU.mult, r=[bre, rim], w=[tb_])
        k.tt(bbi[:], bbi[:], tb_[:], ALU.add, r=[bbi, tb_], w=[bbi])
        BTr, BTi = f("BTr", [128, 16, 128], BF16), f("BTi", [128, 16, 128], BF16)
        fulls = [f("fulls%d" % i, [128, 16, 128], BF16) for i in range(2)]
        nfl = [0]

        def transpose_all(fa, dstT):
            for g4 in range(4):
                p = next_ps()
                pv = p.t.bitcast(BF16)
                for i4 in range(4):
                    k.tr(pv[:, i4 * 128:(i4 + 1) * 128], fa[:, g4 * 4 + i4, :], ident_b[:], r=[fa, ident_b], w=[p],
                         inc=(i4 == 3))
                k.act(dstT[:, g4 * 4:g4 * 4 + 4, :], pv[:, 0:512].rearrange("p (s c) -> p s c", s=4), AF.Copy,
                      r=[p], w=[dstT])

        for src, dstT in ((bbr, BTr), (bbi, BTi)):
            fa = fulls[nfl[0] % 2]
            nfl[0] += 1
            k.memset(fa[:], 0.0, w=[fa], eng="dve")
            for m in range(4):
                for h in range(2):
                    c0 = 32 * m + 16 * h
                    k.ts(fa[:, m::4, c0:c0 + 16], src[:, m::4, :], hmask[:, h:h + 1], None, ALU.mult,
                         r=[src, hmask, fa], w=[fa])
            transpose_all(fa, dstT)
        cnr, cni = f("cnr", [128, 4, 64]), f("cni", [128, 4, 64])
        k.dma(q, cnr[:], P_["ssm_c_re"].ap()[l].rearrange("(c4 gg) c p -> (gg c) c4 p", c4=4), w=[cnr])
        k.dma(q, cni[:], P_["ssm_c_im"].ap()[l].rearrange("(c4 gg) c p -> (gg c) c4 p", c4=4), w=[cni])
        CTr, CTi, CTn = f("CTr", [128, 16, 128], BF16), f("CTi", [128, 16, 128], BF16), f("CTn", [128, 16, 128], BF16)
        for src, dstT, sgn in ((cnr, CTr, 1.0), (cni, CTi, -1.0), (cnr, CTn, -1.0)):
            fa = fulls[nfl[0] % 2]
            nfl[0] += 1
            for m in range(4):
                for h in range(2):
                    j = m * 2 + h
                    k.ts(fa[:, m::4, 64 * h:64 * h + 64], src[:, 0:4, :], chmask[:, j:j + 1], sgn, ALU.mult, ALU.mult,
                         r=[src, chmask, fa], w=[fa])
            transpose_all(fa, dstT)
        drow = f("drow", [2, W])
        dcol = f("dcol", [128, 4, 2])
        k.dma(q, drow[0:1, :], P_["ssm_d"].ap()[l:l + 1, :], w=[drow])
        k.dma(q, drow[1:2, :], P_["ssm_b_glu"].ap()[l:l + 1, :], w=[drow])
        rows_to_cols(drow, 2, 4, dcol[:, :, :], dcol)
        Dg = f("Dg", [128, 4, 128], BF16)
        for c in range(4):
            k.ts(Dg[:, c, :], ident_f[:, :], dcol[:, c, 0:1], None, ALU.mult, r=[ident_f, dcol], w=[Dg])
        wg = f("wg", [128, 4, 512], BF16)
        wgs = f("wgs", [128, 4, 512])
        k.dma(q, wgs[:], P_["ssm_w_glu"].ap()[l].rearrange("(kc p) c -> p kc c", p=128), w=[wgs])
        k.copy(wg[:], wgs[:], r=[wgs], w=[wg], eng="pool")
        cosT, sinT = f("cosT", [128, 16, 512]), f("sinT", [128, 16, 512])
        iota1 = f("iota1", [128, 512])
        k.dma(q, iota1[:], C_["iota1"].ap(), w=[iota1])
        ph = [f("ph%d" % i, [128, 512]) for i in range(2)]
        us = [{"ti": f("usi%d" % i, [128, 512], I32), "tf": f("usf%d" % i, [128, 512]),
               "u": f("usu%d" % i, [128, 512])} for i in range(1)] * 2
        for s in range(16):
            ph_, u_ = ph[s % 2], us[s % 2]
            k.ts(ph_[:], iota1[:], th[:, s:s + 1], None, ALU.mult, r=[iota1, th], w=[ph_])
            sincos(ph_[:], [128, 512], sinT[:, s, :], cosT[:, s, :], u_, [ph_], [sinT, cosT])
        k.S.barrier()
        k.sb_off = mark_s5
        ccol, scol, nscol = f("ccol", [128, 16]), f("scol", [128, 16]), f("nscol", [128, 16])
        k.copy(ccol[:], cosT[:, :, 511], r=[cosT], w=[ccol])
        k.copy(scol[:], sinT[:, :, 511], r=[sinT], w=[scol])
        k.ts(nscol[:], sinT[:, :, 511], -1.0, None, ALU.mult, r=[sinT], w=[nscol])
        ctmp = [f("ctmp%d" % i, [128, 2]) for i in range(2)]
        cr, ci = f("cr", [128, 16]), f("ci", [128, 16])
        k.memset(cr[:], 0.0, w=[cr])
        k.memset(ci[:], 0.0, w=[ci])
        k.S.barrier()
        ub = [f("ub%d" % i, [128, 4, 512], BF16) for i in range(2)]
        zb = [f("zb%d" % i, [128, 4, 512], BF16) for i in range(2)]
        wk = {n: [f("%s%d" % (n, i), [128, 512]) for i in range(2)] for n in
              ("t1", "t2", "t3", "t4", "wre", "wim", "zre", "zim")}
        for a_ in ("d1", "d2", "d3", "d4"):
            wk[a_] = [f("%s%d" % (a_, i), [128, 512], BF16) for i in range(2)]
        gf = f("gf", [128, 4, 512])
        gb = f("gb", [128, 4, 512], BF16)
        y2 = [f("y2%d" % i, [128, 512]) for i in range(2)]
        sg = [f("sg%d" % i, [128, 512]) for i in range(2)]
        yo = [f("yo%d" % i, [128, 4, 512], BF16) for i in range(2)]
        crb = [Buf("cr%d" % s_) for s_ in range(16)]
        cib = [Buf("ci%d" % s_) for s_ in range(16)]

        def load_blk(j):
            k.dma("sp", ub[j % 2][:], suT.ap()[:, j * 512:(j + 1) * 512].rearrange("(c p) t -> p c t", p=128),
                  r=[k.dbuf["suT"]], w=[ub[j % 2]])
            k.dma("sp", zb[j % 2][:], szT.ap()[:, j * 512:(j + 1) * 512].rearrange("(c p) t -> p c t", p=128),
                  r=[k.dbuf["szT"]], w=[zb[j % 2]])

        iters = [(j, c4, s_) for j in range(NB) for c4 in range(4) for s_ in range(4 * c4, 4 * c4 + 4)]

        def stA(i):
            j, c4, s_ = iters[i]
            if c4 == 0 and s_ == 1 and j + 1 < NB:
                load_blk(j + 1)
            ub_ = ub[j % 2]
            W_ = {n: v[i % 2] for n, v in wk.items()}
            pa, pb = psb[2 + (i * 2) % 6], psb[2 + (i * 2 + 1) % 6]
            k.mm(pa[:, :], BTr[:, s_, :], ub_[:, c4, :], True, True, r=[BTr, ub_], w=[pa])
            k.mm(pb[:, :], BTi[:, s_, :], ub_[:, c4, :], True, True, r=[BTi, ub_], w=[pb])
            cs, sn = cosT[:, s_, :], sinT[:, s_, :]
            k.tt(W_["t1"][:], pa[:, :], cs, ALU.mult, r=[pa, cosT], w=[W_["t1"]])
            k.tt(W_["t2"][:], pb[:, :], sn, ALU.mult, r=[pb, sinT], w=[W_["t2"]])
            k.tt(W_["wre"][:], W_["t1"][:], W_["t2"][:], ALU.add, r=[W_["t1"], W_["t2"]], w=[W_["wre"]])
            k.tt(W_["t3"][:], pb[:, :], cs, ALU.mult, r=[pb, cosT], w=[W_["t3"]])
            k.tt(W_["t4"][:], pa[:, :], sn, ALU.mult, r=[pa, sinT], w=[W_["t4"]])
            k.tt(W_["wim"][:], W_["t3"][:], W_["t4"][:], ALU.subtract, r=[W_["t3"], W_["t4"]], w=[W_["wim"]])
            magb = mag[:, s_:s_ + 1].to_broadcast([128, 512])
            k.scan(W_["zre"][:], magb, W_["wre"][:], cr[:, s_:s_ + 1], ALU.mult, ALU.add,
                   r=[mag, W_["wre"], crb[s_]], w=[W_["zre"]])
            k.scan(W_["zim"][:], magb, W_["wim"][:], ci[:, s_:s_ + 1], ALU.mult, ALU.add,
                   r=[mag, W_["wim"], cib[s_]], w=[W_["zim"]])

        def stB(i):
            j, c4, s_ = iters[i]
            W_ = {n: v[i % 2] for n, v in wk.items()}
            cs, sn = cosT[:, s_, :], sinT[:, s_, :]
            k.tt(W_["d1"][:], W_["zre"][:], cs, ALU.mult, r=[W_["zre"], cosT], w=[W_["d1"]], eng="pool")
            k.tt(W_["d2"][:], W_["zim"][:], sn, ALU.mult, r=[W_["zim"], sinT], w=[W_["d2"]], eng="pool")
            k.tt(W_["d3"][:], W_["zre"][:], sn, ALU.mult, r=[W_["zre"], sinT], w=[W_["d3"]], eng="pool")
            k.tt(W_["d4"][:], W_["zim"][:], cs, ALU.mult, r=[W_["zim"], cosT], w=[W_["d4"]], eng="pool")

        def stC(i):
            j, c4, s_ = iters[i]
            ub_, zb_, yo_ = ub[j % 2], zb[j % 2], yo[j % 2]
            W_ = {n: v[i % 2] for n, v in wk.items()}
            yps = psb[c4 % 2]
            ct = ctmp[i % 2]
            k.act(ct[:, 0:1], W_["zim"][:, 511:512], AF.Copy, r=[W_["zim"], nscol], w=[ct], scale=nscol[:, s_:s_ + 1])
            k.act(cr[:, s_:s_ + 1], W_["zre"][:, 511:512], AF.Identity, r=[W_["zre"], ccol, ct], w=[crb[s_]],
                  scale=ccol[:, s_:s_ + 1], bias=ct[:, 0:1])
            k.act(ct[:, 1:2], W_["zim"][:, 511:512], AF.Copy, r=[W_["zim"], ccol], w=[ct], scale=ccol[:, s_:s_ + 1])
            k.act(ci[:, s_:s_ + 1], W_["zre"][:, 511:512], AF.Identity, r=[W_["zre"], scol, ct], w=[cib[s_]],
                  scale=scol[:, s_:s_ + 1], bias=ct[:, 1:2])
            k.mm(yps[:, :], CTr[:, s_, :], W_["d1"][:], start=(s_ == 4 * c4), stop=False, r=[CTr, W_["d1"]], w=[yps],
                 inc=True)
            k.mm(yps[:, :], CTn[:, s_, :], W_["d2"][:], start=False, stop=False, r=[CTn, W_["d2"]], w=[yps], inc=True)
            k.mm(yps[:, :], CTi[:, s_, :], W_["d3"][:], start=False, stop=False, r=[CTi, W_["d3"]], w=[yps], inc=True)
            k.mm(yps[:, :], CTi[:, s_, :], W_["d4"][:], start=False, stop=False, r=[CTi, W_["d4"]], w=[yps], inc=True)
            if s_ != 4 * c4 + 3:
                return
            k.mm(yps[:, :], Dg[:, c4, :], ub_[:, c4, :], start=False, stop=True, r=[Dg, ub_], w=[yps])
            y2_, sg_ = y2[c4 % 2], sg[c4 % 2]
            k.act(y2_[:], yps[:, :], AF.Square, r=[yps], w=[y2_])
            k.ts(y2_[:], y2_[:], 0.044715, 1.0, ALU.mult, ALU.add, r=[y2_], w=[y2_])
            k.tt(y2_[:], y2_[:], yps[:, :], ALU.mult, r=[y2_, yps], w=[y2_])
            k.act(sg_[:], y2_[:], AF.Sigmoid, r=[y2_], w=[sg_], scale=2.0 * math.sqrt(2.0 / PI))
            k.tt(gf[:, c4, :], sg_[:], yps[:, :], ALU.mult, r=[sg_, yps], w=[gf])
            k.act(gb[:, c4, :], gf[:, c4, :], AF.Copy, r=[gf], w=[gb])
            if c4 != 3:
                return
            for jc in range(4):
                pg = psb[2 + jc]
                for ic in range(4):
                    k.mm(pg[:, :], wg[:, ic, jc * 128:(jc + 1) * 128], gb[:, ic, :], start=(ic == 0), stop=(ic == 3),
                         r=[wg, gb], w=[pg])
                sg_ = sg[jc % 2]
                k.act(sg_[:], pg[:, :], AF.Sigmoid, r=[pg, dcol], w=[sg_], bias=dcol[:, jc, 1:2])
                k.tt(sg_[:], sg_[:], gf[:, jc, :], ALU.mult, r=[sg_, gf], w=[sg_])
                k.tt(yo_[:, jc, :], sg_[:], zb_[:, jc, :], ALU.mult, r=[sg_, zb_], w=[yo_])
            k.dma("sp", ysT.ap()[:, j * 512:(j + 1) * 512].rearrange("(c p) t -> p c t", p=128), yo_[:],
                  r=[yo_], w=[k.dbuf["ysT"]])

        load_blk(0)
        stA(0)
        for i in range(len(iters)):
            stB(i)
            if i + 1 < len(iters):
                stA(i + 1)
            stC(i)

    def phase_gates(l, pre):
        f = lambda name, shp, dt=F32: k.sb(name, shp, dt)
        NC = NT
        qT = f("mq", [128, 4, L], BF16)
        kTt = f("mk", [128, 4, L], BF16)
        Vp = f("mVp", [128, NT, 4, 130], BF16)
        pre.update(qT=qT, kTt=kTt, Vp=Vp, mark=k.sb_off)
        mif = f("mif_g", [8, L])
        k.dma("sp", mif[:], mifd.ap(), r=[k.dbuf["mif"]], w=[mif])
        ifb = f("ifb", [8, 1])
        k.dma("sp", ifb[:], P_["if_bias"].ap()[l].rearrange("a (h o) -> (a h) o", o=1), w=[ifb],
              allow_slow_non_contiguous=True)
        k.dma("sp", qT[:], mqT.ap().rearrange("(c p) t -> p c t", p=128), r=[k.dbuf["mqT"]], w=[qT])
        k.dma("sp", kTt[:], mkT.ap().rearrange("(c p) t -> p c t", p=128), r=[k.dbuf["mkT"]], w=[kTt])
        for n in range(NT):
            k.dma("sp", Vp[:, n, :, 0:128], mv.ap()[n * 128:(n + 1) * 128, :].rearrange("p (h d) -> p h d", h=4),
                  r=[k.dbuf["mv"]], w=[Vp])
        pre = f("pre", [8, L])
        lsg = f("lsg", [8, L])
        tmp8 = f("tmp8", [8, L])
        k.ts(pre[:], mif[:], ifb[:, 0:1], None, ALU.add, r=[mif, ifb], w=[pre])
        k.act(tmp8[:], pre[:], AF.Abs, r=[pre], w=[tmp8])
        k.act(tmp8[:], tmp8[:], AF.Exp, r=[tmp8], w=[tmp8], scale=-1.0)
        k.act(tmp8[:], tmp8[:], AF.Ln, r=[tmp8, consts], w=[tmp8], bias=consts[0:8, 3:4])
        k.ts(lsg[:], pre[:], 0.0, None, ALU.min, r=[pre], w=[lsg])
        k.tt(lsg[:], lsg[:], tmp8[:], ALU.subtract, r=[lsg, tmp8], w=[lsg])
        cs8 = tmp8
        k.scan(cs8[:], consts[0:8, 3:4].to_broadcast([8, L]), lsg[:], 0.0, ALU.mult, ALU.add, r=[consts, lsg],
               w=[cs8])
        R8 = lsg
        k.ts(R8[:], cs8[:], rowmask8[:, 1:2], None, ALU.mult, r=[cs8, rowmask8], w=[R8])
        k.stt(R8[:], pre[:], rowmask8[:, 0:1], R8[:], ALU.mult, ALU.add, r=[pre, rowmask8, R8], w=[R8])
        a4, B4 = f("a4", [4, L]), f("B4", [4, L])
        for tb in range(NB):
            p = next_ps()
            k.mm(p[0:4, :], sel8[:, 0:4], R8[:, tb * 512:(tb + 1) * 512], True, True, r=[sel8, R8], w=[p])
            k.copy(a4[:, tb * 512:(tb + 1) * 512], p[0:4, :], r=[p], w=[a4])
            p = next_ps()
            k.mm(p[0:4, :], sel8[:, 4:8], R8[:, tb * 512:(tb + 1) * 512], True, True, r=[sel8, R8], w=[p])
            k.copy(B4[:, tb * 512:(tb + 1) * 512], p[0:4, :], r=[p], w=[B4])
        cm, Mx, Mp, dd = f("cm", [4, NC]), f("Mx", [4, NC]), f("Mp", [4, NC]), f("dd", [4, NC])
        k.reduce(cm[:], a4[:, :].rearrange("p (n t) -> p n t", t=128), ALU.max, AX.X, r=[a4], w=[cm])
        k.scan(Mx[:], cm[:], cm[:], 0.0, ALU.max, ALU.max, r=[cm], w=[Mx])
        k.memset(Mp[:, 0:1], 0.0, w=[Mp])
        if NC > 1:
            k.copy(Mp[:, 1:NC], Mx[:, 0:NC - 1], r=[Mx], w=[Mp])
        k.tt(dd[:], Mp[:], Mx[:], ALU.subtract, r=[Mp, Mx], w=[dd])
        k.act(dd[:], dd[:], AF.Exp, r=[dd], w=[dd])
        Mb = Mx[:, :].unsqueeze(2).to_broadcast([4, NC, 128])
        wrow, frow = a4, B4
        k.tt(wrow[:, :].rearrange("p (n t) -> p n t", t=128), a4[:, :].rearrange("p (n t) -> p n t", t=128), Mb,
             ALU.subtract, r=[a4, Mx], w=[wrow])
        k.act(wrow[:], wrow[:], AF.Exp, r=[wrow], w=[wrow])
        k.tt(frow[:, :].rearrange("p (n t) -> p n t", t=128), B4[:, :].rearrange("p (n t) -> p n t", t=128), Mb,
             ALU.add, r=[B4, Mx], w=[frow])
        k.act(frow[:], frow[:], AF.Exp, r=[frow], w=[frow], scale=-1.0)
        for n0 in range(0, NT, 32):
            nn = min(32, NT - n0)
            p = next_ps()
            for n in range(n0, n0 + nn):
                c0 = (n - n0) * 8
                k.mm(p[:, c0:c0 + 4], wrow[0:4, n * 128:(n + 1) * 128], ident_f[0:4, 0:4], True, True,
                     r=[wrow, ident_f], w=[p], inc=False)
                k.mm(p[:, c0 + 4:c0 + 8], frow[0:4, n * 128:(n + 1) * 128], ident_f[0:4, 0:4], True, True,
                     r=[frow, ident_f], w=[p], inc=(n == n0 + nn - 1))
            k.copy(wf[:, n0:n0 + nn, :], p[:, 0:nn * 8].rearrange("p (n e) -> p n e", e=8), r=[p], w=[wf])
        for h in range(4):
            p = next_ps()
            k.mm(p[:, 0:NC], selrow4[0:4, h * 128:(h + 1) * 128], dd[0:4, :], True, True, r=[selrow4, dd], w=[p])
            k.copy(dbc[:, h, :], p[:, 0:NC], r=[p], w=[dbc])

    def phase_mlstm(l, pre):
        f = lambda name, shp, dt=F32: k.sb(name, shp, dt)
        NC = NT
        SC = 128.0 ** -0.5
        qT, kTt, Vp = pre["qT"], pre["kTt"], pre["Vp"]
        k.memset(Vp[:, :, :, 128:129], 1.0, w=[Vp], eng="dve")
        k.memset(Vp[:, :, :, 129:130], 0.0, w=[Vp], eng="dve")
        for n in range(NT):
            k.tt(Vp[:, n, :, :], Vp[:, n, :, :], wf[:, n, 0:4].unsqueeze(2).to_broadcast([128, 4, 130]), ALU.mult,
                 r=[Vp, wf], w=[Vp])
        gM = f("gM", [128, W])
        k.dma("sp", gM[:], P_["m_norm_g"].ap()[l:l + 1, :].partition_broadcast(128), w=[gM])
        Cst4 = f("Cst4", [128, 4, 130])
        Tm4 = f("Tm4", [128, 4, 130])
        Cs4 = [f("Cs4_%d" % i, [128, 4, 130], BF16) for i in range(2)]
        k.memset(Cst4[:], 0.0, w=[Cst4], eng="dve")
        Sm4 = [f("Sm4_%d" % i, [128, 4, 128], BF16) for i in range(2)]
        kN4 = [f("kN4_%d" % i, [128, 4, 128], BF16) for i in range(2)]
        puS4 = [f("puS4_%d" % i, [128, 4, 129]) for i in range(2)]
        d4 = [f("d4_%d" % i, [128, 8]) for i in range(2)]
        hmt = [f("hmt%d" % i, [128, W]) for i in range(2)]
        mot = [f("mot%d" % i, [128, W], BF16) for i in range(2)]
        mzt = [f("mzt%d" % i, [128, W], BF16) for i in range(2)]
        ws = {"ssq": f("m_ssq", [128, 4]), "sqs": f("m_sqs", [128, 128]), "yb": f("m_yb", [128, W], BF16)}
        stage = [f("mstage%d" % i, [128, 4, 512], BF16) for i in range(2)]
        cz_b = causal01[:, :].unsqueeze(1).to_broadcast([128, 4, 128])

        def stL(n):
            i2 = n % 2
            tk = slice(n * 128, (n + 1) * 128)
            psS = next_ps()
            for h in range(4):
                k.mm(psS[:, h * 128:(h + 1) * 128], kTt[:, h, tk], qT[:, h, tk], True, True, r=[kTt, qT], w=[psS],
                     inc=(h == 3))
            k.tt(Sm4[i2][:], psS[:, :].rearrange("p (h t) -> p h t", h=4), cz_b, ALU.mult, r=[psS, causal01],
                 w=[Sm4[i2]])
            pk = next_ps()
            pkv = pk.t.bitcast(BF16)
            for h in range(4):
                k.tr(pkv[:, h * 128:(h + 1) * 128], kTt[:, h, tk], ident_b[:], r=[kTt, ident_b], w=[pk],
                     inc=(h == 3))
            k.act(kN4[i2][:], pkv[:, 0:512].rearrange("p (h t) -> p h t", h=4), AF.Copy, r=[pk], w=[kN4[i2]])
            for hp in range(2):
                pu = next_ps()
                for hh in range(2):
                    h = hp * 2 + hh
                    k.mm(pu[:, hh * 129:(hh + 1) * 129], kN4[i2][:, h, :], Vp[:, n, h, 0:129], True, True,
                         r=[kN4[i2], Vp], w=[pu], inc=(hh == 1))
                k.act(puS4[i2][:, hp * 2:hp * 2 + 2, :], pu[:, 0:258].rearrange("p (h e) -> p h e", h=2), AF.Copy,
                      r=[pu], w=[puS4[i2]])

        def stR(n):
            i2 = n % 2
            tk = slice(n * 128, (n + 1) * 128)
            hm_ = hmt[n % 2]
            d_ = d4[i2]
            dsc = dbc[:, :, n:n + 1].to_broadcast([128, 4, 130])
            k.tt(Tm4[:], Cst4[:], dsc, ALU.mult, r=[Cst4, dbc], w=[Tm4])
            cs_ = Cs4[i2]
            if n > 0:
                k.act(cs_[:], Tm4[:], AF.Copy, r=[Tm4], w=[cs_])
            k.tt(Cst4[:, :, 0:129], Tm4[:, :, 0:129], puS4[i2][:], ALU.add, r=[Tm4, puS4[i2]], w=[Cst4])
            for hp in range(2):
                po = next_ps()
                for hh in range(2):
                    h = hp * 2 + hh
                    k.mm(po[:, hh * 129:(hh + 1) * 129], Sm4[i2][:, h, :], Vp[:, n, h, 0:129], True, (n == 0),
                         r=[Sm4[i2], Vp], w=[po], inc=(n == 0 and hh == 1))
                    if n > 0:
                        k.mm(po[:, hh * 129:(hh + 1) * 129], qT[:, h, tk], cs_[:, h, 0:129], False, True,
                             r=[qT, cs_], w=[po], inc=(hh == 1))
                pov = po[:, 0:258].rearrange("p (h e) -> p h e", h=2)
                k.act(d_[:, hp * 2:hp * 2 + 2], pov[:, :, 128], AF.Abs, r=[po], w=[d_], scale=SC)
                k.tt(d_[:, hp * 2:hp * 2 + 2], d_[:, hp * 2:hp * 2 + 2], wf[:, n, 4 + hp * 2:6 + hp * 2], ALU.max,
                     r=[d_, wf], w=[d_])
                k.recip(d_[:, 4 + hp * 2:6 + hp * 2], d_[:, hp * 2:hp * 2 + 2], r=[d_], w=[d_])
                k.ts(d_[:, 4 + hp * 2:6 + hp * 2], d_[:, 4 + hp * 2:6 + hp * 2], SC, None, ALU.mult, r=[d_], w=[d_])
                k.tt(hm_[:, hp * 256:(hp + 1) * 256].rearrange("p (h t) -> p h t", h=2), pov[:, :, 0:128],
                     d_[:, 4 + hp * 2:6 + hp * 2].unsqueeze(2).to_broadcast([128, 2, 128]), ALU.mult,
                     r=[po, d_], w=[hm_])

        stL(0)
        for n in range(NT):
            hm_, mo_, mz_ = hmt[n % 2], mot[n % 2], mzt[n % 2]
            k.dma("sp", mo_[:], mo.ap()[n * 128:(n + 1) * 128, :], r=[k.dbuf["mo"]], w=[mo_])
            k.dma("sp", mz_[:], mz.ap()[n * 128:(n + 1) * 128, :], r=[k.dbuf["mz"]], w=[mz_])
            if n + 1 < NT:
                stL(n + 1)
            stR(n)
            k.tt(hm_[:], hm_[:], mo_[:], ALU.mult, r=[hm_, mo_], w=[hm_])
            stg = stage[(n // 4) % 2]
            headnorm_out(hm_, gM, mz_, stg, (n % 4) * 128, ws)
            if n % 4 == 3:
                j = n // 4
                k.dma("sp", ymT.ap()[:, j * 512:(j + 1) * 512].rearrange("(c p) t -> p c t", p=128), stg[:],
                      r=[stg], w=[k.dbuf["ymT"]])

    def phase_attn(l):
        pre_ = {}
        for nm_, shp_, dt_ in (("aq", [128, 4, L], BF16), ("ak", [128, 4, L], BF16), ("aVp", [128, NT, 4, 130], BF16),
                               ("strip0", [128, 1024], F32), ("strip1", [128, 1024], F32),
                               ("strip2", [128, 1024], F32), ("strip3", [128, 1024], F32), ("b31bc", [128, 4], F32),
                               ("negMb", [128, 2, 4], F32), ("farb", [128, 2, 4], F32), ("neglam", [128, 1], F32),
                               ("gAcol", [128, 4, 1], F32), ("ones_b", [128, 128], BF16)):
            pre_[nm_] = k.sb(nm_, shp_, dt_)
        mark_a = k.sb_off
        f = lambda name, shp, dt=F32: pre_[name] if name in pre_ else k.sb(name, shp, dt)
        lam_init = 0.8 - 0.6 * math.exp(-0.3 * l)
        qT = f("aq", [128, 4, L], BF16)
        kTt = f("ak", [128, 4, L], BF16)
        Vp = f("aVp", [128, NT, 4, 130], BF16)
        k.dma("sp", qT[:], aqT.ap().rearrange("(c p) t -> p c t", p=128), r=[k.dbuf["aqT"]], w=[qT])
        k.dma("sp", kTt[:], akT.ap().rearrange("(c p) t -> p c t", p=128), r=[k.dbuf["akT"]], w=[kTt])
        vstg = f("vstg_a", [128, NT, W], BF16)
        for n0 in range(0, NT, 8):
            nn = min(8, NT - n0)
            k.dma("sp", vstg[:, n0:n0 + nn, :],
                  av.ap()[n0 * 128:(n0 + nn) * 128, :].rearrange("(n p) c -> p n c", p=128),
                  r=[k.dbuf["av"]], w=[vstg])
        k.copy(Vp[:, :, :, 0:128], vstg[:, :, :].rearrange("p n (h d) -> p n h d", h=4), r=[vstg], w=[Vp])
        k.memset(Vp[:, :, :, 128:129], 1.0, w=[Vp], eng="dve")
        k.memset(Vp[:, :, :, 129:130], 0.0, w=[Vp], eng="dve")
        strips = [f("strip%d" % h, [128, 1024]) for h in range(4)]
        b31bc = f("b31bc", [128, 4])
        antiI = f("antiI", [128, 128])
        k.dma("sp", antiI[:], C_["antiI"].ap(), w=[antiI])
        srev = [f("srev%d" % i, [128, 1024]) for i in range(2)]
        for h in range(4):
            sr = srev[h % 2]
            k.dma("sp", sr[:], bass.AP(Gd, h * 1152, [[1, 128], [1, 1024]]), r=[k.dbuf["Gd"]], w=[sr])
            for hf in range(2):
                p = next_ps()
                k.mm(p[:, :], antiI[:, :], sr[:, hf * 512:(hf + 1) * 512], True, True, r=[antiI, sr], w=[p])
                k.copy(strips[h][:, hf * 512:(hf + 1) * 512], p[:, :], r=[p], w=[strips[h]])
        k.dma("sp", b31bc[:], P_["rel_bias"].ap()[31:32, :].partition_broadcast(128), w=[b31bc])
        sqb = [f("sqb%d" % i, [128, 512], BF16) for i in range(2)]
        blkmax = f("blkmax", [2, NB])
        stat = f("stat", [2, 8])
        ii = 0
        for qi, src in enumerate((qT, kTt)):
            for h in range(4):
                for tb in range(NB):
                    s_ = sqb[ii % 2]
                    ii += 1
                    k.act(s_[:], src[:, h, tb * 512:(tb + 1) * 512], AF.Square, r=[src], w=[s_])
                    p = next_ps()
                    k.mm(p[0:2, :], blockones_b[:, 0:2], s_[:], True, True, r=[blockones_b, s_], w=[p])
                    k.reduce(blkmax[:, tb:tb + 1], p[0:2, :], ALU.max, AX.X, r=[p], w=[blkmax])
                k.reduce(stat[:, qi * 4 + h:qi * 4 + h + 1], blkmax[:, :], ALU.max, AX.X, r=[blkmax], w=[stat])
        M2 = f("M2", [2, 4])
        k.tt(M2[:], stat[:, 0:4], stat[:, 4:8], ALU.mult, r=[stat], w=[M2])
        k.act(M2[:], M2[:], AF.Sqrt, r=[M2], w=[M2])
        k.ts(M2[:], M2[:], -1.05, -2.0, ALU.mult, ALU.add, r=[M2], w=[M2])
        negMb = f("negMb", [128, 2, 4])
        farb = f("farb", [128, 2, 4])
        selc = f("selc", [2, 2, 128])
        k.memset(selc[:], 0.0, w=[selc])
        for c in range(2):
            k.ts(selc[:, c, :], ident_f[0:2, c:c + 1].to_broadcast([2, 128]), 1.0, None, ALU.mult, r=[ident_f, selc],
                 w=[selc])
        for c in range(2):
            p = next_ps()
            k.mm(p[:, 0:4], selc[0:2, c, :], M2[0:2, 0:4], True, True, r=[selc, M2], w=[p])
            k.copy(negMb[:, c, :], p[:, 0:4], r=[p], w=[negMb])
            k.tt(farb[:, c, :], negMb[:, c, :], b31bc[:, :], ALU.add, r=[negMb, b31bc], w=[farb])
        dl = f("dl", [1, 256])
        k.dma("sp", dl[:], P_["diff_lam"].ap()[l:l + 1].rearrange("o a d -> o (a d)"), w=[dl])
        pr = f("pr", [1, 128])
        s12 = f("s12", [1, 2])
        k.tt(pr[:, 0:64], dl[:, 0:64], dl[:, 64:128], ALU.mult, r=[dl], w=[pr])
        k.tt(pr[:, 64:128], dl[:, 128:192], dl[:, 192:256], ALU.mult, r=[dl], w=[pr])
        k.reduce(s12[:], pr[:, :].rearrange("p (a d) -> p a d", a=2), ALU.add, AX.X, r=[pr], w=[s12])
        k.act(s12[:], s12[:], AF.Exp, r=[s12], w=[s12])
        lamv = f("lamv", [1, 1])
        k.tt(lamv[:], s12[:, 1:2], s12[:, 0:1], ALU.subtract, r=[s12], w=[lamv])
        k.ts(lamv[:], lamv[:], -lam_init, None, ALU.add, r=[lamv], w=[lamv])
        neglam = f("neglam", [128, 1])
        p = next_ps()
        k.mm(p[:, 0:1], ones_row[0:1, :], lamv[0:1, 0:1], True, True, r=[ones_row, lamv], w=[p])
        k.copy(neglam[:], p[:, 0:1], r=[p], w=[neglam])
        garow = f("garow", [1, W])
        gAcol = f("gAcol", [128, 4, 1])
        k.dma("sp", garow[:], P_["diff_norm_g"].ap()[l:l + 1, :], w=[garow])
        rows_to_cols(garow, 1, 4, gAcol[:, :, :], gAcol)
        k.ts(gAcol[:], gAcol[:], 1.0 - lam_init, None, ALU.mult, r=[gAcol], w=[gAcol])
        ones_b = f("ones_b", [128, 128], BF16)
        k.memset(ones_b[:], 1.0, w=[ones_b], eng="dve")
        k.S.barrier()
        k.sb_off = mark_a
        Pt = [f("Pt%d" % i, [128, 512], BF16) for i in range(3)]
        tmpb = [f("tmpb%d" % i, [128, 512]) for i in range(2)]
        rsb = [f("rsb%d" % i, [128, 512]) for i in range(2)]
        o0 = [f("o0_%d" % i, [128, 512]) for i in range(2)]
        t1 = [f("at1_%d" % i, [128, 512]) for i in range(2)]
        haT = [f("haT%d" % i, [128, 512]) for i in range(2)]
        sqb_ = [f("asq%d" % i, [128, 512], BF16) for i in range(2)]
        rstd = [f("arstd%d" % i, [128, 512]) for i in range(2)]
        azt = [f("azt%d" % i, [128, 512], BF16) for i in range(2)]
        stage = [f("astage%d" % i, [128, 512], BF16) for i in range(2)]
        qz = [[f("qz%d_%d" % (i, c), [128, 512], BF16) for c in range(2)] for i in range(2)]
        for i in range(2):
            for c in range(2):
                k.memset(qz[i][c][:], 0.0, w=[qz[i][c]], eng="dve")
        it = 0
        gh = 0
        ic = 0
        pending = []
        my_ep = None
        for g in range(NB):
            for h in range(4):
                qz_ = qz[gh % 2]
                az_ = azt[gh % 2]
                ha_ = haT[gh % 2]
                k.dma("sp", az_[:], azT.ap()[h * 128:(h + 1) * 128, g * 512:(g + 1) * 512], r=[k.dbuf["azT"]],
                      w=[az_])
                for c in range(2):
                    k.copy(qz_[c][c * 64:(c + 1) * 64, :], qT[c * 64:(c + 1) * 64, h, g * 512:(g + 1) * 512],
                           r=[qT], w=[qz_[c]], eng="pool")
                for c in range(2):
                    nkb = 4 * g + 4
                    DEPTH_PF = 2
                    base = it
                    pss = {}
                    accV, accS = psb[(ic % 2) * 2], psb[(ic % 2) * 2 + 1]
                    ic += 1

                    def issue_qk(kb):
                        ps = psb[4 + (base + kb) % 4]
                        pss[kb] = ps
                        k.mm(ps[:, :], kTt[:, h, kb * 128:(kb + 1) * 128], qz_[c][:, :], True, True,
                             r=[kTt, qz_[c]], w=[ps])

                    for kb in range(min(DEPTH_PF, nkb)):
                        issue_qk(kb)
                    for kb in range(nkb):
                        it += 1
                        if kb + DEPTH_PF < nkb:
                            issue_qk(kb + DEPTH_PF)
                        if c == 0 and kb == min(3, nkb - 1) and pending and pending[0] is not my_ep:
                            pending.pop(0)()
                        ps = pss.pop(kb)
                        P_t = Pt[it % 3]
                        j = kb - (4 * g - 1)
                        c0 = 0
                        if j >= 0:
                            tb_ = tmpb[it % 2]
                            c0 = max(j - 1, 0) * 128
                            k.tt(tb_[:, c0:512], ps[:, c0:512], strips[h][:, (4 - j) * 128 + c0:(4 - j) * 128 + 512],
                                 ALU.add, r=[ps, strips[h]], w=[tb_])
                            k.act(P_t[:, c0:512], tb_[:, c0:512], AF.Exp, r=[tb_, negMb], w=[P_t],
                                  bias=negMb[:, c, h:h + 1])
                        else:
                            k.act(P_t[:], ps[:, :], AF.Exp, r=[ps, farb], w=[P_t], bias=farb[:, c, h:h + 1])
                        last = (kb == nkb - 1)
                        k.mm(accV[:, c0:512], Vp[:, kb, h, 0:128], P_t[:, c0:512], start=(kb == 0), stop=last,
                             r=[Vp, P_t], w=[accV], inc=last)
                        k.mm(accS[:, c0:512], ones_b[:, :], P_t[:, c0:512], start=(kb == 0), stop=last,
                             r=[ones_b, P_t], w=[accS], inc=True)
                    rs_ = rsb[ic % 2]
                    k.recip(rs_[:], accS[:, :], r=[accS], w=[rs_])
                    if c == 0:
                        o0_ = o0[gh % 2]
                        k.tt(o0_[:], accV[:, :], rs_[:], ALU.mult, r=[accV, rs_], w=[o0_])
                    else:
                        t_ = t1[gh % 2]
                        k.tt(t_[:], accV[:, :], rs_[:], ALU.mult, r=[accV, rs_], w=[t_])
                        k.stt(ha_[:], t_[:], neglam[:, 0:1], o0_[:], ALU.mult, ALU.add, r=[t_, neglam, o0_], w=[ha_])
                def epilogue(ha_=ha_, az_=az_, h=h, g=g, ghl=gh):
                    sq_ = sqb_[ghl % 2]
                    rd_ = rstd[ghl % 2]
                    stg = stage[ghl % 2]
                    k.act(sq_[:], ha_[:], AF.Square, r=[ha_], w=[sq_])
                    pn = psb[4 + (it + 2) % 4]
                    k.mm(pn[:, :], ones_b[:, :], sq_[:], True, True, r=[ones_b, sq_], w=[pn])
                    k.act(rd_[:], pn[:, :], AF.Ln, r=[pn, consts], w=[rd_], scale=1.0 / 128, bias=epsb)
                    k.act(rd_[:], rd_[:], AF.Exp, r=[rd_], w=[rd_], scale=-0.5)
                    k.tt(ha_[:], ha_[:], rd_[:], ALU.mult, r=[ha_, rd_], w=[ha_])
                    k.stt(stg[:], ha_[:], gAcol[:, h, 0:1], az_[:], ALU.mult, ALU.mult, r=[ha_, gAcol, az_], w=[stg])
                    k.dma("sp", yaT.ap()[h * 128:(h + 1) * 128, g * 512:(g + 1) * 512], stg[:], r=[stg],
                          w=[k.dbuf["yaT"]])

                pending.append(epilogue)
                gh += 1
        while pending:
            pending.pop(0)()

    def phase_merge(l):
        f = lambda name, shp, dt=F32: k.sb(name, shp, dt)
        last = (l == depth - 1)
        norm_alloc()
        wbr = f("wbr", [128, 3, 4, 1024], BF16)
        wo = f("wo", [128, 8, 1024], BF16)
        mark_m = k.sb_off
        wbs = [f("wbs%d" % i, [128, 4, 1024]) for i in range(2)]
        for b in range(3):
            s_ = wbs[b % 2]
            k.dma("sp", s_[:], P_["w_branch"].ap()[l, b].rearrange("(kc p) c -> p kc c", p=128), w=[s_])
            k.copy(wbr[:, b, :, :], s_[:], r=[s_], w=[wbr])
        for hf in range(2):
            s_ = wbs[(3 + hf) % 2]
            k.dma("sp", s_[:], P_["w_out"].ap()[l, hf * 512:(hf + 1) * 512, :].rearrange("(kc p) c -> p kc c", p=128),
                  w=[s_])
            k.copy(wo[:, hf * 4:(hf + 1) * 4, :], s_[:], r=[s_], w=[wo])
        if last:
            k.dma("sp", gbc[:], P_["final_g"].ap().rearrange("(o d) -> o d", o=1).partition_broadcast(128), w=[gbc])
        else:
            k.dma("sp", gbc[:], norm_g.ap()[l + 1:l + 2, :].partition_broadcast(128), w=[gbc])
        k.S.barrier()
        k.sb_off = mark_m
        yb_ = [[f("y%d_%d" % (b, i), [128, 4, 512], BF16) for i in range(2)] for b in range(3)]
        gt = [f("gt%d" % i, [128, 512], BF16) for i in range(6)]
        m1 = [f("m1_%d" % i, [128, 512]) for i in range(2)]
        m2 = [f("m2_%d" % i, [128, 512]) for i in range(2)]
        mT = [f("mT%d" % i, [128, 8, 512], BF16) for i in range(2)]
        xo = [f("xo%d" % i, [128, D_MODEL]) for i in range(2)]
        xn = [f("xn%d" % i, [128, D_MODEL]) for i in range(2)]
        x_src = x_in if l == 0 else xs_d
        srcs = (ysT, ymT, yaT)
        st_ = {"gi": 0}

        def load_y(j):
            for b in range(3):
                k.dma("sp", yb_[b][j % 2][:],
                      srcs[b].ap()[:, j * 512:(j + 1) * 512].rearrange("(c p) t -> p c t", p=128),
                      r=[k.dbuf[srcs[b].name]], w=[yb_[b][j % 2]])

        def branch(j):
            ys_ = [yb_[b][j % 2] for b in range(3)]
            mT_ = mT[j % 2]
            for dc in range(8):
                m1_, m2_ = m1[dc % 2], m2[dc % 2]
                for b in range(3):
                    g_ = gt[st_["gi"] % 6]
                    st_["gi"] += 1
                    r0 = b * 1024 + dc * 128
                    k.dma("sp", g_[:], gT.ap()[r0:r0 + 128, j * 512:(j + 1) * 512], r=[k.dbuf["gT"]], w=[g_])
                    p = next_ps()
                    for cc in range(4):
                        k.mm(p[:, :], wbr[:, b, cc, dc * 128:(dc + 1) * 128], ys_[b][:, cc, :], start=(cc == 0),
                             stop=(cc == 3), r=[wbr, ys_[b]], w=[p])
                    if b == 0:
                        k.tt(m1_[:], p[:, :], g_[:], ALU.mult, r=[p, g_], w=[m1_])
                    else:
                        k.tt(m2_[:], p[:, :], g_[:], ALU.mult, r=[p, g_], w=[m2_])
                        if b == 1:
                            k.tt(m1_[:], m1_[:], m2_[:], ALU.add, r=[m1_, m2_], w=[m1_])
                        else:
                            k.tt(mT_[:, dc, :], m1_[:], m2_[:], ALU.add, r=[m1_, m2_], w=[mT_])

        def outproj(j):
            mT_ = mT[j % 2]
            for t4 in range(4):
                n = j * 4 + t4
                xo_, xn_ = xo[n % 2], xn[n % 2]
                k.dma("sp", xo_[:], x_src.ap()[n * 128:(n + 1) * 128, :], r=[k.dbuf[x_src.name]], w=[xo_])
                for hf in range(2):
                    p = next_ps()
                    for dc in range(8):
                        k.mm(p[:, :], mT_[:, dc, t4 * 128:(t4 + 1) * 128], wo[:, dc, hf * 512:(hf + 1) * 512],
                             start=(dc == 0), stop=(dc == 7), r=[mT_, wo], w=[p])
                    k.tt(xn_[:, hf * 512:(hf + 1) * 512], p[:, :], xo_[:, hf * 512:(hf + 1) * 512], ALU.add,
                         r=[p, xo_], w=[xn_])
                if last:
                    r_, _ = norm_rstd(xn_)
                    k.stt(xo_[:], xn_[:], r_[:, 0:1], gbc[:], ALU.mult, ALU.mult, r=[xn_, r_, gbc], w=[xo_])
                    k.dma("pool", out_d.ap()[n * 128:(n + 1) * 128, :], xo_[:], r=[xo_], w=[k.dbuf["out"]])
                else:
                    k.dma("pool", xs_d.ap()[n * 128:(n + 1) * 128, :], xn_[:], r=[xn_], w=[k.dbuf["xs"]])
                    norm_transpose(xn_, n * 128)

        load_y(0)
        branch(0)
        for j in range(NB):
            if j + 1 < NB:
                load_y(j + 1)
                branch(j + 1)
            outproj(j)

    for l in range(depth):
        if "proj" in phases:
            phase_proj(l)
        k.phase_reset(keep_hT=False)
        if "mlstm" in phases:
            pre_m = {}
            phase_gates(l, pre_m)
            k.S.barrier()
            k.sb_off = pre_m["mark"]
            phase_mlstm(l, pre_m)
        k.phase_reset(keep_hT=False)
        if "s5" in phases:
            phase_s5(l)
        k.phase_reset(keep_hT=False)
        if "attn" in phases:
            phase_attn(l)
        k.phase_reset(keep_hT=True)
        if "merge" in phases:
            phase_merge(l)
        k.phase_reset(keep_hT=True)

    k.S.finish("sp")
    k.S.emit()
    return k


_CACHE = {}


def kernel(**inputs):
    L, depth = SEQ, DEPTH
    if "k" not in _CACHE:
        _CACHE["k"] = build(L, depth)
    kb = _CACHE["k"]
    consts = host_consts()
    x = np.ascontiguousarray(inputs["x"], dtype=np.float32)
    shared = {n: np.ascontiguousarray(inputs[n], dtype=np.float32) for n in PARAM_SHAPES(depth)}
    shared.update(consts)
    in_maps = []
    for c in range(NCORES):
        m = dict(shared)
        m["x"] = x[c]
        in_maps.append(m)
    res = run_bass_kernel_spmd(kb.nc, in_maps, core_ids=list(range(NCORES)))
    return np.stack([np.asarray(r["out"], dtype=np.float32) for r in res.results], axis=0)
```

```python
import math
from contextlib import ExitStack
import numpy as np
import concourse.bass as bass
import concourse.mybir as mybir
from concourse.bass_utils import run_bass_kernel_spmd

F32 = mybir.dt.float32
BF16 = mybir.dt.bfloat16
I32 = mybir.dt.int32
ALU = mybir.AluOpType
AF = mybir.ActivationFunctionType
AX = mybir.AxisListType

D_MODEL = 1024
SEQ = 4096
DEPTH = 2
W = 512
D_IN = 8712
NCORES = 8
EPS = 1e-6


class Buf:
    __slots__ = ("name", "w", "r")

    def __init__(self, name=""):
        self.name = name
        self.w = None
        self.r = {}


class Sched:
    ENGS = (("pe", "tensor"), ("dve", "vector"), ("act", "scalar"), ("pool", "gpsimd"), ("sp", "sync"))
    NSLOT = 8

    def __init__(self, nc):
        self.nc = nc
        self.ops = {e: [] for e, _ in self.ENGS}
        self.cnt = {}
        self.seen = {e: {} for e, _ in self.ENGS}
        self.dma_i = {e: 0 for e, _ in self.ENGS}

    def _waits(self, eng, r, w, extra=None):
        deps = dict(extra or {})

        def add(k, v):
            if v > deps.get(k, 0):
                deps[k] = v

        for b in r:
            if b.w is not None:
                add(*b.w)
        for b in w:
            if b.w is not None and b.w[0] != eng:
                add(*b.w)
            for k, v in b.r.items():
                if k != eng:
                    add(k, v)
        waits = []
        seen = self.seen[eng]
        for k, v in deps.items():
            if k == "pe" and eng == "pe":
                continue
            if seen.get(k, 0) >= v:
                continue
            seen[k] = v
            waits.append((k, v))
        return waits

    def op(self, eng, fn, r=(), w=(), inc=True):
        waits = self._waits(eng, r, w)
        n = self.cnt.get(eng, 0) + 1
        if inc:
            self.cnt[eng] = n
        self.ops[eng].append((waits, fn, (eng, 1, n) if inc else None))
        for b in r:
            if b.r.get(eng, 0) < n:
                b.r[eng] = n
        for b in w:
            b.w = (eng, n)
            b.r = {}

    def dma(self, q, out, in_, r=(), w=(), **kw):
        i = self.dma_i[q]
        self.dma_i[q] += 1
        key = ("dma", q, i % self.NSLOT)
        n = self.cnt.get(key, 0) + 16
        extra = {key: n - 16} if n > 16 else None
        waits = self._waits(q, r, w, extra)
        self.cnt[key] = n
        self.ops[q].append((waits, lambda e: e.dma_start(out=out, in_=in_, **kw), (key, 16, n)))
        for b in r:
            b.r[key] = n
        for b in w:
            b.w = (key, n)
            b.r = {}

    def barrier(self):
        snap = dict(self.cnt)
        for eng, _ in self.ENGS:
            waits = [(k, v) for k, v in snap.items() if self.seen[eng].get(k, 0) < v and k != eng]
            for k, v in waits:
                self.seen[eng][k] = v
            if waits:
                self.ops[eng].append((waits, None, None))

    def finish(self, q="sp"):
        waits = [(k, v) for k, v in self.cnt.items() if self.seen[q].get(k, 0) < v]
        self.ops[q].append((waits, None, None))

    def emit(self):
        nc = self.nc
        targets = {}
        for eng, _ in self.ENGS:
            for waits, fn, inc in self.ops[eng]:
                for wk, wv in waits:
                    targets.setdefault(wk, set()).add(wv)
        rank = {}
        for key, vs in targets.items():
            step = 16 if isinstance(key, tuple) else 1
            rank[key] = {v: (i + 1) * step for i, v in enumerate(sorted(vs))}
        with ExitStack() as st:
            sems = {}
            for i, key in enumerate(self.cnt):
                sems[key] = st.enter_context(nc.semaphore("s%d" % i))
            block = st.enter_context(nc.Block())
            for eng, attr in self.ENGS:
                ops = self.ops[eng]
                if not ops:
                    continue

                def body(e, ops=ops):
                    for waits, fn, inc in ops:
                        for wk, wv in waits:
                            e.wait_ge(sems[wk], rank[wk][wv])
                        if fn is not None:
                            ins = fn(e)
                            if inc is not None and inc[2] in targets.get(inc[0], ()):
                                ins.then_inc(sems[inc[0]], inc[1])

                getattr(block, attr)(body)


class T:
    def __init__(self, t, name):
        self.t = t
        self.b = Buf(name)

    def __getitem__(self, k):
        return self.t[k]


class K:
    def __init__(self, L, depth, dbg=False):
        self.L = L
        self.depth = depth
        self.dbg = dbg
        self.nc = bass.Bass("TRN2", target_bir_lowering=False)
        self.S = Sched(self.nc)
        self.dram = {}
        self.dbuf = {}
        self.nps = 0
        self.sb_off = 16640
        self.sb_n = 0
        self.sb_mark = 0

    def sb(self, name, shape, dt=F32):
        nbytes = int(np.prod(shape[1:])) * (2 if dt == BF16 else 4)
        off = (self.sb_off + 63) // 64 * 64
        self.sb_off = off + nbytes
        assert self.sb_off <= 16640 + 206 * 1024, ("SBUF overflow", name, self.sb_off)
        self.sb_n += 1
        nm = "%s_%d" % (name, self.sb_n)
        return T(self.nc.alloc_sbuf_tensor_at(nm, list(shape), dt, offset=off), nm)

    def phase_mark(self):
        self.sb_mark = self.sb_off

    def phase_reset(self, keep_hT=True):
        self.S.barrier()
        self.sb_off = self.sb_mark if keep_hT else self.sb_mark0

    def ps(self, name, shape=(128, 512), dt=F32):
        return T(self.nc.alloc_psum_tensor(name, list(shape), dt), name)

    def din(self, name, shape, dt=F32):
        t = self.nc.dram_tensor(name, list(shape), dt, kind="ExternalInput")
        self.dram[name] = t
        self.dbuf[name] = Buf(name)
        return t

    def dscratch(self, name, shape, dt=F32, out=False):
        kind = "ExternalOutput" if (out or self.dbg) else "Internal"
        t = self.nc.dram_tensor(name, list(shape), dt, kind=kind)
        self.dram[name] = t
        self.dbuf[name] = Buf(name)
        return t

    def op(self, eng, fn, r=(), w=(), inc=True):
        self.S.op(eng, fn, [x.b if isinstance(x, T) else x for x in r],
                  [x.b if isinstance(x, T) else x for x in w], inc)

    def dma(self, q, out, in_, r=(), w=(), **kw):
        self.S.dma(q, out, in_, [x.b if isinstance(x, T) else x for x in r],
                   [x.b if isinstance(x, T) else x for x in w], **kw)

    def mm(self, out, lhsT, rhs, start, stop, r=(), w=(), inc=None):
        if inc is None:
            inc = stop
        self.op("pe", lambda e: e.matmul(out, lhsT, rhs, start=start, stop=stop), r=r, w=w, inc=inc)

    def act(self, out, in_, func, r=(), w=(), bias=None, scale=1.0, accum=None, eng="act"):
        kw = {}
        if bias is not None:
            kw["bias"] = bias
        if accum is not None:
            kw["accum_out"] = accum
        self.op(eng, lambda e: e.activation(out=out, in_=in_, func=func, scale=scale, **kw), r=r, w=w)

    def tt(self, out, in0, in1, op, r=(), w=(), eng="dve"):
        self.op(eng, lambda e: e.tensor_tensor(out=out, in0=in0, in1=in1, op=op), r=r, w=w)

    def ts(self, out, in0, s1, s2, op0, op1=None, r=(), w=(), eng="dve", accum=None):
        kw = {}
        if op1 is not None:
            kw["op1"] = op1
        if accum is not None:
            kw["accum_out"] = accum
        self.op(eng, lambda e: e.tensor_scalar(out=out, in0=in0, scalar1=s1, scalar2=s2, op0=op0, **kw), r=r, w=w)

    def stt(self, out, in0, scalar, in1, op0, op1, r=(), w=(), eng="dve"):
        self.op(eng, lambda e: e.scalar_tensor_tensor(out=out, in0=in0, scalar=scalar, in1=in1, op0=op0, op1=op1),
                r=r, w=w)

    def copy(self, out, in_, r=(), w=(), eng="dve"):
        self.op(eng, lambda e: e.tensor_copy(out=out, in_=in_), r=r, w=w)

    def memset(self, ap, val, w=(), eng="pool"):
        self.op(eng, lambda e: e.memset(ap, val), w=w)

    def scan(self, out, d0, d1, init, op0, op1, r=(), w=(), eng="dve"):
        self.op(eng, lambda e: e.tensor_tensor_scan(out=out, data0=d0, data1=d1, initial=init, op0=op0, op1=op1),
                r=r, w=w)

    def reduce(self, out, in_, op, axis, r=(), w=(), eng="dve"):
        self.op(eng, lambda e: e.tensor_reduce(out=out, in_=in_, axis=axis, op=op), r=r, w=w)

    def recip(self, out, in_, r=(), w=(), eng="dve"):
        self.op(eng, lambda e: e.reciprocal(out=out, in_=in_), r=r, w=w)

    def tr(self, out, in_, ident, r=(), w=(), inc=True):
        self.op("pe", lambda e: e.transpose(out, in_, ident), r=r, w=w, inc=inc)


def host_consts():
    c = {}
    c["ident"] = np.eye(128, dtype=np.float32)
    c["iota1"] = np.tile(np.arange(1, 513, dtype=np.float32)[None, :], (128, 1))
    s_ = np.arange(128)
    c["causal01"] = (s_[:, None] <= s_[None, :]).astype(np.float32)
    c["chmask"] = (s_[:, None] // 16 == np.arange(8)[None, :]).astype(np.float32)
    c["hmask"] = (s_[:, None] // 64 == np.arange(2)[None, :]).astype(np.float32)
    sel8 = np.zeros((8, 8), np.float32)
    for h in range(4):
        sel8[h, h] = 1.0
        sel8[4 + h, h] = -1.0
        sel8[4 + h, 4 + h] = 1.0
    c["sel8"] = sel8
    rm = np.zeros((8, 2), np.float32)
    rm[0:4, 0] = 1.0
    rm[4:8, 1] = 1.0
    c["rowmask8"] = rm
    sr4 = np.zeros((4, 4 * 128), np.float32)
    for h in range(4):
        sr4[h, h * 128:(h + 1) * 128] = 1.0
    c["selrow4"] = sr4
    bo = np.zeros((128, 2), np.float32)
    bo[0:64, 0] = 1.0
    bo[64:128, 1] = 1.0
    c["blockones"] = bo
    e = np.arange(1152)
    d = e - 511
    n = np.maximum(d, 0)
    large = 16 + (np.log(np.maximum(n, 1).astype(np.float32) / np.float32(16)) / np.float32(math.log(128 / 16))
                  * np.float32(16)).astype(np.int32)
    large = np.minimum(large, 31)
    bucket = np.where(n < 16, n, large)
    oh = np.zeros((32, 1152), np.float32)
    oh[bucket, e] = 1.0
    oh[:, d < 0] = 0.0
    c["oh"] = oh
    c["maskvec"] = np.where(d < 0, -30000.0, 0.0).astype(np.float32)[None, :]
    c["ones_row"] = np.ones((1, 128), np.float32)
    c["antiI"] = np.ascontiguousarray(np.eye(128, dtype=np.float32)[::-1])
    return c


CONST_SHAPES = {"ident": [128, 128], "iota1": [128, 512], "causal01": [128, 128], "chmask": [128, 8],
                "hmask": [128, 2], "sel8": [8, 8], "rowmask8": [8, 2], "selrow4": [4, 512],
                "blockones": [128, 2], "oh": [32, 1152], "maskvec": [1, 1152], "ones_row": [1, 128], "antiI": [128, 128]}

PARAM_SHAPES = lambda depth: {
    "norm_g": [depth, 1024], "w_in": [depth, 1024, D_IN], "conv_w": [depth, 4, 1024], "conv_b": [depth, 1024],
    "if_bias": [depth, 2, 4], "m_norm_g": [depth, 512], "ssm_lam_re": [depth, 32, 64],
    "ssm_lam_im": [depth, 32, 64], "ssm_b_re": [depth, 32, 64, 16], "ssm_b_im": [depth, 32, 64, 16],
    "ssm_c_re": [depth, 32, 16, 64], "ssm_c_im": [depth, 32, 16, 64], "ssm_d": [depth, 512],
    "ssm_log_step": [depth, 32], "ssm_w_glu": [depth, 512, 512], "ssm_b_glu": [depth, 512],
    "diff_lam": [depth, 4, 64], "diff_norm_g": [depth, 512], "rel_bias": [32, 4],
    "w_branch": [depth, 3, 512, 1024], "w_out": [depth, 1024, 1024], "final_g": [1024]}

PI = math.pi


def build(L=SEQ, depth=DEPTH, dbg=False, phases=("proj", "s5", "mlstm", "attn", "merge")):
    k = K(L, depth, dbg)
    nc = k.nc
    NT = L // 128
    NB = L // 512
    x_in = k.din("x", [L, D_MODEL])
    P_ = {n: k.din(n, shp) for n, shp in PARAM_SHAPES(depth).items()}
    C_ = {n: k.din(n, shp) for n, shp in CONST_SHAPES.items()}
    norm_g, w_in, conv_w, conv_b = P_["norm_g"], P_["w_in"], P_["conv_w"], P_["conv_b"]
    out_d = k.dscratch("out", [L, D_MODEL], F32, out=True)
    xs_d = k.dscratch("xs", [L, D_MODEL], F32)

    suT = k.dscratch("suT", [W, L], BF16)
    szT = k.dscratch("szT", [W, L], BF16)
    mqT = k.dscratch("mqT", [W, L], BF16)
    mkT = k.dscratch("mkT", [W, L], BF16)
    mv = k.dscratch("mv", [L, W], BF16)
    mo = k.dscratch("mo", [L, W], BF16)
    mz = k.dscratch("mz", [L, W], BF16)
    aqT = k.dscratch("aqT", [W, L], BF16)
    akT = k.dscratch("akT", [W, L], BF16)
    av = k.dscratch("av", [L, W], BF16)
    azT = k.dscratch("azT", [W, L], BF16)
    gT = k.dscratch("gT", [3 * D_MODEL, L], BF16)
    mifd = k.dscratch("mif", [8, L], F32)
    ysT = k.dscratch("ysT", [W, L], BF16)
    ymT = k.dscratch("ymT", [W, L], BF16)
    yaT = k.dscratch("yaT", [W, L], BF16)
    Gd = k.dscratch("Gd", [4, 1152], F32)

    def cload(name, dt=F32, parts=None):
        shp = CONST_SHAPES[name]
        t = k.sb(name, shp, F32)
        k.dma("sp", t[:], C_[name].ap(), w=[t])
        if dt == BF16:
            tb = k.sb(name + "_b", shp, BF16)
            k.copy(tb[:], t[:], r=[t], w=[tb])
            return t, tb
        return t

    ident_f, ident_b = cload("ident", BF16)
    causal01 = cload("causal01")
    chmask = cload("chmask")
    hmask = cload("hmask")
    sel8 = cload("sel8")
    rowmask8 = cload("rowmask8")
    selrow4 = cload("selrow4")
    blockones_f, blockones_b = cload("blockones", BF16)
    ones_row = cload("ones_row")
    gbc = k.sb("gbc", [128, D_MODEL], F32)
    wf = k.sb("wf", [128, NT, 8])
    dbc = k.sb("dbc", [128, 4, NT])
    consts = k.sb("consts", [128, 8], F32)
    CV = [EPS, -PI, 0.0, 1.0, 0.5 * PI]
    for i, v in enumerate(CV):
        k.memset(consts[:, i:i + 1], v, w=[consts])
    epsb = consts[:, 0:1]
    negpi = consts[:, 1:2]
    k.sb_mark0 = k.sb_off
    hT = k.sb("hT", [128, 8, L], BF16)
    psb = [k.ps("ps%d" % i) for i in range(8)]

    def next_ps():
        p = psb[k.nps % 8]
        k.nps += 1
        return p

    def rows_to_cols(src, R, nchunk, dst_ap, dstT):
        p = next_ps()
        for c in range(nchunk):
            k.mm(p[:, c * R:(c + 1) * R], src[0:R, c * 128:(c + 1) * 128], ident_f[0:R, 0:R],
                 start=True, stop=True, r=[src, ident_f], w=[p], inc=(c == nchunk - 1))
        k.copy(dst_ap, p[:, 0:nchunk * R].rearrange("p (c r) -> p c r", r=R), r=[p], w=[dstT])

    nrm = {}

    def norm_alloc():
        nrm["hb"] = [k.sb("hb%d" % i, [128, D_MODEL], BF16) for i in range(2)]
        nrm["sq"] = k.sb("sq", [128, D_MODEL], F32)
        nrm["ss"] = [k.sb("ss%d" % i, [128, 1], F32) for i in range(2)]
        nrm["rstd"] = [k.sb("rstd%d" % i, [128, 1], F32) for i in range(2)]
        nrm["i"] = 0

    def norm_rstd(xtile):
        i = nrm["i"]
        nrm["i"] += 1
        s_, r_ = nrm["ss"][i % 2], nrm["rstd"][i % 2]
        sq = nrm["sq"]
        k.act(sq[:], xtile[:], AF.Square, r=[xtile], w=[sq, s_], accum=s_[:])
        k.act(r_[:], s_[:], AF.Sqrt, r=[s_, consts], w=[r_], scale=1.0 / D_MODEL, bias=epsb)
        k.recip(r_[:], r_[:], r=[r_], w=[r_])
        return r_, i

    def norm_transpose(xtile, tok0, defer=False):
        r_, i = norm_rstd(xtile)
        h_ = nrm["hb"][i % 2]
        k.stt(h_[:], xtile[:], r_[:, 0:1], gbc[:], ALU.mult, ALU.mult, r=[xtile, r_, gbc], w=[h_])
        p = next_ps()
        pv = p.t.bitcast(BF16)
        for c in range(8):
            k.tr(pv[:, c * 128:(c + 1) * 128], h_[:, c * 128:(c + 1) * 128], ident_b[:], r=[h_, ident_b], w=[p],
                 inc=(c == 7))
        fin = lambda: k.act(hT[:, :, tok0:tok0 + 128], pv[:, :].rearrange("p (c t) -> p c t", c=8), AF.Copy,
                            r=[p], w=[hT])
        if defer:
            return fin
        fin()

    def headnorm_out(ht, gtile, gate, stage, col0, ws):
        ssq, sqs, yb = ws["ssq"], ws["sqs"], ws["yb"]
        for h in range(4):
            k.act(sqs[:, :], ht[:, h * 128:(h + 1) * 128], AF.Square, r=[ht], w=[sqs, ssq], accum=ssq[:, h:h + 1])
        k.act(ssq[:, :], ssq[:, :], AF.Sqrt, r=[ssq, consts], w=[ssq], scale=1.0 / 128, bias=epsb)
        k.recip(ssq[:, :], ssq[:, :], r=[ssq], w=[ssq])
        for h in range(4):
            k.stt(ht[:, h * 128:(h + 1) * 128], ht[:, h * 128:(h + 1) * 128], ssq[:, h:h + 1],
                  gtile[:, h * 128:(h + 1) * 128], ALU.mult, ALU.mult, r=[ht, ssq, gtile], w=[ht])
        k.tt(yb[:, :], ht[:, :], gate[:, :], ALU.mult, r=[ht, gate], w=[yb])
        p = next_ps()
        pv = p.t.bitcast(BF16)
        for h in range(4):
            k.tr(pv[:, h * 128:(h + 1) * 128], yb[:, h * 128:(h + 1) * 128], ident_b[:], r=[yb, ident_b], w=[p],
                 inc=(h == 3))
        k.act(stage[:, :, col0:col0 + 128], pv[:, 0:512].rearrange("p (c t) -> p c t", c=4), AF.Copy,
              r=[p], w=[stage])

    k.phase_mark()
    if "attn" in phases:
        tab = k.sb("tab", [32, 4], F32)
        oh = k.sb("oh", [32, 1152], F32)
        mvec = k.sb("mvec", [1, 1152], F32)
        Gs = k.sb("Gs", [4, 1152], F32)
        k.dma("sp", tab[:], P_["rel_bias"].ap(), w=[tab])
        k.dma("sp", oh[:], C_["oh"].ap(), w=[oh])
        k.dma("sp", mvec[:], C_["maskvec"].ap(), w=[mvec])
        for c0 in range(0, 1152, 384):
            p = next_ps()
            k.mm(p[0:4, 0:384], tab[:, :], oh[:, c0:c0 + 384], start=True, stop=False, r=[tab, oh], w=[p])
            k.mm(p[0:4, 0:384], ones_row[0:1, 0:4], mvec[0:1, c0:c0 + 384], start=False, stop=True,
                 r=[ones_row, mvec], w=[p])
            k.copy(Gs[:, c0:c0 + 384], p[0:4, 0:384], r=[p], w=[Gs])
        k.dma("sp", Gd.ap(), Gs[:], r=[Gs], w=[k.dbuf["Gd"]])
    k.phase_reset()

    def phase0():
        norm_alloc()
        xt = [k.sb("xt%d" % i, [128, D_MODEL], F32) for i in range(3)]
        k.dma("sp", gbc[:], norm_g.ap()[0:1, :].partition_broadcast(128), w=[gbc])
        pend = None
        for i in range(NT):
            xt_ = xt[i % 3]
            k.dma("sp", xt_[:], x_in.ap()[i * 128:(i + 1) * 128, :], w=[xt_])
            fin = norm_transpose(xt_, i * 128, defer=True)
            if pend is not None:
                pend()
            pend = fin
        pend()

    phase0()
    k.phase_reset()

    def phase_proj(l):
        wst = [k.sb("wst%d" % i, [128, 8, 512], F32) for i in range(2)]
        wbf = [k.sb("wbf%d" % i, [128, 8, 512], BF16) for i in range(2)]
        stT = [k.sb("stT%d" % i, [128, L], BF16) for i in range(2)]
        stN = [k.sb("stN%d" % i, [128, 512], BF16) for i in range(6)]
        cbuf = k.sb("cbuf", [128, L + 8], F32)
        cacc = k.sb("cacc", [128, L], F32)
        cw = k.sb("cw", [128, 8, 5], F32)
        crow = k.sb("crow", [5, 2 * W], F32)
        st = {"wi": 0, "ti": 0, "ni": 0, "ev": 0}
        mif = k.sb("mif_sb", [8, L], F32)

        def load_w(col0, ncols=512):
            i = st["wi"]
            st["wi"] += 1
            ws, wb = wst[i % 2], wbf[i % 2]
            k.dma("pool", ws[:, :, 0:ncols],
                  w_in.ap()[l, :, col0:col0 + ncols].rearrange("(kc p) c -> p kc c", p=128), w=[ws])
            k.copy(wb[:, :, 0:ncols], ws[:, :, 0:ncols], r=[ws], w=[wb], eng="pool")
            return wb

        def evac(out_ap, in_ap, func, r, w, scale=1.0):
            if func is None:
                st["ev"] += 1
                if st["ev"] % 2 == 0:
                    if scale == 1.0:
                        k.copy(out_ap, in_ap, r=r, w=w)
                    else:
                        k.ts(out_ap, in_ap, scale, None, ALU.mult, r=r, w=w)
                    return
                func = AF.Copy
            k.act(out_ap, in_ap, func, r=r, w=w, scale=scale)

        def proj_T_block(wb, col0, dst, row0, func, scale=1.0, conv=None):
            for c in range(4):
                if conv is None:
                    stg = stT[st["ti"] % 2]
                    st["ti"] += 1
                for tb in range(NB):
                    p = next_ps()
                    for kc in range(8):
                        k.mm(p[:, :], wb[:, kc, c * 128:(c + 1) * 128], hT[:, kc, tb * 512:(tb + 1) * 512],
                             start=(kc == 0), stop=(kc == 7), r=[wb, hT], w=[p])
                    if conv is None:
                        evac(stg[:, tb * 512:(tb + 1) * 512], p[:, :], func, r=[p], w=[stg], scale=scale)
                    else:
                        k.act(cbuf[:, 8 + tb * 512:8 + (tb + 1) * 512], p[:, :], AF.Copy, r=[p], w=[cbuf])
                if conv is not None:
                    cc = conv * 4 + c
                    stg = stT[st["ti"] % 2]
                    st["ti"] += 1
                    k.ts(cacc[:], cbuf[:, 8:8 + L], cw[:, cc, 3:4], cw[:, cc, 4:5], ALU.mult, ALU.add,
                         r=[cbuf, cw], w=[cacc])
                    for j in range(3):
                        k.stt(cacc[:], cbuf[:, 5 + j:5 + j + L], cw[:, cc, j:j + 1], cacc[:], ALU.mult, ALU.add,
                              r=[cbuf, cw, cacc], w=[cacc])
                    k.act(stg[:], cacc[:], AF.Silu, r=[cacc], w=[stg])
                k.dma("sp", dst.ap()[row0 + c * 128:row0 + (c + 1) * 128, :], stg[:], r=[stg], w=[k.dbuf[dst.name]])

        def proj_N_block(wb, col0, dst, func):
            for t in range(NT):
                p = next_ps()
                stg = stN[st["ni"] % 6]
                st["ni"] += 1
                for kc in range(8):
                    k.mm(p[:, :], hT[:, kc, t * 128:(t + 1) * 128], wb[:, kc, :],
                         start=(kc == 0), stop=(kc == 7), r=[wb, hT], w=[p])
                evac(stg[:], p[:, :], func, r=[p], w=[stg])
                k.dma("sp", dst.ap()[t * 128:(t + 1) * 128, :], stg[:], r=[stg], w=[k.dbuf[dst.name]])

        def proj_if(wb, col0=3584):
            for tb in range(NB):
                p = next_ps()
                for kc in range(8):
                    k.mm(p[0:8, :], wb[:, kc, 0:8], hT[:, kc, tb * 512:(tb + 1) * 512],
                         start=(kc == 0), stop=(kc == 7), r=[wb, hT], w=[p])
                k.act(mif[:, tb * 512:(tb + 1) * 512], p[0:8, :], AF.Copy, r=[p], w=[mif])

        k.memset(cbuf[:, 0:8], 0.0, w=[cbuf])
        k.dma("sp", crow[0:4, :], conv_w.ap()[l], w=[crow])
        k.dma("sp", crow[4:5, :], conv_b.ap()[l:l + 1, :], w=[crow])
        rows_to_cols(crow, 5, 8, cw[:, :, :], cw)
        specs = [(proj_if, 3584, 8, ()),
                 (proj_T_block, 0, 512, (suT, 0, None)),
                 (proj_T_block, 512, 512, (szT, 0, AF.Silu)),
                 (proj_T_block, 1024, 512, (mqT, 0, None, 1.0, 0)),
                 (proj_T_block, 1536, 512, (mkT, 0, None, 1.0, 1)),
                 (proj_N_block, 2048, 512, (mv, None)),
                 (proj_N_block, 2560, 512, (mo, AF.Sigmoid)),
                 (proj_N_block, 3072, 512, (mz, AF.Silu)),
                 (proj_T_block, 3592, 512, (aqT, 0, None, 0.125)),
                 (proj_T_block, 4104, 512, (akT, 0, None)),
                 (proj_N_block, 4616, 512, (av, None)),
                 (proj_T_block, 5128, 512, (azT, 0, AF.Silu))]
        for gb in range(6):
            specs.append((proj_T_block, 5640 + gb * 512, 512, (gT, gb * 512, AF.Sigmoid)))
        wbl = [None] * len(specs)
        wbl[0] = load_w(specs[0][1], specs[0][2])
        for i, (fn, col0, ncols, args) in enumerate(specs):
            if i + 1 < len(specs):
                wbl[i + 1] = load_w(specs[i + 1][1], specs[i + 1][2])
            fn(wbl[i], col0, *args)
            if i == 0:
                k.dma("sp", mifd.ap(), mif[:], r=[mif], w=[k.dbuf["mif"]])

    def sincos(ph_ap, shape, sin_out, cos_out, u, r, wT):
        ti = u["ti"]
        tf = u["tf"]
        uu = u["u"]
        for which, out_ap in (("sin", sin_out), ("cos", cos_out)):
            if which == "sin":
                k.ts(ti[:], ph_ap, 1.0 / (2.0 * PI), None, ALU.mult, r=r, w=[ti])
            else:
                k.ts(ti[:], ph_ap, 1.0 / (2.0 * PI), 0.25, ALU.mult, ALU.add, r=r, w=[ti])
            k.copy(tf[:], ti[:], r=[ti], w=[tf])
            k.stt(uu[:], tf[:], -2.0 * PI, ph_ap, ALU.mult, ALU.add, r=[tf] + list(r), w=[uu])
            if which == "sin":
                k.ts(uu[:], uu[:], -PI + 1e-5, PI - 1e-5, ALU.max, ALU.min, r=[uu], w=[uu])
                k.act(out_ap, uu[:], AF.Sin, r=[uu], w=wT)
            else:
                k.ts(uu[:], uu[:], -1.5 * PI + 1e-5, 0.5 * PI - 1e-5, ALU.max, ALU.min, r=[uu], w=[uu])
                k.act(out_ap, uu[:], AF.Sin, r=[uu, consts], w=wT, bias=consts[:, 4:5])

    def phase_s5(l):
        pre_ = {}
        for nm_, shp_, dt_ in (("mag", [128, 16], F32), ("th", [128, 16], F32), ("BTr", [128, 16, 128], BF16),
                               ("BTi", [128, 16, 128], BF16), ("CTr", [128, 16, 128], BF16),
                               ("CTi", [128, 16, 128], BF16), ("CTn", [128, 16, 128], BF16), ("Dg", [128, 4, 128], BF16), ("dcol", [128, 4, 2], F32),
                               ("wg", [128, 4, 512], BF16), ("cosT", [128, 16, 512], F32),
                               ("sinT", [128, 16, 512], F32)):
            pre_[nm_] = k.sb(nm_, shp_, dt_)
        mark_s5 = k.sb_off
        f = lambda name, shp, dt=F32: pre_[name] if name in pre_ else k.sb(name, shp, dt)
        lre, lim, lst = f("lre", [128, 16]), f("lim", [128, 16]), f("lst", [128, 16])
        mag, th, are, aim = f("mag", [128, 16]), f("th", [128, 16]), f("are", [128, 16]), f("aim", [128, 16])
        den, rre, rim, t16 = (f("den", [128, 16]), f("rre", [128, 16]),
                              f("rim", [128, 16]), f("t16", [128, 16]))
        u16 = {"ti": f("u16i", [128, 16], I32), "tf": f("u16f", [128, 16]), "u": f("u16u", [128, 16])}
        q = "sp"
        k.dma(q, lre[:], P_["ssm_lam_re"].ap()[l].rearrange("(s g) p -> (g p) s", g=2), w=[lre],
              allow_slow_non_contiguous=True)
        k.dma(q, lim[:], P_["ssm_lam_im"].ap()[l].rearrange("(s g) p -> (g p) s", g=2), w=[lim],
              allow_slow_non_contiguous=True)
        for h in range(2):
            k.dma(q, lst[h * 64:(h + 1) * 64, :],
                  P_["ssm_log_step"].ap()[l].rearrange("(s g) -> g s", g=2)[h:h + 1, :].partition_broadcast(64),
                  w=[lst], allow_slow_non_contiguous=True)
        k.act(lst[:], lst[:], AF.Exp, r=[lst], w=[lst])
        k.tt(mag[:], lre[:], lst[:], ALU.mult, r=[lre, lst], w=[mag])
        k.act(mag[:], mag[:], AF.Exp, r=[mag], w=[mag])
        k.tt(th[:], lim[:], lst[:], ALU.mult, r=[lim, lst], w=[th])
        sincos(th[:], [128, 16], aim[:], are[:], u16, [th], [aim, are])
        k.tt(are[:], are[:], mag[:], ALU.mult, r=[are, mag], w=[are])
        k.tt(aim[:], aim[:], mag[:], ALU.mult, r=[aim, mag], w=[aim])
        k.tt(den[:], lre[:], lre[:], ALU.mult, r=[lre], w=[den])
        k.tt(t16[:], lim[:], lim[:], ALU.mult, r=[lim], w=[t16])
        k.tt(den[:], den[:], t16[:], ALU.add, r=[den, t16], w=[den])
        k.recip(den[:], den[:], r=[den], w=[den])
        nre = f("nre", [128, 16])
        k.ts(nre[:], are[:], -1.0, None, ALU.add, r=[are], w=[nre])
        k.tt(rre[:], nre[:], lre[:], ALU.mult, r=[nre, lre], w=[rre])
        k.tt(t16[:], aim[:], lim[:], ALU.mult, r=[aim, lim], w=[t16])
        k.tt(rre[:], rre[:], t16[:], ALU.add, r=[rre, t16], w=[rre])
        k.tt(rre[:], rre[:], den[:], ALU.mult, r=[rre, den], w=[rre])
        k.tt(rim[:], aim[:], lre[:], ALU.mult, r=[aim, lre], w=[rim])
        k.tt(t16[:], nre[:], lim[:], ALU.mult, r=[nre, lim], w=[t16])
        k.tt(rim[:], rim[:], t16[:], ALU.subtract, r=[rim, t16], w=[rim])
        k.tt(rim[:], rim[:], den[:], ALU.mult, r=[rim, den], w=[rim])
        bre, bim = f("bre", [128, 16, 16]), f("bim", [128, 16, 16])
        bbr, bbi, tb_ = f("bbr", [128, 16, 16]), f("bbi", [128, 16, 16]), f("tb_", [128, 16, 16])
        k.dma(q, bre[:], P_["ssm_b_re"].ap()[l].rearrange("(s g) p c -> (g p) s c", g=2), w=[bre])
        k.dma(q, bim[:], P_["ssm_b_im"].ap()[l].rearrange("(s g) p c -> (g p) s c", g=2), w=[bim])
        rre_b = rre[:, :].unsqueeze(2).to_broadcast([128, 16, 16])
        rim_b = rim[:, :].unsqueeze(2).to_broadcast([128, 16, 16])
        k.tt(bbr[:], bre[:], rre_b, ALU.mult, r=[bre, rre], w=[bbr])
        k.tt(tb_[:], bim[:], rim_b, ALU.mult, r=[bim, rim], w=[tb_])
        k.tt(bbr[:], bbr[:], tb_[:], ALU.subtract, r=[bbr, tb_], w=[bbr])
        k.tt(bbi[:], bim[:], rre_b, ALU.mult, r=[bim, rre], w=[bbi])
        k.tt(tb_[:], bre[:], rim_b, ALU.mult, r=[bre, rim], w=[tb_])
        k.tt(bbi[:], bbi[:], tb_[:], ALU.add, r=[bbi, tb_], w=[bbi])
        BTr, BTi = f("BTr", [128, 16, 128], BF16), f("BTi", [128, 16, 128], BF16)
        fulls = [f("fulls%d" % i, [128, 16, 128], BF16) for i in range(2)]
        nfl = [0]

        def transpose_all(fa, dstT):
            for g4 in range(4):
                p = next_ps()
                pv = p.t.bitcast(BF16)
                for i4 in range(4):
                    k.tr(pv[:, i4 * 128:(i4 + 1) * 128], fa[:, g4 * 4 + i4, :], ident_b[:], r=[fa, ident_b], w=[p],
                         inc=(i4 == 3))
                k.act(dstT[:, g4 * 4:g4 * 4 + 4, :], pv[:, 0:512].rearrange("p (s c) -> p s c", s=4), AF.Copy,
                      r=[p], w=[dstT])

        for src, dstT in ((bbr, BTr), (bbi, BTi)):
            fa = fulls[nfl[0] % 2]
            nfl[0] += 1
            k.memset(fa[:], 0.0, w=[fa], eng="dve")
            for m in range(4):
                for h in range(2):
                    c0 = 32 * m + 16 * h
                    k.ts(fa[:, m::4, c0:c0 + 16], src[:, m::4, :], hmask[:, h:h + 1], None, ALU.mult,
                         r=[src, hmask, fa], w=[fa])
            transpose_all(fa, dstT)
        cnr, cni = f("cnr", [128, 4, 64]), f("cni", [128, 4, 64])
        k.dma(q, cnr[:], P_["ssm_c_re"].ap()[l].rearrange("(c4 gg) c p -> (gg c) c4 p", c4=4), w=[cnr])
        k.dma(q, cni[:], P_["ssm_c_im"].ap()[l].rearrange("(c4 gg) c p -> (gg c) c4 p", c4=4), w=[cni])
        CTr, CTi, CTn = f("CTr", [128, 16, 128], BF16), f("CTi", [128, 16, 128], BF16), f("CTn", [128, 16, 128], BF16)
        for src, dstT, sgn in ((cnr, CTr, 1.0), (cni, CTi, -1.0), (cnr, CTn, -1.0)):
            fa = fulls[nfl[0] % 2]
            nfl[0] += 1
            for m in range(4):
                for h in range(2):
                    j = m * 2 + h
                    k.ts(fa[:, m::4, 64 * h:64 * h + 64], src[:, 0:4, :], chmask[:, j:j + 1], sgn, ALU.mult, ALU.mult,
                         r=[src, chmask, fa], w=[fa])
            transpose_all(fa, dstT)
        drow = f("drow", [2, W])
        dcol = f("dcol", [128, 4, 2])
        k.dma(q, drow[0:1, :], P_["ssm_d"].ap()[l:l + 1, :], w=[drow])
        k.dma(q, drow[1:2, :], P_["ssm_b_glu"].ap()[l:l + 1, :], w=[drow])
        rows_to_cols(drow, 2, 4, dcol[:, :, :], dcol)
        Dg = f("Dg", [128, 4, 128], BF16)
        for c in range(4):
            k.ts(Dg[:, c, :], ident_f[:, :], dcol[:, c, 0:1], None, ALU.mult, r=[ident_f, dcol], w=[Dg])
        wg = f("wg", [128, 4, 512], BF16)
        wgs = f("wgs", [128, 4, 512])
        k.dma(q, wgs[:], P_["ssm_w_glu"].ap()[l].rearrange("(kc p) c -> p kc c", p=128), w=[wgs])
        k.copy(wg[:], wgs[:], r=[wgs], w=[wg], eng="pool")
        cosT, sinT = f("cosT", [128, 16, 512]), f("sinT", [128, 16, 512])
        iota1 = f("iota1", [128, 512])
        k.dma(q, iota1[:], C_["iota1"].ap(), w=[iota1])
        ph = [f("ph%d" % i, [128, 512]) for i in range(2)]
        us = [{"ti": f("usi%d" % i, [128, 512], I32), "tf": f("usf%d" % i, [128, 512]),
               "u": f("usu%d" % i, [128, 512])} for i in range(1)] * 2
        for s in range(16):
            ph_, u_ = ph[s % 2], us[s % 2]
            k.ts(ph_[:], iota1[:], th[:, s:s + 1], None, ALU.mult, r=[iota1, th], w=[ph_])
            sincos(ph_[:], [128, 512], sinT[:, s, :], cosT[:, s, :], u_, [ph_], [sinT, cosT])
        k.S.barrier()
        k.sb_off = mark_s5
        ccol, scol, nscol = f("ccol", [128, 16]), f("scol", [128, 16]), f("nscol", [128, 16])
        k.copy(ccol[:], cosT[:, :, 511], r=[cosT], w=[ccol])
        k.copy(scol[:], sinT[:, :, 511], r=[sinT], w=[scol])
        k.ts(nscol[:], sinT[:, :, 511], -1.0, None, ALU.mult, r=[sinT], w=[nscol])
        ctmp = [f("ctmp%d" % i, [128, 2]) for i in range(2)]
        cr, ci = f("cr", [128, 16]), f("ci", [128, 16])
        k.memset(cr[:], 0.0, w=[cr])
        k.memset(ci[:], 0.0, w=[ci])
        k.S.barrier()
        ub = [f("ub%d" % i, [128, 4, 512], BF16) for i in range(2)]
        zb = [f("zb%d" % i, [128, 4, 512], BF16) for i in range(2)]
        wk = {n: [f("%s%d" % (n, i), [128, 512]) for i in range(2)] for n in
              ("t1", "t2", "t3", "t4", "wre", "wim", "zre", "zim")}
        for a_ in ("d1", "d2", "d3", "d4"):
            wk[a_] = [f("%s%d" % (a_, i), [128, 512], BF16) for i in range(2)]
        gf = f("gf", [128, 4, 512])
        gb = f("gb", [128, 4, 512], BF16)
        y2 = [f("y2%d" % i, [128, 512]) for i in range(2)]
        sg = [f("sg%d" % i, [128, 512]) for i in range(2)]
        yo = [f("yo%d" % i, [128, 4, 512], BF16) for i in range(2)]
        crb = [Buf("cr%d" % s_) for s_ in range(16)]
        cib = [Buf("ci%d" % s_) for s_ in range(16)]

        def load_blk(j):
            k.dma("sp", ub[j % 2][:], suT.ap()[:, j * 512:(j + 1) * 512].rearrange("(c p) t -> p c t", p=128),
                  r=[k.dbuf["suT"]], w=[ub[j % 2]])
            k.dma("sp", zb[j % 2][:], szT.ap()[:, j * 512:(j + 1) * 512].rearrange("(c p) t -> p c t", p=128),
                  r=[k.dbuf["szT"]], w=[zb[j % 2]])

        iters = [(j, c4, s_) for j in range(NB) for c4 in range(4) for s_ in range(4 * c4, 4 * c4 + 4)]

        def stA(i):
            j, c4, s_ = iters[i]
            if c4 == 0 and s_ == 1 and j + 1 < NB:
                load_blk(j + 1)
            ub_ = ub[j % 2]
            W_ = {n: v[i % 2] for n, v in wk.items()}
            pa, pb = psb[2 + (i * 2) % 6], psb[2 + (i * 2 + 1) % 6]
            k.mm(pa[:, :], BTr[:, s_, :], ub_[:, c4, :], True, True, r=[BTr, ub_], w=[pa])
            k.mm(pb[:, :], BTi[:, s_, :], ub_[:, c4, :], True, True, r=[BTi, ub_], w=[pb])
            cs, sn = cosT[:, s_, :], sinT[:, s_, :]
            k.tt(W_["t1"][:], pa[:, :], cs, ALU.mult, r=[pa, cosT], w=[W_["t1"]])
            k.tt(W_["t2"][:], pb[:, :], sn, ALU.mult, r=[pb, sinT], w=[W_["t2"]])
            k.tt(W_["wre"][:], W_["t1"][:], W_["t2"][:], ALU.add, r=[W_["t1"], W_["t2"]], w=[W_["wre"]])
            k.tt(W_["t3"][:], pb[:, :], cs, ALU.mult, r=[pb, cosT], w=[W_["t3"]])
            k.tt(W_["t4"][:], pa[:, :], sn, ALU.mult, r=[pa, sinT], w=[W_["t4"]])
            k.tt(W_["wim"][:], W_["t3"][:], W_["t4"][:], ALU.subtract, r=[W_["t3"], W_["t4"]], w=[W_["wim"]])
            magb = mag[:, s_:s_ + 1].to_broadcast([128, 512])
            k.scan(W_["zre"][:], magb, W_["wre"][:], cr[:, s_:s_ + 1], ALU.mult, ALU.add,
                   r=[mag, W_["wre"], crb[s_]], w=[W_["zre"]])
            k.scan(W_["zim"][:], magb, W_["wim"][:], ci[:, s_:s_ + 1], ALU.mult, ALU.add,
                   r=[mag, W_["wim"], cib[s_]], w=[W_["zim"]])

        def stB(i):
            j, c4, s_ = iters[i]
            W_ = {n: v[i % 2] for n, v in wk.items()}
            cs, sn = cosT[:, s_, :], sinT[:, s_, :]
            k.tt(W_["d1"][:], W_["zre"][:], cs, ALU.mult, r=[W_["zre"], cosT], w=[W_["d1"]], eng="pool")
            k.tt(W_["d2"][:], W_["zim"][:], sn, ALU.mult, r=[W_["zim"], sinT], w=[W_["d2"]], eng="pool")
            k.tt(W_["d3"][:], W_["zre"][:], sn, ALU.mult, r=[W_["zre"], sinT], w=[W_["d3"]], eng="pool")
            k.tt(W_["d4"][:], W_["zim"][:], cs, ALU.mult, r=[W_["zim"], cosT], w=[W_["d4"]], eng="pool")

        def stC(i):
            j, c4, s_ = iters[i]
            ub_, zb_, yo_ = ub[j % 2], zb[j % 2], yo[j % 2]
            W_ = {n: v[i % 2] for n, v in wk.items()}
            yps = psb[c4 % 2]
            ct = ctmp[i % 2]
            k.act(ct[:, 0:1], W_["zim"][:, 511:512], AF.Copy, r=[W_["zim"], nscol], w=[ct], scale=nscol[:, s_:s_ + 1])
            k.act(cr[:, s_:s_ + 1], W_["zre"][:, 511:512], AF.Identity, r=[W_["zre"], ccol, ct], w=[crb[s_]],
                  scale=ccol[:, s_:s_ + 1], bias=ct[:, 0:1])
            k.act(ct[:, 1:2], W_["zim"][:, 511:512], AF.Copy, r=[W_["zim"], ccol], w=[ct], scale=ccol[:, s_:s_ + 1])
            k.act(ci[:, s_:s_ + 1], W_["zre"][:, 511:512], AF.Identity, r=[W_["zre"], scol, ct], w=[cib[s_]],
                  scale=scol[:, s_:s_ + 1], bias=ct[:, 1:2])
            k.mm(yps[:, :], CTr[:, s_, :], W_["d1"][:], start=(s_ == 4 * c4), stop=False, r=[CTr, W_["d1"]], w=[yps],
                 inc=True)
            k.mm(yps[:, :], CTn[:, s_, :], W_["d2"][:], start=False, stop=False, r=[CTn, W_["d2"]], w=[yps], inc=True)
            k.mm(yps[:, :], CTi[:, s_, :], W_["d3"][:], start=False, stop=False, r=[CTi, W_["d3"]], w=[yps], inc=True)
            k.mm(yps[:, :], CTi[:, s_, :], W_["d4"][:], start=False, stop=False, r=[CTi, W_["d4"]], w=[yps], inc=True)
            if s_ != 4 * c4 + 3:
                return
            k.mm(yps[:, :], Dg[:, c4, :], ub_[:, c4, :], start=False, stop=True, r=[Dg, ub_], w=[yps])
            y2_, sg_ = y2[c4 % 2], sg[c4 % 2]
            k.act(y2_[:], yps[:, :], AF.Square, r=[yps], w=[y2_])
            k.ts(y2_[:], y2_[:], 0.044715, 1.0, ALU.mult, ALU.add, r=[y2_], w=[y2_])
            k.tt(y2_[:], y2_[:], yps[:, :], ALU.mult, r=[y2_, yps], w=[y2_])
            k.act(sg_[:], y2_[:], AF.Sigmoid, r=[y2_], w=[sg_], scale=2.0 * math.sqrt(2.0 / PI))
            k.tt(gf[:, c4, :], sg_[:], yps[:, :], ALU.mult, r=[sg_, yps], w=[gf])
            k.act(gb[:, c4, :], gf[:, c4, :], AF.Copy, r=[gf], w=[gb])
            if c4 != 3:
                return
            for jc in range(4):
                pg = psb[2 + jc]
                for ic in range(4):
                    k.mm(pg[:, :], wg[:, ic, jc * 128:(jc + 1) * 128], gb[:, ic, :], start=(ic == 0), stop=(ic == 3),
                         r=[wg, gb], w=[pg])
                sg_ = sg[jc % 2]
                k.act(sg_[:], pg[:, :], AF.Sigmoid, r=[pg, dcol], w=[sg_], bias=dcol[:, jc, 1:2])
                k.tt(sg_[:], sg_[:], gf[:, jc, :], ALU.mult, r=[sg_, gf], w=[sg_])
                k.tt(yo_[:, jc, :], sg_[:], zb_[:, jc, :], ALU.mult, r=[sg_, zb_], w=[yo_])
            k.dma("sp", ysT.ap()[:, j * 512:(j + 1) * 512].rearrange("(c p) t -> p c t", p=128), yo_[:],
                  r=[yo_], w=[k.dbuf["ysT"]])

        load_blk(0)
        stA(0)
        for i in range(len(iters)):
            stB(i)
            if i + 1 < len(iters):
                stA(i + 1)
            stC(i)

    def phase_gates(l, pre):
        f = lambda name, shp, dt=F32: k.sb(name, shp, dt)
        NC = NT
        qT = f("mq", [128, 4, L], BF16)
        kTt = f("mk", [128, 4, L], BF16)
        Vp = f("mVp", [128, NT, 4, 130], BF16)
        pre.update(qT=qT, kTt=kTt, Vp=Vp, mark=k.sb_off)
        mif = f("mif_g", [8, L])
        k.dma("sp", mif[:], mifd.ap(), r=[k.dbuf["mif"]], w=[mif])
        ifb = f("ifb", [8, 1])
        k.dma("sp", ifb[:], P_["if_bias"].ap()[l].rearrange("a (h o) -> (a h) o", o=1), w=[ifb],
              allow_slow_non_contiguous=True)
        k.dma("sp", qT[:], mqT.ap().rearrange("(c p) t -> p c t", p=128), r=[k.dbuf["mqT"]], w=[qT])
        k.dma("sp", kTt[:], mkT.ap().rearrange("(c p) t -> p c t", p=128), r=[k.dbuf["mkT"]], w=[kTt])
        for n in range(NT):
            k.dma("sp", Vp[:, n, :, 0:128], mv.ap()[n * 128:(n + 1) * 128, :].rearrange("p (h d) -> p h d", h=4),
                  r=[k.dbuf["mv"]], w=[Vp])
        pre = f("pre", [8, L])
        lsg = f("lsg", [8, L])
        tmp8 = f("tmp8", [8, L])
        k.ts(pre[:], mif[:], ifb[:, 0:1], None, ALU.add, r=[mif, ifb], w=[pre])
        k.act(tmp8[:], pre[:], AF.Abs, r=[pre], w=[tmp8])
        k.act(tmp8[:], tmp8[:], AF.Exp, r=[tmp8], w=[tmp8], scale=-1.0)
        k.act(tmp8[:], tmp8[:], AF.Ln, r=[tmp8, consts], w=[tmp8], bias=consts[0:8, 3:4])
        k.ts(lsg[:], pre[:], 0.0, None, ALU.min, r=[pre], w=[lsg])
        k.tt(lsg[:], lsg[:], tmp8[:], ALU.subtract, r=[lsg, tmp8], w=[lsg])
        cs8 = tmp8
        k.scan(cs8[:], consts[0:8, 3:4].to_broadcast([8, L]), lsg[:], 0.0, ALU.mult, ALU.add, r=[consts, lsg],
               w=[cs8])
        R8 = lsg
        k.ts(R8[:], cs8[:], rowmask8[:, 1:2], None, ALU.mult, r=[cs8, rowmask8], w=[R8])
        k.stt(R8[:], pre[:], rowmask8[:, 0:1], R8[:], ALU.mult, ALU.add, r=[pre, rowmask8, R8], w=[R8])
        a4, B4 = f("a4", [4, L]), f("B4", [4, L])
        for tb in range(NB):
            p = next_ps()
            k.mm(p[0:4, :], sel8[:, 0:4], R8[:, tb * 512:(tb + 1) * 512], True, True, r=[sel8, R8], w=[p])
            k.copy(a4[:, tb * 512:(tb + 1) * 512], p[0:4, :], r=[p], w=[a4])
            p = next_ps()
            k.mm(p[0:4, :], sel8[:, 4:8], R8[:, tb * 512:(tb + 1) * 512], True, True, r=[sel8, R8], w=[p])
            k.copy(B4[:, tb * 512:(tb + 1) * 512], p[0:4, :], r=[p], w=[B4])
        cm, Mx, Mp, dd = f("cm", [4, NC]), f("Mx", [4, NC]), f("Mp", [4, NC]), f("dd", [4, NC])
        k.reduce(cm[:], a4[:, :].rearrange("p (n t) -> p n t", t=128), ALU.max, AX.X, r=[a4], w=[cm])
        k.scan(Mx[:], cm[:], cm[:], 0.0, ALU.max, ALU.max, r=[cm], w=[Mx])
        k.memset(Mp[:, 0:1], 0.0, w=[Mp])
        if NC > 1:
            k.copy(Mp[:, 1:NC], Mx[:, 0:NC - 1], r=[Mx], w=[Mp])
        k.tt(dd[:], Mp[:], Mx[:], ALU.subtract, r=[Mp, Mx], w=[dd])
        k.act(dd[:], dd[:], AF.Exp, r=[dd], w=[dd])
        Mb = Mx[:, :].unsqueeze(2).to_broadcast([4, NC, 128])
        wrow, frow = a4, B4
        k.tt(wrow[:, :].rearrange("p (n t) -> p n t", t=128), a4[:, :].rearrange("p (n t) -> p n t", t=128), Mb,
             ALU.subtract, r=[a4, Mx], w=[wrow])
        k.act(wrow[:], wrow[:], AF.Exp, r=[wrow], w=[wrow])
        k.tt(frow[:, :].rearrange("p (n t) -> p n t", t=128), B4[:, :].rearrange("p (n t) -> p n t", t=128), Mb,
             ALU.add, r=[B4, Mx], w=[frow])
        k.act(frow[:], frow[:], AF.Exp, r=[frow], w=[frow], scale=-1.0)
        for n0 in range(0, NT, 32):
            nn = min(32, NT - n0)
            p = next_ps()
            for n in range(n0, n0 + nn):
                c0 = (n - n0) * 8
                k.mm(p[:, c0:c0 + 4], wrow[0:4, n * 128:(n + 1) * 128], ident_f[0:4, 0:4], True, True,
                     r=[wrow, ident_f], w=[p], inc=False)
                k.mm(p[:, c0 + 4:c0 + 8], frow[0:4, n * 128:(n + 1) * 128], ident_f[0:4, 0:4], True, True,
                     r=[frow, ident_f], w=[p], inc=(n == n0 + nn - 1))
            k.copy(wf[:, n0:n0 + nn, :], p[:, 0:nn * 8].rearrange("p (n e) -> p n e", e=8), r=[p], w=[wf])
        for h in range(4):
            p = next_ps()
            k.mm(p[:, 0:NC], selrow4[0:4, h * 128:(h + 1) * 128], dd[0:4, :], True, True, r=[selrow4, dd], w=[p])
            k.copy(dbc[:, h, :], p[:, 0:NC], r=[p], w=[dbc])

    def phase_mlstm(l, pre):
        f = lambda name, shp, dt=F32: k.sb(name, shp, dt)
        NC = NT
        SC = 128.0 ** -0.5
        qT, kTt, Vp = pre["qT"], pre["kTt"], pre["Vp"]
        k.memset(Vp[:, :, :, 128:129], 1.0, w=[Vp], eng="dve")
        k.memset(Vp[:, :, :, 129:130], 0.0, w=[Vp], eng="dve")
        for n in range(NT):
            k.tt(Vp[:, n, :, :], Vp[:, n, :, :], wf[:, n, 0:4].unsqueeze(2).to_broadcast([128, 4, 130]), ALU.mult,
                 r=[Vp, wf], w=[Vp])
        gM = f("gM", [128, W])
        k.dma("sp", gM[:], P_["m_norm_g"].ap()[l:l + 1, :].partition_broadcast(128), w=[gM])
        Cst4 = f("Cst4", [128, 4, 130])
        Tm4 = f("Tm4", [128, 4, 130])
        Cs4 = [f("Cs4_%d" % i, [128, 4, 130], BF16) for i in range(2)]
        k.memset(Cst4[:], 0.0, w=[Cst4], eng="dve")
        Sm4 = [f("Sm4_%d" % i, [128, 4, 128], BF16) for i in range(2)]
        kN4 = [f("kN4_%d" % i, [128, 4, 128], BF16) for i in range(2)]
        puS4 = [f("puS4_%d" % i, [128, 4, 129]) for i in range(2)]
        d4 = [f("d4_%d" % i, [128, 8]) for i in range(2)]
        hmt = [f("hmt%d" % i, [128, W]) for i in range(2)]
        mot = [f("mot%d" % i, [128, W], BF16) for i in range(2)]
        mzt = [f("mzt%d" % i, [128, W], BF16) for i in range(2)]
        ws = {"ssq": f("m_ssq", [128, 4]), "sqs": f("m_sqs", [128, 128]), "yb": f("m_yb", [128, W], BF16)}
        stage = [f("mstage%d" % i, [128, 4, 512], BF16) for i in range(2)]
        cz_b = causal01[:, :].unsqueeze(1).to_broadcast([128, 4, 128])

        def stL(n):
            i2 = n % 2
            tk = slice(n * 128, (n + 1) * 128)
            psS = next_ps()
            for h in range(4):
                k.mm(psS[:, h * 128:(h + 1) * 128], kTt[:, h, tk], qT[:, h, tk], True, True, r=[kTt, qT], w=[psS],
                     inc=(h == 3))
            k.tt(Sm4[i2][:], psS[:, :].rearrange("p (h t) -> p h t", h=4), cz_b, ALU.mult, r=[psS, causal01],
                 w=[Sm4[i2]])
            pk = next_ps()
            pkv = pk.t.bitcast(BF16)
            for h in range(4):
                k.tr(pkv[:, h * 128:(h + 1) * 128], kTt[:, h, tk], ident_b[:], r=[kTt, ident_b], w=[pk],
                     inc=(h == 3))
            k.act(kN4[i2][:], pkv[:, 0:512].rearrange("p (h t) -> p h t", h=4), AF.Copy, r=[pk], w=[kN4[i2]])
            for hp in range(2):
                pu = next_ps()
                for hh in range(2):
                    h = hp * 2 + hh
                    k.mm(pu[:, hh * 129:(hh + 1) * 129], kN4[i2][:, h, :], Vp[:, n, h, 0:129], True, True,
                         r=[kN4[i2], Vp], w=[pu], inc=(hh == 1))
                k.act(puS4[i2][:, hp * 2:hp * 2 + 2, :], pu[:, 0:258].rearrange("p (h e) -> p h e", h=2), AF.Copy,
                      r=[pu], w=[puS4[i2]])

        def stR(n):
            i2 = n % 2
            tk = slice(n * 128, (n + 1) * 128)
            hm_ = hmt[n % 2]
            d_ = d4[i2]
            dsc = dbc[:, :, n:n + 1].to_broadcast([128, 4, 130])
            k.tt(Tm4[:], Cst4[:], dsc, ALU.mult, r=[Cst4, dbc], w=[Tm4])
            cs_ = Cs4[i2]
            if n > 0:
                k.act(cs_[:], Tm4[:], AF.Copy, r=[Tm4], w=[cs_])
            k.tt(Cst4[:, :, 0:129], Tm4[:, :, 0:129], puS4[i2][:], ALU.add, r=[Tm4, puS4[i2]], w=[Cst4])
            for hp in range(2):
                po = next_ps()
                for hh in range(2):
                    h = hp * 2 + hh
                    k.mm(po[:, hh * 129:(hh + 1) * 129], Sm4[i2][:, h, :], Vp[:, n, h, 0:129], True, (n == 0),
                         r=[Sm4[i2], Vp], w=[po], inc=(n == 0 and hh == 1))
                    if n > 0:
                        k.mm(po[:, hh * 129:(hh + 1) * 129], qT[:, h, tk], cs_[:, h, 0:129], False, True,
                             r=[qT, cs_], w=[po], inc=(hh == 1))
                pov = po[:, 0:258].rearrange("p (h e) -> p h e", h=2)
                k.act(d_[:, hp * 2:hp * 2 + 2], pov[:, :, 128], AF.Abs, r=[po], w=[d_], scale=SC)
                k.tt(d_[:, hp * 2:hp * 2 + 2], d_[:, hp * 2:hp * 2 + 2], wf[:, n, 4 + hp * 2:6 + hp * 2], ALU.max,
                     r=[d_, wf], w=[d_])
                k.recip(d_[:, 4 + hp * 2:6 + hp * 2], d_[:, hp * 2:hp * 2 + 2], r=[d_], w=[d_])
                k.ts(d_[:, 4 + hp * 2:6 + hp * 2], d_[:, 4 + hp * 2:6 + hp * 2], SC, None, ALU.mult, r=[d_], w=[d_])
                k.tt(hm_[:, hp * 256:(hp + 1) * 256].rearrange("p (h t) -> p h t", h=2), pov[:, :, 0:128],
                     d_[:, 4 + hp * 2:6 + hp * 2].unsqueeze(2).to_broadcast([128, 2, 128]), ALU.mult,
                     r=[po, d_], w=[hm_])

        stL(0)
        for n in range(NT):
            hm_, mo_, mz_ = hmt[n % 2], mot[n % 2], mzt[n % 2]
            k.dma("sp", mo_[:], mo.ap()[n * 128:(n + 1) * 128, :], r=[k.dbuf["mo"]], w=[mo_])
            k.dma("sp", mz_[:], mz.ap()[n * 128:(n + 1) * 128, :], r=[k.dbuf["mz"]], w=[mz_])
            if n + 1 < NT:
                stL(n + 1)
            stR(n)
            k.tt(hm_[:], hm_[:], mo_[:], ALU.mult, r=[hm_, mo_], w=[hm_])
            stg = stage[(n // 4) % 2]
            headnorm_out(hm_, gM, mz_, stg, (n % 4) * 128, ws)
            if n % 4 == 3:
                j = n // 4
                k.dma("sp", ymT.ap()[:, j * 512:(j + 1) * 512].rearrange("(c p) t -> p c t", p=128), stg[:],
                      r=[stg], w=[k.dbuf["ymT"]])

    def phase_attn(l):
        pre_ = {}
        for nm_, shp_, dt_ in (("aq", [128, 4, L], BF16), ("ak", [128, 4, L], BF16), ("aVp", [128, NT, 4, 130], BF16),
                               ("strip0", [128, 1024], F32), ("strip1", [128, 1024], F32),
                               ("strip2", [128, 1024], F32), ("strip3", [128, 1024], F32), ("b31bc", [128, 4], F32),
                               ("negMb", [128, 2, 4], F32), ("farb", [128, 2, 4], F32), ("neglam", [128, 1], F32),
                               ("gAcol", [128, 4, 1], F32), ("ones_b", [128, 128], BF16)):
            pre_[nm_] = k.sb(nm_, shp_, dt_)
        mark_a = k.sb_off
        f = lambda name, shp, dt=F32: pre_[name] if name in pre_ else k.sb(name, shp, dt)
        lam_init = 0.8 - 0.6 * math.exp(-0.3 * l)
        qT = f("aq", [128, 4, L], BF16)
        kTt = f("ak", [128, 4, L], BF16)
        Vp = f("aVp", [128, NT, 4, 130], BF16)
        k.dma("sp", qT[:], aqT.ap().rearrange("(c p) t -> p c t", p=128), r=[k.dbuf["aqT"]], w=[qT])
        k.dma("sp", kTt[:], akT.ap().rearrange("(c p) t -> p c t", p=128), r=[k.dbuf["akT"]], w=[kTt])
        vstg = f("vstg_a", [128, NT, W], BF16)
        for n0 in range(0, NT, 8):
            nn = min(8, NT - n0)
            k.dma("sp", vstg[:, n0:n0 + nn, :],
                  av.ap()[n0 * 128:(n0 + nn) * 128, :].rearrange("(n p) c -> p n c", p=128),
                  r=[k.dbuf["av"]], w=[vstg])
        k.copy(Vp[:, :, :, 0:128], vstg[:, :, :].rearrange("p n (h d) -> p n h d", h=4), r=[vstg], w=[Vp])
        k.memset(Vp[:, :, :, 128:129], 1.0, w=[Vp], eng="dve")
        k.memset(Vp[:, :, :, 129:130], 0.0, w=[Vp], eng="dve")
        strips = [f("strip%d" % h, [128, 1024]) for h in range(4)]
        b31bc = f("b31bc", [128, 4])
        antiI = f("antiI", [128, 128])
        k.dma("sp", antiI[:], C_["antiI"].ap(), w=[antiI])
        srev = [f("srev%d" % i, [128, 1024]) for i in range(2)]
        for h in range(4):
            sr = srev[h % 2]
            k.dma("sp", sr[:], bass.AP(Gd, h * 1152, [[1, 128], [1, 1024]]), r=[k.dbuf["Gd"]], w=[sr])
            for hf in range(2):
                p = next_ps()
                k.mm(p[:, :], antiI[:, :], sr[:, hf * 512:(hf + 1) * 512], True, True, r=[antiI, sr], w=[p])
                k.copy(strips[h][:, hf * 512:(hf + 1) * 512], p[:, :], r=[p], w=[strips[h]])
        k.dma("sp", b31bc[:], P_["rel_bias"].ap()[31:32, :].partition_broadcast(128), w=[b31bc])
        sqb = [f("sqb%d" % i, [128, 512], BF16) for i in range(2)]
        blkmax = f("blkmax", [2, NB])
        stat = f("stat", [2, 8])
        ii = 0
        for qi, src in enumerate((qT, kTt)):
            for h in range(4):
                for tb in range(NB):
                    s_ = sqb[ii % 2]
                    ii += 1
                    k.act(s_[:], src[:, h, tb * 512:(tb + 1) * 512], AF.Square, r=[src], w=[s_])
                    p = next_ps()
                    k.mm(p[0:2, :], blockones_b[:, 0:2], s_[:], True, True, r=[blockones_b, s_], w=[p])
                    k.reduce(blkmax[:, tb:tb + 1], p[0:2, :], ALU.max, AX.X, r=[p], w=[blkmax])
                k.reduce(stat[:, qi * 4 + h:qi * 4 + h + 1], blkmax[:, :], ALU.max, AX.X, r=[blkmax], w=[stat])
        M2 = f("M2", [2, 4])
        k.tt(M2[:], stat[:, 0:4], stat[:, 4:8], ALU.mult, r=[stat], w=[M2])
        k.act(M2[:], M2[:], AF.Sqrt, r=[M2], w=[M2])
        k.ts(M2[:], M2[:], -1.05, -2.0, ALU.mult, ALU.add, r=[M2], w=[M2])
        negMb = f("negMb", [128, 2, 4])
        farb = f("farb", [128, 2, 4])
        selc = f("selc", [2, 2, 128])
        k.memset(selc[:], 0.0, w=[selc])
        for c in range(2):
            k.ts(selc[:, c, :], ident_f[0:2, c:c + 1].to_broadcast([2, 128]), 1.0, None, ALU.mult, r=[ident_f, selc],
                 w=[selc])
        for c in range(2):
            p = next_ps()
            k.mm(p[:, 0:4], selc[0:2, c, :], M2[0:2, 0:4], True, True, r=[selc, M2], w=[p])
            k.copy(negMb[:, c, :], p[:, 0:4], r=[p], w=[negMb])
            k.tt(farb[:, c, :], negMb[:, c, :], b31bc[:, :], ALU.add, r=[negMb, b31bc], w=[farb])
        dl = f("dl", [1, 256])
        k.dma("sp", dl[:], P_["diff_lam"].ap()[l:l + 1].rearrange("o a d -> o (a d)"), w=[dl])
        pr = f("pr", [1, 128])
        s12 = f("s12", [1, 2])
        k.tt(pr[:, 0:64], dl[:, 0:64], dl[:, 64:128], ALU.mult, r=[dl], w=[pr])
        k.tt(pr[:, 64:128], dl[:, 128:192], dl[:, 192:256], ALU.mult, r=[dl], w=[pr])
        k.reduce(s12[:], pr[:, :].rearrange("p (a d) -> p a d", a=2), ALU.add, AX.X, r=[pr], w=[s12])
        k.act(s12[:], s12[:], AF.Exp, r=[s12], w=[s12])
        lamv = f("lamv", [1, 1])
        k.tt(lamv[:], s12[:, 1:2], s12[:, 0:1], ALU.subtract, r=[s12], w=[lamv])
        k.ts(lamv[:], lamv[:], -lam_init, None, ALU.add, r=[lamv], w=[lamv])
        neglam = f("neglam", [128, 1])
        p = next_ps()
        k.mm(p[:, 0:1], ones_row[0:1, :], lamv[0:1, 0:1], True, True, r=[ones_row, lamv], w=[p])
        k.copy(neglam[:], p[:, 0:1], r=[p], w=[neglam])
        garow = f("garow", [1, W])
        gAcol = f("gAcol", [128, 4, 1])
        k.dma("sp", garow[:], P_["diff_norm_g"].ap()[l:l + 1, :], w=[garow])
        rows_to_cols(garow, 1, 4, gAcol[:, :, :], gAcol)
        k.ts(gAcol[:], gAcol[:], 1.0 - lam_init, None, ALU.mult, r=[gAcol], w=[gAcol])
        ones_b = f("ones_b", [128, 128], BF16)
        k.memset(ones_b[:], 1.0, w=[ones_b], eng="dve")
        k.S.barrier()
        k.sb_off = mark_a
        Pt = [f("Pt%d" % i, [128, 512], BF16) for i in range(3)]
        tmpb = [f("tmpb%d" % i, [128, 512]) for i in range(2)]
        rsb = [f("rsb%d" % i, [128, 512]) for i in range(2)]
        o0 = [f("o0_%d" % i, [128, 512]) for i in range(2)]
        t1 = [f("at1_%d" % i, [128, 512]) for i in range(2)]
        haT = [f("haT%d" % i, [128, 512]) for i in range(2)]
        sqb_ = [f("asq%d" % i, [128, 512], BF16) for i in range(2)]
        rstd = [f("arstd%d" % i, [128, 512]) for i in range(2)]
        azt = [f("azt%d" % i, [128, 512], BF16) for i in range(2)]
        stage = [f("astage%d" % i, [128, 512], BF16) for i in range(2)]
        qz = [[f("qz%d_%d" % (i, c), [128, 512], BF16) for c in range(2)] for i in range(2)]
        for i in range(2):
            for c in range(2):
                k.memset(qz[i][c][:], 0.0, w=[qz[i][c]], eng="dve")
        it = 0
        gh = 0
        ic = 0
        pending = []
        my_ep = None
        for g in range(NB):
            for h in range(4):
                qz_ = qz[gh % 2]
                az_ = azt[gh % 2]
                ha_ = haT[gh % 2]
                k.dma("sp", az_[:], azT.ap()[h * 128:(h + 1) * 128, g * 512:(g + 1) * 512], r=[k.dbuf["azT"]],
                      w=[az_])
                for c in range(2):
                    k.copy(qz_[c][c * 64:(c + 1) * 64, :], qT[c * 64:(c + 1) * 64, h, g * 512:(g + 1) * 512],
                           r=[qT], w=[qz_[c]], eng="pool")
                for c in range(2):
                    nkb = 4 * g + 4
                    DEPTH_PF = 2
                    base = it
                    pss = {}
                    accV, accS = psb[(ic % 2) * 2], psb[(ic % 2) * 2 + 1]
                    ic += 1

                    def issue_qk(kb):
                        ps = psb[4 + (base + kb) % 4]
                        pss[kb] = ps
                        k.mm(ps[:, :], kTt[:, h, kb * 128:(kb + 1) * 128], qz_[c][:, :], True, True,
                             r=[kTt, qz_[c]], w=[ps])

                    for kb in range(min(DEPTH_PF, nkb)):
                        issue_qk(kb)
                    for kb in range(nkb):
                        it += 1
                        if kb + DEPTH_PF < nkb:
                            issue_qk(kb + DEPTH_PF)
                        if c == 0 and kb == min(3, nkb - 1) and pending and pending[0] is not my_ep:
                            pending.pop(0)()
                        ps = pss.pop(kb)
                        P_t = Pt[it % 3]
                        j = kb - (4 * g - 1)
                        c0 = 0
                        if j >= 0:
                            tb_ = tmpb[it % 2]
                            c0 = max(j - 1, 0) * 128
                            k.tt(tb_[:, c0:512], ps[:, c0:512], strips[h][:, (4 - j) * 128 + c0:(4 - j) * 128 + 512],
                                 ALU.add, r=[ps, strips[h]], w=[tb_])
                            k.act(P_t[:, c0:512], tb_[:, c0:512], AF.Exp, r=[tb_, negMb], w=[P_t],
                                  bias=negMb[:, c, h:h + 1])
                        else:
                            k.act(P_t[:], ps[:, :], AF.Exp, r=[ps, farb], w=[P_t], bias=farb[:, c, h:h + 1])
                        last = (kb == nkb - 1)
                        k.mm(accV[:, c0:512], Vp[:, kb, h, 0:128], P_t[:, c0:512], start=(kb == 0), stop=last,
                             r=[Vp, P_t], w=[accV], inc=last)
                        k.mm(accS[:, c0:512], ones_b[:, :], P_t[:, c0:512], start=(kb == 0), stop=last,
                             r=[ones_b, P_t], w=[accS], inc=True)
                    rs_ = rsb[ic % 2]
                    k.recip(rs_[:], accS[:, :], r=[accS], w=[rs_])
                    if c == 0:
                        o0_ = o0[gh % 2]
                        k.tt(o0_[:], accV[:, :], rs_[:], ALU.mult, r=[accV, rs_], w=[o0_])
                    else:
                        t_ = t1[gh % 2]
                        k.tt(t_[:], accV[:, :], rs_[:], ALU.mult, r=[accV, rs_], w=[t_])
                        k.stt(ha_[:], t_[:], neglam[:, 0:1], o0_[:], ALU.mult, ALU.add, r=[t_, neglam, o0_], w=[ha_])
                def epilogue(ha_=ha_, az_=az_, h=h, g=g, ghl=gh):
                    sq_ = sqb_[ghl % 2]
                    rd_ = rstd[ghl % 2]
                    stg = stage[ghl % 2]
                    k.act(sq_[:], ha_[:], AF.Square, r=[ha_], w=[sq_])
                    pn = psb[4 + (it + 2) % 4]
                    k.mm(pn[:, :], ones_b[:, :], sq_[:], True, True, r=[ones_b, sq_], w=[pn])
                    k.act(rd_[:], pn[:, :], AF.Ln, r=[pn, consts], w=[rd_], scale=1.0 / 128, bias=epsb)
                    k.act(rd_[:], rd_[:], AF.Exp, r=[rd_], w=[rd_], scale=-0.5)
                    k.tt(ha_[:], ha_[:], rd_[:], ALU.mult, r=[ha_, rd_], w=[ha_])
                    k.stt(stg[:], ha_[:], gAcol[:, h, 0:1], az_[:], ALU.mult, ALU.mult, r=[ha_, gAcol, az_], w=[stg])
                    k.dma("sp", yaT.ap()[h * 128:(h + 1) * 128, g * 512:(g + 1) * 512], stg[:], r=[stg],
                          w=[k.dbuf["yaT"]])

                pending.append(epilogue)
                gh += 1
        while pending:
            pending.pop(0)()

    def phase_merge(l):
        f = lambda name, shp, dt=F32: k.sb(name, shp, dt)
        last = (l == depth - 1)
        norm_alloc()
        wbr = f("wbr", [128, 3, 4, 1024], BF16)
        wo = f("wo", [128, 8, 1024], BF16)
        mark_m = k.sb_off
        wbs = [f("wbs%d" % i, [128, 4, 1024]) for i in range(2)]
        for b in range(3):
            s_ = wbs[b % 2]
            k.dma("sp", s_[:], P_["w_branch"].ap()[l, b].rearrange("(kc p) c -> p kc c", p=128), w=[s_])
            k.copy(wbr[:, b, :, :], s_[:], r=[s_], w=[wbr])
        for hf in range(2):
            s_ = wbs[(3 + hf) % 2]
            k.dma("sp", s_[:], P_["w_out"].ap()[l, hf * 512:(hf + 1) * 512, :].rearrange("(kc p) c -> p kc c", p=128),
                  w=[s_])
            k.copy(wo[:, hf * 4:(hf + 1) * 4, :], s_[:], r=[s_], w=[wo])
        if last:
            k.dma("sp", gbc[:], P_["final_g"].ap().rearrange("(o d) -> o d", o=1).partition_broadcast(128), w=[gbc])
        else:
            k.dma("sp", gbc[:], norm_g.ap()[l + 1:l + 2, :].partition_broadcast(128), w=[gbc])
        k.S.barrier()
        k.sb_off = mark_m
        yb_ = [[f("y%d_%d" % (b, i), [128, 4, 512], BF16) for i in range(2)] for b in range(3)]
        gt = [f("gt%d" % i, [128, 512], BF16) for i in range(6)]
        m1 = [f("m1_%d" % i, [128, 512]) for i in range(2)]
        m2 = [f("m2_%d" % i, [128, 512]) for i in range(2)]
        mT = [f("mT%d" % i, [128, 8, 512], BF16) for i in range(2)]
        xo = [f("xo%d" % i, [128, D_MODEL]) for i in range(3)]
        xn = [f("xn%d" % i, [128, D_MODEL]) for i in range(3)]
        x_src = x_in if l == 0 else xs_d
        srcs = (ysT, ymT, yaT)
        st_ = {"gi": 0}

        def load_y(j):
            for b in range(3):
                k.dma("sp", yb_[b][j % 2][:],
                      srcs[b].ap()[:, j * 512:(j + 1) * 512].rearrange("(c p) t -> p c t", p=128),
                      r=[k.dbuf[srcs[b].name]], w=[yb_[b][j % 2]])

        def branch(j):
            ys_ = [yb_[b][j % 2] for b in range(3)]
            mT_ = mT[j % 2]
            for dc in range(8):
                m1_, m2_ = m1[dc % 2], m2[dc % 2]
                for b in range(3):
                    g_ = gt[st_["gi"] % 6]
                    st_["gi"] += 1
                    r0 = b * 1024 + dc * 128
                    k.dma("sp", g_[:], gT.ap()[r0:r0 + 128, j * 512:(j + 1) * 512], r=[k.dbuf["gT"]], w=[g_])
                    p = next_ps()
                    for cc in range(4):
                        k.mm(p[:, :], wbr[:, b, cc, dc * 128:(dc + 1) * 128], ys_[b][:, cc, :], start=(cc == 0),
                             stop=(cc == 3), r=[wbr, ys_[b]], w=[p])
                    if b == 0:
                        k.tt(m1_[:], p[:, :], g_[:], ALU.mult, r=[p, g_], w=[m1_])
                    else:
                        k.tt(m2_[:], p[:, :], g_[:], ALU.mult, r=[p, g_], w=[m2_])
                        if b == 1:
                            k.tt(m1_[:], m1_[:], m2_[:], ALU.add, r=[m1_, m2_], w=[m1_])
                        else:
                            k.tt(mT_[:, dc, :], m1_[:], m2_[:], ALU.add, r=[m1_, m2_], w=[mT_])

        def outproj(j):
            mT_ = mT[j % 2]
            for t4 in range(4):
                n = j * 4 + t4
                xo_, xn_ = xo[n % 3], xn[n % 3]
                k.dma("sp", xo_[:], x_src.ap()[n * 128:(n + 1) * 128, :], r=[k.dbuf[x_src.name]], w=[xo_])
                for hf in range(2):
                    p = next_ps()
                    for dc in range(8):
                        k.mm(p[:, :], mT_[:, dc, t4 * 128:(t4 + 1) * 128], wo[:, dc, hf * 512:(hf + 1) * 512],
                             start=(dc == 0), stop=(dc == 7), r=[mT_, wo], w=[p])
                    k.tt(xn_[:, hf * 512:(hf + 1) * 512], p[:, :], xo_[:, hf * 512:(hf + 1) * 512], ALU.add,
                         r=[p, xo_], w=[xn_])
                if last:
                    r_, _ = norm_rstd(xn_)
                    k.stt(xo_[:], xn_[:], r_[:, 0:1], gbc[:], ALU.mult, ALU.mult, r=[xn_, r_, gbc], w=[xo_])
                    k.dma("pool", out_d.ap()[n * 128:(n + 1) * 128, :], xo_[:], r=[xo_], w=[k.dbuf["out"]])
                else:
                    k.dma("pool", xs_d.ap()[n * 128:(n + 1) * 128, :], xn_[:], r=[xn_], w=[k.dbuf["xs"]])
                    norm_transpose(xn_, n * 128)

        load_y(0)
        branch(0)
        for j in range(NB):
            if j + 1 < NB:
                load_y(j + 1)
                branch(j + 1)
            outproj(j)

    for l in range(depth):
        if "proj" in phases:
            phase_proj(l)
        k.phase_reset(keep_hT=False)
        if "mlstm" in phases:
            pre_m = {}
            phase_gates(l, pre_m)
            k.S.barrier()
            k.sb_off = pre_m["mark"]
            phase_mlstm(l, pre_m)
        k.phase_reset(keep_hT=False)
        if "s5" in phases:
            phase_s5(l)
        k.phase_reset(keep_hT=False)
        if "attn" in phases:
            phase_attn(l)
        k.phase_reset(keep_hT=True)
        if "merge" in phases:
            phase_merge(l)
        k.phase_reset(keep_hT=True)

    k.S.finish("sp")
    k.S.emit()
    return k


_CACHE = {}


def kernel(**inputs):
    L, depth = SEQ, DEPTH
    if "k" not in _CACHE:
        _CACHE["k"] = build(L, depth)
    kb = _CACHE["k"]
    consts = host_consts()
    x = np.ascontiguousarray(inputs["x"], dtype=np.float32)
    shared = {n: np.ascontiguousarray(inputs[n], dtype=np.float32) for n in PARAM_SHAPES(depth)}
    shared.update(consts)
    in_maps = []
    for c in range(NCORES):
        m = dict(shared)
        m["x"] = x[c]
        in_maps.append(m)
    res = run_bass_kernel_spmd(kb.nc, in_maps, core_ids=list(range(NCORES)))
    return np.stack([np.asarray(r["out"], dtype=np.float32) for r in res.results], axis=0)
```
